# Optimizing a Trainium2 kernel written in Bass

```python
import math
import jax
import jax.numpy as jnp
from jax import lax
import numpy as np

D_MODEL = 1024
BATCH = 4
SEQ = 8192
DEPTH = 2

CTX_LEN = 256
GRID_W = 64
EPS = 1e-6

DN_HEADS = 4
DN_HEAD_DIM = 128
DN_WIDTH = DN_HEADS * DN_HEAD_DIM
DN_CHUNK = 64
POOL_WINDOWS = (2, 4, 8, 16)
POOL_GROUPS = len(POOL_WINDOWS)
POOL_WIDTH = D_MODEL // 4
POOL_GROUP_DIM = POOL_WIDTH // POOL_GROUPS
SC_WIDTH = D_MODEL // 4
N_BRANCH = 3
D_FF = ((8 * D_MODEL + 3 * 256 - 1) // (3 * 256)) * 256

OFF_Z = 3 * DN_WIDTH
OFF_A = OFF_Z + DN_WIDTH
OFF_BETA = OFF_A + 2 * DN_HEADS
OFF_POOL = OFF_BETA + 2 * DN_HEADS
OFF_SC = OFF_POOL + POOL_WIDTH
OFF_GATE = OFF_SC + 3 * SC_WIDTH
N_IN = OFF_GATE + N_BRANCH * D_MODEL
IN_SPLITS = (OFF_Z, OFF_A, OFF_BETA, OFF_POOL, OFF_SC, OFF_GATE)

kernel_name = "hybrid_parallel_deltanet_pool_shortconv_dit"


def rmsnorm(x, g):
    xf = x.astype(jnp.float32)
    y = xf * lax.rsqrt(jnp.mean(xf * xf, axis=-1, keepdims=True) + EPS)
    return (y * g.astype(jnp.float32)).astype(x.dtype)


def l2norm(x):
    return x * lax.rsqrt(jnp.sum(x * x, axis=-1, keepdims=True) + EPS)


def conv3(x, w):
    xp = jnp.pad(x, ((0, 0), (1, 1), (0, 0)))
    return xp[:, :-2] * w[0] + xp[:, 1:-1] * w[1] + xp[:, 2:] * w[2]


def decay_gate(p_a, p_b, a_log, dt_bias):
    bn, l, _ = p_a.shape
    a = p_a.astype(jnp.float32).reshape(bn, l, 2, DN_HEADS)
    g = -jnp.exp(a_log.astype(jnp.float32)) * jax.nn.softplus(a + dt_bias.astype(jnp.float32))
    beta = jax.nn.sigmoid(p_b.astype(jnp.float32).reshape(bn, l, 2, DN_HEADS))
    return g, beta


def gated_delta_chunked(q, k, v, g, beta, s0):
    bn, l, h, _ = k.shape
    dv = v.shape[-1]
    n = l // DN_CHUNK

    def chunks(t):
        t = t.reshape(bn, n, DN_CHUNK, h, *t.shape[3:])
        return jnp.moveaxis(t, (1, 3), (0, 2))

    kc, vc, bc = chunks(k), chunks(v), chunks(beta)
    gc = jnp.cumsum(chunks(g), axis=-1)
    idx = jnp.arange(DN_CHUNK)
    incl = idx[:, None] >= idx[None, :]
    strict = idx[:, None] > idx[None, :]
    decay = jnp.exp(jnp.where(incl, gc[..., :, None] - gc[..., None, :], -jnp.inf))
    kb = kc * bc[..., None]
    a = jnp.einsum('nbhid,nbhjd->nbhij', kb, kc) * jnp.where(strict, decay, 0.0)
    eye = jnp.eye(DN_CHUNK, dtype=jnp.float32)
    rhs = jnp.concatenate([vc * bc[..., None], kb * jnp.exp(gc)[..., None]], axis=-1)
    sol = lax.linalg.triangular_solve(eye + a, rhs, left_side=True, lower=True, unit_diagonal=True)
    u, w = sol[..., :dv], sol[..., dv:]
    g_last = gc[..., -1]
    k_state = kc * jnp.exp(g_last[..., None] - gc)[..., None]
    with_output = q is not None
    xs = (u, w, k_state, g_last)
    if with_output:
        qc = chunks(q)
        q_dec = qc * jnp.exp(gc)[..., None]
        a_qk = jnp.einsum('nbhid,nbhjd->nbhij', qc, kc) * decay
        xs = xs + (q_dec, a_qk)

    def step(s, xs_i):
        u_i, w_i, k_i, gl_i = xs_i[:4]
        v_new = u_i - jnp.einsum('bhck,bhkv->bhcv', w_i, s)
        s_next = s * jnp.exp(gl_i)[..., None, None] + jnp.einsum('bhck,bhcv->bhkv', k_i, v_new)
        if with_output:
            qd_i, aqk_i = xs_i[4:]
            o_i = jnp.einsum('bhck,bhkv->bhcv', qd_i, s) + jnp.einsum('bhij,bhjv->bhiv', aqk_i, v_new)
            return s_next, o_i
        return s_next, None

    s_fin, o = lax.scan(step, s0, xs)
    if with_output:
        o = jnp.moveaxis(o, (0, 2), (1, 3)).reshape(bn, l, h, dv)
    return o, s_fin


def dn_bidir(q, k, v, g, beta, s0_f, s0_b):
    rev = lambda t: jnp.flip(t, axis=1)
    o_f, s_f = gated_delta_chunked(q, k, v, g[:, :, 0], beta[:, :, 0], s0_f)
    o_b, s_b = gated_delta_chunked(None if q is None else rev(q), rev(k), rev(v),
                                   rev(g[:, :, 1]), rev(beta[:, :, 1]), s0_b)
    o = None if q is None else o_f + rev(o_b)
    return o, s_f, s_b


def box_mean(x, w, axis):
    l = x.shape[axis]
    lo = w // 2
    hi = w - 1 - lo
    cs = jnp.cumsum(x, axis=axis)
    cs = jnp.concatenate([jnp.zeros_like(lax.slice_in_dim(cs, 0, 1, axis=axis)), cs], axis=axis)
    pos = jnp.arange(l)
    start = jnp.clip(pos - lo, 0, l)
    end = jnp.clip(pos + hi + 1, 0, l)
    total = jnp.take(cs, end, axis=axis) - jnp.take(cs, start, axis=axis)
    shape = [1] * x.ndim
    shape[axis] = l
    return total / (end - start).astype(x.dtype).reshape(shape)


def pool_mixer(u, pool_w, pool_scale, rows):
    bn, l, _ = u.shape
    uf = u.astype(jnp.float32)
    outs = []
    for gi, w in enumerate(POOL_WINDOWS):
        ug = uf[..., gi * POOL_GROUP_DIM:(gi + 1) * POOL_GROUP_DIM]
        if rows is None:
            m = box_mean(ug, w, 1)
        else:
            ug2 = ug.reshape(bn, rows, GRID_W, POOL_GROUP_DIM)
            m = box_mean(box_mean(ug2, w, 1), w, 2).reshape(bn, l, POOL_GROUP_DIM)
        outs.append(m - ug)
    d = jnp.stack(outs, axis=2).astype(u.dtype)
    y = jnp.einsum('blgc,gcd->blgd', d, pool_w).reshape(bn, l, POOL_WIDTH)
    return y * pool_scale


def shortconv_mixer(p, conv_w):
    xin, gate_b, gate_c = jnp.split(p, 3, axis=-1)
    return gate_b * conv3(gate_c * xin, conv_w)


def mixer(h, lp, s0_f, s0_b, rows):
    bn, l, _ = h.shape
    p = h @ lp['w_in']
    p_qkv, p_z, p_a, p_b, p_pool, p_sc, p_gate = jnp.split(p, IN_SPLITS, axis=-1)
    qkv = jax.nn.silu(conv3(p_qkv, lp['dn_conv_w'])).astype(jnp.float32)
    qkv = qkv.reshape(bn, l, 3, DN_HEADS, DN_HEAD_DIM)
    q = l2norm(qkv[:, :, 0]) * (DN_HEAD_DIM ** -0.5)
    k = l2norm(qkv[:, :, 1])
    v = qkv[:, :, 2]
    g, beta = decay_gate(p_a, p_b, lp['dn_a_log'], lp['dn_dt_bias'])
    o, s_f, s_b = dn_bidir(q, k, v, g, beta, s0_f, s0_b)
    z = p_z.astype(jnp.float32).reshape(bn, l, DN_HEADS, DN_HEAD_DIM)
    o = rmsnorm(o, lp['dn_norm_g']) * jax.nn.silu(z)
    y_a = o.reshape(bn, l, DN_WIDTH).astype(h.dtype) @ lp['w_br_a']
    y_b = pool_mixer(p_pool, lp['pool_w'], lp['pool_scale'], rows) @ lp['w_br_b']
    y_c = shortconv_mixer(p_sc, lp['sc_conv_w']) @ lp['w_br_c']
    gates = jax.nn.sigmoid(p_gate.reshape(bn, l, N_BRANCH, D_MODEL))
    y = gates[:, :, 0] * y_a + gates[:, :, 1] * y_b + gates[:, :, 2] * y_c
    return y @ lp['w_o'], s_f, s_b


def context_states(h, lp, s0):
    bn, l, _ = h.shape
    w = lp['w_in']
    kv = jax.nn.silu(conv3(h @ w[:, DN_WIDTH:OFF_Z], lp['dn_conv_w'][:, DN_WIDTH:])).astype(jnp.float32)
    kv = kv.reshape(bn, l, 2, DN_HEADS, DN_HEAD_DIM)
    k = l2norm(kv[:, :, 0])
    v = kv[:, :, 1]
    p_a, p_b = jnp.split(h @ w[:, OFF_A:OFF_POOL], 2, axis=-1)
    g, beta = decay_gate(p_a, p_b, lp['dn_a_log'], lp['dn_dt_bias'])
    _, s_f, s_b = dn_bidir(None, k, v, g, beta, s0, s0)
    return s_f, s_b


def swiglu(h, w_gu, w_down):
    gate, up = jnp.split(h @ w_gu, 2, axis=-1)
    return (jax.nn.silu(gate) * up) @ w_down


def modulate(x, norm_g, shift, scale):
    return rmsnorm(x, norm_g) * (1 + scale) + shift


def setup_inputs(seed: int = 0) -> dict:
    key = jax.random.key(seed)
    ks = jax.random.split(key, 24)
    f32 = jnp.float32
    nrm = lambda k, shape, s: jax.random.normal(k, shape, f32) * s
    L = DEPTH
    dt = jnp.exp(jax.random.uniform(ks[9], (L, 2, DN_HEADS), f32, math.log(1e-3), math.log(1e-1)))
    return {
        'x': nrm(ks[0], (BATCH, SEQ, D_MODEL), 1.0),
        'c': nrm(ks[1], (BATCH, D_MODEL), 1.0),
        'ctx': nrm(ks[2], (BATCH, CTX_LEN, D_MODEL), 1.0),
        'c_ctx': nrm(ks[3], (D_MODEL,), 1.0),
        'w_ada': nrm(ks[4], (L, D_MODEL, 6 * D_MODEL), 0.5 * D_MODEL ** -0.5),
        'b_ada': nrm(ks[5], (L, 6 * D_MODEL), 0.01),
        'norm1_g': 1.0 + nrm(ks[6], (L, D_MODEL), 0.1),
        'norm2_g': 1.0 + nrm(ks[7], (L, D_MODEL), 0.1),
        'w_in': nrm(ks[8], (L, D_MODEL, N_IN), D_MODEL ** -0.5),
        'dn_conv_w': nrm(ks[10], (L, 3, 3 * DN_WIDTH), 3 ** -0.5),
        'dn_a_log': jnp.log(jax.random.uniform(ks[11], (L, 2, DN_HEADS), f32, 1.0, 16.0)),
        'dn_dt_bias': dt + jnp.log(-jnp.expm1(-dt)),
        'dn_norm_g': 1.0 + nrm(ks[12], (L, DN_HEAD_DIM), 0.1),
        'pool_w': nrm(ks[13], (L, POOL_GROUPS, POOL_GROUP_DIM, POOL_GROUP_DIM), POOL_GROUP_DIM ** -0.5),
        'pool_scale': 1.0 + nrm(ks[14], (L, POOL_WIDTH), 0.1),
        'sc_conv_w': nrm(ks[15], (L, 3, SC_WIDTH), 3 ** -0.5),
        'w_br_a': nrm(ks[16], (L, DN_WIDTH, D_MODEL), DN_WIDTH ** -0.5),
        'w_br_b': nrm(ks[17], (L, POOL_WIDTH, D_MODEL), POOL_WIDTH ** -0.5),
        'w_br_c': nrm(ks[18], (L, SC_WIDTH, D_MODEL), SC_WIDTH ** -0.5),
        'w_o': nrm(ks[19], (L, D_MODEL, D_MODEL), D_MODEL ** -0.5),
        'w_gu': nrm(ks[20], (L, D_MODEL, 2 * D_FF), D_MODEL ** -0.5),
        'w_down': nrm(ks[21], (L, D_FF, D_MODEL), D_FF ** -0.5),
        'final_norm_g': 1.0 + nrm(ks[22], (D_MODEL,), 0.1),
    }


def reference(x, c, ctx, c_ctx, w_ada, b_ada, norm1_g, norm2_g, w_in, dn_conv_w, dn_a_log,
              dn_dt_bias, dn_norm_g, pool_w, pool_scale, sc_conv_w, w_br_a, w_br_b, w_br_c,
              w_o, w_gu, w_down, final_norm_g):
    bn = x.shape[0]
    rows = x.shape[1] // GRID_W
    s0 = jnp.zeros((bn, DN_HEADS, DN_HEAD_DIM, DN_HEAD_DIM), jnp.float32)
    for l in range(DEPTH):
        lp = {'w_in': w_in[l], 'dn_conv_w': dn_conv_w[l], 'dn_a_log': dn_a_log[l],
              'dn_dt_bias': dn_dt_bias[l], 'dn_norm_g': dn_norm_g[l], 'pool_w': pool_w[l],
              'pool_scale': pool_scale[l], 'sc_conv_w': sc_conv_w[l], 'w_br_a': w_br_a[l],
              'w_br_b': w_br_b[l], 'w_br_c': w_br_c[l], 'w_o': w_o[l]}
        mod = jax.nn.silu(c) @ w_ada[l] + b_ada[l]
        sh1, sc1, g1, sh2, sc2, g2 = jnp.split(mod[:, None, :], 6, axis=-1)
        mod_c = jax.nn.silu(c_ctx) @ w_ada[l] + b_ada[l]
        sh1c, sc1c, g1c, sh2c, sc2c, g2c = jnp.split(mod_c, 6)
        hc = modulate(ctx, norm1_g[l], sh1c, sc1c)
        if l == DEPTH - 1:
            s_f, s_b = context_states(hc, lp, s0)
        else:
            mix_c, s_f, s_b = mixer(hc, lp, s0, s0, None)
            ctx = ctx + g1c * mix_c
            ctx = ctx + g2c * swiglu(modulate(ctx, norm2_g[l], sh2c, sc2c), w_gu[l], w_down[l])
        h = modulate(x, norm1_g[l], sh1, sc1)
        mix, _, _ = mixer(h, lp, s_f, s_b, rows)
        x = x + g1 * mix
        x = x + g2 * swiglu(modulate(x, norm2_g[l], sh2, sc2), w_gu[l], w_down[l])
    return rmsnorm(x, final_norm_g)
```

```python
import numpy as np
import ml_dtypes
from contextlib import ExitStack
import concourse.bass as bass
import concourse.mybir as mybir
from concourse.bass_utils import run_bass_kernel_spmd

F32 = mybir.dt.float32
BF16 = mybir.dt.bfloat16
AF = mybir.ActivationFunctionType
ALU = mybir.AluOpType

D = 1024
KC = 8
NCTX = 256
GW = 64
DFF = 2816
FC = DFF // 128
N_IN = 6160
EPS = 1e-6
NEG = -30000.0
ENGS = ("tensor", "vector", "scalar", "gpsimd", "sync")
NDS = 4
NCST = 13


class Buf:
    __slots__ = ("w", "r")

    def __init__(self):
        self.w = None
        self.r = []


class Sched:
    def __init__(self, nc, pname=""):
        self.nc = nc
        self.pname = pname
        self.ops = {e: [] for e in ENGS}
        self.keys = list(ENGS) + ["dma_%s_%d" % (q, i) for q in ("sync", "gpsimd", "scalar") for i in range(NDS)]
        self.cnt = {k: 0 for k in self.keys}
        self.seen = {e: {k: 0 for k in self.keys} for e in ENGS}
        self.dma_rr = {"sync": 0, "gpsimd": 0, "scalar": 0}
        self.sems = {}

    def alloc(self):
        for k in self.keys:
            self.sems[k] = self.nc.alloc_semaphore(name="s_%s_%s" % (self.pname, k))

    def _deps(self, eng, reads, writes):
        need = {}

        def add(tok):
            if tok is None:
                return
            p, c = tok
            if need.get(p, 0) < c:
                need[p] = c
        for b in reads:
            add(b.w)
        for b in writes:
            add(b.w)
            for t in b.r:
                add(t)
        waits = []
        for p, c in need.items():
            if eng == "tensor" and p == "tensor":
                continue
            if self.seen[eng][p] < c:
                self.seen[eng][p] = c
                waits.append((p, c))
        return waits

    def _commit(self, tok, reads, writes):
        for b in reads:
            b.r.append(tok)
        for b in writes:
            b.w = tok
            b.r = []

    def op(self, eng, fn, reads=(), writes=()):
        waits = self._deps(eng, reads, writes)
        self.cnt[eng] += 1
        tok = (eng, self.cnt[eng])
        self.ops[eng].append((waits, fn, (eng, 1)))
        self._commit(tok, reads, writes)
        return tok

    def dma(self, q, fn, reads=(), writes=()):
        waits = self._deps(q, reads, writes)
        key = "dma_%s_%d" % (q, self.dma_rr[q])
        self.dma_rr[q] = (self.dma_rr[q] + 1) % NDS
        self.cnt[key] += 1
        tok = (key, self.cnt[key])
        self.ops[q].append((waits, fn, (key, 16)))
        self._commit(tok, reads, writes)
        return tok

    def emit(self):
        nc = self.nc
        mult = {k: (16 if k.startswith("dma_") else 1) for k in self.keys}
        with nc.Block() as block:
            def run(engname):
                def body(e):
                    for waits, fn, (sk, inc) in self.ops[engname]:
                        for p, c in waits:
                            e.wait_ge(self.sems[p], c * mult[p])
                        fn(e).then_inc(self.sems[sk], inc)
                    if engname == "sync":
                        for k in self.keys:
                            if self.cnt[k] > 0:
                                e.wait_ge(self.sems[k], self.cnt[k] * mult[k])
                return body
            block.tensor(run("tensor"))
            block.vector(run("vector"))
            block.scalar(run("scalar"))
            block.gpsimd(run("gpsimd"))
            block.sync(run("sync"))


class Phase:
    def __init__(self, nc, name):
        self.nc = nc
        self.name = name
        self.es = ExitStack()
        self.S = Sched(nc, name)
        self.n = 0

    def __enter__(self):
        self.es.enter_context(self.nc.cleanup_on_exit())
        self.S.alloc()
        return self

    def __exit__(self, *a):
        if a[0] is None:
            self.S.emit()
        self.es.close()
        return False

    def sb(self, shape, dt, name=None):
        self.n += 1
        return self.es.enter_context(self.nc.sbuf_tensor("%s_%s%d" % (self.name, name or "t", self.n), list(shape), dt))

    def ps(self, shape, dt, name=None):
        self.n += 1
        return self.es.enter_context(self.nc.psum_tensor("%s_%s%d" % (self.name, name or "p", self.n), list(shape), dt))

    def dma(self, q, out, in_, r=(), w=()):
        return self.S.dma(q, lambda e: e.dma_start(out=out, in_=in_), r, w)

    def mm(self, out, lhsT, rhs, start=True, stop=True, r=(), w=()):
        return self.S.op("tensor", lambda e: e.matmul(out, lhsT=lhsT, rhs=rhs, start=start, stop=stop), r, w)

    def tr(self, out, in_, ident, r=(), w=()):
        return self.S.op("tensor", lambda e: e.transpose(out, in_, ident), r, w)

    def act(self, out, in_, func, bias=None, scale=None, r=(), w=()):
        kw = {}
        if bias is not None:
            kw["bias"] = bias
        if scale is not None:
            kw["scale"] = scale
        return self.S.op("scalar", lambda e: e.activation(out=out, in_=in_, func=func, **kw), r, w)

    def tt(self, eng, out, in0, in1, op, r=(), w=()):
        return self.S.op(eng, lambda e: e.tensor_tensor(out=out, in0=in0, in1=in1, op=op), r, w)

    def ts(self, eng, out, in0, s1, op0, s2=None, op1=None, r=(), w=()):
        if op1 is None:
            return self.S.op(eng, lambda e: e.tensor_scalar(out=out, in0=in0, scalar1=s1, scalar2=None, op0=op0), r, w)
        return self.S.op(eng, lambda e: e.tensor_scalar(out=out, in0=in0, scalar1=s1, scalar2=s2, op0=op0, op1=op1), r, w)

    def stt(self, eng, out, in0, scalar, in1, op0, op1, r=(), w=()):
        return self.S.op(eng, lambda e: e.scalar_tensor_tensor(out=out, in0=in0, scalar=scalar, in1=in1, op0=op0, op1=op1), r, w)

    def cp(self, eng, out, in_, r=(), w=()):
        if eng == "scalar":
            return self.S.op(eng, lambda e: e.copy(out=out, in_=in_), r, w)
        return self.S.op(eng, lambda e: e.tensor_copy(out=out, in_=in_), r, w)

    def memset(self, eng, ap, val, r=(), w=()):
        return self.S.op(eng, lambda e: e.memset(ap, val), r, w)


class Rot:
    def __init__(self, ph, n, shape, dt, name, psum=False):
        mk = ph.ps if psum else ph.sb
        self.t = [mk(shape, dt, name) for _ in range(n)]
        self.b = [Buf() for _ in range(n)]
        self.i = -1

    def next(self):
        self.i = (self.i + 1) % len(self.t)
        return self.t[self.i], self.b[self.i]


def token_groups(NT, G):
    gs = [(1, 0, NCTX)]
    t = NCTX
    while t < NCTX + NT:
        gs.append((0, t, G))
        t += G
    return gs


def build(NT, stop_after=None, dbg=()):
    NTA = NCTX + NT
    NTILE = NTA // 128
    ROWS = NT // GW
    nc = bass.Bass("TRN2", target_bir_lowering=False)

    def din(name, shape, dt=F32):
        return nc.dram_tensor(name, list(shape), dt, kind="ExternalInput").ap()

    def scratch(name, shape, dt=F32):
        kind = "ExternalOutput" if name in dbg else "Internal"
        return nc.dram_tensor(name, list(shape), dt, kind=kind).ap()

    x_d = din("x", [NT, D]); ctx_d = din("ctx", [NCTX, D]); cc_d = din("cc", [128, KC, 2])
    w_ada_d = din("w_ada", [2, D, 6 * D]); b_adaT_d = din("b_adaT", [2, 128, 48])
    n1g_d = din("n1g", [2, 128, KC]); n2g_d = din("n2g", [2, 128, KC]); fng_d = din("fng", [128, KC])
    w_in_d = din("w_in", [2, D, N_IN]); dcw_d = din("dcw", [2, 128, 12, 3])
    alog_d = din("alog", [2, 128, 8]); dtb_d = din("dtb", [2, 128, 8]); dng_d = din("dng", [2, 128, 1])
    poolw_d = din("poolw", [2, 4, 64, 64]); pscale_d = din("pscale", [2, 128, 2]); scw_d = din("scw", [2, 128, 2, 3])
    wbra_d = din("w_br_a", [2, 512, D]); wbrb_d = din("w_br_b", [2, 256, D]); wbrc_d = din("w_br_c", [2, 256, D])
    wo_d = din("w_o", [2, D, D]); wgu_d = din("w_gu", [2, D, 2 * DFF]); wdn_d = din("w_down", [2, DFF, D])
    cst_d = din("cst", [128, NCST, 128]); cntl_d = din("cnt_lat", [2, 128, NT]); cntc_d = din("cnt_ctx", [2, 128, NCTX])
    out_d = nc.dram_tensor("out", [NT, D], F32, kind="ExternalOutput").ap()

    xT = scratch("xT", [KC, 128, NTA])
    pqkvT = scratch("pqkvT", [12, 128, NTA], BF16); szT = scratch("szT", [4, 128, NTA], BF16)
    ppoolT = scratch("ppoolT", [2, 128, NTA], BF16); pscT = scratch("pscT", [6, 128, NTA], BF16)
    gatesT = scratch("gatesT", [24, 128, NTA], BF16)
    abS = scratch("abS", [NTILE, 128, 24])
    qnT = scratch("qnT", [4, 128, NTA], BF16); knT = scratch("knT", [4, 128, NTA], BF16)
    kTM = scratch("kTM", [NTILE, 128, 512], BF16); vTM = scratch("vTM", [NTILE, 128, 512], BF16)
    oTd = [scratch("oTf", [4, 128, NTA]), scratch("oTb", [4, 128, NTA])]
    dT = scratch("dT", [2, 128, NTA], BF16)
    modd = scratch("modd", [2, 128, 6 * KC, 2])
    sfin = scratch("sfin", [2, 2, 4, 128, 128])
    dbgbuf = scratch("dbgbuf", [2, 128, 8, 128])

    def done(tag):
        return stop_after == tag

    with Phase(nc, "p0") as ph:
        cst = ph.sb([128, NCST, 128], F32, "cst"); b_cst = Buf()
        ph.dma("sync", cst[:], cst_d, w=[b_cst])
        cc = ph.sb([128, KC, 2], F32); b_cc = Buf()
        ph.dma("sync", cc[:], cc_d, w=[b_cc])
        scc = ph.sb([128, KC, 2], F32); b_scc = Buf()
        ph.act(scc[:], cc[:], AF.Silu, r=[b_cc], w=[b_scc])
        wrot = Rot(ph, 2, [128, KC, 768], F32, "wada")
        modps = ph.ps([128, 48, 2], F32); b_modps = Buf()
        for l in range(2):
            badaT = ph.sb([128, 48], F32); b_bada = Buf()
            ph.dma("sync", badaT[:], b_adaT_d[l], w=[b_bada])
            for pc in range(8):
                wt, wb = wrot.next()
                ph.dma("sync", wt[:], w_ada_d[l, :, pc * 768:(pc + 1) * 768].rearrange("(k p) n -> p k n", p=128), w=[wb])
                for mi in range(6):
                    m = pc * 6 + mi
                    for k in range(KC):
                        ph.mm(modps[:, m, :], wt[:, k, mi * 128:(mi + 1) * 128], scc[:, k, :], start=(k == 0), stop=(k == KC - 1),
                              r=[wb, b_scc], w=[b_modps])
            mods = ph.sb([128, 48, 2], F32); b_mods = Buf()
            for j in range(2):
                ph.tt("vector", mods[:, :, j], modps[:, :, j], badaT[:], ALU.add, r=[b_modps, b_bada], w=[b_mods])
            ph.dma("sync", modd[l], mods[:], r=[b_mods])
        xrot = Rot(ph, 3, [128, D], F32, "xin")
        trps = Rot(ph, 2, [128, KC, 128], F32, "trps", psum=True)
        orot = Rot(ph, 3, [128, KC, 128], F32, "xo")
        for ti in range(NTILE):
            xt, xb = xrot.next()
            src = ctx_d[ti * 128:(ti + 1) * 128, :] if ti < NCTX // 128 else x_d[ti * 128 - NCTX:(ti + 1) * 128 - NCTX, :]
            ph.dma("sync", xt[:], src, w=[xb])
            pt, pb = trps.next()
            for k in range(KC):
                ph.mm(pt[:, k, :], xt[:, k * 128:(k + 1) * 128], cst[:, 0, :], r=[xb, b_cst], w=[pb])
            ot, ob = orot.next()
            ph.cp("vector" if ti % 2 == 0 else "scalar", ot[:], pt[:], r=[pb], w=[ob])
            ph.dma("sync", xT[:, :, ti * 128:(ti + 1) * 128].rearrange("k p t -> p k t"), ot[:], r=[ob])
    if done("p0"):
        return nc

    for l in range(2):
        last = (l == 1)
        with Phase(nc, "p1_%d" % l) as ph:
            cst = ph.sb([128, NCST, 128], F32, "cst"); b_cst = Buf()
            ph.dma("sync", cst[:], cst_d, w=[b_cst])
            ones_bf = ph.sb([128, 128], BF16); b_ones = Buf()
            ph.cp("vector", ones_bf[:], cst[:, 1, :], r=[b_cst], w=[b_ones])
            win = ph.sb([128, KC, N_IN], BF16, "win"); b_win = [Buf() for _ in range(KC)]
            for k in range(KC):
                ph.dma("gpsimd", win[:, k, :], w_in_d[l, k * 128:(k + 1) * 128, :], w=[b_win[k]])
            mods = ph.sb([128, 48, 2], F32); b_mods = Buf()
            ph.dma("sync", mods[:], modd[l], w=[b_mods])
            n1g = ph.sb([128, KC], F32); b_n1g = Buf()
            ph.dma("sync", n1g[:], n1g_d[l], w=[b_n1g])
            A1 = ph.sb([128, 2, KC], F32); b_A1 = Buf()
            for j in range(2):
                ph.stt("vector", A1[:, j, :], mods[:, 8:16, j], 1.0, n1g[:], ALU.add, ALU.mult, r=[b_mods, b_n1g], w=[b_A1])
            alog = ph.sb([128, 8], F32); dtb = ph.sb([128, 8], F32); b_al = Buf(); b_dtb = Buf()
            ph.dma("sync", alog[:], alog_d[l], w=[b_al])
            ph.dma("sync", dtb[:], dtb_d[l], w=[b_dtb])
            negea = ph.sb([128, 8], F32); b_negea = Buf()
            ph.act(negea[:], alog[:], AF.Exp, r=[b_al], w=[b_negea])
            ph.ts("vector", negea[:], negea[:], -1.0, ALU.mult, r=[b_negea], w=[b_negea])

            xrot = Rot(ph, 2, [128, KC, 512], F32, "xg")
            sqrot = Rot(ph, 1, [128, KC, 512], BF16, "sq")
            hrot = Rot(ph, 2, [128, KC, 512], BF16, "hT")
            tmprot = Rot(ph, 2, [128, 512], F32, "tmp")
            rsrot = Rot(ph, 2, [128, 512], F32, "rstd")
            stf = Rot(ph, 4, [128, 512], BF16, "stf")
            stb = Rot(ph, 4, [128, 512], BF16, "stb")
            abrot = Rot(ph, 2, [128, 24], F32, "ab")
            abt = Rot(ph, 2, [128, 8], F32, "abt")
            ssps = Rot(ph, 1, [128, 512], F32, "ssps", psum=True)
            accps = Rot(ph, 5, [128, 512], F32, "acc", psum=True)
            abps = Rot(ph, 2, [128, 16], F32, "abps", psum=True)

            chunks = []
            for c in range(12):
                chunks.append((c * 128, "copy", pqkvT, c))
            for c in range(4):
                chunks.append((1536 + c * 128, "silu", szT, c))
            for c in range(2):
                chunks.append((2064 + c * 128, "copy", ppoolT, c))
            for c in range(6):
                chunks.append((2320 + c * 128, "copy", pscT, c))
            for c in range(24):
                chunks.append((3088 + c * 128, "sigm", gatesT, c))

            for (j, t0, G) in token_groups(NT, 512):
                xg, xb = xrot.next()
                ph.dma("sync", xg[:, :, 0:G], xT[:, :, t0:t0 + G].rearrange("k p t -> p k t"), w=[xb])
                sq, sqb = sqrot.next()
                ph.act(sq[:, :, 0:G], xg[:, :, 0:G], AF.Square, r=[xb], w=[sqb])
                sp, spb = ssps.next()
                for k in range(KC):
                    ph.mm(sp[:, 0:G], ones_bf[:], sq[:, k, 0:G], start=(k == 0), stop=(k == KC - 1), r=[b_ones, sqb], w=[spb])
                rs, rsb = rsrot.next()
                ph.act(rs[:, 0:G], sp[:, 0:G], AF.Ln, bias=EPS, scale=1.0 / D, r=[spb], w=[rsb])
                ph.act(rs[:, 0:G], rs[:, 0:G], AF.Exp, scale=-0.5, r=[rsb], w=[rsb])
                hT, hb = hrot.next()
                for k in range(KC):
                    tm, tmb = tmprot.next()
                    ph.stt("vector", tm[:, 0:G], xg[:, k, 0:G], A1[:, j, k:k + 1], rs[:, 0:G], ALU.mult, ALU.mult,
                           r=[xb, b_A1, rsb], w=[tmb])
                    ph.act(hT[:, k, 0:G], tm[:, 0:G], AF.Identity, bias=mods[:, k, j:j + 1], r=[tmb, b_mods], w=[hb])
                for s in range(G // 128):
                    ap_, apb = abps.next()
                    for k in range(KC):
                        ph.mm(ap_[:], hT[:, k, s * 128:(s + 1) * 128], win[:, k, 2048:2064], start=(k == 0), stop=(k == KC - 1),
                              r=[hb, b_win[k]], w=[apb])
                    ab, abb = abrot.next()
                    at, atb = abt.next()
                    ph.tt("vector", at[:], ap_[:, 0:8], dtb[:], ALU.add, r=[apb, b_dtb], w=[atb])
                    ph.act(at[:], at[:], AF.Exp, r=[atb], w=[atb])
                    ph.act(at[:], at[:], AF.Ln, bias=1.0, r=[atb], w=[atb])
                    ph.tt("vector", ab[:, 0:8], at[:], negea[:], ALU.mult, r=[atb, b_negea], w=[abb])
                    ph.act(ab[:, 8:16], ap_[:, 8:16], AF.Sigmoid, w=[apb, abb])
                    ph.ts("vector", ab[:, 16:24], ab[:, 8:16], -1.0, ALU.mult, r=[abb], w=[abb])
                    ph.dma("sync", abS[(t0 // 128) + s], ab[:], r=[abb])
                for ci, (c0, kind, dst, dc) in enumerate(chunks):
                    acc, accb = accps.next()
                    for k in range(KC):
                        ph.mm(acc[:, 0:G], win[:, k, c0:c0 + 128], hT[:, k, 0:G], start=(k == 0), stop=(k == KC - 1),
                              r=[hb, b_win[k]], w=[accb])
                    if kind == "copy":
                        st, sb_ = stf.next()
                        ph.cp("vector", st[:, 0:G], acc[:, 0:G], r=[accb], w=[sb_])
                        ph.dma("sync", dst[dc, :, t0:t0 + G], st[:, 0:G], r=[sb_])
                    else:
                        st, sb_ = stb.next()
                        ph.act(st[:, 0:G], acc[:, 0:G], AF.Silu if kind == "silu" else AF.Sigmoid, r=[accb], w=[sb_])
                        ph.dma("scalar", dst[dc, :, t0:t0 + G], st[:, 0:G], r=[sb_])
        if done("p1_%d" % l):
            return nc


        with Phase(nc, "p2a_%d" % l) as ph:
            cst = ph.sb([128, NCST, 128], F32, "cst"); b_cst = Buf()
            ph.dma("sync", cst[:], cst_d, w=[b_cst])
            ones_bf = ph.sb([128, 128], BF16); ident_bf = ph.sb([128, 128], BF16); b_cb = Buf()
            ph.cp("vector", ones_bf[:], cst[:, 1, :], r=[b_cst], w=[b_cb])
            ph.cp("vector", ident_bf[:], cst[:, 0, :], r=[b_cst], w=[b_cb])
            cw = ph.sb([128, 12, 3], F32); b_cw = Buf()
            ph.dma("sync", cw[:], dcw_d[l], w=[b_cw])
            pqrot = Rot(ph, 2, [128, 12, 514], BF16, "pq")
            srot = Rot(ph, 1, [128, 8, 512], F32, "s")
            vrot = Rot(ph, 2, [128, 4, 512], BF16, "vT")
            qkrot = Rot(ph, 2, [128, 8, 512], BF16, "qkn")
            tmrot = Rot(ph, 3, [128, 512], F32, "tm")
            sqrot = Rot(ph, 2, [128, 512], BF16, "sq")
            rrot = Rot(ph, 2, [128, 512], F32, "r")
            ssps = Rot(ph, 2, [128, 512], F32, "ss", psum=True)
            trps = Rot(ph, 4, [128, 4, 128], BF16, "trp", psum=True)
            tmo = Rot(ph, 4, [128, 512], BF16, "tmo")
            QB = float(np.log(128.0 ** -0.5))
            for (j, t0, G) in token_groups(NT, 512):
                s_lo, s_hi = (0, NCTX) if j == 1 else (NCTX, NTA)
                pq, pqb = pqrot.next()
                a = max(t0 - 1, s_lo); b = min(t0 + G + 1, s_hi)
                if a > t0 - 1:
                    ph.memset("gpsimd", pq[:, :, 0:1], 0.0, w=[pqb])
                if b < t0 + G + 1:
                    ph.memset("gpsimd", pq[:, :, G + 1:G + 2], 0.0, w=[pqb])
                ph.dma("sync", pq[:, :, a - (t0 - 1):b - (t0 - 1)], pqkvT[:, :, a:b].rearrange("c p t -> p c t"), w=[pqb])
                st, sb_ = srot.next()
                vT_, vb = vrot.next()
                qk, qkb = qkrot.next()
                for c in range(12):
                    tm, tmb = tmrot.next()
                    ph.ts("gpsimd", tm[:, 0:G], pq[:, c, 1:G + 1], cw[:, c, 1:2], ALU.mult, r=[pqb, b_cw], w=[tmb])
                    ph.stt("vector", tm[:, 0:G], pq[:, c, 0:G], cw[:, c, 0:1], tm[:, 0:G], ALU.mult, ALU.add, r=[pqb, b_cw, tmb], w=[tmb])
                    ph.stt("vector", tm[:, 0:G], pq[:, c, 2:G + 2], cw[:, c, 2:3], tm[:, 0:G], ALU.mult, ALU.add, r=[pqb, b_cw, tmb], w=[tmb])
                    if c < 8:
                        ph.act(st[:, c, 0:G], tm[:, 0:G], AF.Silu, r=[tmb], w=[sb_])
                    else:
                        ph.act(vT_[:, c - 8, 0:G], tm[:, 0:G], AF.Silu, r=[tmb], w=[vb])
                for c in range(8):
                    sq, sqb = sqrot.next()
                    ph.tt("gpsimd", sq[:, 0:G], st[:, c, 0:G], st[:, c, 0:G], ALU.mult, r=[sb_], w=[sqb])
                    sp, spb = ssps.next()
                    ph.mm(sp[:, 0:G], ones_bf[:], sq[:, 0:G], r=[b_cb, sqb], w=[spb])
                    rr, rb = rrot.next()
                    ph.act(rr[:, 0:G], sp[:, 0:G], AF.Ln, bias=EPS, r=[spb], w=[rb])
                    ph.act(rr[:, 0:G], rr[:, 0:G], AF.Exp, scale=-0.5, bias=(QB if c < 4 else 0.0), r=[rb], w=[rb])
                    ph.tt("vector", qk[:, c, 0:G], st[:, c, 0:G], rr[:, 0:G], ALU.mult, r=[sb_, rb], w=[qkb])
                ph.dma("sync", qnT[:, :, t0:t0 + G].rearrange("h p t -> p h t"), qk[:, 0:4, 0:G], r=[qkb])
                ph.dma("sync", knT[:, :, t0:t0 + G].rearrange("h p t -> p h t"), qk[:, 4:8, 0:G], r=[qkb])
                for s in range(G // 128):
                    for which in range(2):
                        tp, tpb = trps.next()
                        for h in range(4):
                            src = qk[:, 4 + h, s * 128:(s + 1) * 128] if which == 0 else vT_[:, h, s * 128:(s + 1) * 128]
                            ph.tr(tp[:, h, :], src, ident_bf[:], r=[qkb if which == 0 else vb, b_cb], w=[tpb])
                        to, tob = tmo.next()
                        ph.cp("scalar" if which == 0 else "vector", to[:].rearrange("p (h t) -> p h t", h=4), tp[:], r=[tpb], w=[tob])
                        ph.dma("sync", (kTM if which == 0 else vTM)[t0 // 128 + s], to[:], r=[tob])
        if done("p2a_%d" % l):
            return nc

        with Phase(nc, "p2c_%d" % l) as ph:
            for (j, c0, Rr, Wd, cnt_src) in ((1, 0, 1, NCTX, cntc_d), (0, NCTX, ROWS, GW, cntl_d)):
                n_tok = Rr * Wd
                RP = Rr + 16 if Rr > 1 else 1
                WP = Wd + 16
                X = ph.sb([128, n_tok], BF16, "pX"); PA = ph.sb([128, RP, WP], F32, "pA"); PB = ph.sb([128, RP, WP], F32, "pB")
                CN = ph.sb([128, n_tok], F32, "pC"); DO = ph.sb([128, n_tok], BF16, "pD")
                bX = Buf(); bC = Buf(); bA = [Buf(), Buf()]; bB = [Buf(), Buf()]; bD = [Buf(), Buf()]
                r0 = 8 if Rr > 1 else 0
                for c in range(2):
                    ph.dma("sync", X[:], ppoolT[c, :, c0:c0 + n_tok], w=[bX])
                    ph.dma("sync", CN[:], cnt_src[c], w=[bC])
                    ph.memset("gpsimd", PA[:], 0.0, w=bA)
                    ph.cp("vector", PA[:, r0:r0 + Rr, 8:8 + Wd], X[:].rearrange("p (r w) -> p r w", w=Wd), r=[bX], w=bA)
                    for half in range(2):
                        eng = "gpsimd" if half == 0 else "vector"
                        w_ = POOL_WINDOWS[2 * c + half]
                        lo = w_ // 2
                        nl = int(np.log2(w_))
                        pr = slice(half * 64, (half + 1) * 64)
                        src, dst, bs, bd = PA, PB, bA[half], bB[half]
                        for lv in range(nl):
                            sft = 1 << lv
                            ph.tt(eng, dst[pr, :, 0:WP - sft], src[pr, :, 0:WP - sft], src[pr, :, sft:WP], ALU.add, r=[bs], w=[bd])
                            src, dst, bs, bd = dst, src, bd, bs
                        if Rr > 1:
                            for lv in range(nl):
                                sft = 1 << lv
                                ph.tt(eng, dst[pr, 0:RP - sft, :], src[pr, 0:RP - sft, :], src[pr, sft:RP, :], ALU.add, r=[bs], w=[bd])
                                src, dst, bs, bd = dst, src, bd, bs
                        ro = r0 - lo if Rr > 1 else 0
                        co = 8 - lo
                        Mv = src[pr, ro:ro + Rr, co:co + Wd]
                        tflat = dst[pr].rearrange("p r w -> p (r w)")[:, 0:n_tok]
                        ph.tt(eng, tflat.rearrange("p (r w) -> p r w", w=Wd), Mv, CN[pr].rearrange("p (r w) -> p r w", w=Wd), ALU.mult,
                              r=[bs, bC], w=[bd])
                        ph.tt(eng, DO[pr], tflat, X[pr], ALU.subtract, r=[bd, bX], w=[bD[half]])
                    ph.dma("sync", dT[c, :, c0:c0 + n_tok], DO[:], r=bD)
        if done("p2c_%d" % l):
            return nc


        with Phase(nc, "p2b_%d" % l) as ph:
            cst = ph.sb([128, NCST, 128], F32, "cst"); b_cst = Buf()
            ph.dma("sync", cst[:], cst_d, w=[b_cst])
            ident_bf = ph.sb([128, 128], BF16); b_ib = Buf()
            ph.cp("vector", ident_bf[:], cst[:, 0, :], r=[b_cst], w=[b_ib])
            S32 = [ph.sb([128, 4, 128], F32, "S32") for _ in range(2)]
            Sbf = [ph.sb([128, 4, 128], BF16, "Sbf") for _ in range(2)]
            bS32 = [[Buf() for _ in range(4)] for _ in range(2)]
            bSbf = [[Buf() for _ in range(4)] for _ in range(2)]
            for d in range(2):
                ph.memset("vector", S32[d][:], 0.0, w=bS32[d])
                ph.memset("gpsimd", Sbf[d][:], 0.0, w=bSbf[d])
            banks = Rot(ph, 8, [128, 512], F32, "bank", psum=True)

            def trbank():
                t_, b_ = banks.next()
                return t_[:].bitcast(BF16), b_
            T = {}
            TB = {}

            def tl(name, d, par, shape, dt):
                key = (name, d, par)
                if key not in T:
                    T[key] = ph.sb(shape, dt, name)
                return T[key]

            def tb(name, d, par, h=0):
                return TB.setdefault((name, d, par, h), Buf())
            order = [list(range(NTILE)), [1, 0] + list(range(NTILE - 1, 1, -1))]

            def prep(step):
                par = step % 2
                ctxs = []
                for d in range(2):
                    ti = order[d][step]
                    tc = slice(ti * 128, (ti + 1) * 128)
                    qn = tl("qn", d, par, [128, 4, 128], BF16); kn = tl("kn", d, par, [128, 4, 128], BF16)
                    kT = tl("kT", d, par, [128, 512], BF16); vT_ = tl("vT", d, par, [128, 512], BF16)
                    ab = tl("ab", d, par, [128, 24], F32); sm = tl("sm", d, par, [128, 16], F32)
                    ph.dma("sync", qn[:], qnT[:, :, tc].rearrange("h p t -> p h t"), w=[tb("qn", d, par)])
                    ph.dma("sync", kn[:], knT[:, :, tc].rearrange("h p t -> p h t"), w=[tb("kn", d, par)])
                    ph.dma("sync", kT[:], kTM[ti], w=[tb("kT", d, par)])
                    ph.dma("sync", vT_[:], vTM[ti], w=[tb("vT", d, par)])
                    ph.dma("sync", ab[:], abS[ti], w=[tb("ab", d, par)])
                    g4 = ab[:, d * 4:(d + 1) * 4]
                    bsm = tb("sm", d, par); bab = tb("ab", d, par)
                    pS, bpS = banks.next()
                    ph.mm(pS[:, 0:8], cst[:, 2 + d, :], ab[:, 0:8], r=[b_cst, bab], w=[bpS])
                    ph.mm(pS[:, 8:16], cst[:, 1, :], ab[:, 0:8], r=[b_cst, bab], w=[bpS])
                    gcs = pS[:, d * 4:d * 4 + 4]
                    gls = pS[:, 8 + d * 4:12 + d * 4]
                    ph.ts("vector", sm[:, 0:4], gcs, -1.0, ALU.mult, w=[bpS, bsm])
                    ph.act(sm[:, 4:8], gcs, AF.Exp, w=[bpS, bsm])
                    ph.act(sm[:, 8:12], gls, AF.Exp, w=[bpS, bsm])
                    ph.tt("vector", sm[:, 12:16], gls, sm[:, 0:4], ALU.add, w=[bpS, bsm])
                    ph.act(sm[:, 12:16], sm[:, 12:16], AF.Exp, w=[bsm])
                    ctxs.append((d, ti, tc, qn, kn, kT, vT_, ab, sm))
                for (d, ti, tc, qn, kn, kT, vT_, ab, sm) in ctxs:
                    Gbc = tl("Gbc", d, par, [128, 4, 128], F32)
                    for h in range(4):
                        ph.act(Gbc[:, h, :], cst[:, 1, :], AF.Identity, scale=ab[:, d * 4 + h:d * 4 + h + 1],
                               r=[b_cst, tb("ab", d, par)], w=[tb("Gbc", d, par, h)])
                C = {}
                for (d, ti, tc, qn, kn, kT, vT_, ab, sm) in ctxs:
                    C[d] = dict(ti=ti, tc=tc, qn=qn, kn=kn, kT=kT, vT=vT_, ab=ab, sm=sm, Gbc=T[("Gbc", d, par)],
                                EQ=tl("EQ", d, par, [128, 4, 128], F32), E2T=tl("E2T", d, par, [128, 4, 128], F32),
                                EsT=tl("EsT", d, par, [128, 4, 128], F32), M0t=tl("M0t", d, par, [128, 4, 128], BF16),
                                A0t=tl("A0t", d, par, [128, 4, 128], BF16), AMb=tl("AMb", d, par, [128, 4, 2, 128], BF16),
                                AM1=tl("AM1", d, par, [128, 4, 2, 128], BF16), A2t=tl("A2t", d, par, [128, 4, 128], BF16),
                                PP=[tl("Pa", d, par, [128, 4, 128], BF16), tl("Pb", d, par, [128, 4, 128], BF16)],
                                PT=tl("PT", d, par, [128, 4, 128], BF16), Xt=tl("Xt", d, par, [128, 4, 128], BF16),
                                aqkT=tl("aqkT", d, par, [128, 4, 128], BF16), qdecT=tl("qdecT", d, par, [128, 4, 128], BF16),
                                kegc=tl("kegc", d, par, [128, 4, 128], BF16), kst=tl("kst", d, par, [128, 4, 128], BF16),
                                wTp=tl("wTp", d, par, [128, 4, 128], BF16), u=tl("u", d, par, [128, 4, 128], F32),
                                b4=ab[:, 8 + d * 4:12 + d * 4])
                DHS = [(d, h) for d in range(2) for h in range(4)]
                bk = {}
                for (d, h) in DHS:
                    c = C[d]
                    pR, bR = banks.next(); bk[(d, h)] = (pR, bR)
                    ph.mm(pR[:, 0:128], c["Gbc"][:, h, :], cst[:, 2 + d, :], r=[tb("Gbc", d, par, h), b_cst], w=[bR])
                    ph.mm(pR[:, 128:256], c["Gbc"][:, h, :], cst[:, 2 + d, :], start=True, stop=False, r=[tb("Gbc", d, par, h), b_cst], w=[bR])
                    ph.mm(pR[:, 128:256], cst[:, 0, :], cst[:, 4 + d, :], start=False, stop=True, r=[b_cst], w=[bR])
                for (d, h) in DHS:
                    c = C[d]; pR, bR = bk[(d, h)]
                    ph.act(c["EQ"][:, h, :], pR[:, 0:128], AF.Exp, w=[bR, tb("EQ", d, par, h)])
                    ph.act(c["E2T"][:, h, :], pR[:, 128:256], AF.Exp, bias=c["sm"][:, h:h + 1], r=[tb("sm", d, par)], w=[bR, tb("E2T", d, par, h)])
                for (d, h) in DHS:
                    c = C[d]
                    ph.tt("gpsimd", c["EsT"][:, h, :], c["E2T"][:, h, :], cst[:, 6 + d, :], ALU.mult, r=[tb("E2T", d, par, h), b_cst], w=[tb("EsT", d, par, h)])
                    ph.tt("gpsimd", c["qdecT"][:, h, :], c["qn"][:, h, :], c["EQ"][:, h, :], ALU.mult, r=[tb("qn", d, par), tb("EQ", d, par, h)], w=[tb("qdecT", d, par, h)])
                for (d, h) in DHS:
                    c = C[d]
                    pG, bG = banks.next(); bk[(d, h)] = (pG, bG)
                    ph.mm(pG[:, 0:128], c["kn"][:, h, :], c["kn"][:, h, :], r=[tb("kn", d, par)], w=[bG])
                    ph.mm(pG[:, 128:256], c["kn"][:, h, :], c["qn"][:, h, :], r=[tb("kn", d, par), tb("qn", d, par)], w=[bG])
                for (d, h) in DHS:
                    c = C[d]; pG, bG = bk[(d, h)]
                    ph.stt("vector", c["M0t"][:, h, :], pG[:, 0:128], c["b4"][:, h:h + 1], c["EsT"][:, h, :], ALU.mult, ALU.mult,
                           r=[tb("ab", d, par), tb("EsT", d, par, h)], w=[bG, tb("M0t", d, par, h)])
                    ph.tt("vector", c["aqkT"][:, h, :], pG[:, 128:256], c["E2T"][:, h, :], ALU.mult,
                          r=[tb("E2T", d, par, h)], w=[bG, tb("aqkT", d, par, h)])
                for (d, h) in DHS:
                    c = C[d]
                    pT_, bT_ = trbank(); bk[(d, h)] = (pT_, bT_)
                    ph.tr(pT_[:, 0:128], c["M0t"][:, h, :], ident_bf[:], r=[tb("M0t", d, par, h), b_ib], w=[bT_])
                for (d, h) in DHS:
                    c = C[d]; pT_, bT_ = bk[(d, h)]
                    ph.cp("scalar", c["A0t"][:, h, :], pT_[:, 0:128], w=[bT_, tb("A0t", d, par, h)])
                for (d, h) in DHS:
                    c = C[d]
                    ph.tt("gpsimd", c["AMb"][:, h, 1, :], c["M0t"][:, h, :], cst[:, 8, :], ALU.mult, r=[tb("M0t", d, par, h), b_cst], w=[tb("AMb", d, par, h)])
                    ph.tt("gpsimd", c["AMb"][:, h, 0, :], c["A0t"][:, h, :], cst[:, 8, :], ALU.mult, r=[tb("A0t", d, par, h), b_cst], w=[tb("AMb", d, par, h)])
                    ph.tt("gpsimd", c["PP"][0][:, h, :], cst[:, 0, :], c["AMb"][:, h, 1, :], ALU.subtract, r=[b_cst, tb("AMb", d, par, h)], w=[tb("P0", d, par, h)])
                for (d, h) in DHS:
                    c = C[d]
                    pA, bA_ = banks.next(); bk[(d, h)] = (pA, bA_)
                    ph.mm(pA[:, 0:128], c["AMb"][:, h, 1, :], c["AMb"][:, h, 0, :], r=[tb("AMb", d, par, h)], w=[bA_])
                    ph.mm(pA[:, 128:256], c["AMb"][:, h, 0, :], c["AMb"][:, h, 1, :], r=[tb("AMb", d, par, h)], w=[bA_])
                for (d, h) in DHS:
                    c = C[d]; pA, bA_ = bk[(d, h)]
                    ph.cp("scalar", c["AM1"][:, h, :, :], pA[:, 0:256].rearrange("p (a b) -> p a b", a=2), w=[bA_, tb("AM1", d, par, h)])
                for (d, h) in DHS:
                    c = C[d]
                    pP, bP_ = banks.next(); bk[(d, h)] = (pP, bP_)
                    ph.mm(pP[:, 0:128], c["AM1"][:, h, 0, :], c["PP"][0][:, h, :], r=[tb("AM1", d, par, h), tb("P0", d, par, h)], w=[bP_])
                for (d, h) in DHS:
                    c = C[d]; pP, bP_ = bk[(d, h)]
                    ph.tt("vector", c["PP"][1][:, h, :], pP[:, 0:128], c["PP"][0][:, h, :], ALU.add, r=[tb("P0", d, par, h)], w=[bP_, tb("P1", d, par, h)])
                for (d, h) in DHS:
                    c = C[d]
                    pA, bA_ = banks.next(); bk[(d, h)] = (pA, bA_)
                    ph.mm(pA[:, 0:128], c["AM1"][:, h, 1, :], c["AM1"][:, h, 0, :], r=[tb("AM1", d, par, h)], w=[bA_])
                for (d, h) in DHS:
                    c = C[d]; pA, bA_ = bk[(d, h)]
                    ph.cp("scalar", c["A2t"][:, h, :], pA[:, 0:128], w=[bA_, tb("A2t", d, par, h)])
                for (d, h) in DHS:
                    c = C[d]
                    pP, bP_ = banks.next(); bk[(d, h)] = (pP, bP_)
                    ph.mm(pP[:, 0:128], c["A2t"][:, h, :], c["PP"][1][:, h, :], r=[tb("A2t", d, par, h), tb("P1", d, par, h)], w=[bP_])
                for (d, h) in DHS:
                    c = C[d]; pP, bP_ = bk[(d, h)]
                    ph.tt("vector", c["PP"][0][:, h, :], pP[:, 0:128], c["PP"][1][:, h, :], ALU.add, r=[tb("P1", d, par, h)], w=[bP_, tb("P0", d, par, h)])
                for mi in range(4):
                    cur = mi % 2
                    nxt = 1 - cur
                    bk2 = {}
                    for (d, h) in DHS:
                        c = C[d]
                        pT_, bT_ = trbank(); bk[(d, h)] = (pT_, bT_)
                        ph.tr(pT_[:, 0:128], c["PP"][cur][:, h, :], ident_bf[:], r=[tb("P%d" % cur, d, par, h), b_ib], w=[bT_])
                    for (d, h) in DHS:
                        c = C[d]; pT_, bT_ = bk[(d, h)]
                        ph.cp("scalar", c["PT"][:, h, :], pT_[:, 0:128], w=[bT_, tb("PT", d, par, h)])
                    for (d, h) in DHS:
                        c = C[d]
                        pX, bX_ = banks.next(); bk2[(d, h)] = (pX, bX_)
                        ph.mm(pX[:, 0:128], c["A0t"][:, h, :], c["PP"][cur][:, h, :], r=[tb("A0t", d, par, h), tb("P%d" % cur, d, par, h)], w=[bX_])
                    for (d, h) in DHS:
                        c = C[d]; pX, bX_ = bk2[(d, h)]
                        ph.tt("vector", c["Xt"][:, h, :], pX[:, 0:128], cst[:, 9 + mi, :], ALU.mult, r=[b_cst], w=[bX_, tb("Xt", d, par, h)])
                    for (d, h) in DHS:
                        c = C[d]
                        pY, bY_ = banks.next(); bk[(d, h)] = (pY, bY_)
                        ph.mm(pY[:, 0:128], c["PT"][:, h, :], c["Xt"][:, h, :], r=[tb("PT", d, par, h), tb("Xt", d, par, h)], w=[bY_])
                    for (d, h) in DHS:
                        c = C[d]; pY, bY_ = bk[(d, h)]
                        ph.tt("vector", c["PP"][nxt][:, h, :], c["PP"][cur][:, h, :], pY[:, 0:128], ALU.subtract,
                              r=[tb("P%d" % cur, d, par, h)], w=[bY_, tb("P%d" % nxt, d, par, h)])
                for (d, h) in DHS:
                    c = C[d]
                    hc = slice(h * 128, (h + 1) * 128)
                    ph.act(c["kegc"][:, h, :], c["kT"][:, hc], AF.Identity, scale=c["sm"][:, 4 + h:5 + h], r=[tb("kT", d, par), tb("sm", d, par)], w=[tb("kegc", d, par, h)])
                    ph.act(c["kst"][:, h, :], c["kT"][:, hc], AF.Identity, scale=c["sm"][:, 12 + h:13 + h], r=[tb("kT", d, par), tb("sm", d, par)], w=[tb("kst", d, par, h)])
                for (d, h) in DHS:
                    c = C[d]
                    pW, bW = banks.next(); bk[(d, h)] = (pW, bW)
                    ph.mm(pW[:, 0:128], c["kegc"][:, h, :], c["PP"][0][:, h, :], r=[tb("kegc", d, par, h), tb("P0", d, par, h)], w=[bW])
                    ph.mm(pW[:, 128:256], c["PP"][0][:, h, :], c["vT"][:, h * 128:(h + 1) * 128], r=[tb("P0", d, par, h), tb("vT", d, par)], w=[bW])
                for (d, h) in DHS:
                    c = C[d]; pW, bW = bk[(d, h)]
                    ph.cp("scalar", c["wTp"][:, h, :], pW[:, 0:128], w=[bW, tb("wTp", d, par, h)])
                    ph.ts("vector", c["u"][:, h, :], pW[:, 128:256], c["b4"][:, h:h + 1], ALU.mult, r=[tb("ab", d, par)], w=[bW, tb("u", d, par, h)])
                return C

            def rec(step, C):
                par = step % 2
                DHS = [(d, h) for d in range(2) for h in range(4)]
                bk = {}
                bk2 = {}
                for d in range(2):
                    C[d]["vnew"] = tl("vnew", d, par, [128, 4, 128], BF16)
                    C[d]["oTs"] = tl("oTs", d, par, [128, 4, 128], F32)
                    C[d]["nb4"] = C[d]["ab"][:, 16 + d * 4:20 + d * 4]
                for (d, h) in DHS:
                    c = C[d]
                    pW, bW = banks.next(); bk[(d, h)] = (pW, bW)
                    ph.mm(pW[:, 0:128], c["wTp"][:, h, :], Sbf[d][:, h, :], r=[tb("wTp", d, par, h), bSbf[d][h]], w=[bW])
                for (d, h) in DHS:
                    c = C[d]; pW, bW = bk[(d, h)]
                    ph.stt("vector", c["vnew"][:, h, :], pW[:, 0:128], c["nb4"][:, h:h + 1], c["u"][:, h, :], ALU.mult, ALU.add,
                           r=[tb("ab", d, par), tb("u", d, par, h)], w=[bW, tb("vnew", d, par, h)])
                for (d, h) in DHS:
                    c = C[d]
                    pO, bO = banks.next(); bk2[(d, h)] = (pO, bO)
                    ph.mm(pO[:, 0:128], Sbf[d][:, h, :], c["qdecT"][:, h, :], start=True, stop=False, r=[bSbf[d][h], tb("qdecT", d, par, h)], w=[bO])
                    ph.mm(pO[:, 0:128], c["vnew"][:, h, :], c["aqkT"][:, h, :], start=False, stop=True, r=[tb("vnew", d, par, h), tb("aqkT", d, par, h)], w=[bO])
                    ph.mm(pO[:, 128:256], c["kst"][:, h, :], c["vnew"][:, h, :], r=[tb("kst", d, par, h), tb("vnew", d, par, h)], w=[bO])
                for (d, h) in DHS:
                    c = C[d]; pO, bO = bk2[(d, h)]
                    ph.stt("vector", S32[d][:, h, :], S32[d][:, h, :], c["sm"][:, 8 + h:9 + h], pO[:, 128:256], ALU.mult, ALU.add,
                           r=[tb("sm", d, par)], w=[bO, bS32[d][h]])
                    ph.cp("scalar", c["oTs"][:, h, :], pO[:, 0:128], w=[bO, tb("oTs", d, par, h)])
                for (d, h) in DHS:
                    ph.cp("gpsimd", Sbf[d][:, h, :], S32[d][:, h, :], r=[bS32[d][h]], w=[bSbf[d][h]])
                for d in range(2):
                    ph.dma("sync", oTd[d][:, :, C[d]["tc"]].rearrange("h p t -> p h t"), C[d]["oTs"][:], r=[tb("oTs", d, par, h) for h in range(4)])

            prev = prep(0)
            for step in range(NTILE):
                nxt_ = prep(step + 1) if step + 1 < NTILE else None
                rec(step, prev)
                prev = nxt_
            if "sfin" in dbg:
                for d in range(2):
                    ph.dma("sync", sfin[l, d].rearrange("h p t -> p h t"), S32[d][:], r=bS32[d])
        if done("p2b_%d" % l):
            return nc


        with Phase(nc, "p3_%d" % l) as ph:
            cst = ph.sb([128, NCST, 128], F32, "cst"); b_cst = Buf()
            ph.dma("sync", cst[:], cst_d, w=[b_cst])
            ones_bf = ph.sb([128, 128], BF16); b_cb = Buf()
            ph.cp("vector", ones_bf[:], cst[:, 1, :], r=[b_cst], w=[b_cb])
            wa = ph.sb([128, 4, D], BF16, "wa"); wbb = ph.sb([128, 2, D], BF16, "wb"); wc = ph.sb([128, 2, D], BF16, "wc")
            wo = ph.sb([128, KC, D], BF16, "wo"); b_w = Buf()
            ph.dma("gpsimd", wa[:], wbra_d[l].rearrange("(k p) n -> p k n", p=128), w=[b_w])
            ph.dma("gpsimd", wbb[:], wbrb_d[l].rearrange("(k p) n -> p k n", p=128), w=[b_w])
            ph.dma("gpsimd", wc[:], wbrc_d[l].rearrange("(k p) n -> p k n", p=128), w=[b_w])
            ph.dma("gpsimd", wo[:], wo_d[l].rearrange("(k p) n -> p k n", p=128), w=[b_w])
            pw = ph.sb([128, 2, 128], BF16, "pw"); b_pw = Buf()
            ph.memset("vector", pw[:], 0.0, w=[b_pw])
            for c in range(2):
                for half in range(2):
                    ph.dma("gpsimd", pw[half * 64:(half + 1) * 64, c, half * 64:(half + 1) * 64], poolw_d[l, 2 * c + half], w=[b_pw])
            psc_ = ph.sb([128, 2], F32); scw = ph.sb([128, 2, 3], F32); dng = ph.sb([128, 1], F32); b_sm = Buf()
            ph.dma("sync", psc_[:], pscale_d[l], w=[b_sm])
            ph.dma("sync", scw[:], scw_d[l], w=[b_sm])
            ph.dma("sync", dng[:], dng_d[l], w=[b_sm])
            mods = ph.sb([128, 48, 2], F32); b_mods = Buf()
            ph.dma("sync", mods[:], modd[l], w=[b_mods])

            G3 = 512
            ofr = Rot(ph, 1, [128, 4, G3], F32, "of"); obr = Rot(ph, 1, [128, 4, G3], F32, "ob")
            szr = Rot(ph, 1, [128, 4, G3], BF16, "sz"); onr = Rot(ph, 1, [128, 4, G3], BF16, "on")
            sqr = Rot(ph, 2, [128, G3], BF16, "sq"); rr_ = Rot(ph, 2, [128, G3], F32, "r"); tmr = Rot(ph, 3, [128, G3], F32, "tm")
            dr = Rot(ph, 1, [128, 2, G3], BF16, "d"); ybr = Rot(ph, 1, [128, 2, G3], BF16, "yb")
            scr = Rot(ph, 1, [128, 6, G3 + 2], BF16, "sc"); cxr = Rot(ph, 1, [128, 2, G3 + 2], F32, "cx"); ycr = Rot(ph, 1, [128, 2, G3], BF16, "yc")
            gtr = Rot(ph, 1, [128, 24, G3], BF16, "gt"); xgr = Rot(ph, 1, [128, KC, G3], F32, "xg")
            yr = Rot(ph, 1, [128, KC, G3], BF16, "y"); xor_ = Rot(ph, 1, [128, KC, G3], F32, "xo")
            ps1 = Rot(ph, 2, [128, G3], F32, "ps1", psum=True)
            ps3 = Rot(ph, 5, [128, G3], F32, "ps3", psum=True)
            for (j, t0, G) in token_groups(NT, G3):
                if last and j == 1:
                    continue
                s_lo, s_hi = (0, NCTX) if j == 1 else (NCTX, NTA)
                of_, ofb = ofr.next(); ob_, obb = obr.next(); sz, szb = szr.next(); on, onb = onr.next()
                ph.dma("sync", of_[:, :, 0:G], oTd[0][:, :, t0:t0 + G].rearrange("h p t -> p h t"), w=[ofb])
                ph.dma("sync", ob_[:, :, 0:G], oTd[1][:, :, t0:t0 + G].rearrange("h p t -> p h t"), w=[obb])
                ph.dma("sync", sz[:, :, 0:G], szT[:, :, t0:t0 + G].rearrange("h p t -> p h t"), w=[szb])
                ph.tt("gpsimd", of_[:, :, 0:G], of_[:, :, 0:G], ob_[:, :, 0:G], ALU.add, r=[ofb, obb], w=[ofb])
                for h in range(4):
                    sq, sqb = sqr.next()
                    ph.act(sq[:, 0:G], of_[:, h, 0:G], AF.Square, r=[ofb], w=[sqb])
                    p1, p1b = ps1.next()
                    ph.mm(p1[:, 0:G], ones_bf[:], sq[:, 0:G], r=[b_cb, sqb], w=[p1b])
                    rr, rb = rr_.next()
                    ph.act(rr[:, 0:G], p1[:, 0:G], AF.Ln, bias=EPS, scale=1.0 / 128, r=[p1b], w=[rb])
                    ph.act(rr[:, 0:G], rr[:, 0:G], AF.Exp, scale=-0.5, r=[rb], w=[rb])
                    tm, tmb = tmr.next()
                    ph.stt("vector", tm[:, 0:G], of_[:, h, 0:G], dng[:, 0:1], rr[:, 0:G], ALU.mult, ALU.mult, r=[ofb, b_sm, rb], w=[tmb])
                    ph.tt("gpsimd", on[:, h, 0:G], tm[:, 0:G], sz[:, h, 0:G], ALU.mult, r=[tmb, szb], w=[onb])
                dd, ddb = dr.next(); yb, ybb = ybr.next()
                ph.dma("sync", dd[:, :, 0:G], dT[:, :, t0:t0 + G].rearrange("c p t -> p c t"), w=[ddb])
                for c in range(2):
                    p1, p1b = ps1.next()
                    ph.mm(p1[:, 0:G], pw[:, c, :], dd[:, c, 0:G], r=[b_pw, ddb], w=[p1b])
                    ph.ts("vector", yb[:, c, 0:G], p1[:, 0:G], psc_[:, c:c + 1], ALU.mult, r=[p1b, b_sm], w=[ybb])
                sc, scb = scr.next(); cx, cxb = cxr.next(); yc, ycb = ycr.next()
                a = max(t0 - 1, s_lo); b = min(t0 + G + 1, s_hi)
                if a > t0 - 1:
                    ph.memset("gpsimd", sc[:, :, 0:1], 0.0, w=[scb])
                if b < t0 + G + 1:
                    ph.memset("gpsimd", sc[:, :, G + 1:G + 2], 0.0, w=[scb])
                ph.dma("sync", sc[:, :, a - (t0 - 1):b - (t0 - 1)], pscT[:, :, a:b].rearrange("c p t -> p c t"), w=[scb])
                ph.tt("gpsimd", cx[:, :, 0:G + 2], sc[:, 4:6, 0:G + 2], sc[:, 0:2, 0:G + 2], ALU.mult, r=[scb], w=[cxb])
                for c in range(2):
                    tm, tmb = tmr.next()
                    ph.ts("gpsimd", tm[:, 0:G], cx[:, c, 1:G + 1], scw[:, c, 1:2], ALU.mult, r=[cxb, b_sm], w=[tmb])
                    ph.stt("vector", tm[:, 0:G], cx[:, c, 0:G], scw[:, c, 0:1], tm[:, 0:G], ALU.mult, ALU.add, r=[cxb, b_sm, tmb], w=[tmb])
                    ph.stt("vector", tm[:, 0:G], cx[:, c, 2:G + 2], scw[:, c, 2:3], tm[:, 0:G], ALU.mult, ALU.add, r=[cxb, b_sm, tmb], w=[tmb])
                    ph.tt("gpsimd", yc[:, c, 0:G], tm[:, 0:G], sc[:, 2 + c, 1:G + 1], ALU.mult, r=[tmb, scb], w=[ycb])
                gt, gtb = gtr.next(); xg, xb = xgr.next(); y, yb_ = yr.next(); xo, xob = xor_.next()
                ph.dma("sync", gt[:, :, 0:G], gatesT[:, :, t0:t0 + G].rearrange("c p t -> p c t"), w=[gtb])
                ph.dma("sync", xg[:, :, 0:G], xT[:, :, t0:t0 + G].rearrange("k p t -> p k t"), w=[xb])
                for m in range(KC):
                    mc = slice(m * 128, (m + 1) * 128)
                    pa, pab = ps3.next(); pb_, pbb = ps3.next(); pc, pcb = ps3.next()
                    for h in range(4):
                        ph.mm(pa[:, 0:G], wa[:, h, mc], on[:, h, 0:G], start=(h == 0), stop=(h == 3), r=[b_w, onb], w=[pab])
                    for c in range(2):
                        ph.mm(pb_[:, 0:G], wbb[:, c, mc], yb[:, c, 0:G], start=(c == 0), stop=(c == 1), r=[b_w, ybb], w=[pbb])
                    for c in range(2):
                        ph.mm(pc[:, 0:G], wc[:, c, mc], yc[:, c, 0:G], start=(c == 0), stop=(c == 1), r=[b_w, ycb], w=[pcb])
                    t1, t1b = tmr.next(); t2, t2b = tmr.next(); t3, t3b = tmr.next()
                    ph.tt("vector", t1[:, 0:G], pa[:, 0:G], gt[:, m, 0:G], ALU.mult, r=[pab, gtb], w=[t1b])
                    ph.tt("vector", t2[:, 0:G], pb_[:, 0:G], gt[:, 8 + m, 0:G], ALU.mult, r=[pbb, gtb], w=[t2b])
                    ph.tt("vector", t3[:, 0:G], pc[:, 0:G], gt[:, 16 + m, 0:G], ALU.mult, r=[pcb, gtb], w=[t3b])
                    ph.tt("gpsimd", t1[:, 0:G], t1[:, 0:G], t2[:, 0:G], ALU.add, r=[t1b, t2b], w=[t1b])
                    ph.tt("gpsimd", y[:, m, 0:G], t1[:, 0:G], t3[:, 0:G], ALU.add, r=[t1b, t3b], w=[yb_])
                for m in range(KC):
                    mc = slice(m * 128, (m + 1) * 128)
                    pa, pab = ps3.next()
                    for k in range(KC):
                        ph.mm(pa[:, 0:G], wo[:, k, mc], y[:, k, 0:G], start=(k == 0), stop=(k == KC - 1), r=[b_w, yb_], w=[pab])
                    ph.stt("vector", xo[:, m, 0:G], pa[:, 0:G], mods[:, 16 + m, j:j + 1], xg[:, m, 0:G], ALU.mult, ALU.add,
                           r=[pab, b_mods, xb], w=[xob])
                ph.dma("sync", xT[:, :, t0:t0 + G].rearrange("k p t -> p k t"), xo[:, :, 0:G], r=[xob])
        if done("p3_%d" % l):
            return nc

        with Phase(nc, "p4_%d" % l) as ph:
            onesf = ph.sb([128, 128], F32, "onesf"); b_cst = Buf()
            ph.dma("sync", onesf[:], cst_d[:, 1, :], w=[b_cst])
            ones_bf = ph.sb([128, 128], BF16); b_cb = Buf()
            ph.cp("vector", ones_bf[:], onesf[:], r=[b_cst], w=[b_cb])
            wgu = ph.sb([128, KC, 2 * DFF], BF16, "wgu"); wdn = ph.sb([128, FC, D], BF16, "wdn"); b_wg = [Buf() for _ in range(KC)]; b_wd = Buf()
            for k in range(KC):
                ph.dma("gpsimd", wgu[:, k, :], wgu_d[l, k * 128:(k + 1) * 128, :], w=[b_wg[k]])
            for f0 in range(0, FC, 11):
                ph.dma("gpsimd", wdn[:, f0:f0 + 11, :], wdn_d[l, f0 * 128:(f0 + 11) * 128, :].rearrange("(f p) n -> p f n", p=128), w=[b_wd])
            mods = ph.sb([128, 48, 2], F32); b_mods = Buf()
            ph.dma("sync", mods[:], modd[l], w=[b_mods])
            n2g = ph.sb([128, KC], F32); b_n2g = Buf()
            ph.dma("sync", n2g[:], n2g_d[l], w=[b_n2g])
            A2 = ph.sb([128, 2, KC], F32); b_A2 = Buf()
            for j in range(2):
                ph.stt("vector", A2[:, j, :], mods[:, 32:40, j], 1.0, n2g[:], ALU.add, ALU.mult, r=[b_mods, b_n2g], w=[b_A2])
            G4 = 512
            xgr = Rot(ph, 2, [128, KC, G4], F32, "xg")
            hr = Rot(ph, 1, [128, KC, G4], BF16, "hT"); tmr = Rot(ph, 2, [128, G4], F32, "tm"); rsr = Rot(ph, 1, [128, G4], F32, "rs")
            acr = Rot(ph, 1, [128, FC, G4], BF16, "act")
            ssp = Rot(ph, 1, [128, G4], F32, "ssp", psum=True)
            gup = Rot(ph, 6, [128, G4], F32, "gup", psum=True)
            for (j, t0, G) in token_groups(NT, G4):
                if last and j == 1:
                    continue
                xg, xb = xgr.next()
                ph.dma("sync", xg[:, :, 0:G], xT[:, :, t0:t0 + G].rearrange("k p t -> p k t"), w=[xb])
                ac, acb = acr.next()
                sq, sqb = ac, acb
                ph.act(sq[:, 0:KC, 0:G], xg[:, :, 0:G], AF.Square, r=[xb], w=[sqb])
                sp, spb = ssp.next()
                for k in range(KC):
                    ph.mm(sp[:, 0:G], ones_bf[:], sq[:, k, 0:G], start=(k == 0), stop=(k == KC - 1), r=[b_cb, sqb], w=[spb])
                rs, rsb = rsr.next()
                ph.act(rs[:, 0:G], sp[:, 0:G], AF.Ln, bias=EPS, scale=1.0 / D, r=[spb], w=[rsb])
                ph.act(rs[:, 0:G], rs[:, 0:G], AF.Exp, scale=-0.5, r=[rsb], w=[rsb])
                hT, hb = hr.next()
                for k in range(KC):
                    tm, tmb = tmr.next()
                    ph.stt("vector", tm[:, 0:G], xg[:, k, 0:G], A2[:, j, k:k + 1], rs[:, 0:G], ALU.mult, ALU.mult, r=[xb, b_A2, rsb], w=[tmb])
                    ph.act(hT[:, k, 0:G], tm[:, 0:G], AF.Identity, bias=mods[:, 24 + k, j:j + 1], r=[tmb, b_mods], w=[hb])
                for f in range(FC):
                    pg, pgb = gup.next(); pu, pub = gup.next()
                    for k in range(KC):
                        ph.mm(pg[:, 0:G], wgu[:, k, f * 128:(f + 1) * 128], hT[:, k, 0:G], start=(k == 0), stop=(k == KC - 1), r=[b_wg[k], hb], w=[pgb])
                    for k in range(KC):
                        ph.mm(pu[:, 0:G], wgu[:, k, DFF + f * 128:DFF + (f + 1) * 128], hT[:, k, 0:G], start=(k == 0), stop=(k == KC - 1), r=[b_wg[k], hb], w=[pub])
                    tm, tmb = tmr.next()
                    ph.act(tm[:, 0:G], pg[:, 0:G], AF.Silu, r=[pgb], w=[tmb])
                    ph.tt("vector", ac[:, f, 0:G], pu[:, 0:G], tm[:, 0:G], ALU.mult, r=[pub, tmb], w=[acb])
                for m in range(KC):
                    pd, pdb = gup.next()
                    for f in range(FC):
                        ph.mm(pd[:, 0:G], wdn[:, f, m * 128:(m + 1) * 128], ac[:, f, 0:G], start=(f == 0), stop=(f == FC - 1), r=[b_wd, acb], w=[pdb])
                    ph.stt("vector", xg[:, m, 0:G], pd[:, 0:G], mods[:, 40 + m, j:j + 1], xg[:, m, 0:G], ALU.mult, ALU.add,
                           r=[pdb, b_mods], w=[xb])
                ph.dma("sync", xT[:, :, t0:t0 + G].rearrange("k p t -> p k t"), xg[:, :, 0:G], r=[xb])
        if done("p4_%d" % l):
            return nc

    with Phase(nc, "pf") as ph:
        cst = ph.sb([128, NCST, 128], F32, "cst"); b_cst = Buf()
        ph.dma("sync", cst[:], cst_d, w=[b_cst])
        ones_bf = ph.sb([128, 128], BF16); b_cb = Buf()
        ph.cp("vector", ones_bf[:], cst[:, 1, :], r=[b_cst], w=[b_cb])
        fng = ph.sb([128, KC], F32); b_fng = Buf()
        ph.dma("sync", fng[:], fng_d, w=[b_fng])
        GF = 512
        xgr = Rot(ph, 2, [128, KC, GF], F32, "xg"); sqr = Rot(ph, 1, [128, KC, GF], BF16, "sq"); rsr = Rot(ph, 1, [128, GF], F32, "rs")
        yr = Rot(ph, 2, [128, KC, GF], F32, "y"); otr = Rot(ph, 3, [128, D], F32, "ot")
        ssp = Rot(ph, 1, [128, GF], F32, "ssp", psum=True)
        trp = Rot(ph, 3, [128, KC, 128], F32, "trp", psum=True)
        out_waits = []
        for (j, t0, G) in token_groups(NT, GF):
            if j == 1:
                continue
            xg, xb = xgr.next()
            ph.dma("sync", xg[:], xT[:, :, t0:t0 + G].rearrange("k p t -> p k t"), w=[xb])
            sq, sqb = sqr.next()
            ph.act(sq[:], xg[:], AF.Square, r=[xb], w=[sqb])
            sp, spb = ssp.next()
            for k in range(KC):
                ph.mm(sp[:], ones_bf[:], sq[:, k, :], start=(k == 0), stop=(k == KC - 1), r=[b_cb, sqb], w=[spb])
            rs, rsb = rsr.next()
            ph.act(rs[:], sp[:], AF.Ln, bias=EPS, scale=1.0 / D, r=[spb], w=[rsb])
            ph.act(rs[:], rs[:], AF.Exp, scale=-0.5, r=[rsb], w=[rsb])
            y, yb = yr.next()
            for k in range(KC):
                ph.stt("vector", y[:, k, :], xg[:, k, :], fng[:, k:k + 1], rs[:], ALU.mult, ALU.mult,
                       r=[xb, b_fng, rsb], w=[yb])
            for s_ in range(G // 128):
                tp, tpb = trp.next()
                for k in range(KC):
                    ph.mm(tp[:, k, :], y[:, k, s_ * 128:(s_ + 1) * 128], cst[:, 0, :], r=[yb, b_cst], w=[tpb])
                ot, otb = otr.next()
                ph.cp("scalar" if s_ % 2 == 0 else "vector", ot[:].rearrange("p (k f) -> p k f", k=KC), tp[:], r=[tpb], w=[otb])
                ph.dma("sync", out_d[t0 - NCTX + s_ * 128:t0 - NCTX + (s_ + 1) * 128, :], ot[:], r=[otb])

    return nc


POOL_WINDOWS = (2, 4, 8, 16)


def _consts(NT):
    idx = np.arange(128)
    cst = np.zeros((128, NCST, 128), np.float32)
    cst[:, 0, :] = np.eye(128)
    cst[:, 1, :] = 1.0
    cst[:, 2, :] = (idx[:, None] <= idx[None, :])
    cst[:, 3, :] = (idx[:, None] >= idx[None, :])
    cst[:, 4, :] = np.where(idx[None, :] >= idx[:, None], 0.0, NEG)
    cst[:, 5, :] = np.where(idx[None, :] <= idx[:, None], 0.0, NEG)
    cst[:, 6, :] = (idx[None, :] > idx[:, None])
    cst[:, 7, :] = (idx[None, :] < idx[:, None])

    def blk(sz):
        return (idx[:, None] // sz == idx[None, :] // sz).astype(np.float32)
    cst[:, 8, :] = blk(8)
    for n, sz in enumerate((8, 16, 32, 64)):
        cst[:, 9 + n, :] = blk(2 * sz) - blk(sz)

    def cnt1d(n, w):
        lo = w // 2
        hi = w - 1 - lo
        pos = np.arange(n)
        return (np.clip(pos + hi + 1, 0, n) - np.clip(pos - lo, 0, n)).astype(np.float64)
    rows = NT // GW
    cnt_lat = np.zeros((2, 128, NT), np.float32)
    cnt_ctx = np.zeros((2, 128, NCTX), np.float32)
    for c in range(2):
        for half in range(2):
            w = POOL_WINDOWS[2 * c + half]
            cl = 1.0 / (cnt1d(rows, w)[:, None] * cnt1d(GW, w)[None, :])
            cnt_lat[c, half * 64:(half + 1) * 64, :] = cl.reshape(1, NT)
            cnt_ctx[c, half * 64:(half + 1) * 64, :] = (1.0 / cnt1d(NCTX, w))[None, :]
    return cst, cnt_lat, cnt_ctx


def col(v):
    v = np.asarray(v, np.float32)
    return np.ascontiguousarray(v.reshape(-1, 128).T)


def prep_shared(inp, NT):
    f = lambda a: np.ascontiguousarray(np.asarray(a, np.float32))
    cst, cnt_lat, cnt_ctx = _consts(NT)
    sh = {
        "w_ada": f(inp["w_ada"]),
        "b_adaT": np.stack([col(inp["b_ada"][l]) for l in range(2)]),
        "n1g": np.stack([col(inp["norm1_g"][l]) for l in range(2)]),
        "n2g": np.stack([col(inp["norm2_g"][l]) for l in range(2)]),
        "fng": col(inp["final_norm_g"]),
        "w_in": f(inp["w_in"]),
        "dcw": np.stack([np.stack([col(np.asarray(inp["dn_conv_w"])[l, t]) for t in range(3)], axis=-1) for l in range(2)]),
        "alog": np.stack([np.broadcast_to(np.asarray(inp["dn_a_log"], np.float32)[l].reshape(1, 8), (128, 8)) for l in range(2)]).copy(),
        "dtb": np.stack([np.broadcast_to(np.asarray(inp["dn_dt_bias"], np.float32)[l].reshape(1, 8), (128, 8)) for l in range(2)]).copy(),
        "dng": np.asarray(inp["dn_norm_g"], np.float32).reshape(2, 128, 1).copy(),
        "poolw": f(inp["pool_w"]),
        "pscale": np.stack([col(inp["pool_scale"][l]) for l in range(2)]),
        "scw": np.stack([np.stack([col(np.asarray(inp["sc_conv_w"])[l, t]) for t in range(3)], axis=-1) for l in range(2)]),
        "w_br_a": f(inp["w_br_a"]), "w_br_b": f(inp["w_br_b"]), "w_br_c": f(inp["w_br_c"]),
        "w_o": f(inp["w_o"]), "w_gu": f(inp["w_gu"]), "w_down": f(inp["w_down"]),
        "cst": cst, "cnt_lat": cnt_lat, "cnt_ctx": cnt_ctx,
    }
    return sh


def prep_core(inp, sh, b):
    m = dict(sh)
    m["x"] = np.ascontiguousarray(np.asarray(inp["x"], np.float32)[b])
    m["ctx"] = np.ascontiguousarray(np.asarray(inp["ctx"], np.float32)[b])
    m["cc"] = np.ascontiguousarray(np.stack([col(np.asarray(inp["c"])[b]), col(inp["c_ctx"])], axis=-1))
    return m


def kernel(**inputs):
    x = np.asarray(inputs["x"])
    B, NT, _ = x.shape
    nc = build(NT)
    sh = prep_shared(inputs, NT)
    in_maps = [prep_core(inputs, sh, c % B) for c in range(8)]
    res = run_bass_kernel_spmd(nc, in_maps, core_ids=list(range(8)))
    return np.stack([res.results[b]["out"] for b in range(B)]).astype(np.float32)
```

```python
import numpy as np
import ml_dtypes
from contextlib import ExitStack
import concourse.bass as bass
import concourse.mybir as mybir
from concourse.bass_utils import run_bass_kernel_spmd

F32 = mybir.dt.float32
BF16 = mybir.dt.bfloat16
AF = mybir.ActivationFunctionType
ALU = mybir.AluOpType

D = 1024
KC = 8
NCTX = 256
GW = 64
DFF = 2816
FC = DFF // 128
N_IN = 6160
EPS = 1e-6
NEG = -30000.0
ENGS = ("tensor", "vector", "scalar", "gpsimd", "sync")
NDS = 12
NCST = 13


class Buf:
    __slots__ = ("w", "r")

    def __init__(self):
        self.w = None
        self.r = []


class Sched:
    def __init__(self, nc, pname=""):
        self.nc = nc
        self.pname = pname
        self.ops = {e: [] for e in ENGS}
        self.keys = list(ENGS) + ["dma_%s_%d" % (q, i) for q in ("sync", "gpsimd", "scalar") for i in range(NDS)]
        self.cnt = {k: 0 for k in self.keys}
        self.seen = {e: {k: 0 for k in self.keys} for e in ENGS}
        self.dma_rr = {"sync": 0, "gpsimd": 0, "scalar": 0}
        self.sems = {}

    def alloc(self):
        for k in self.keys:
            self.sems[k] = self.nc.alloc_semaphore(name="s_%s_%s" % (self.pname, k))

    def _deps(self, eng, reads, writes):
        need = {}

        def add(tok):
            if tok is None:
                return
            p, c = tok
            if need.get(p, 0) < c:
                need[p] = c
        for b in reads:
            add(b.w)
        for b in writes:
            add(b.w)
            for t in b.r:
                add(t)
        waits = []
        for p, c in need.items():
            if eng == "tensor" and p == "tensor":
                continue
            if self.seen[eng][p] < c:
                self.seen[eng][p] = c
                waits.append((p, c))
        return waits

    def _commit(self, tok, reads, writes):
        for b in reads:
            b.r.append(tok)
        for b in writes:
            b.w = tok
            b.r = []

    def op(self, eng, fn, reads=(), writes=()):
        waits = self._deps(eng, reads, writes)
        self.cnt[eng] += 1
        tok = (eng, self.cnt[eng])
        self.ops[eng].append((waits, fn, (eng, 1)))
        self._commit(tok, reads, writes)
        return tok

    def dma(self, q, fn, reads=(), writes=()):
        waits = self._deps(q, reads, writes)
        key = "dma_%s_%d" % (q, self.dma_rr[q])
        self.dma_rr[q] = (self.dma_rr[q] + 1) % NDS
        if self.cnt[key] > 0 and self.seen[q][key] < self.cnt[key]:
            self.seen[q][key] = self.cnt[key]
            waits.append((key, self.cnt[key]))
        self.cnt[key] += 1
        tok = (key, self.cnt[key])
        self.ops[q].append((waits, fn, (key, 16)))
        self._commit(tok, reads, writes)
        return tok

    def emit(self):
        nc = self.nc
        mult = {k: (16 if k.startswith("dma_") else 1) for k in self.keys}
        with nc.Block() as block:
            def run(engname):
                def body(e):
                    for waits, fn, (sk, inc) in self.ops[engname]:
                        for p, c in waits:
                            e.wait_ge(self.sems[p], c * mult[p])
                        fn(e).then_inc(self.sems[sk], inc)
                    if engname == "sync":
                        for k in self.keys:
                            if self.cnt[k] > 0:
                                e.wait_ge(self.sems[k], self.cnt[k] * mult[k])
                return body
            block.tensor(run("tensor"))
            block.vector(run("vector"))
            block.scalar(run("scalar"))
            block.gpsimd(run("gpsimd"))
            block.sync(run("sync"))


class Phase:
    def __init__(self, nc, name):
        self.nc = nc
        self.name = name
        self.es = ExitStack()
        self.S = Sched(nc, name)
        self.n = 0

    def __enter__(self):
        self.es.enter_context(self.nc.cleanup_on_exit())
        self.S.alloc()
        return self

    def __exit__(self, *a):
        if a[0] is None:
            self.S.emit()
        self.es.close()
        return False

    def sb(self, shape, dt, name=None):
        self.n += 1
        return self.es.enter_context(self.nc.sbuf_tensor("%s_%s%d" % (self.name, name or "t", self.n), list(shape), dt))

    def ps(self, shape, dt, name=None):
        self.n += 1
        return self.es.enter_context(self.nc.psum_tensor("%s_%s%d" % (self.name, name or "p", self.n), list(shape), dt))

    def dma(self, q, out, in_, r=(), w=()):
        return self.S.dma(q, lambda e: e.dma_start(out=out, in_=in_), r, w)

    def mm(self, out, lhsT, rhs, start=True, stop=True, r=(), w=()):
        return self.S.op("tensor", lambda e: e.matmul(out, lhsT=lhsT, rhs=rhs, start=start, stop=stop), r, w)

    def tr(self, out, in_, ident, r=(), w=()):
        return self.S.op("tensor", lambda e: e.transpose(out, in_, ident), r, w)

    def act(self, out, in_, func, bias=None, scale=None, r=(), w=()):
        kw = {}
        if bias is not None:
            kw["bias"] = bias
        if scale is not None:
            kw["scale"] = scale
        return self.S.op("scalar", lambda e: e.activation(out=out, in_=in_, func=func, **kw), r, w)

    def tt(self, eng, out, in0, in1, op, r=(), w=()):
        return self.S.op(eng, lambda e: e.tensor_tensor(out=out, in0=in0, in1=in1, op=op), r, w)

    def ts(self, eng, out, in0, s1, op0, s2=None, op1=None, r=(), w=()):
        if op1 is None:
            return self.S.op(eng, lambda e: e.tensor_scalar(out=out, in0=in0, scalar1=s1, scalar2=None, op0=op0), r, w)
        return self.S.op(eng, lambda e: e.tensor_scalar(out=out, in0=in0, scalar1=s1, scalar2=s2, op0=op0, op1=op1), r, w)

    def stt(self, eng, out, in0, scalar, in1, op0, op1, r=(), w=()):
        return self.S.op(eng, lambda e: e.scalar_tensor_tensor(out=out, in0=in0, scalar=scalar, in1=in1, op0=op0, op1=op1), r, w)

    def cp(self, eng, out, in_, r=(), w=()):
        if eng == "scalar":
            return self.S.op(eng, lambda e: e.copy(out=out, in_=in_), r, w)
        return self.S.op(eng, lambda e: e.tensor_copy(out=out, in_=in_), r, w)

    def memset(self, eng, ap, val, r=(), w=()):
        return self.S.op(eng, lambda e: e.memset(ap, val), r, w)


class Rot:
    def __init__(self, ph, n, shape, dt, name, psum=False):
        mk = ph.ps if psum else ph.sb
        self.t = [mk(shape, dt, name) for _ in range(n)]
        self.b = [Buf() for _ in range(n)]
        self.i = -1

    def next(self):
        self.i = (self.i + 1) % len(self.t)
        return self.t[self.i], self.b[self.i]


def token_groups(NT, G):
    gs = [(1, 0, NCTX)]
    t = NCTX
    while t < NCTX + NT:
        gs.append((0, t, G))
        t += G
    return gs


def build(NT, stop_after=None, dbg=()):
    NTA = NCTX + NT
    NTILE = NTA // 128
    ROWS = NT // GW
    nc = bass.Bass("TRN2", target_bir_lowering=False)

    def din(name, shape, dt=F32):
        return nc.dram_tensor(name, list(shape), dt, kind="ExternalInput").ap()

    def scratch(name, shape, dt=F32):
        kind = "ExternalOutput" if name in dbg else "Internal"
        return nc.dram_tensor(name, list(shape), dt, kind=kind).ap()

    x_d = din("x", [NT, D]); ctx_d = din("ctx", [NCTX, D]); cc_d = din("cc", [128, KC, 2])
    w_ada_d = din("w_ada", [2, D, 6 * D]); b_adaT_d = din("b_adaT", [2, 128, 48])
    n1g_d = din("n1g", [2, 128, KC]); n2g_d = din("n2g", [2, 128, KC]); fng_d = din("fng", [128, KC])
    w_in_d = din("w_in", [2, D, N_IN]); dcw_d = din("dcw", [2, 128, 12, 3])
    alog_d = din("alog", [2, 128, 8]); dtb_d = din("dtb", [2, 128, 8]); dng_d = din("dng", [2, 128, 1])
    poolw_d = din("poolw", [2, 4, 64, 64]); pscale_d = din("pscale", [2, 128, 2]); scw_d = din("scw", [2, 128, 2, 3])
    wbra_d = din("w_br_a", [2, 512, D]); wbrb_d = din("w_br_b", [2, 256, D]); wbrc_d = din("w_br_c", [2, 256, D])
    wo_d = din("w_o", [2, D, D]); wgu_d = din("w_gu", [2, D, 2 * DFF]); wdn_d = din("w_down", [2, DFF, D])
    cst_d = din("cst", [128, NCST, 128]); cntl_d = din("cnt_lat", [2, 128, NT]); cntc_d = din("cnt_ctx", [2, 128, NCTX])
    out_d = nc.dram_tensor("out", [NT, D], F32, kind="ExternalOutput").ap()

    xT = scratch("xT", [KC, 128, NTA])
    pqkvT = scratch("pqkvT", [12, 128, NTA], BF16); szT = scratch("szT", [4, 128, NTA], BF16)
    ppoolT = scratch("ppoolT", [2, 128, NTA], BF16); pscT = scratch("pscT", [6, 128, NTA], BF16)
    gatesT = scratch("gatesT", [24, 128, NTA], BF16)
    abS = scratch("abS", [NTILE, 128, 24])
    qnT = scratch("qnT", [4, 128, NTA], BF16); knT = scratch("knT", [4, 128, NTA], BF16)
    kTM = scratch("kTM", [NTILE, 128, 512], BF16); vTM = scratch("vTM", [NTILE, 128, 512], BF16)
    oTd = [scratch("oTf", [4, 128, NTA], BF16), scratch("oTb", [4, 128, NTA], BF16)]
    dT = scratch("dT", [2, 128, NTA], BF16)
    modd = scratch("modd", [2, 128, 6 * KC, 2])
    sfin = scratch("sfin", [2, 2, 4, 128, 128])
    dbgbuf = scratch("dbgbuf", [2, 128, 8, 128])

    def done(tag):
        return stop_after == tag

    with Phase(nc, "p0") as ph:
        cst = ph.sb([128, NCST, 128], F32, "cst"); b_cst = Buf()
        ph.dma("sync", cst[:], cst_d, w=[b_cst])
        cc = ph.sb([128, KC, 2], F32); b_cc = Buf()
        ph.dma("sync", cc[:], cc_d, w=[b_cc])
        scc = ph.sb([128, KC, 2], F32); b_scc = Buf()
        ph.act(scc[:], cc[:], AF.Silu, r=[b_cc], w=[b_scc])
        wrot = Rot(ph, 2, [128, KC, 768], F32, "wada")
        modps = ph.ps([128, 48, 2], F32); b_modps = Buf()
        for l in range(2):
            badaT = ph.sb([128, 48], F32); b_bada = Buf()
            ph.dma("sync", badaT[:], b_adaT_d[l], w=[b_bada])
            for pc in range(8):
                wt, wb = wrot.next()
                ph.dma("sync" if pc % 2 == 0 else "scalar", wt[:], w_ada_d[l, :, pc * 768:(pc + 1) * 768].rearrange("(k p) n -> p k n", p=128), w=[wb])
                for mi in range(6):
                    m = pc * 6 + mi
                    for k in range(KC):
                        ph.mm(modps[:, m, :], wt[:, k, mi * 128:(mi + 1) * 128], scc[:, k, :], start=(k == 0), stop=(k == KC - 1),
                              r=[wb, b_scc], w=[b_modps])
            mods = ph.sb([128, 48, 2], F32); b_mods = Buf()
            for j in range(2):
                ph.tt("vector", mods[:, :, j], modps[:, :, j], badaT[:], ALU.add, r=[b_modps, b_bada], w=[b_mods])
            ph.dma("sync", modd[l], mods[:], r=[b_mods])
        xrot = Rot(ph, 3, [128, D], F32, "xin")
        trps = Rot(ph, 2, [128, KC, 128], F32, "trps", psum=True)
        orot = Rot(ph, 3, [128, KC, 128], F32, "xo")
        for ti in range(NTILE):
            xt, xb = xrot.next()
            src = ctx_d[ti * 128:(ti + 1) * 128, :] if ti < NCTX // 128 else x_d[ti * 128 - NCTX:(ti + 1) * 128 - NCTX, :]
            ph.dma("sync", xt[:], src, w=[xb])
            pt, pb = trps.next()
            for k in range(KC):
                ph.mm(pt[:, k, :], xt[:, k * 128:(k + 1) * 128], cst[:, 0, :], r=[xb, b_cst], w=[pb])
            ot, ob = orot.next()
            ph.cp("vector" if ti % 2 == 0 else "scalar", ot[:], pt[:], r=[pb], w=[ob])
            ph.dma("sync", xT[:, :, ti * 128:(ti + 1) * 128].rearrange("k p t -> p k t"), ot[:], r=[ob])
    if done("p0"):
        return nc

    for l in range(2):
        last = (l == 1)
        with Phase(nc, "p1_%d" % l) as ph:
            cst = ph.sb([128, NCST, 128], F32, "cst"); b_cst = Buf()
            ph.dma("sync", cst[:], cst_d, w=[b_cst])
            ones_bf = ph.sb([128, 128], BF16); b_ones = Buf()
            ph.cp("vector", ones_bf[:], cst[:, 1, :], r=[b_cst], w=[b_ones])
            win = ph.sb([128, KC, N_IN], BF16, "win"); b_win = [Buf() for _ in range(KC)]
            for k in range(KC):
                ph.dma("gpsimd", win[:, k, :], w_in_d[l, k * 128:(k + 1) * 128, :], w=[b_win[k]])
            mods = ph.sb([128, 48, 2], F32); b_mods = Buf()
            ph.dma("sync", mods[:], modd[l], w=[b_mods])
            n1g = ph.sb([128, KC], F32); b_n1g = Buf()
            ph.dma("sync", n1g[:], n1g_d[l], w=[b_n1g])
            A1 = ph.sb([128, 2, KC], F32); b_A1 = Buf()
            for j in range(2):
                ph.stt("vector", A1[:, j, :], mods[:, 8:16, j], 1.0, n1g[:], ALU.add, ALU.mult, r=[b_mods, b_n1g], w=[b_A1])
            alog = ph.sb([128, 8], F32); dtb = ph.sb([128, 8], F32); b_al = Buf(); b_dtb = Buf()
            ph.dma("sync", alog[:], alog_d[l], w=[b_al])
            ph.dma("sync", dtb[:], dtb_d[l], w=[b_dtb])
            negea = ph.sb([128, 8], F32); b_negea = Buf()
            ph.act(negea[:], alog[:], AF.Exp, r=[b_al], w=[b_negea])
            ph.ts("vector", negea[:], negea[:], -1.0, ALU.mult, r=[b_negea], w=[b_negea])

            xrot = Rot(ph, 2, [128, KC, 512], F32, "xg")
            sqrot = Rot(ph, 1, [128, KC, 512], BF16, "sq")
            hrot = Rot(ph, 2, [128, KC, 512], BF16, "hT")
            tmprot = Rot(ph, 2, [128, 512], F32, "tmp")
            rsrot = Rot(ph, 2, [128, 512], F32, "rstd")
            stf = Rot(ph, 4, [128, 512], BF16, "stf")
            stb = Rot(ph, 4, [128, 512], BF16, "stb")
            abrot = Rot(ph, 2, [128, 24], F32, "ab")
            abt = Rot(ph, 2, [128, 8], F32, "abt")
            ssps = Rot(ph, 1, [128, 512], F32, "ssps", psum=True)
            accps = Rot(ph, 5, [128, 512], F32, "acc", psum=True)
            abps = Rot(ph, 2, [128, 16], F32, "abps", psum=True)

            chunks = []
            for c in range(12):
                chunks.append((c * 128, "copy", pqkvT, c))
            for c in range(4):
                chunks.append((1536 + c * 128, "silu", szT, c))
            for c in range(2):
                chunks.append((2064 + c * 128, "copy", ppoolT, c))
            for c in range(6):
                chunks.append((2320 + c * 128, "copy", pscT, c))
            for c in range(24):
                chunks.append((3088 + c * 128, "sigm", gatesT, c))

            for (j, t0, G) in token_groups(NT, 512):
                xg, xb = xrot.next()
                ph.dma("sync", xg[:, :, 0:G], xT[:, :, t0:t0 + G].rearrange("k p t -> p k t"), w=[xb])
                sq, sqb = sqrot.next()
                ph.act(sq[:, :, 0:G], xg[:, :, 0:G], AF.Square, r=[xb], w=[sqb])
                sp, spb = ssps.next()
                for k in range(KC):
                    ph.mm(sp[:, 0:G], ones_bf[:], sq[:, k, 0:G], start=(k == 0), stop=(k == KC - 1), r=[b_ones, sqb], w=[spb])
                rs, rsb = rsrot.next()
                ph.act(rs[:, 0:G], sp[:, 0:G], AF.Ln, bias=EPS, scale=1.0 / D, r=[spb], w=[rsb])
                ph.act(rs[:, 0:G], rs[:, 0:G], AF.Exp, scale=-0.5, r=[rsb], w=[rsb])
                hT, hb = hrot.next()
                for k in range(KC):
                    tm, tmb = tmprot.next()
                    ph.stt("vector", tm[:, 0:G], xg[:, k, 0:G], A1[:, j, k:k + 1], rs[:, 0:G], ALU.mult, ALU.mult,
                           r=[xb, b_A1, rsb], w=[tmb])
                    ph.act(hT[:, k, 0:G], tm[:, 0:G], AF.Identity, bias=mods[:, k, j:j + 1], r=[tmb, b_mods], w=[hb])
                for s in range(G // 128):
                    ap_, apb = abps.next()
                    for k in range(KC):
                        ph.mm(ap_[:], hT[:, k, s * 128:(s + 1) * 128], win[:, k, 2048:2064], start=(k == 0), stop=(k == KC - 1),
                              r=[hb, b_win[k]], w=[apb])
                    ab, abb = abrot.next()
                    at, atb = abt.next()
                    ph.tt("vector", at[:], ap_[:, 0:8], dtb[:], ALU.add, r=[apb, b_dtb], w=[atb])
                    ph.act(at[:], at[:], AF.Exp, r=[atb], w=[atb])
                    ph.act(at[:], at[:], AF.Ln, bias=1.0, r=[atb], w=[atb])
                    ph.tt("vector", ab[:, 0:8], at[:], negea[:], ALU.mult, r=[atb, b_negea], w=[abb])
                    ph.act(ab[:, 8:16], ap_[:, 8:16], AF.Sigmoid, w=[apb, abb])
                    ph.ts("vector", ab[:, 16:24], ab[:, 8:16], -1.0, ALU.mult, r=[abb], w=[abb])
                    ph.dma("sync", abS[(t0 // 128) + s], ab[:], r=[abb])
                for ci, (c0, kind, dst, dc) in enumerate(chunks):
                    acc, accb = accps.next()
                    for k in range(KC):
                        ph.mm(acc[:, 0:G], win[:, k, c0:c0 + 128], hT[:, k, 0:G], start=(k == 0), stop=(k == KC - 1),
                              r=[hb, b_win[k]], w=[accb])
                    if kind == "copy":
                        st, sb_ = stf.next()
                        ph.cp("vector", st[:, 0:G], acc[:, 0:G], r=[accb], w=[sb_])
                        ph.dma("sync", dst[dc, :, t0:t0 + G], st[:, 0:G], r=[sb_])
                    else:
                        st, sb_ = stb.next()
                        ph.act(st[:, 0:G], acc[:, 0:G], AF.Silu if kind == "silu" else AF.Sigmoid, r=[accb], w=[sb_])
                        ph.dma("scalar", dst[dc, :, t0:t0 + G], st[:, 0:G], r=[sb_])
        if done("p1_%d" % l):
            return nc


        with Phase(nc, "p2a_%d" % l) as ph:
            cst = ph.sb([128, NCST, 128], F32, "cst"); b_cst = Buf()
            ph.dma("sync", cst[:], cst_d, w=[b_cst])
            ones_bf = ph.sb([128, 128], BF16); ident_bf = ph.sb([128, 128], BF16); b_cb = Buf()
            ph.cp("vector", ones_bf[:], cst[:, 1, :], r=[b_cst], w=[b_cb])
            ph.cp("vector", ident_bf[:], cst[:, 0, :], r=[b_cst], w=[b_cb])
            cw = ph.sb([128, 12, 3], F32); b_cw = Buf()
            ph.dma("sync", cw[:], dcw_d[l], w=[b_cw])
            pqrot = Rot(ph, 2, [128, 12, 514], BF16, "pq")
            srot = Rot(ph, 1, [128, 8, 512], F32, "s")
            vrot = Rot(ph, 2, [128, 4, 512], BF16, "vT")
            qkrot = Rot(ph, 2, [128, 8, 512], BF16, "qkn")
            tmrot = Rot(ph, 3, [128, 512], F32, "tm")
            sqrot = Rot(ph, 2, [128, 512], BF16, "sq")
            rrot = Rot(ph, 2, [128, 512], F32, "r")
            ssps = Rot(ph, 2, [128, 512], F32, "ss", psum=True)
            trps = Rot(ph, 4, [128, 4, 128], BF16, "trp", psum=True)
            tmo = Rot(ph, 4, [128, 512], BF16, "tmo")
            QB = float(np.log(128.0 ** -0.5))
            for (j, t0, G) in token_groups(NT, 512):
                s_lo, s_hi = (0, NCTX) if j == 1 else (NCTX, NTA)
                pq, pqb = pqrot.next()
                a = max(t0 - 1, s_lo); b = min(t0 + G + 1, s_hi)
                if a > t0 - 1:
                    ph.memset("gpsimd", pq[:, :, 0:1], 0.0, w=[pqb])
                if b < t0 + G + 1:
                    ph.memset("gpsimd", pq[:, :, G + 1:G + 2], 0.0, w=[pqb])
                ph.dma("sync", pq[:, :, a - (t0 - 1):b - (t0 - 1)], pqkvT[:, :, a:b].rearrange("c p t -> p c t"), w=[pqb])
                st, sb_ = srot.next()
                vT_, vb = vrot.next()
                qk, qkb = qkrot.next()
                for c in range(12):
                    tm, tmb = tmrot.next()
                    ph.act(tm[:, 0:G], pq[:, c, 1:G + 1], AF.Identity, scale=cw[:, c, 1:2], r=[pqb, b_cw], w=[tmb])
                    ph.stt("vector", tm[:, 0:G], pq[:, c, 0:G], cw[:, c, 0:1], tm[:, 0:G], ALU.mult, ALU.add, r=[pqb, b_cw, tmb], w=[tmb])
                    ph.stt("vector", tm[:, 0:G], pq[:, c, 2:G + 2], cw[:, c, 2:3], tm[:, 0:G], ALU.mult, ALU.add, r=[pqb, b_cw, tmb], w=[tmb])
                    if c < 8:
                        ph.act(st[:, c, 0:G], tm[:, 0:G], AF.Silu, r=[tmb], w=[sb_])
                    else:
                        ph.act(vT_[:, c - 8, 0:G], tm[:, 0:G], AF.Silu, r=[tmb], w=[vb])
                for c in range(8):
                    sq, sqb = sqrot.next()
                    ph.tt("gpsimd", sq[:, 0:G], st[:, c, 0:G], st[:, c, 0:G], ALU.mult, r=[sb_], w=[sqb])
                    sp, spb = ssps.next()
                    ph.mm(sp[:, 0:G], ones_bf[:], sq[:, 0:G], r=[b_cb, sqb], w=[spb])
                    rr, rb = rrot.next()
                    ph.act(rr[:, 0:G], sp[:, 0:G], AF.Ln, bias=EPS, r=[spb], w=[rb])
                    ph.act(rr[:, 0:G], rr[:, 0:G], AF.Exp, scale=-0.5, bias=(QB if c < 4 else 0.0), r=[rb], w=[rb])
                    ph.tt("vector", qk[:, c, 0:G], st[:, c, 0:G], rr[:, 0:G], ALU.mult, r=[sb_, rb], w=[qkb])
                ph.dma("sync", qnT[:, :, t0:t0 + G].rearrange("h p t -> p h t"), qk[:, 0:4, 0:G], r=[qkb])
                ph.dma("sync", knT[:, :, t0:t0 + G].rearrange("h p t -> p h t"), qk[:, 4:8, 0:G], r=[qkb])
                for s in range(G // 128):
                    for which in range(2):
                        tp, tpb = trps.next()
                        for h in range(4):
                            src = qk[:, 4 + h, s * 128:(s + 1) * 128] if which == 0 else vT_[:, h, s * 128:(s + 1) * 128]
                            ph.tr(tp[:, h, :], src, ident_bf[:], r=[qkb if which == 0 else vb, b_cb], w=[tpb])
                        to, tob = tmo.next()
                        ph.cp("scalar" if which == 0 else "vector", to[:].rearrange("p (h t) -> p h t", h=4), tp[:], r=[tpb], w=[tob])
                        ph.dma("sync", (kTM if which == 0 else vTM)[t0 // 128 + s], to[:], r=[tob])
        if done("p2a_%d" % l):
            return nc

        with Phase(nc, "p2c_%d" % l) as ph:
            for (j, c0, Rr, Wd, cnt_src) in ((1, 0, 1, NCTX, cntc_d), (0, NCTX, ROWS, GW, cntl_d)):
                n_tok = Rr * Wd
                RP = Rr + 16 if Rr > 1 else 1
                WP = Wd + 16
                X = ph.sb([128, n_tok], BF16, "pX"); PA = ph.sb([128, RP, WP], F32, "pA"); PB = ph.sb([128, RP, WP], F32, "pB")
                CN = ph.sb([128, n_tok], F32, "pC"); DO = ph.sb([128, n_tok], BF16, "pD")
                bX = Buf(); bC = Buf(); bA = [Buf(), Buf()]; bB = [Buf(), Buf()]; bD = [Buf(), Buf()]
                r0 = 8 if Rr > 1 else 0
                for c in range(2):
                    ph.dma("sync", X[:], ppoolT[c, :, c0:c0 + n_tok], w=[bX])
                    ph.dma("sync", CN[:], cnt_src[c], w=[bC])
                    ph.memset("gpsimd", PA[:], 0.0, w=bA)
                    ph.cp("vector", PA[:, r0:r0 + Rr, 8:8 + Wd], X[:].rearrange("p (r w) -> p r w", w=Wd), r=[bX], w=bA)
                    for half in range(2):
                        eng = "gpsimd" if half == 0 else "vector"
                        w_ = POOL_WINDOWS[2 * c + half]
                        lo = w_ // 2
                        nl = int(np.log2(w_))
                        pr = slice(half * 64, (half + 1) * 64)
                        src, dst, bs, bd = PA, PB, bA[half], bB[half]
                        for lv in range(nl):
                            sft = 1 << lv
                            ph.tt(eng, dst[pr, :, 0:WP - sft], src[pr, :, 0:WP - sft], src[pr, :, sft:WP], ALU.add, r=[bs], w=[bd])
                            src, dst, bs, bd = dst, src, bd, bs
                        if Rr > 1:
                            for lv in range(nl):
                                sft = 1 << lv
                                ph.tt(eng, dst[pr, 0:RP - sft, :], src[pr, 0:RP - sft, :], src[pr, sft:RP, :], ALU.add, r=[bs], w=[bd])
                                src, dst, bs, bd = dst, src, bd, bs
                        ro = r0 - lo if Rr > 1 else 0
                        co = 8 - lo
                        Mv = src[pr, ro:ro + Rr, co:co + Wd]
                        tflat = dst[pr].rearrange("p r w -> p (r w)")[:, 0:n_tok]
                        ph.tt(eng, tflat.rearrange("p (r w) -> p r w", w=Wd), Mv, CN[pr].rearrange("p (r w) -> p r w", w=Wd), ALU.mult,
                              r=[bs, bC], w=[bd])
                        ph.tt(eng, DO[pr], tflat, X[pr], ALU.subtract, r=[bd, bX], w=[bD[half]])
                    ph.dma("sync", dT[c, :, c0:c0 + n_tok], DO[:], r=bD)
        if done("p2c_%d" % l):
            return nc


        with Phase(nc, "p2b_%d" % l) as ph:
            cst = ph.sb([128, NCST, 128], F32, "cst"); b_cst = Buf()
            ph.dma("sync", cst[:], cst_d, w=[b_cst])
            ident_bf = ph.sb([128, 128], BF16); b_ib = Buf()
            ph.cp("vector", ident_bf[:], cst[:, 0, :], r=[b_cst], w=[b_ib])
            S32 = [ph.sb([128, 4, 128], F32, "S32") for _ in range(2)]
            Sbf = [ph.sb([128, 4, 128], BF16, "Sbf") for _ in range(2)]
            bS32 = [[Buf() for _ in range(4)] for _ in range(2)]
            bSbf = [[Buf() for _ in range(4)] for _ in range(2)]
            for d in range(2):
                ph.memset("vector", S32[d][:], 0.0, w=bS32[d])
                ph.memset("gpsimd", Sbf[d][:], 0.0, w=bSbf[d])
            banks = Rot(ph, 8, [128, 512], F32, "bank", psum=True)

            def trbank():
                t_, b_ = banks.next()
                return t_[:].bitcast(BF16), b_
            T = {}
            TB = {}

            def tl(name, d, par, shape, dt):
                key = (name, d, par)
                if key not in T:
                    T[key] = ph.sb(shape, dt, name)
                return T[key]

            def tb(name, d, par, h=0):
                return TB.setdefault((name, d, par, h), Buf())
            order = [list(range(NTILE)), [1, 0] + list(range(NTILE - 1, 1, -1))]

            def prep(step):
                par = step % 2
                ctxs = []
                for d in range(2):
                    ti = order[d][step]
                    tc = slice(ti * 128, (ti + 1) * 128)
                    qn = tl("qn", d, par, [128, 4, 128], BF16); kn = tl("kn", d, par, [128, 4, 128], BF16)
                    kT = tl("kT", d, par, [128, 512], BF16); vT_ = tl("vT", d, par, [128, 512], BF16)
                    ab = tl("ab", d, par, [128, 24], F32); sm = tl("sm", d, par, [128, 16], F32)
                    ph.dma("sync", qn[:], qnT[:, :, tc].rearrange("h p t -> p h t"), w=[tb("qn", d, par)])
                    ph.dma("sync", kn[:], knT[:, :, tc].rearrange("h p t -> p h t"), w=[tb("kn", d, par)])
                    ph.dma("sync", kT[:], kTM[ti], w=[tb("kT", d, par)])
                    ph.dma("sync", vT_[:], vTM[ti], w=[tb("vT", d, par)])
                    ph.dma("sync", ab[:], abS[ti], w=[tb("ab", d, par)])
                    g4 = ab[:, d * 4:(d + 1) * 4]
                    bsm = tb("sm", d, par); bab = tb("ab", d, par)
                    pS, bpS = banks.next()
                    ph.mm(pS[:, 0:8], cst[:, 2 + d, :], ab[:, 0:8], r=[b_cst, bab], w=[bpS])
                    ph.mm(pS[:, 8:16], cst[:, 1, :], ab[:, 0:8], r=[b_cst, bab], w=[bpS])
                    gcs = pS[:, d * 4:d * 4 + 4]
                    gls = pS[:, 8 + d * 4:12 + d * 4]
                    ph.ts("vector", sm[:, 0:4], gcs, -1.0, ALU.mult, w=[bpS, bsm])
                    ph.act(sm[:, 4:8], gcs, AF.Exp, w=[bpS, bsm])
                    ph.act(sm[:, 8:12], gls, AF.Exp, w=[bpS, bsm])
                    ph.tt("vector", sm[:, 12:16], gls, sm[:, 0:4], ALU.add, w=[bpS, bsm])
                    ph.act(sm[:, 12:16], sm[:, 12:16], AF.Exp, w=[bsm])
                    ctxs.append((d, ti, tc, qn, kn, kT, vT_, ab, sm))
                for (d, ti, tc, qn, kn, kT, vT_, ab, sm) in ctxs:
                    Gbc = tl("Gbc", d, par, [128, 4, 128], F32)
                    for h in range(4):
                        ph.act(Gbc[:, h, :], cst[:, 1, :], AF.Identity, scale=ab[:, d * 4 + h:d * 4 + h + 1],
                               r=[b_cst, tb("ab", d, par)], w=[tb("Gbc", d, par, h)])
                C = {}
                for (d, ti, tc, qn, kn, kT, vT_, ab, sm) in ctxs:
                    C[d] = dict(ti=ti, tc=tc, qn=qn, kn=kn, kT=kT, vT=vT_, ab=ab, sm=sm, Gbc=T[("Gbc", d, par)],
                                EQ=tl("EQ", d, par, [128, 4, 128], F32), E2T=tl("E2T", d, par, [128, 4, 128], F32),
                                EsT=tl("EsT", d, par, [128, 4, 128], F32), M0t=tl("M0t", d, par, [128, 4, 128], BF16),
                                A0t=tl("A0t", d, par, [128, 4, 128], BF16), AMb=tl("AMb", d, par, [128, 4, 2, 128], BF16),
                                AM1=tl("AM1", d, par, [128, 4, 2, 128], BF16), A2t=tl("A2t", d, par, [128, 4, 128], BF16),
                                PP=[tl("Pa", d, par, [128, 4, 128], BF16), tl("Pb", d, par, [128, 4, 128], BF16)],
                                PT=tl("PT", d, par, [128, 4, 128], BF16), Xt=tl("Xt", d, par, [128, 4, 128], BF16),
                                aqkT=tl("aqkT", d, par, [128, 4, 128], BF16), qdecT=tl("qdecT", d, par, [128, 4, 128], BF16),
                                kegc=tl("kegc", d, par, [128, 4, 128], BF16), kst=tl("kst", d, par, [128, 4, 128], BF16),
                                wTp=tl("wTp", d, par, [128, 4, 128], BF16), u=tl("u", d, par, [128, 4, 128], F32),
                                b4=ab[:, 8 + d * 4:12 + d * 4])
                DHS = [(d, h) for d in range(2) for h in range(4)]
                bk = {}
                for (d, h) in DHS:
                    c = C[d]
                    pR, bR = banks.next(); bk[(d, h)] = (pR, bR)
                    ph.mm(pR[:, 0:128], c["Gbc"][:, h, :], cst[:, 2 + d, :], r=[tb("Gbc", d, par, h), b_cst], w=[bR])
                    ph.mm(pR[:, 128:256], c["Gbc"][:, h, :], cst[:, 2 + d, :], start=True, stop=False, r=[tb("Gbc", d, par, h), b_cst], w=[bR])
                    ph.mm(pR[:, 128:256], cst[:, 0, :], cst[:, 4 + d, :], start=False, stop=True, r=[b_cst], w=[bR])
                for (d, h) in DHS:
                    c = C[d]; pR, bR = bk[(d, h)]
                    ph.act(c["EQ"][:, h, :], pR[:, 0:128], AF.Exp, w=[bR, tb("EQ", d, par, h)])
                    ph.act(c["E2T"][:, h, :], pR[:, 128:256], AF.Exp, bias=c["sm"][:, h:h + 1], r=[tb("sm", d, par)], w=[bR, tb("E2T", d, par, h)])
                for (d, h) in DHS:
                    c = C[d]
                    ph.tt("gpsimd", c["EsT"][:, h, :], c["E2T"][:, h, :], cst[:, 6 + d, :], ALU.mult, r=[tb("E2T", d, par, h), b_cst], w=[tb("EsT", d, par, h)])
                    ph.tt("gpsimd", c["qdecT"][:, h, :], c["qn"][:, h, :], c["EQ"][:, h, :], ALU.mult, r=[tb("qn", d, par), tb("EQ", d, par, h)], w=[tb("qdecT", d, par, h)])
                for (d, h) in DHS:
                    c = C[d]
                    pG, bG = banks.next(); bk[(d, h)] = (pG, bG)
                    ph.mm(pG[:, 0:128], c["kn"][:, h, :], c["kn"][:, h, :], r=[tb("kn", d, par)], w=[bG])
                    ph.mm(pG[:, 128:256], c["kn"][:, h, :], c["qn"][:, h, :], r=[tb("kn", d, par), tb("qn", d, par)], w=[bG])
                for (d, h) in DHS:
                    c = C[d]; pG, bG = bk[(d, h)]
                    ph.stt("vector", c["M0t"][:, h, :], pG[:, 0:128], c["b4"][:, h:h + 1], c["EsT"][:, h, :], ALU.mult, ALU.mult,
                           r=[tb("ab", d, par), tb("EsT", d, par, h)], w=[bG, tb("M0t", d, par, h)])
                    ph.tt("vector", c["aqkT"][:, h, :], pG[:, 128:256], c["E2T"][:, h, :], ALU.mult,
                          r=[tb("E2T", d, par, h)], w=[bG, tb("aqkT", d, par, h)])
                for (d, h) in DHS:
                    c = C[d]
                    pT_, bT_ = trbank(); bk[(d, h)] = (pT_, bT_)
                    ph.tr(pT_[:, 0:128], c["M0t"][:, h, :], ident_bf[:], r=[tb("M0t", d, par, h), b_ib], w=[bT_])
                for (d, h) in DHS:
                    c = C[d]; pT_, bT_ = bk[(d, h)]
                    ph.cp("scalar", c["A0t"][:, h, :], pT_[:, 0:128], w=[bT_, tb("A0t", d, par, h)])
                for (d, h) in DHS:
                    c = C[d]
                    ph.tt("gpsimd", c["AMb"][:, h, 1, :], c["M0t"][:, h, :], cst[:, 8, :], ALU.mult, r=[tb("M0t", d, par, h), b_cst], w=[tb("AMb", d, par, h)])
                    ph.tt("gpsimd", c["AMb"][:, h, 0, :], c["A0t"][:, h, :], cst[:, 8, :], ALU.mult, r=[tb("A0t", d, par, h), b_cst], w=[tb("AMb", d, par, h)])
                    ph.tt("gpsimd", c["PP"][0][:, h, :], cst[:, 0, :], c["AMb"][:, h, 1, :], ALU.subtract, r=[b_cst, tb("AMb", d, par, h)], w=[tb("P0", d, par, h)])
                for (d, h) in DHS:
                    c = C[d]
                    pA, bA_ = banks.next(); bk[(d, h)] = (pA, bA_)
                    ph.mm(pA[:, 0:128], c["AMb"][:, h, 1, :], c["AMb"][:, h, 0, :], r=[tb("AMb", d, par, h)], w=[bA_])
                    ph.mm(pA[:, 128:256], c["AMb"][:, h, 0, :], c["AMb"][:, h, 1, :], r=[tb("AMb", d, par, h)], w=[bA_])
                for (d, h) in DHS:
                    c = C[d]; pA, bA_ = bk[(d, h)]
                    ph.cp("scalar", c["AM1"][:, h, :, :], pA[:, 0:256].rearrange("p (a b) -> p a b", a=2), w=[bA_, tb("AM1", d, par, h)])
                for (d, h) in DHS:
                    c = C[d]
                    pP, bP_ = banks.next(); bk[(d, h)] = (pP, bP_)
                    ph.mm(pP[:, 0:128], c["AM1"][:, h, 0, :], c["PP"][0][:, h, :], r=[tb("AM1", d, par, h), tb("P0", d, par, h)], w=[bP_])
                for (d, h) in DHS:
                    c = C[d]; pP, bP_ = bk[(d, h)]
                    ph.tt("vector", c["PP"][1][:, h, :], pP[:, 0:128], c["PP"][0][:, h, :], ALU.add, r=[tb("P0", d, par, h)], w=[bP_, tb("P1", d, par, h)])
                for (d, h) in DHS:
                    c = C[d]
                    pA, bA_ = banks.next(); bk[(d, h)] = (pA, bA_)
                    ph.mm(pA[:, 0:128], c["AM1"][:, h, 1, :], c["AM1"][:, h, 0, :], r=[tb("AM1", d, par, h)], w=[bA_])
                for (d, h) in DHS:
                    c = C[d]; pA, bA_ = bk[(d, h)]
                    ph.cp("scalar", c["A2t"][:, h, :], pA[:, 0:128], w=[bA_, tb("A2t", d, par, h)])
                for (d, h) in DHS:
                    c = C[d]
                    pP, bP_ = banks.next(); bk[(d, h)] = (pP, bP_)
                    ph.mm(pP[:, 0:128], c["A2t"][:, h, :], c["PP"][1][:, h, :], r=[tb("A2t", d, par, h), tb("P1", d, par, h)], w=[bP_])
                for (d, h) in DHS:
                    c = C[d]; pP, bP_ = bk[(d, h)]
                    ph.tt("vector", c["PP"][0][:, h, :], pP[:, 0:128], c["PP"][1][:, h, :], ALU.add, r=[tb("P1", d, par, h)], w=[bP_, tb("P0", d, par, h)])
                for mi in range(4):
                    cur = mi % 2
                    nxt = 1 - cur
                    bk2 = {}
                    for (d, h) in DHS:
                        c = C[d]
                        pT_, bT_ = trbank(); bk[(d, h)] = (pT_, bT_)
                        ph.tr(pT_[:, 0:128], c["PP"][cur][:, h, :], ident_bf[:], r=[tb("P%d" % cur, d, par, h), b_ib], w=[bT_])
                    for (d, h) in DHS:
                        c = C[d]; pT_, bT_ = bk[(d, h)]
                        ph.cp("scalar", c["PT"][:, h, :], pT_[:, 0:128], w=[bT_, tb("PT", d, par, h)])
                    for (d, h) in DHS:
                        c = C[d]
                        pX, bX_ = banks.next(); bk2[(d, h)] = (pX, bX_)
                        ph.mm(pX[:, 0:128], c["A0t"][:, h, :], c["PP"][cur][:, h, :], r=[tb("A0t", d, par, h), tb("P%d" % cur, d, par, h)], w=[bX_])
                    for (d, h) in DHS:
                        c = C[d]; pX, bX_ = bk2[(d, h)]
                        ph.tt("vector", c["Xt"][:, h, :], pX[:, 0:128], cst[:, 9 + mi, :], ALU.mult, r=[b_cst], w=[bX_, tb("Xt", d, par, h)])
                    for (d, h) in DHS:
                        c = C[d]
                        pY, bY_ = banks.next(); bk[(d, h)] = (pY, bY_)
                        ph.mm(pY[:, 0:128], c["PT"][:, h, :], c["Xt"][:, h, :], r=[tb("PT", d, par, h), tb("Xt", d, par, h)], w=[bY_])
                    for (d, h) in DHS:
                        c = C[d]; pY, bY_ = bk[(d, h)]
                        ph.tt("vector", c["PP"][nxt][:, h, :], c["PP"][cur][:, h, :], pY[:, 0:128], ALU.subtract,
                              r=[tb("P%d" % cur, d, par, h)], w=[bY_, tb("P%d" % nxt, d, par, h)])
                for (d, h) in DHS:
                    c = C[d]
                    hc = slice(h * 128, (h + 1) * 128)
                    ph.act(c["kegc"][:, h, :], c["kT"][:, hc], AF.Identity, scale=c["sm"][:, 4 + h:5 + h], r=[tb("kT", d, par), tb("sm", d, par)], w=[tb("kegc", d, par, h)])
                    ph.act(c["kst"][:, h, :], c["kT"][:, hc], AF.Identity, scale=c["sm"][:, 12 + h:13 + h], r=[tb("kT", d, par), tb("sm", d, par)], w=[tb("kst", d, par, h)])
                for (d, h) in DHS:
                    c = C[d]
                    pW, bW = banks.next(); bk[(d, h)] = (pW, bW)
                    ph.mm(pW[:, 0:128], c["kegc"][:, h, :], c["PP"][0][:, h, :], r=[tb("kegc", d, par, h), tb("P0", d, par, h)], w=[bW])
                    ph.mm(pW[:, 128:256], c["PP"][0][:, h, :], c["vT"][:, h * 128:(h + 1) * 128], r=[tb("P0", d, par, h), tb("vT", d, par)], w=[bW])
                for (d, h) in DHS:
                    c = C[d]; pW, bW = bk[(d, h)]
                    ph.cp("scalar", c["wTp"][:, h, :], pW[:, 0:128], w=[bW, tb("wTp", d, par, h)])
                    ph.ts("vector", c["u"][:, h, :], pW[:, 128:256], c["b4"][:, h:h + 1], ALU.mult, r=[tb("ab", d, par)], w=[bW, tb("u", d, par, h)])
                return C

            def rec(step, C):
                par = step % 2
                DHS = [(d, h) for d in range(2) for h in range(4)]
                bk = {}
                bk2 = {}
                for d in range(2):
                    C[d]["vnew"] = tl("vnew", d, par, [128, 4, 128], BF16)
                    C[d]["oTs"] = tl("oTs", d, par, [128, 4, 128], BF16)
                    C[d]["nb4"] = C[d]["ab"][:, 16 + d * 4:20 + d * 4]
                for (d, h) in DHS:
                    c = C[d]
                    pW, bW = banks.next(); bk[(d, h)] = (pW, bW)
                    ph.mm(pW[:, 0:128], c["wTp"][:, h, :], Sbf[d][:, h, :], r=[tb("wTp", d, par, h), bSbf[d][h]], w=[bW])
                for (d, h) in DHS:
                    c = C[d]; pW, bW = bk[(d, h)]
                    ph.stt("vector", c["vnew"][:, h, :], pW[:, 0:128], c["nb4"][:, h:h + 1], c["u"][:, h, :], ALU.mult, ALU.add,
                           r=[tb("ab", d, par), tb("u", d, par, h)], w=[bW, tb("vnew", d, par, h)])
                for (d, h) in DHS:
                    c = C[d]
                    pO, bO = banks.next(); bk2[(d, h)] = (pO, bO)
                    ph.mm(pO[:, 0:128], Sbf[d][:, h, :], c["qdecT"][:, h, :], start=True, stop=False, r=[bSbf[d][h], tb("qdecT", d, par, h)], w=[bO])
                    ph.mm(pO[:, 0:128], c["vnew"][:, h, :], c["aqkT"][:, h, :], start=False, stop=True, r=[tb("vnew", d, par, h), tb("aqkT", d, par, h)], w=[bO])
                    ph.mm(pO[:, 128:256], c["kst"][:, h, :], c["vnew"][:, h, :], r=[tb("kst", d, par, h), tb("vnew", d, par, h)], w=[bO])
                for (d, h) in DHS:
                    c = C[d]; pO, bO = bk2[(d, h)]
                    ph.stt("vector", S32[d][:, h, :], S32[d][:, h, :], c["sm"][:, 8 + h:9 + h], pO[:, 128:256], ALU.mult, ALU.add,
                           r=[tb("sm", d, par)], w=[bO, bS32[d][h]])
                    ph.cp("scalar", c["oTs"][:, h, :], pO[:, 0:128], w=[bO, tb("oTs", d, par, h)])
                for (d, h) in DHS:
                    ph.cp("gpsimd", Sbf[d][:, h, :], S32[d][:, h, :], r=[bS32[d][h]], w=[bSbf[d][h]])
                for d in range(2):
                    ph.dma("sync", oTd[d][:, :, C[d]["tc"]].rearrange("h p t -> p h t"), C[d]["oTs"][:], r=[tb("oTs", d, par, h) for h in range(4)])

            prev = prep(0)
            for step in range(NTILE):
                nxt_ = prep(step + 1) if step + 1 < NTILE else None
                rec(step, prev)
                prev = nxt_
            if "sfin" in dbg:
                for d in range(2):
                    ph.dma("sync", sfin[l, d].rearrange("h p t -> p h t"), S32[d][:], r=bS32[d])
        if done("p2b_%d" % l):
            return nc


        with Phase(nc, "p3_%d" % l) as ph:
            cst = ph.sb([128, NCST, 128], F32, "cst"); b_cst = Buf()
            ph.dma("sync", cst[:], cst_d, w=[b_cst])
            ones_bf = ph.sb([128, 128], BF16); b_cb = Buf()
            ph.cp("vector", ones_bf[:], cst[:, 1, :], r=[b_cst], w=[b_cb])
            wa = ph.sb([128, 4, D], BF16, "wa"); wbb = ph.sb([128, 2, D], BF16, "wb"); wc = ph.sb([128, 2, D], BF16, "wc")
            wo = ph.sb([128, KC, D], BF16, "wo"); b_w = Buf()
            ph.dma("gpsimd", wa[:], wbra_d[l].rearrange("(k p) n -> p k n", p=128), w=[b_w])
            ph.dma("gpsimd", wbb[:], wbrb_d[l].rearrange("(k p) n -> p k n", p=128), w=[b_w])
            ph.dma("gpsimd", wc[:], wbrc_d[l].rearrange("(k p) n -> p k n", p=128), w=[b_w])
            ph.dma("gpsimd", wo[:], wo_d[l].rearrange("(k p) n -> p k n", p=128), w=[b_w])
            pw = ph.sb([128, 2, 128], BF16, "pw"); b_pw = Buf()
            ph.memset("vector", pw[:], 0.0, w=[b_pw])
            for c in range(2):
                for half in range(2):
                    ph.dma("gpsimd", pw[half * 64:(half + 1) * 64, c, half * 64:(half + 1) * 64], poolw_d[l, 2 * c + half], w=[b_pw])
            psc_ = ph.sb([128, 2], F32); scw = ph.sb([128, 2, 3], F32); dng = ph.sb([128, 1], F32); b_sm = Buf()
            ph.dma("sync", psc_[:], pscale_d[l], w=[b_sm])
            ph.dma("sync", scw[:], scw_d[l], w=[b_sm])
            ph.dma("sync", dng[:], dng_d[l], w=[b_sm])
            mods = ph.sb([128, 48, 2], F32); b_mods = Buf()
            ph.dma("sync", mods[:], modd[l], w=[b_mods])

            G3 = 512
            ofr = Rot(ph, 1, [128, 4, G3], BF16, "of"); obr = Rot(ph, 1, [128, 4, G3], BF16, "ob"); osr = Rot(ph, 1, [128, 4, G3], F32, "os")
            szr = Rot(ph, 1, [128, 4, G3], BF16, "sz"); onr = Rot(ph, 1, [128, 4, G3], BF16, "on")
            sqr = Rot(ph, 2, [128, G3], BF16, "sq"); rr_ = Rot(ph, 2, [128, G3], F32, "r"); tmr = Rot(ph, 3, [128, G3], F32, "tm")
            dr = Rot(ph, 1, [128, 2, G3], BF16, "d"); ybr = Rot(ph, 1, [128, 2, G3], BF16, "yb")
            scr = Rot(ph, 1, [128, 6, G3 + 2], BF16, "sc"); cxr = Rot(ph, 1, [128, 2, G3 + 2], F32, "cx"); ycr = Rot(ph, 1, [128, 2, G3], BF16, "yc")
            gtr = Rot(ph, 1, [128, 24, G3], BF16, "gt"); xgr = Rot(ph, 1, [128, KC, G3], F32, "xg")
            yr = Rot(ph, 1, [128, KC, G3], BF16, "y"); xor_ = Rot(ph, 1, [128, KC, G3], F32, "xo")
            ps1 = Rot(ph, 2, [128, G3], F32, "ps1", psum=True)
            ps3 = Rot(ph, 5, [128, G3], F32, "ps3", psum=True)
            for (j, t0, G) in token_groups(NT, G3):
                if last and j == 1:
                    continue
                s_lo, s_hi = (0, NCTX) if j == 1 else (NCTX, NTA)
                of_, ofb = ofr.next(); ob_, obb = obr.next(); sz, szb = szr.next(); on, onb = onr.next()
                ph.dma("scalar", of_[:, :, 0:G], oTd[0][:, :, t0:t0 + G].rearrange("h p t -> p h t"), w=[ofb])
                ph.dma("scalar", ob_[:, :, 0:G], oTd[1][:, :, t0:t0 + G].rearrange("h p t -> p h t"), w=[obb])
                ph.dma("sync", sz[:, :, 0:G], szT[:, :, t0:t0 + G].rearrange("h p t -> p h t"), w=[szb])
                osum, osb = osr.next()
                ph.tt("gpsimd", osum[:, :, 0:G], of_[:, :, 0:G], ob_[:, :, 0:G], ALU.add, r=[ofb, obb], w=[osb])
                of_, ofb = osum, osb
                for h in range(4):
                    sq, sqb = sqr.next()
                    ph.act(sq[:, 0:G], of_[:, h, 0:G], AF.Square, r=[ofb], w=[sqb])
                    p1, p1b = ps1.next()
                    ph.mm(p1[:, 0:G], ones_bf[:], sq[:, 0:G], r=[b_cb, sqb], w=[p1b])
                    rr, rb = rr_.next()
                    ph.act(rr[:, 0:G], p1[:, 0:G], AF.Ln, bias=EPS, scale=1.0 / 128, r=[p1b], w=[rb])
                    ph.act(rr[:, 0:G], rr[:, 0:G], AF.Exp, scale=-0.5, r=[rb], w=[rb])
                    tm, tmb = tmr.next()
                    ph.stt("vector", tm[:, 0:G], of_[:, h, 0:G], dng[:, 0:1], rr[:, 0:G], ALU.mult, ALU.mult, r=[ofb, b_sm, rb], w=[tmb])
                    ph.tt("gpsimd", on[:, h, 0:G], tm[:, 0:G], sz[:, h, 0:G], ALU.mult, r=[tmb, szb], w=[onb])
                dd, ddb = dr.next(); yb, ybb = ybr.next()
                ph.dma("sync", dd[:, :, 0:G], dT[:, :, t0:t0 + G].rearrange("c p t -> p c t"), w=[ddb])
                for c in range(2):
                    p1, p1b = ps1.next()
                    ph.mm(p1[:, 0:G], pw[:, c, :], dd[:, c, 0:G], r=[b_pw, ddb], w=[p1b])
                    ph.ts("vector", yb[:, c, 0:G], p1[:, 0:G], psc_[:, c:c + 1], ALU.mult, r=[p1b, b_sm], w=[ybb])
                sc, scb = scr.next(); cx, cxb = cxr.next(); yc, ycb = ycr.next()
                a = max(t0 - 1, s_lo); b = min(t0 + G + 1, s_hi)
                if a > t0 - 1:
                    ph.memset("gpsimd", sc[:, :, 0:1], 0.0, w=[scb])
                if b < t0 + G + 1:
                    ph.memset("gpsimd", sc[:, :, G + 1:G + 2], 0.0, w=[scb])
                ph.dma("sync", sc[:, :, a - (t0 - 1):b - (t0 - 1)], pscT[:, :, a:b].rearrange("c p t -> p c t"), w=[scb])
                ph.tt("gpsimd", cx[:, :, 0:G + 2], sc[:, 4:6, 0:G + 2], sc[:, 0:2, 0:G + 2], ALU.mult, r=[scb], w=[cxb])
                for c in range(2):
                    tm, tmb = tmr.next()
                    ph.act(tm[:, 0:G], cx[:, c, 1:G + 1], AF.Identity, scale=scw[:, c, 1:2], r=[cxb, b_sm], w=[tmb])
                    ph.stt("vector", tm[:, 0:G], cx[:, c, 0:G], scw[:, c, 0:1], tm[:, 0:G], ALU.mult, ALU.add, r=[cxb, b_sm, tmb], w=[tmb])
                    ph.stt("vector", tm[:, 0:G], cx[:, c, 2:G + 2], scw[:, c, 2:3], tm[:, 0:G], ALU.mult, ALU.add, r=[cxb, b_sm, tmb], w=[tmb])
                    ph.tt("gpsimd", yc[:, c, 0:G], tm[:, 0:G], sc[:, 2 + c, 1:G + 1], ALU.mult, r=[tmb, scb], w=[ycb])
                gt, gtb = gtr.next(); xg, xb = xgr.next(); y, yb_ = yr.next(); xo, xob = xor_.next()
                ph.dma("scalar", gt[:, :, 0:G], gatesT[:, :, t0:t0 + G].rearrange("c p t -> p c t"), w=[gtb])
                ph.dma("sync", xg[:, :, 0:G], xT[:, :, t0:t0 + G].rearrange("k p t -> p k t"), w=[xb])
                for m in range(KC):
                    mc = slice(m * 128, (m + 1) * 128)
                    pa, pab = ps3.next(); pb_, pbb = ps3.next(); pc, pcb = ps3.next()
                    for h in range(4):
                        ph.mm(pa[:, 0:G], wa[:, h, mc], on[:, h, 0:G], start=(h == 0), stop=(h == 3), r=[b_w, onb], w=[pab])
                    for c in range(2):
                        ph.mm(pb_[:, 0:G], wbb[:, c, mc], yb[:, c, 0:G], start=(c == 0), stop=(c == 1), r=[b_w, ybb], w=[pbb])
                    for c in range(2):
                        ph.mm(pc[:, 0:G], wc[:, c, mc], yc[:, c, 0:G], start=(c == 0), stop=(c == 1), r=[b_w, ycb], w=[pcb])
                    t1, t1b = tmr.next(); t2, t2b = tmr.next(); t3, t3b = tmr.next()
                    ph.tt("vector", t1[:, 0:G], pa[:, 0:G], gt[:, m, 0:G], ALU.mult, r=[pab, gtb], w=[t1b])
                    ph.tt("vector", t2[:, 0:G], pb_[:, 0:G], gt[:, 8 + m, 0:G], ALU.mult, r=[pbb, gtb], w=[t2b])
                    ph.tt("vector", t3[:, 0:G], pc[:, 0:G], gt[:, 16 + m, 0:G], ALU.mult, r=[pcb, gtb], w=[t3b])
                    ph.tt("gpsimd", t1[:, 0:G], t1[:, 0:G], t2[:, 0:G], ALU.add, r=[t1b, t2b], w=[t1b])
                    ph.tt("gpsimd", y[:, m, 0:G], t1[:, 0:G], t3[:, 0:G], ALU.add, r=[t1b, t3b], w=[yb_])
                for m in range(KC):
                    mc = slice(m * 128, (m + 1) * 128)
                    pa, pab = ps3.next()
                    for k in range(KC):
                        ph.mm(pa[:, 0:G], wo[:, k, mc], y[:, k, 0:G], start=(k == 0), stop=(k == KC - 1), r=[b_w, yb_], w=[pab])
                    ph.stt("vector", xo[:, m, 0:G], pa[:, 0:G], mods[:, 16 + m, j:j + 1], xg[:, m, 0:G], ALU.mult, ALU.add,
                           r=[pab, b_mods, xb], w=[xob])
                ph.dma("sync", xT[:, :, t0:t0 + G].rearrange("k p t -> p k t"), xo[:, :, 0:G], r=[xob])
        if done("p3_%d" % l):
            return nc

        with Phase(nc, "p4_%d" % l) as ph:
            onesf = ph.sb([128, 128], F32, "onesf"); b_cst = Buf()
            ph.dma("sync", onesf[:], cst_d[:, 1, :], w=[b_cst])
            ones_bf = ph.sb([128, 128], BF16); b_cb = Buf()
            ph.cp("vector", ones_bf[:], onesf[:], r=[b_cst], w=[b_cb])
            wgu = ph.sb([128, KC, 2 * DFF], BF16, "wgu"); wdn = ph.sb([128, FC, D], BF16, "wdn"); b_wg = [Buf() for _ in range(KC)]; b_wd = Buf()
            for k in range(KC):
                ph.dma("gpsimd", wgu[:, k, :], wgu_d[l, k * 128:(k + 1) * 128, :], w=[b_wg[k]])
            for f0 in range(0, FC, 11):
                ph.dma("gpsimd", wdn[:, f0:f0 + 11, :], wdn_d[l, f0 * 128:(f0 + 11) * 128, :].rearrange("(f p) n -> p f n", p=128), w=[b_wd])
            mods = ph.sb([128, 48, 2], F32); b_mods = Buf()
            ph.dma("sync", mods[:], modd[l], w=[b_mods])
            n2g = ph.sb([128, KC], F32); b_n2g = Buf()
            ph.dma("sync", n2g[:], n2g_d[l], w=[b_n2g])
            A2 = ph.sb([128, 2, KC], F32); b_A2 = Buf()
            for j in range(2):
                ph.stt("vector", A2[:, j, :], mods[:, 32:40, j], 1.0, n2g[:], ALU.add, ALU.mult, r=[b_mods, b_n2g], w=[b_A2])
            G4 = 512
            xgr = Rot(ph, 2, [128, KC, G4], F32, "xg")
            hr = Rot(ph, 1, [128, KC, G4], BF16, "hT"); tmr = Rot(ph, 2, [128, G4], F32, "tm"); rsr = Rot(ph, 1, [128, G4], F32, "rs")
            acr = Rot(ph, 1, [128, FC, G4], BF16, "act")
            ssp = Rot(ph, 1, [128, G4], F32, "ssp", psum=True)
            gup = Rot(ph, 6, [128, G4], F32, "gup", psum=True)
            for (j, t0, G) in token_groups(NT, G4):
                if last and j == 1:
                    continue
                xg, xb = xgr.next()
                ph.dma("sync", xg[:, :, 0:G], xT[:, :, t0:t0 + G].rearrange("k p t -> p k t"), w=[xb])
                ac, acb = acr.next()
                sq, sqb = ac, acb
                ph.act(sq[:, 0:KC, 0:G], xg[:, :, 0:G], AF.Square, r=[xb], w=[sqb])
                sp, spb = ssp.next()
                for k in range(KC):
                    ph.mm(sp[:, 0:G], ones_bf[:], sq[:, k, 0:G], start=(k == 0), stop=(k == KC - 1), r=[b_cb, sqb], w=[spb])
                rs, rsb = rsr.next()
                ph.act(rs[:, 0:G], sp[:, 0:G], AF.Ln, bias=EPS, scale=1.0 / D, r=[spb], w=[rsb])
                ph.act(rs[:, 0:G], rs[:, 0:G], AF.Exp, scale=-0.5, r=[rsb], w=[rsb])
                hT, hb = hr.next()
                for k in range(KC):
                    tm, tmb = tmr.next()
                    ph.stt("vector", tm[:, 0:G], xg[:, k, 0:G], A2[:, j, k:k + 1], rs[:, 0:G], ALU.mult, ALU.mult, r=[xb, b_A2, rsb], w=[tmb])
                    ph.act(hT[:, k, 0:G], tm[:, 0:G], AF.Identity, bias=mods[:, 24 + k, j:j + 1], r=[tmb, b_mods], w=[hb])
                for f in range(FC):
                    pg, pgb = gup.next(); pu, pub = gup.next()
                    for k in range(KC):
                        ph.mm(pg[:, 0:G], wgu[:, k, f * 128:(f + 1) * 128], hT[:, k, 0:G], start=(k == 0), stop=(k == KC - 1), r=[b_wg[k], hb], w=[pgb])
                    for k in range(KC):
                        ph.mm(pu[:, 0:G], wgu[:, k, DFF + f * 128:DFF + (f + 1) * 128], hT[:, k, 0:G], start=(k == 0), stop=(k == KC - 1), r=[b_wg[k], hb], w=[pub])
                    tm, tmb = tmr.next()
                    ph.act(tm[:, 0:G], pg[:, 0:G], AF.Silu, r=[pgb], w=[tmb])
                    ph.tt("vector", ac[:, f, 0:G], pu[:, 0:G], tm[:, 0:G], ALU.mult, r=[pub, tmb], w=[acb])
                for m in range(KC):
                    pd, pdb = gup.next()
                    for f in range(FC):
                        ph.mm(pd[:, 0:G], wdn[:, f, m * 128:(m + 1) * 128], ac[:, f, 0:G], start=(f == 0), stop=(f == FC - 1), r=[b_wd, acb], w=[pdb])
                    ph.stt("vector", xg[:, m, 0:G], pd[:, 0:G], mods[:, 40 + m, j:j + 1], xg[:, m, 0:G], ALU.mult, ALU.add,
                           r=[pdb, b_mods], w=[xb])
                ph.dma("sync", xT[:, :, t0:t0 + G].rearrange("k p t -> p k t"), xg[:, :, 0:G], r=[xb])
        if done("p4_%d" % l):
            return nc

    with Phase(nc, "pf") as ph:
        cst = ph.sb([128, NCST, 128], F32, "cst"); b_cst = Buf()
        ph.dma("sync", cst[:], cst_d, w=[b_cst])
        ones_bf = ph.sb([128, 128], BF16); b_cb = Buf()
        ph.cp("vector", ones_bf[:], cst[:, 1, :], r=[b_cst], w=[b_cb])
        fng = ph.sb([128, KC], F32); b_fng = Buf()
        ph.dma("sync", fng[:], fng_d, w=[b_fng])
        GF = 512
        xgr = Rot(ph, 2, [128, KC, GF], F32, "xg"); sqr = Rot(ph, 1, [128, KC, GF], BF16, "sq"); rsr = Rot(ph, 1, [128, GF], F32, "rs")
        yr = Rot(ph, 2, [128, KC, GF], F32, "y"); otr = Rot(ph, 3, [128, D], F32, "ot")
        ssp = Rot(ph, 1, [128, GF], F32, "ssp", psum=True)
        trp = Rot(ph, 3, [128, KC, 128], F32, "trp", psum=True)
        out_waits = []
        for (j, t0, G) in token_groups(NT, GF):
            if j == 1:
                continue
            xg, xb = xgr.next()
            ph.dma("sync", xg[:], xT[:, :, t0:t0 + G].rearrange("k p t -> p k t"), w=[xb])
            sq, sqb = sqr.next()
            ph.act(sq[:], xg[:], AF.Square, r=[xb], w=[sqb])
            sp, spb = ssp.next()
            for k in range(KC):
                ph.mm(sp[:], ones_bf[:], sq[:, k, :], start=(k == 0), stop=(k == KC - 1), r=[b_cb, sqb], w=[spb])
            rs, rsb = rsr.next()
            ph.act(rs[:], sp[:], AF.Ln, bias=EPS, scale=1.0 / D, r=[spb], w=[rsb])
            ph.act(rs[:], rs[:], AF.Exp, scale=-0.5, r=[rsb], w=[rsb])
            y, yb = yr.next()
            for k in range(KC):
                ph.stt("vector", y[:, k, :], xg[:, k, :], fng[:, k:k + 1], rs[:], ALU.mult, ALU.mult,
                       r=[xb, b_fng, rsb], w=[yb])
            for s_ in range(G // 128):
                tp, tpb = trp.next()
                for k in range(KC):
                    ph.mm(tp[:, k, :], y[:, k, s_ * 128:(s_ + 1) * 128], cst[:, 0, :], r=[yb, b_cst], w=[tpb])
                ot, otb = otr.next()
                ph.cp("scalar" if s_ % 2 == 0 else "vector", ot[:].rearrange("p (k f) -> p k f", k=KC), tp[:], r=[tpb], w=[otb])
                ph.dma("scalar" if s_ % 2 == 0 else "sync", out_d[t0 - NCTX + s_ * 128:t0 - NCTX + (s_ + 1) * 128, :], ot[:], r=[otb])

    return nc


POOL_WINDOWS = (2, 4, 8, 16)


def _consts(NT):
    idx = np.arange(128)
    cst = np.zeros((128, NCST, 128), np.float32)
    cst[:, 0, :] = np.eye(128)
    cst[:, 1, :] = 1.0
    cst[:, 2, :] = (idx[:, None] <= idx[None, :])
    cst[:, 3, :] = (idx[:, None] >= idx[None, :])
    cst[:, 4, :] = np.where(idx[None, :] >= idx[:, None], 0.0, NEG)
    cst[:, 5, :] = np.where(idx[None, :] <= idx[:, None], 0.0, NEG)
    cst[:, 6, :] = (idx[None, :] > idx[:, None])
    cst[:, 7, :] = (idx[None, :] < idx[:, None])

    def blk(sz):
        return (idx[:, None] // sz == idx[None, :] // sz).astype(np.float32)
    cst[:, 8, :] = blk(8)
    for n, sz in enumerate((8, 16, 32, 64)):
        cst[:, 9 + n, :] = blk(2 * sz) - blk(sz)

    def cnt1d(n, w):
        lo = w // 2
        hi = w - 1 - lo
        pos = np.arange(n)
        return (np.clip(pos + hi + 1, 0, n) - np.clip(pos - lo, 0, n)).astype(np.float64)
    rows = NT // GW
    cnt_lat = np.zeros((2, 128, NT), np.float32)
    cnt_ctx = np.zeros((2, 128, NCTX), np.float32)
    for c in range(2):
        for half in range(2):
            w = POOL_WINDOWS[2 * c + half]
            cl = 1.0 / (cnt1d(rows, w)[:, None] * cnt1d(GW, w)[None, :])
            cnt_lat[c, half * 64:(half + 1) * 64, :] = cl.reshape(1, NT)
            cnt_ctx[c, half * 64:(half + 1) * 64, :] = (1.0 / cnt1d(NCTX, w))[None, :]
    return cst, cnt_lat, cnt_ctx


def col(v):
    v = np.asarray(v, np.float32)
    return np.ascontiguousarray(v.reshape(-1, 128).T)


def prep_shared(inp, NT):
    f = lambda a: np.ascontiguousarray(np.asarray(a, np.float32))
    cst, cnt_lat, cnt_ctx = _consts(NT)
    sh = {
        "w_ada": f(inp["w_ada"]),
        "b_adaT": np.stack([col(inp["b_ada"][l]) for l in range(2)]),
        "n1g": np.stack([col(inp["norm1_g"][l]) for l in range(2)]),
        "n2g": np.stack([col(inp["norm2_g"][l]) for l in range(2)]),
        "fng": col(inp["final_norm_g"]),
        "w_in": f(inp["w_in"]),
        "dcw": np.stack([np.stack([col(np.asarray(inp["dn_conv_w"])[l, t]) for t in range(3)], axis=-1) for l in range(2)]),
        "alog": np.stack([np.broadcast_to(np.asarray(inp["dn_a_log"], np.float32)[l].reshape(1, 8), (128, 8)) for l in range(2)]).copy(),
        "dtb": np.stack([np.broadcast_to(np.asarray(inp["dn_dt_bias"], np.float32)[l].reshape(1, 8), (128, 8)) for l in range(2)]).copy(),
        "dng": np.asarray(inp["dn_norm_g"], np.float32).reshape(2, 128, 1).copy(),
        "poolw": f(inp["pool_w"]),
        "pscale": np.stack([col(inp["pool_scale"][l]) for l in range(2)]),
        "scw": np.stack([np.stack([col(np.asarray(inp["sc_conv_w"])[l, t]) for t in range(3)], axis=-1) for l in range(2)]),
        "w_br_a": f(inp["w_br_a"]), "w_br_b": f(inp["w_br_b"]), "w_br_c": f(inp["w_br_c"]),
        "w_o": f(inp["w_o"]), "w_gu": f(inp["w_gu"]), "w_down": f(inp["w_down"]),
        "cst": cst, "cnt_lat": cnt_lat, "cnt_ctx": cnt_ctx,
    }
    return sh


def prep_core(inp, sh, b):
    m = dict(sh)
    m["x"] = np.ascontiguousarray(np.asarray(inp["x"], np.float32)[b])
    m["ctx"] = np.ascontiguousarray(np.asarray(inp["ctx"], np.float32)[b])
    m["cc"] = np.ascontiguousarray(np.stack([col(np.asarray(inp["c"])[b]), col(inp["c_ctx"])], axis=-1))
    return m


def kernel(**inputs):
    x = np.asarray(inputs["x"])
    B, NT, _ = x.shape
    nc = build(NT)
    sh = prep_shared(inputs, NT)
    in_maps = [prep_core(inputs, sh, c % B) for c in range(8)]
    res = run_bass_kernel_spmd(nc, in_maps, core_ids=list(range(8)))
    return np.stack([res.results[b]["out"] for b in range(B)]).astype(np.float32)
```

```python
import numpy as np
import ml_dtypes
from contextlib import ExitStack
import concourse.bass as bass
import concourse.mybir as mybir
from concourse.bass_utils import run_bass_kernel_spmd

F32 = mybir.dt.float32
BF16 = mybir.dt.bfloat16
AF = mybir.ActivationFunctionType
ALU = mybir.AluOpType

D = 1024
KC = 8
NCTX = 256
GW = 64
DFF = 2816
FC = DFF // 128
N_IN = 6160
EPS = 1e-6
NEG = -30000.0
ENGS = ("tensor", "vector", "scalar", "gpsimd", "sync")
NDS = 12
NCST = 13


class Buf:
    __slots__ = ("w", "r")

    def __init__(self):
        self.w = None
        self.r = []


class Sched:
    def __init__(self, nc, pname=""):
        self.nc = nc
        self.pname = pname
        self.ops = {e: [] for e in ENGS}
        self.keys = list(ENGS) + ["dma_%s_%d" % (q, i) for q in ("sync", "gpsimd", "scalar") for i in range(NDS)]
        self.cnt = {k: 0 for k in self.keys}
        self.seen = {e: {k: 0 for k in self.keys} for e in ENGS}
        self.dma_rr = {"sync": 0, "gpsimd": 0, "scalar": 0}
        self.sems = {}

    def alloc(self):
        for k in self.keys:
            self.sems[k] = self.nc.alloc_semaphore(name="s_%s_%s" % (self.pname, k))

    def _deps(self, eng, reads, writes):
        need = {}

        def add(tok):
            if tok is None:
                return
            p, c = tok
            if need.get(p, 0) < c:
                need[p] = c
        for b in reads:
            add(b.w)
        for b in writes:
            add(b.w)
            for t in b.r:
                add(t)
        waits = []
        for p, c in need.items():
            if eng == "tensor" and p == "tensor":
                continue
            if self.seen[eng][p] < c:
                self.seen[eng][p] = c
                waits.append((p, c))
        return waits

    def _commit(self, tok, reads, writes):
        for b in reads:
            b.r.append(tok)
        for b in writes:
            b.w = tok
            b.r = []

    def op(self, eng, fn, reads=(), writes=()):
        waits = self._deps(eng, reads, writes)
        self.cnt[eng] += 1
        tok = (eng, self.cnt[eng])
        self.ops[eng].append((waits, fn, (eng, 1)))
        self._commit(tok, reads, writes)
        return tok

    def dma(self, q, fn, reads=(), writes=()):
        waits = self._deps(q, reads, writes)
        key = "dma_%s_%d" % (q, self.dma_rr[q])
        self.dma_rr[q] = (self.dma_rr[q] + 1) % NDS
        if self.cnt[key] > 0 and self.seen[q][key] < self.cnt[key]:
            self.seen[q][key] = self.cnt[key]
            waits.append((key, self.cnt[key]))
        self.cnt[key] += 1
        tok = (key, self.cnt[key])
        self.ops[q].append((waits, fn, (key, 16)))
        self._commit(tok, reads, writes)
        return tok

    def emit(self):
        nc = self.nc
        mult = {k: (16 if k.startswith("dma_") else 1) for k in self.keys}
        with nc.Block() as block:
            def run(engname):
                def body(e):
                    for waits, fn, (sk, inc) in self.ops[engname]:
                        for p, c in waits:
                            e.wait_ge(self.sems[p], c * mult[p])
                        fn(e).then_inc(self.sems[sk], inc)
                    if engname == "sync":
                        for k in self.keys:
                            if self.cnt[k] > 0:
                                e.wait_ge(self.sems[k], self.cnt[k] * mult[k])
                return body
            block.tensor(run("tensor"))
            block.vector(run("vector"))
            block.scalar(run("scalar"))
            block.gpsimd(run("gpsimd"))
            block.sync(run("sync"))


class Phase:
    def __init__(self, nc, name):
        self.nc = nc
        self.name = name
        self.es = ExitStack()
        self.S = Sched(nc, name)
        self.n = 0

    def __enter__(self):
        self.es.enter_context(self.nc.cleanup_on_exit())
        self.S.alloc()
        return self

    def __exit__(self, *a):
        if a[0] is None:
            self.S.emit()
        self.es.close()
        return False

    def sb(self, shape, dt, name=None):
        self.n += 1
        return self.es.enter_context(self.nc.sbuf_tensor("%s_%s%d" % (self.name, name or "t", self.n), list(shape), dt))

    def ps(self, shape, dt, name=None):
        self.n += 1
        return self.es.enter_context(self.nc.psum_tensor("%s_%s%d" % (self.name, name or "p", self.n), list(shape), dt))

    def dma(self, q, out, in_, r=(), w=()):
        return self.S.dma(q, lambda e: e.dma_start(out=out, in_=in_), r, w)

    def mm(self, out, lhsT, rhs, start=True, stop=True, r=(), w=()):
        return self.S.op("tensor", lambda e: e.matmul(out, lhsT=lhsT, rhs=rhs, start=start, stop=stop), r, w)

    def tr(self, out, in_, ident, r=(), w=()):
        return self.S.op("tensor", lambda e: e.transpose(out, in_, ident), r, w)

    def act(self, out, in_, func, bias=None, scale=None, r=(), w=()):
        kw = {}
        if bias is not None:
            kw["bias"] = bias
        if scale is not None:
            kw["scale"] = scale
        return self.S.op("scalar", lambda e: e.activation(out=out, in_=in_, func=func, **kw), r, w)

    def tt(self, eng, out, in0, in1, op, r=(), w=()):
        return self.S.op(eng, lambda e: e.tensor_tensor(out=out, in0=in0, in1=in1, op=op), r, w)

    def ts(self, eng, out, in0, s1, op0, s2=None, op1=None, r=(), w=()):
        if op1 is None:
            return self.S.op(eng, lambda e: e.tensor_scalar(out=out, in0=in0, scalar1=s1, scalar2=None, op0=op0), r, w)
        return self.S.op(eng, lambda e: e.tensor_scalar(out=out, in0=in0, scalar1=s1, scalar2=s2, op0=op0, op1=op1), r, w)

    def stt(self, eng, out, in0, scalar, in1, op0, op1, r=(), w=()):
        return self.S.op(eng, lambda e: e.scalar_tensor_tensor(out=out, in0=in0, scalar=scalar, in1=in1, op0=op0, op1=op1), r, w)

    def cp(self, eng, out, in_, r=(), w=()):
        if eng == "scalar":
            return self.S.op(eng, lambda e: e.copy(out=out, in_=in_), r, w)
        return self.S.op(eng, lambda e: e.tensor_copy(out=out, in_=in_), r, w)

    def memset(self, eng, ap, val, r=(), w=()):
        return self.S.op(eng, lambda e: e.memset(ap, val), r, w)


class Rot:
    def __init__(self, ph, n, shape, dt, name, psum=False):
        mk = ph.ps if psum else ph.sb
        self.t = [mk(shape, dt, name) for _ in range(n)]
        self.b = [Buf() for _ in range(n)]
        self.i = -1

    def next(self):
        self.i = (self.i + 1) % len(self.t)
        return self.t[self.i], self.b[self.i]


def token_groups(NT, G):
    gs = [(1, 0, NCTX)]
    t = NCTX
    while t < NCTX + NT:
        gs.append((0, t, G))
        t += G
    return gs


def build(NT, stop_after=None, dbg=()):
    NTA = NCTX + NT
    NTILE = NTA // 128
    ROWS = NT // GW
    nc = bass.Bass("TRN2", target_bir_lowering=False)

    def din(name, shape, dt=F32):
        return nc.dram_tensor(name, list(shape), dt, kind="ExternalInput").ap()

    def scratch(name, shape, dt=F32):
        kind = "ExternalOutput" if name in dbg else "Internal"
        return nc.dram_tensor(name, list(shape), dt, kind=kind).ap()

    x_d = din("x", [NT, D]); ctx_d = din("ctx", [NCTX, D]); cc_d = din("cc", [128, KC, 2])
    w_ada_d = din("w_ada", [2, D, 6 * D]); b_adaT_d = din("b_adaT", [2, 128, 48])
    n1g_d = din("n1g", [2, 128, KC]); n2g_d = din("n2g", [2, 128, KC]); fng_d = din("fng", [128, KC])
    w_in_d = din("w_in", [2, D, N_IN]); dcw_d = din("dcw", [2, 128, 12, 3])
    alog_d = din("alog", [2, 128, 8]); dtb_d = din("dtb", [2, 128, 8]); dng_d = din("dng", [2, 128, 1])
    poolw_d = din("poolw", [2, 4, 64, 64]); pscale_d = din("pscale", [2, 128, 2]); scw_d = din("scw", [2, 128, 2, 3])
    wbra_d = din("w_br_a", [2, 512, D]); wbrb_d = din("w_br_b", [2, 256, D]); wbrc_d = din("w_br_c", [2, 256, D])
    wo_d = din("w_o", [2, D, D]); wgu_d = din("w_gu", [2, D, 2 * DFF]); wdn_d = din("w_down", [2, DFF, D])
    cst_d = din("cst", [128, NCST, 128]); cntl_d = din("cnt_lat", [2, 128, NT]); cntc_d = din("cnt_ctx", [2, 128, NCTX])
    out_d = nc.dram_tensor("out", [NT, D], F32, kind="ExternalOutput").ap()

    xT = scratch("xT", [KC, 128, NTA])
    pqkvT = scratch("pqkvT", [12, 128, NTA], BF16); szT = scratch("szT", [4, 128, NTA], BF16)
    ppoolT = scratch("ppoolT", [2, 128, NTA], BF16); pscT = scratch("pscT", [6, 128, NTA], BF16)
    gatesT = scratch("gatesT", [24, 128, NTA], BF16)
    abS = scratch("abS", [NTILE, 128, 24])
    qnT = scratch("qnT", [4, 128, NTA], BF16); knT = scratch("knT", [4, 128, NTA], BF16)
    kTM = scratch("kTM", [NTILE, 128, 512], BF16); vTM = scratch("vTM", [NTILE, 128, 512], BF16)
    oTd = [scratch("oTf", [4, 128, NTA], BF16), scratch("oTb", [4, 128, NTA], BF16)]
    dT = scratch("dT", [2, 128, NTA], BF16)
    modd = scratch("modd", [2, 128, 6 * KC, 2])
    sfin = scratch("sfin", [2, 2, 4, 128, 128])
    dbgbuf = scratch("dbgbuf", [2, 128, 8, 128])

    def done(tag):
        return stop_after == tag

    with Phase(nc, "p0") as ph:
        cst = ph.sb([128, NCST, 128], F32, "cst"); b_cst = Buf()
        ph.dma("sync", cst[:], cst_d, w=[b_cst])
        cc = ph.sb([128, KC, 2], F32); b_cc = Buf()
        ph.dma("sync", cc[:], cc_d, w=[b_cc])
        scc = ph.sb([128, KC, 2], F32); b_scc = Buf()
        ph.act(scc[:], cc[:], AF.Silu, r=[b_cc], w=[b_scc])
        wrot = Rot(ph, 2, [128, KC, 768], F32, "wada")
        modps = ph.ps([128, 48, 2], F32); b_modps = Buf()
        for l in range(2):
            badaT = ph.sb([128, 48], F32); b_bada = Buf()
            ph.dma("sync", badaT[:], b_adaT_d[l], w=[b_bada])
            for pc in range(8):
                wt, wb = wrot.next()
                ph.dma("sync" if pc % 2 == 0 else "scalar", wt[:], w_ada_d[l, :, pc * 768:(pc + 1) * 768].rearrange("(k p) n -> p k n", p=128), w=[wb])
                for mi in range(6):
                    m = pc * 6 + mi
                    for k in range(KC):
                        ph.mm(modps[:, m, :], wt[:, k, mi * 128:(mi + 1) * 128], scc[:, k, :], start=(k == 0), stop=(k == KC - 1),
                              r=[wb, b_scc], w=[b_modps])
            mods = ph.sb([128, 48, 2], F32); b_mods = Buf()
            for j in range(2):
                ph.tt("vector", mods[:, :, j], modps[:, :, j], badaT[:], ALU.add, r=[b_modps, b_bada], w=[b_mods])
            ph.dma("sync", modd[l], mods[:], r=[b_mods])
        xrot = Rot(ph, 3, [128, D], F32, "xin")
        trps = Rot(ph, 2, [128, KC, 128], F32, "trps", psum=True)
        orot = Rot(ph, 3, [128, KC, 128], F32, "xo")
        for ti in range(NTILE):
            xt, xb = xrot.next()
            src = ctx_d[ti * 128:(ti + 1) * 128, :] if ti < NCTX // 128 else x_d[ti * 128 - NCTX:(ti + 1) * 128 - NCTX, :]
            ph.dma("sync", xt[:], src, w=[xb])
            pt, pb = trps.next()
            for k in range(KC):
                ph.mm(pt[:, k, :], xt[:, k * 128:(k + 1) * 128], cst[:, 0, :], r=[xb, b_cst], w=[pb])
            ot, ob = orot.next()
            ph.cp("vector" if ti % 2 == 0 else "scalar", ot[:], pt[:], r=[pb], w=[ob])
            ph.dma("sync", xT[:, :, ti * 128:(ti + 1) * 128].rearrange("k p t -> p k t"), ot[:], r=[ob])
    if done("p0"):
        return nc

    for l in range(2):
        last = (l == 1)
        with Phase(nc, "p1_%d" % l) as ph:
            cst = ph.sb([128, NCST, 128], F32, "cst"); b_cst = Buf()
            ph.dma("sync", cst[:], cst_d, w=[b_cst])
            ones_bf = ph.sb([128, 128], BF16); b_ones = Buf()
            ph.cp("vector", ones_bf[:], cst[:, 1, :], r=[b_cst], w=[b_ones])
            win = ph.sb([128, KC, N_IN], BF16, "win"); b_win = [Buf() for _ in range(KC)]
            for k in range(KC):
                ph.dma("gpsimd", win[:, k, :], w_in_d[l, k * 128:(k + 1) * 128, :], w=[b_win[k]])
            mods = ph.sb([128, 48, 2], F32); b_mods = Buf()
            ph.dma("sync", mods[:], modd[l], w=[b_mods])
            n1g = ph.sb([128, KC], F32); b_n1g = Buf()
            ph.dma("sync", n1g[:], n1g_d[l], w=[b_n1g])
            A1 = ph.sb([128, 2, KC], F32); b_A1 = Buf()
            for j in range(2):
                ph.stt("vector", A1[:, j, :], mods[:, 8:16, j], 1.0, n1g[:], ALU.add, ALU.mult, r=[b_mods, b_n1g], w=[b_A1])
            alog = ph.sb([128, 8], F32); dtb = ph.sb([128, 8], F32); b_al = Buf(); b_dtb = Buf()
            ph.dma("sync", alog[:], alog_d[l], w=[b_al])
            ph.dma("sync", dtb[:], dtb_d[l], w=[b_dtb])
            negea = ph.sb([128, 8], F32); b_negea = Buf()
            ph.act(negea[:], alog[:], AF.Exp, r=[b_al], w=[b_negea])
            ph.ts("vector", negea[:], negea[:], -1.0, ALU.mult, r=[b_negea], w=[b_negea])

            xrot = Rot(ph, 2, [128, KC, 512], F32, "xg")
            sqrot = Rot(ph, 1, [128, KC, 512], BF16, "sq")
            hrot = Rot(ph, 2, [128, KC, 512], BF16, "hT")
            tmprot = Rot(ph, 2, [128, 512], F32, "tmp")
            rsrot = Rot(ph, 2, [128, 512], F32, "rstd")
            stf = Rot(ph, 4, [128, 512], BF16, "stf")
            stb = Rot(ph, 4, [128, 512], BF16, "stb")
            abrot = Rot(ph, 2, [128, 24], F32, "ab")
            abt = Rot(ph, 2, [128, 8], F32, "abt")
            ssps = Rot(ph, 1, [128, 512], F32, "ssps", psum=True)
            accps = Rot(ph, 5, [128, 512], F32, "acc", psum=True)
            abps = Rot(ph, 2, [128, 16], F32, "abps", psum=True)

            chunks = []
            for c in range(12):
                chunks.append((c * 128, "copy", pqkvT, c))
            for c in range(4):
                chunks.append((1536 + c * 128, "silu", szT, c))
            for c in range(2):
                chunks.append((2064 + c * 128, "copy", ppoolT, c))
            for c in range(6):
                chunks.append((2320 + c * 128, "copy", pscT, c))
            for c in range(24):
                chunks.append((3088 + c * 128, "sigm", gatesT, c))

            for (j, t0, G) in token_groups(NT, 512):
                xg, xb = xrot.next()
                ph.dma("sync", xg[:, :, 0:G], xT[:, :, t0:t0 + G].rearrange("k p t -> p k t"), w=[xb])
                sq, sqb = sqrot.next()
                ph.act(sq[:, :, 0:G], xg[:, :, 0:G], AF.Square, r=[xb], w=[sqb])
                sp, spb = ssps.next()
                for k in range(KC):
                    ph.mm(sp[:, 0:G], ones_bf[:], sq[:, k, 0:G], start=(k == 0), stop=(k == KC - 1), r=[b_ones, sqb], w=[spb])
                rs, rsb = rsrot.next()
                ph.act(rs[:, 0:G], sp[:, 0:G], AF.Ln, bias=EPS, scale=1.0 / D, r=[spb], w=[rsb])
                ph.act(rs[:, 0:G], rs[:, 0:G], AF.Exp, scale=-0.5, r=[rsb], w=[rsb])
                hT, hb = hrot.next()
                for k in range(KC):
                    tm, tmb = tmprot.next()
                    ph.stt("vector", tm[:, 0:G], xg[:, k, 0:G], A1[:, j, k:k + 1], rs[:, 0:G], ALU.mult, ALU.mult,
                           r=[xb, b_A1, rsb], w=[tmb])
                    ph.act(hT[:, k, 0:G], tm[:, 0:G], AF.Identity, bias=mods[:, k, j:j + 1], r=[tmb, b_mods], w=[hb])
                for s in range(G // 128):
                    ap_, apb = abps.next()
                    for k in range(KC):
                        ph.mm(ap_[:], hT[:, k, s * 128:(s + 1) * 128], win[:, k, 2048:2064], start=(k == 0), stop=(k == KC - 1),
                              r=[hb, b_win[k]], w=[apb])
                    ab, abb = abrot.next()
                    at, atb = abt.next()
                    ph.tt("vector", at[:], ap_[:, 0:8], dtb[:], ALU.add, r=[apb, b_dtb], w=[atb])
                    ph.act(at[:], at[:], AF.Exp, r=[atb], w=[atb])
                    ph.act(at[:], at[:], AF.Ln, bias=1.0, r=[atb], w=[atb])
                    ph.tt("vector", ab[:, 0:8], at[:], negea[:], ALU.mult, r=[atb, b_negea], w=[abb])
                    ph.act(ab[:, 8:16], ap_[:, 8:16], AF.Sigmoid, w=[apb, abb])
                    ph.ts("vector", ab[:, 16:24], ab[:, 8:16], -1.0, ALU.mult, r=[abb], w=[abb])
                    ph.dma("sync", abS[(t0 // 128) + s], ab[:], r=[abb])
                for ci, (c0, kind, dst, dc) in enumerate(chunks):
                    acc, accb = accps.next()
                    for k in range(KC):
                        ph.mm(acc[:, 0:G], win[:, k, c0:c0 + 128], hT[:, k, 0:G], start=(k == 0), stop=(k == KC - 1),
                              r=[hb, b_win[k]], w=[accb])
                    if kind == "copy":
                        st, sb_ = stf.next()
                        ph.cp("vector", st[:, 0:G], acc[:, 0:G], r=[accb], w=[sb_])
                        ph.dma("sync", dst[dc, :, t0:t0 + G], st[:, 0:G], r=[sb_])
                    else:
                        st, sb_ = stb.next()
                        ph.act(st[:, 0:G], acc[:, 0:G], AF.Silu if kind == "silu" else AF.Sigmoid, r=[accb], w=[sb_])
                        ph.dma("scalar", dst[dc, :, t0:t0 + G], st[:, 0:G], r=[sb_])
        if done("p1_%d" % l):
            return nc


        with Phase(nc, "p2a_%d" % l) as ph:
            cst = ph.sb([128, NCST, 128], F32, "cst"); b_cst = Buf()
            ph.dma("sync", cst[:], cst_d, w=[b_cst])
            ones_bf = ph.sb([128, 128], BF16); ident_bf = ph.sb([128, 128], BF16); b_cb = Buf()
            ph.cp("vector", ones_bf[:], cst[:, 1, :], r=[b_cst], w=[b_cb])
            ph.cp("vector", ident_bf[:], cst[:, 0, :], r=[b_cst], w=[b_cb])
            cw = ph.sb([128, 12, 3], F32); b_cw = Buf()
            ph.dma("sync", cw[:], dcw_d[l], w=[b_cw])
            pqrot = Rot(ph, 2, [128, 12, 514], BF16, "pq")
            srot = Rot(ph, 1, [128, 8, 512], F32, "s")
            vrot = Rot(ph, 2, [128, 4, 512], BF16, "vT")
            qkrot = Rot(ph, 2, [128, 8, 512], BF16, "qkn")
            tmrot = Rot(ph, 3, [128, 512], F32, "tm")
            sqrot = Rot(ph, 2, [128, 512], BF16, "sq")
            rrot = Rot(ph, 2, [128, 512], F32, "r")
            ssps = Rot(ph, 2, [128, 512], F32, "ss", psum=True)
            trps = Rot(ph, 4, [128, 4, 128], BF16, "trp", psum=True)
            tmo = Rot(ph, 4, [128, 512], BF16, "tmo")
            QB = float(np.log(128.0 ** -0.5))
            for (j, t0, G) in token_groups(NT, 512):
                s_lo, s_hi = (0, NCTX) if j == 1 else (NCTX, NTA)
                pq, pqb = pqrot.next()
                a = max(t0 - 1, s_lo); b = min(t0 + G + 1, s_hi)
                if a > t0 - 1:
                    ph.memset("gpsimd", pq[:, :, 0:1], 0.0, w=[pqb])
                if b < t0 + G + 1:
                    ph.memset("gpsimd", pq[:, :, G + 1:G + 2], 0.0, w=[pqb])
                ph.dma("sync", pq[:, :, a - (t0 - 1):b - (t0 - 1)], pqkvT[:, :, a:b].rearrange("c p t -> p c t"), w=[pqb])
                st, sb_ = srot.next()
                vT_, vb = vrot.next()
                qk, qkb = qkrot.next()
                for c in range(12):
                    tm, tmb = tmrot.next()
                    ph.act(tm[:, 0:G], pq[:, c, 1:G + 1], AF.Identity, scale=cw[:, c, 1:2], r=[pqb, b_cw], w=[tmb])
                    ph.stt("vector", tm[:, 0:G], pq[:, c, 0:G], cw[:, c, 0:1], tm[:, 0:G], ALU.mult, ALU.add, r=[pqb, b_cw, tmb], w=[tmb])
                    ph.stt("vector", tm[:, 0:G], pq[:, c, 2:G + 2], cw[:, c, 2:3], tm[:, 0:G], ALU.mult, ALU.add, r=[pqb, b_cw, tmb], w=[tmb])
                    if c < 8:
                        ph.act(st[:, c, 0:G], tm[:, 0:G], AF.Silu, r=[tmb], w=[sb_])
                    else:
                        ph.act(vT_[:, c - 8, 0:G], tm[:, 0:G], AF.Silu, r=[tmb], w=[vb])
                for c in range(8):
                    sq, sqb = sqrot.next()
                    ph.tt("gpsimd", sq[:, 0:G], st[:, c, 0:G], st[:, c, 0:G], ALU.mult, r=[sb_], w=[sqb])
                    sp, spb = ssps.next()
                    ph.mm(sp[:, 0:G], ones_bf[:], sq[:, 0:G], r=[b_cb, sqb], w=[spb])
                    rr, rb = rrot.next()
                    ph.act(rr[:, 0:G], sp[:, 0:G], AF.Ln, bias=EPS, r=[spb], w=[rb])
                    ph.act(rr[:, 0:G], rr[:, 0:G], AF.Exp, scale=-0.5, bias=(QB if c < 4 else 0.0), r=[rb], w=[rb])
                    ph.tt("vector", qk[:, c, 0:G], st[:, c, 0:G], rr[:, 0:G], ALU.mult, r=[sb_, rb], w=[qkb])
                ph.dma("sync", qnT[:, :, t0:t0 + G].rearrange("h p t -> p h t"), qk[:, 0:4, 0:G], r=[qkb])
                ph.dma("sync", knT[:, :, t0:t0 + G].rearrange("h p t -> p h t"), qk[:, 4:8, 0:G], r=[qkb])
                for s in range(G // 128):
                    for which in range(2):
                        tp, tpb = trps.next()
                        for h in range(4):
                            src = qk[:, 4 + h, s * 128:(s + 1) * 128] if which == 0 else vT_[:, h, s * 128:(s + 1) * 128]
                            ph.tr(tp[:, h, :], src, ident_bf[:], r=[qkb if which == 0 else vb, b_cb], w=[tpb])
                        to, tob = tmo.next()
                        ph.cp("scalar" if which == 0 else "vector", to[:].rearrange("p (h t) -> p h t", h=4), tp[:], r=[tpb], w=[tob])
                        ph.dma("sync", (kTM if which == 0 else vTM)[t0 // 128 + s], to[:], r=[tob])
        if done("p2a_%d" % l):
            return nc

        with Phase(nc, "p2c_%d" % l) as ph:
            for (j, c0, Rr, Wd, cnt_src) in ((1, 0, 1, NCTX, cntc_d), (0, NCTX, ROWS, GW, cntl_d)):
                n_tok = Rr * Wd
                RP = Rr + 16 if Rr > 1 else 1
                WP = Wd + 16
                X = ph.sb([128, n_tok], BF16, "pX"); PA = ph.sb([128, RP, WP], F32, "pA"); PB = ph.sb([128, RP, WP], F32, "pB")
                CN = ph.sb([128, n_tok], F32, "pC"); DO = ph.sb([128, n_tok], BF16, "pD")
                bX = Buf(); bC = Buf(); bA = [Buf(), Buf()]; bB = [Buf(), Buf()]; bD = [Buf(), Buf()]
                r0 = 8 if Rr > 1 else 0
                for c in range(2):
                    ph.dma("sync", X[:], ppoolT[c, :, c0:c0 + n_tok], w=[bX])
                    ph.dma("sync", CN[:], cnt_src[c], w=[bC])
                    ph.memset("gpsimd", PA[:], 0.0, w=bA)
                    ph.cp("vector", PA[:, r0:r0 + Rr, 8:8 + Wd], X[:].rearrange("p (r w) -> p r w", w=Wd), r=[bX], w=bA)
                    for half in range(2):
                        eng = "gpsimd" if half == 0 else "vector"
                        w_ = POOL_WINDOWS[2 * c + half]
                        lo = w_ // 2
                        nl = int(np.log2(w_))
                        pr = slice(half * 64, (half + 1) * 64)
                        src, dst, bs, bd = PA, PB, bA[half], bB[half]
                        for lv in range(nl):
                            sft = 1 << lv
                            ph.tt(eng, dst[pr, :, 0:WP - sft], src[pr, :, 0:WP - sft], src[pr, :, sft:WP], ALU.add, r=[bs], w=[bd])
                            src, dst, bs, bd = dst, src, bd, bs
                        if Rr > 1:
                            for lv in range(nl):
                                sft = 1 << lv
                                ph.tt(eng, dst[pr, 0:RP - sft, :], src[pr, 0:RP - sft, :], src[pr, sft:RP, :], ALU.add, r=[bs], w=[bd])
                                src, dst, bs, bd = dst, src, bd, bs
                        ro = r0 - lo if Rr > 1 else 0
                        co = 8 - lo
                        Mv = src[pr, ro:ro + Rr, co:co + Wd]
                        tflat = dst[pr].rearrange("p r w -> p (r w)")[:, 0:n_tok]
                        ph.tt(eng, tflat.rearrange("p (r w) -> p r w", w=Wd), Mv, CN[pr].rearrange("p (r w) -> p r w", w=Wd), ALU.mult,
                              r=[bs, bC], w=[bd])
                        ph.tt(eng, DO[pr], tflat, X[pr], ALU.subtract, r=[bd, bX], w=[bD[half]])
                    ph.dma("sync", dT[c, :, c0:c0 + n_tok], DO[:], r=bD)
        if done("p2c_%d" % l):
            return nc


        with Phase(nc, "p2b_%d" % l) as ph:
            cst = ph.sb([128, NCST, 128], F32, "cst"); b_cst = Buf()
            ph.dma("sync", cst[:], cst_d, w=[b_cst])
            ident_bf = ph.sb([128, 128], BF16); b_ib = Buf()
            ph.cp("vector", ident_bf[:], cst[:, 0, :], r=[b_cst], w=[b_ib])
            S32 = [ph.sb([128, 4, 128], F32, "S32") for _ in range(2)]
            Sbf = [ph.sb([128, 4, 128], BF16, "Sbf") for _ in range(2)]
            bS32 = [[Buf() for _ in range(4)] for _ in range(2)]
            bSbf = [[Buf() for _ in range(4)] for _ in range(2)]
            for d in range(2):
                ph.memset("vector", S32[d][:], 0.0, w=bS32[d])
                ph.memset("gpsimd", Sbf[d][:], 0.0, w=bSbf[d])
            banks = Rot(ph, 8, [128, 512], F32, "bank", psum=True)

            def trbank():
                t_, b_ = banks.next()
                return t_[:].bitcast(BF16), b_
            T = {}
            TB = {}

            def tl(name, d, par, shape, dt):
                key = (name, d, par)
                if key not in T:
                    T[key] = ph.sb(shape, dt, name)
                return T[key]

            def tb(name, d, par, h=0):
                return TB.setdefault((name, d, par, h), Buf())
            order = [list(range(NTILE)), [1, 0] + list(range(NTILE - 1, 1, -1))]

            REC = ("ab", "sm", "wTp", "u", "kst", "qdecT", "aqkT", "vnew", "oTs")

            def prep(steps):
                C = {}
                for slot, step in enumerate(steps):
                    for d in range(2):
                        ti = order[d][step]
                        tc = slice(ti * 128, (ti + 1) * 128)
                        rk = step % 4

                        def mk(name, shape, dt, _d=d, _slot=slot, _rk=rk):
                            return tl(name, _d, ("r", _rk) if name in REC else ("p", _slot), shape, dt)

                        def B(name, h=0, _d=d, _slot=slot, _rk=rk):
                            return tb(name, _d, ("r", _rk) if name in REC else ("p", _slot), h)
                        ab = mk("ab", [128, 24], F32)
                        c = dict(step=step, ti=ti, tc=tc, B=B, ab=ab, sm=mk("sm", [128, 16], F32),
                                 qn=mk("qn", [128, 4, 128], BF16), kn=mk("kn", [128, 4, 128], BF16),
                                 kT=mk("kT", [128, 512], BF16), vT=mk("vT", [128, 512], BF16),
                                 Gbc=mk("Gbc", [128, 4, 128], F32), EQ=mk("EQ", [128, 4, 128], F32), E2T=mk("E2T", [128, 4, 128], F32),
                                 EsT=mk("EsT", [128, 4, 128], F32), M0t=mk("M0t", [128, 4, 128], BF16),
                                 A0t=mk("A0t", [128, 4, 128], BF16), AMb=mk("AMb", [128, 4, 2, 128], BF16),
                                 AM1=mk("AM1", [128, 4, 2, 128], BF16), A2t=mk("A2t", [128, 4, 128], BF16),
                                 PP=[mk("Pa", [128, 4, 128], BF16), mk("Pb", [128, 4, 128], BF16)],
                                 PT=mk("PT", [128, 4, 128], BF16), Xt=mk("Xt", [128, 4, 128], BF16),
                                 aqkT=mk("aqkT", [128, 4, 128], BF16), qdecT=mk("qdecT", [128, 4, 128], BF16),
                                 kegc=mk("kegc", [128, 4, 128], BF16), kst=mk("kst", [128, 4, 128], BF16),
                                 wTp=mk("wTp", [128, 4, 128], BF16), u=mk("u", [128, 4, 128], F32),
                                 vnew=mk("vnew", [128, 4, 128], BF16), oTs=mk("oTs", [128, 4, 128], BF16),
                                 b4=ab[:, 8 + d * 4:12 + d * 4], nb4=ab[:, 16 + d * 4:20 + d * 4])
                        C[(slot, d)] = c
                        ph.dma("sync", c["qn"][:], qnT[:, :, tc].rearrange("h p t -> p h t"), w=[B("qn")])
                        ph.dma("sync", c["kn"][:], knT[:, :, tc].rearrange("h p t -> p h t"), w=[B("kn")])
                        ph.dma("sync", c["kT"][:], kTM[ti], w=[B("kT")])
                        ph.dma("sync", c["vT"][:], vTM[ti], w=[B("vT")])
                        ph.dma("sync", ab[:], abS[ti], w=[B("ab")])
                        sm = c["sm"]
                        bsm = B("sm"); bab = B("ab")
                        pS, bpS = banks.next()
                        ph.mm(pS[:, 0:8], cst[:, 2 + d, :], ab[:, 0:8], r=[b_cst, bab], w=[bpS])
                        ph.mm(pS[:, 8:16], cst[:, 1, :], ab[:, 0:8], r=[b_cst, bab], w=[bpS])
                        gcs = pS[:, d * 4:d * 4 + 4]
                        gls = pS[:, 8 + d * 4:12 + d * 4]
                        ph.ts("vector", sm[:, 0:4], gcs, -1.0, ALU.mult, w=[bpS, bsm])
                        ph.act(sm[:, 4:8], gcs, AF.Exp, w=[bpS, bsm])
                        ph.act(sm[:, 8:12], gls, AF.Exp, w=[bpS, bsm])
                        ph.tt("vector", sm[:, 12:16], gls, sm[:, 0:4], ALU.add, w=[bpS, bsm])
                        ph.act(sm[:, 12:16], sm[:, 12:16], AF.Exp, w=[bsm])
                DHS = [(slot, d, h) for slot in range(len(steps)) for d in range(2) for h in range(4)]
                bk = {}

                def every(fn):
                    for ch in DHS:
                        fn(C[(ch[0], ch[1])], *ch)

                def stage(pe, cons):
                    for i in range(0, len(DHS), 8):
                        for ch in DHS[i:i + 8]:
                            pe(C[(ch[0], ch[1])], *ch)
                        for ch in DHS[i:i + 8]:
                            cons(C[(ch[0], ch[1])], *ch)

                def bank(ch, tr_=False):
                    bk[ch] = trbank() if tr_ else banks.next()
                    return bk[ch]

                every(lambda c, slot, d, h: ph.act(c["Gbc"][:, h, :], cst[:, 1, :], AF.Identity, scale=c["ab"][:, d * 4 + h:d * 4 + h + 1],
                                                   r=[b_cst, c["B"]("ab")], w=[c["B"]("Gbc", h)]))

                def pe_R(c, slot, d, h):
                    pR, bR = bank((slot, d, h))
                    ph.mm(pR[:, 0:128], c["Gbc"][:, h, :], cst[:, 2 + d, :], r=[c["B"]("Gbc", h), b_cst], w=[bR])
                    ph.mm(pR[:, 128:256], c["Gbc"][:, h, :], cst[:, 2 + d, :], start=True, stop=False, r=[c["B"]("Gbc", h), b_cst], w=[bR])
                    ph.mm(pR[:, 128:256], cst[:, 0, :], cst[:, 4 + d, :], start=False, stop=True, r=[b_cst], w=[bR])

                def co_R(c, slot, d, h):
                    pR, bR = bk[(slot, d, h)]
                    ph.act(c["EQ"][:, h, :], pR[:, 0:128], AF.Exp, w=[bR, c["B"]("EQ", h)])
                    ph.act(c["E2T"][:, h, :], pR[:, 128:256], AF.Exp, bias=c["sm"][:, h:h + 1], r=[c["B"]("sm")], w=[bR, c["B"]("E2T", h)])
                stage(pe_R, co_R)

                def po_E(c, slot, d, h):
                    ph.tt("gpsimd", c["EsT"][:, h, :], c["E2T"][:, h, :], cst[:, 6 + d, :], ALU.mult, r=[c["B"]("E2T", h), b_cst], w=[c["B"]("EsT", h)])
                    ph.tt("gpsimd", c["qdecT"][:, h, :], c["qn"][:, h, :], c["EQ"][:, h, :], ALU.mult, r=[c["B"]("qn"), c["B"]("EQ", h)], w=[c["B"]("qdecT", h)])
                every(po_E)

                def pe_G(c, slot, d, h):
                    pG, bG = bank((slot, d, h))
                    ph.mm(pG[:, 0:128], c["kn"][:, h, :], c["kn"][:, h, :], r=[c["B"]("kn")], w=[bG])
                    ph.mm(pG[:, 128:256], c["kn"][:, h, :], c["qn"][:, h, :], r=[c["B"]("kn"), c["B"]("qn")], w=[bG])

                def co_G(c, slot, d, h):
                    pG, bG = bk[(slot, d, h)]
                    ph.stt("vector", c["M0t"][:, h, :], pG[:, 0:128], c["b4"][:, h:h + 1], c["EsT"][:, h, :], ALU.mult, ALU.mult,
                           r=[c["B"]("ab"), c["B"]("EsT", h)], w=[bG, c["B"]("M0t", h)])
                    ph.tt("vector", c["aqkT"][:, h, :], pG[:, 128:256], c["E2T"][:, h, :], ALU.mult,
                          r=[c["B"]("E2T", h)], w=[bG, c["B"]("aqkT", h)])
                stage(pe_G, co_G)

                def pe_T(c, slot, d, h):
                    pT_, bT_ = bank((slot, d, h), True)
                    ph.tr(pT_[:, 0:128], c["M0t"][:, h, :], ident_bf[:], r=[c["B"]("M0t", h), b_ib], w=[bT_])

                def co_T(c, slot, d, h):
                    pT_, bT_ = bk[(slot, d, h)]
                    ph.cp("scalar", c["A0t"][:, h, :], pT_[:, 0:128], w=[bT_, c["B"]("A0t", h)])
                stage(pe_T, co_T)

                def po_M(c, slot, d, h):
                    ph.tt("gpsimd", c["AMb"][:, h, 1, :], c["M0t"][:, h, :], cst[:, 8, :], ALU.mult, r=[c["B"]("M0t", h), b_cst], w=[c["B"]("AMb", h)])
                    ph.tt("gpsimd", c["AMb"][:, h, 0, :], c["A0t"][:, h, :], cst[:, 8, :], ALU.mult, r=[c["B"]("A0t", h), b_cst], w=[c["B"]("AMb", h)])
                    ph.tt("gpsimd", c["PP"][0][:, h, :], cst[:, 0, :], c["AMb"][:, h, 1, :], ALU.subtract, r=[b_cst, c["B"]("AMb", h)], w=[c["B"]("P0", h)])
                every(po_M)

                def pe_A1(c, slot, d, h):
                    pA, bA_ = bank((slot, d, h))
                    ph.mm(pA[:, 0:128], c["AMb"][:, h, 1, :], c["AMb"][:, h, 0, :], r=[c["B"]("AMb", h)], w=[bA_])
                    ph.mm(pA[:, 128:256], c["AMb"][:, h, 0, :], c["AMb"][:, h, 1, :], r=[c["B"]("AMb", h)], w=[bA_])

                def co_A1(c, slot, d, h):
                    pA, bA_ = bk[(slot, d, h)]
                    ph.cp("scalar", c["AM1"][:, h, :, :], pA[:, 0:256].rearrange("p (a b) -> p a b", a=2), w=[bA_, c["B"]("AM1", h)])
                stage(pe_A1, co_A1)

                def pe_P1(c, slot, d, h):
                    pP, bP_ = bank((slot, d, h))
                    ph.mm(pP[:, 0:128], c["AM1"][:, h, 0, :], c["PP"][0][:, h, :], r=[c["B"]("AM1", h), c["B"]("P0", h)], w=[bP_])

                def co_P1(c, slot, d, h):
                    pP, bP_ = bk[(slot, d, h)]
                    ph.tt("vector", c["PP"][1][:, h, :], pP[:, 0:128], c["PP"][0][:, h, :], ALU.add, r=[c["B"]("P0", h)], w=[bP_, c["B"]("P1", h)])
                stage(pe_P1, co_P1)

                def pe_A2(c, slot, d, h):
                    pA, bA_ = bank((slot, d, h))
                    ph.mm(pA[:, 0:128], c["AM1"][:, h, 1, :], c["AM1"][:, h, 0, :], r=[c["B"]("AM1", h)], w=[bA_])

                def co_A2(c, slot, d, h):
                    pA, bA_ = bk[(slot, d, h)]
                    ph.cp("scalar", c["A2t"][:, h, :], pA[:, 0:128], w=[bA_, c["B"]("A2t", h)])
                stage(pe_A2, co_A2)

                def pe_P2(c, slot, d, h):
                    pP, bP_ = bank((slot, d, h))
                    ph.mm(pP[:, 0:128], c["A2t"][:, h, :], c["PP"][1][:, h, :], r=[c["B"]("A2t", h), c["B"]("P1", h)], w=[bP_])

                def co_P2(c, slot, d, h):
                    pP, bP_ = bk[(slot, d, h)]
                    ph.tt("vector", c["PP"][0][:, h, :], pP[:, 0:128], c["PP"][1][:, h, :], ALU.add, r=[c["B"]("P1", h)], w=[bP_, c["B"]("P0", h)])
                stage(pe_P2, co_P2)

                for mi in range(4):
                    cur = mi % 2
                    nxt = 1 - cur

                    def pe_PT(c, slot, d, h, cur=cur):
                        pT_, bT_ = bank((slot, d, h), True)
                        ph.tr(pT_[:, 0:128], c["PP"][cur][:, h, :], ident_bf[:], r=[c["B"]("P%d" % cur, h), b_ib], w=[bT_])

                    def co_PT(c, slot, d, h):
                        pT_, bT_ = bk[(slot, d, h)]
                        ph.cp("scalar", c["PT"][:, h, :], pT_[:, 0:128], w=[bT_, c["B"]("PT", h)])
                    stage(pe_PT, co_PT)

                    def pe_X(c, slot, d, h, cur=cur):
                        pX, bX_ = bank((slot, d, h))
                        ph.mm(pX[:, 0:128], c["A0t"][:, h, :], c["PP"][cur][:, h, :], r=[c["B"]("A0t", h), c["B"]("P%d" % cur, h)], w=[bX_])

                    def co_X(c, slot, d, h, mi=mi):
                        pX, bX_ = bk[(slot, d, h)]
                        ph.tt("vector", c["Xt"][:, h, :], pX[:, 0:128], cst[:, 9 + mi, :], ALU.mult, r=[b_cst], w=[bX_, c["B"]("Xt", h)])
                    stage(pe_X, co_X)

                    def pe_Y(c, slot, d, h):
                        pY, bY_ = bank((slot, d, h))
                        ph.mm(pY[:, 0:128], c["PT"][:, h, :], c["Xt"][:, h, :], r=[c["B"]("PT", h), c["B"]("Xt", h)], w=[bY_])

                    def co_Y(c, slot, d, h, cur=cur, nxt=nxt):
                        pY, bY_ = bk[(slot, d, h)]
                        ph.tt("vector", c["PP"][nxt][:, h, :], c["PP"][cur][:, h, :], pY[:, 0:128], ALU.subtract,
                              r=[c["B"]("P%d" % cur, h)], w=[bY_, c["B"]("P%d" % nxt, h)])
                    stage(pe_Y, co_Y)

                def ac_K(c, slot, d, h):
                    hc = slice(h * 128, (h + 1) * 128)
                    ph.act(c["kegc"][:, h, :], c["kT"][:, hc], AF.Identity, scale=c["sm"][:, 4 + h:5 + h], r=[c["B"]("kT"), c["B"]("sm")], w=[c["B"]("kegc", h)])
                    ph.act(c["kst"][:, h, :], c["kT"][:, hc], AF.Identity, scale=c["sm"][:, 12 + h:13 + h], r=[c["B"]("kT"), c["B"]("sm")], w=[c["B"]("kst", h)])
                every(ac_K)

                def pe_W(c, slot, d, h):
                    pW, bW = bank((slot, d, h))
                    ph.mm(pW[:, 0:128], c["kegc"][:, h, :], c["PP"][0][:, h, :], r=[c["B"]("kegc", h), c["B"]("P0", h)], w=[bW])
                    ph.mm(pW[:, 128:256], c["PP"][0][:, h, :], c["vT"][:, h * 128:(h + 1) * 128], r=[c["B"]("P0", h), c["B"]("vT")], w=[bW])

                def co_W(c, slot, d, h):
                    pW, bW = bk[(slot, d, h)]
                    ph.cp("scalar", c["wTp"][:, h, :], pW[:, 0:128], w=[bW, c["B"]("wTp", h)])
                    ph.ts("vector", c["u"][:, h, :], pW[:, 128:256], c["b4"][:, h:h + 1], ALU.mult, r=[c["B"]("ab")], w=[bW, c["B"]("u", h)])
                stage(pe_W, co_W)
                return C

            def rec(C, slot):
                DH2 = [(d, h) for d in range(2) for h in range(4)]
                bk = {}
                bk2 = {}
                for (d, h) in DH2:
                    c = C[(slot, d)]
                    pW, bW = banks.next(); bk[(d, h)] = (pW, bW)
                    ph.mm(pW[:, 0:128], c["wTp"][:, h, :], Sbf[d][:, h, :], r=[c["B"]("wTp", h), bSbf[d][h]], w=[bW])
                for (d, h) in DH2:
                    c = C[(slot, d)]; pW, bW = bk[(d, h)]
                    ph.stt("vector", c["vnew"][:, h, :], pW[:, 0:128], c["nb4"][:, h:h + 1], c["u"][:, h, :], ALU.mult, ALU.add,
                           r=[c["B"]("ab"), c["B"]("u", h)], w=[bW, c["B"]("vnew", h)])
                for (d, h) in DH2:
                    c = C[(slot, d)]
                    pO, bO = banks.next(); bk2[(d, h)] = (pO, bO)
                    ph.mm(pO[:, 0:128], Sbf[d][:, h, :], c["qdecT"][:, h, :], start=True, stop=False, r=[bSbf[d][h], c["B"]("qdecT", h)], w=[bO])
                    ph.mm(pO[:, 0:128], c["vnew"][:, h, :], c["aqkT"][:, h, :], start=False, stop=True, r=[c["B"]("vnew", h), c["B"]("aqkT", h)], w=[bO])
                    ph.mm(pO[:, 128:256], c["kst"][:, h, :], c["vnew"][:, h, :], r=[c["B"]("kst", h), c["B"]("vnew", h)], w=[bO])
                for (d, h) in DH2:
                    c = C[(slot, d)]; pO, bO = bk2[(d, h)]
                    ph.stt("vector", S32[d][:, h, :], S32[d][:, h, :], c["sm"][:, 8 + h:9 + h], pO[:, 128:256], ALU.mult, ALU.add,
                           r=[c["B"]("sm")], w=[bO, bS32[d][h]])
                    ph.cp("scalar", c["oTs"][:, h, :], pO[:, 0:128], w=[bO, c["B"]("oTs", h)])
                for (d, h) in DH2:
                    ph.cp("gpsimd", Sbf[d][:, h, :], S32[d][:, h, :], r=[bS32[d][h]], w=[bSbf[d][h]])
                for d in range(2):
                    c = C[(slot, d)]
                    ph.dma("sync", oTd[d][:, :, c["tc"]].rearrange("h p t -> p h t"), c["oTs"][:], r=[c["B"]("oTs", h) for h in range(4)])

            pairs = [(s_, s_ + 1) for s_ in range(0, NTILE, 2)]
            prev = prep(pairs[0])
            for pi in range(len(pairs)):
                nxt_ = prep(pairs[pi + 1]) if pi + 1 < len(pairs) else None
                for slot in range(2):
                    rec(prev, slot)
                prev = nxt_
            if "sfin" in dbg:
                for d in range(2):
                    ph.dma("sync", sfin[l, d].rearrange("h p t -> p h t"), S32[d][:], r=bS32[d])
        if done("p2b_%d" % l):
            return nc


        with Phase(nc, "p3_%d" % l) as ph:
            cst = ph.sb([128, NCST, 128], F32, "cst"); b_cst = Buf()
            ph.dma("sync", cst[:], cst_d, w=[b_cst])
            ones_bf = ph.sb([128, 128], BF16); b_cb = Buf()
            ph.cp("vector", ones_bf[:], cst[:, 1, :], r=[b_cst], w=[b_cb])
            wa = ph.sb([128, 4, D], BF16, "wa"); wbb = ph.sb([128, 2, D], BF16, "wb"); wc = ph.sb([128, 2, D], BF16, "wc")
            wo = ph.sb([128, KC, D], BF16, "wo"); b_w = Buf()
            ph.dma("gpsimd", wa[:], wbra_d[l].rearrange("(k p) n -> p k n", p=128), w=[b_w])
            ph.dma("gpsimd", wbb[:], wbrb_d[l].rearrange("(k p) n -> p k n", p=128), w=[b_w])
            ph.dma("gpsimd", wc[:], wbrc_d[l].rearrange("(k p) n -> p k n", p=128), w=[b_w])
            ph.dma("gpsimd", wo[:], wo_d[l].rearrange("(k p) n -> p k n", p=128), w=[b_w])
            pw = ph.sb([128, 2, 128], BF16, "pw"); b_pw = Buf()
            ph.memset("vector", pw[:], 0.0, w=[b_pw])
            for c in range(2):
                for half in range(2):
                    ph.dma("gpsimd", pw[half * 64:(half + 1) * 64, c, half * 64:(half + 1) * 64], poolw_d[l, 2 * c + half], w=[b_pw])
            psc_ = ph.sb([128, 2], F32); scw = ph.sb([128, 2, 3], F32); dng = ph.sb([128, 1], F32); b_sm = Buf()
            ph.dma("sync", psc_[:], pscale_d[l], w=[b_sm])
            ph.dma("sync", scw[:], scw_d[l], w=[b_sm])
            ph.dma("sync", dng[:], dng_d[l], w=[b_sm])
            mods = ph.sb([128, 48, 2], F32); b_mods = Buf()
            ph.dma("sync", mods[:], modd[l], w=[b_mods])

            G3 = 512
            ofr = Rot(ph, 1, [128, 4, G3], BF16, "of"); obr = Rot(ph, 1, [128, 4, G3], BF16, "ob"); osr = Rot(ph, 1, [128, 4, G3], F32, "os")
            szr = Rot(ph, 1, [128, 4, G3], BF16, "sz"); onr = Rot(ph, 1, [128, 4, G3], BF16, "on")
            sqr = Rot(ph, 2, [128, G3], BF16, "sq"); rr_ = Rot(ph, 2, [128, G3], F32, "r"); tmr = Rot(ph, 3, [128, G3], F32, "tm")
            dr = Rot(ph, 1, [128, 2, G3], BF16, "d"); ybr = Rot(ph, 1, [128, 2, G3], BF16, "yb")
            scr = Rot(ph, 1, [128, 6, G3 + 2], BF16, "sc"); cxr = Rot(ph, 1, [128, 2, G3 + 2], F32, "cx"); ycr = Rot(ph, 1, [128, 2, G3], BF16, "yc")
            gtr = Rot(ph, 1, [128, 24, G3], BF16, "gt"); xgr = Rot(ph, 1, [128, KC, G3], F32, "xg")
            yr = Rot(ph, 1, [128, KC, G3], BF16, "y"); xor_ = Rot(ph, 1, [128, KC, G3], F32, "xo")
            ps1 = Rot(ph, 2, [128, G3], F32, "ps1", psum=True)
            ps3 = Rot(ph, 5, [128, G3], F32, "ps3", psum=True)
            for (j, t0, G) in token_groups(NT, G3):
                if last and j == 1:
                    continue
                s_lo, s_hi = (0, NCTX) if j == 1 else (NCTX, NTA)
                of_, ofb = ofr.next(); ob_, obb = obr.next(); sz, szb = szr.next(); on, onb = onr.next()
                ph.dma("scalar", of_[:, :, 0:G], oTd[0][:, :, t0:t0 + G].rearrange("h p t -> p h t"), w=[ofb])
                ph.dma("scalar", ob_[:, :, 0:G], oTd[1][:, :, t0:t0 + G].rearrange("h p t -> p h t"), w=[obb])
                ph.dma("sync", sz[:, :, 0:G], szT[:, :, t0:t0 + G].rearrange("h p t -> p h t"), w=[szb])
                osum, osb = osr.next()
                ph.tt("gpsimd", osum[:, :, 0:G], of_[:, :, 0:G], ob_[:, :, 0:G], ALU.add, r=[ofb, obb], w=[osb])
                of_, ofb = osum, osb
                for h in range(4):
                    sq, sqb = sqr.next()
                    ph.act(sq[:, 0:G], of_[:, h, 0:G], AF.Square, r=[ofb], w=[sqb])
                    p1, p1b = ps1.next()
                    ph.mm(p1[:, 0:G], ones_bf[:], sq[:, 0:G], r=[b_cb, sqb], w=[p1b])
                    rr, rb = rr_.next()
                    ph.act(rr[:, 0:G], p1[:, 0:G], AF.Ln, bias=EPS, scale=1.0 / 128, r=[p1b], w=[rb])
                    ph.act(rr[:, 0:G], rr[:, 0:G], AF.Exp, scale=-0.5, r=[rb], w=[rb])
                    tm, tmb = tmr.next()
                    ph.stt("vector", tm[:, 0:G], of_[:, h, 0:G], dng[:, 0:1], rr[:, 0:G], ALU.mult, ALU.mult, r=[ofb, b_sm, rb], w=[tmb])
                    ph.tt("gpsimd", on[:, h, 0:G], tm[:, 0:G], sz[:, h, 0:G], ALU.mult, r=[tmb, szb], w=[onb])
                dd, ddb = dr.next(); yb, ybb = ybr.next()
                ph.dma("sync", dd[:, :, 0:G], dT[:, :, t0:t0 + G].rearrange("c p t -> p c t"), w=[ddb])
                for c in range(2):
                    p1, p1b = ps1.next()
                    ph.mm(p1[:, 0:G], pw[:, c, :], dd[:, c, 0:G], r=[b_pw, ddb], w=[p1b])
                    ph.ts("vector", yb[:, c, 0:G], p1[:, 0:G], psc_[:, c:c + 1], ALU.mult, r=[p1b, b_sm], w=[ybb])
                sc, scb = scr.next(); cx, cxb = cxr.next(); yc, ycb = ycr.next()
                a = max(t0 - 1, s_lo); b = min(t0 + G + 1, s_hi)
                if a > t0 - 1:
                    ph.memset("gpsimd", sc[:, :, 0:1], 0.0, w=[scb])
                if b < t0 + G + 1:
                    ph.memset("gpsimd", sc[:, :, G + 1:G + 2], 0.0, w=[scb])
                ph.dma("sync", sc[:, :, a - (t0 - 1):b - (t0 - 1)], pscT[:, :, a:b].rearrange("c p t -> p c t"), w=[scb])
                ph.tt("gpsimd", cx[:, :, 0:G + 2], sc[:, 4:6, 0:G + 2], sc[:, 0:2, 0:G + 2], ALU.mult, r=[scb], w=[cxb])
                for c in range(2):
                    tm, tmb = tmr.next()
                    ph.act(tm[:, 0:G], cx[:, c, 1:G + 1], AF.Identity, scale=scw[:, c, 1:2], r=[cxb, b_sm], w=[tmb])
                    ph.stt("vector", tm[:, 0:G], cx[:, c, 0:G], scw[:, c, 0:1], tm[:, 0:G], ALU.mult, ALU.add, r=[cxb, b_sm, tmb], w=[tmb])
                    ph.stt("vector", tm[:, 0:G], cx[:, c, 2:G + 2], scw[:, c, 2:3], tm[:, 0:G], ALU.mult, ALU.add, r=[cxb, b_sm, tmb], w=[tmb])
                    ph.tt("gpsimd", yc[:, c, 0:G], tm[:, 0:G], sc[:, 2 + c, 1:G + 1], ALU.mult, r=[tmb, scb], w=[ycb])
                gt, gtb = gtr.next(); xg, xb = xgr.next(); y, yb_ = yr.next(); xo, xob = xor_.next()
                ph.dma("scalar", gt[:, :, 0:G], gatesT[:, :, t0:t0 + G].rearrange("c p t -> p c t"), w=[gtb])
                ph.dma("sync", xg[:, :, 0:G], xT[:, :, t0:t0 + G].rearrange("k p t -> p k t"), w=[xb])
                for m in range(KC):
                    mc = slice(m * 128, (m + 1) * 128)
                    pa, pab = ps3.next(); pb_, pbb = ps3.next(); pc, pcb = ps3.next()
                    for h in range(4):
                        ph.mm(pa[:, 0:G], wa[:, h, mc], on[:, h, 0:G], start=(h == 0), stop=(h == 3), r=[b_w, onb], w=[pab])
                    for c in range(2):
                        ph.mm(pb_[:, 0:G], wbb[:, c, mc], yb[:, c, 0:G], start=(c == 0), stop=(c == 1), r=[b_w, ybb], w=[pbb])
                    for c in range(2):
                        ph.mm(pc[:, 0:G], wc[:, c, mc], yc[:, c, 0:G], start=(c == 0), stop=(c == 1), r=[b_w, ycb], w=[pcb])
                    t1, t1b = tmr.next(); t2, t2b = tmr.next(); t3, t3b = tmr.next()
                    ph.tt("vector", t1[:, 0:G], pa[:, 0:G], gt[:, m, 0:G], ALU.mult, r=[pab, gtb], w=[t1b])
                    ph.tt("vector", t2[:, 0:G], pb_[:, 0:G], gt[:, 8 + m, 0:G], ALU.mult, r=[pbb, gtb], w=[t2b])
                    ph.tt("vector", t3[:, 0:G], pc[:, 0:G], gt[:, 16 + m, 0:G], ALU.mult, r=[pcb, gtb], w=[t3b])
                    ph.tt("gpsimd", t1[:, 0:G], t1[:, 0:G], t2[:, 0:G], ALU.add, r=[t1b, t2b], w=[t1b])
                    ph.tt("gpsimd", y[:, m, 0:G], t1[:, 0:G], t3[:, 0:G], ALU.add, r=[t1b, t3b], w=[yb_])
                for m in range(KC):
                    mc = slice(m * 128, (m + 1) * 128)
                    pa, pab = ps3.next()
                    for k in range(KC):
                        ph.mm(pa[:, 0:G], wo[:, k, mc], y[:, k, 0:G], start=(k == 0), stop=(k == KC - 1), r=[b_w, yb_], w=[pab])
                    ph.stt("vector", xo[:, m, 0:G], pa[:, 0:G], mods[:, 16 + m, j:j + 1], xg[:, m, 0:G], ALU.mult, ALU.add,
                           r=[pab, b_mods, xb], w=[xob])
                ph.dma("sync", xT[:, :, t0:t0 + G].rearrange("k p t -> p k t"), xo[:, :, 0:G], r=[xob])
        if done("p3_%d" % l):
            return nc

        with Phase(nc, "p4_%d" % l) as ph:
            onesf = ph.sb([128, 128], F32, "onesf"); b_cst = Buf()
            ph.dma("sync", onesf[:], cst_d[:, 1, :], w=[b_cst])
            ones_bf = ph.sb([128, 128], BF16); b_cb = Buf()
            ph.cp("vector", ones_bf[:], onesf[:], r=[b_cst], w=[b_cb])
            wgu = ph.sb([128, KC, 2 * DFF], BF16, "wgu"); wdn = ph.sb([128, FC, D], BF16, "wdn"); b_wg = [Buf() for _ in range(KC)]; b_wd = Buf()
            for k in range(KC):
                ph.dma("gpsimd", wgu[:, k, :], wgu_d[l, k * 128:(k + 1) * 128, :], w=[b_wg[k]])
            for f0 in range(0, FC, 11):
                ph.dma("gpsimd", wdn[:, f0:f0 + 11, :], wdn_d[l, f0 * 128:(f0 + 11) * 128, :].rearrange("(f p) n -> p f n", p=128), w=[b_wd])
            mods = ph.sb([128, 48, 2], F32); b_mods = Buf()
            ph.dma("sync", mods[:], modd[l], w=[b_mods])
            n2g = ph.sb([128, KC], F32); b_n2g = Buf()
            ph.dma("sync", n2g[:], n2g_d[l], w=[b_n2g])
            A2 = ph.sb([128, 2, KC], F32); b_A2 = Buf()
            for j in range(2):
                ph.stt("vector", A2[:, j, :], mods[:, 32:40, j], 1.0, n2g[:], ALU.add, ALU.mult, r=[b_mods, b_n2g], w=[b_A2])
            G4 = 512
            xgr = Rot(ph, 2, [128, KC, G4], F32, "xg")
            hr = Rot(ph, 1, [128, KC, G4], BF16, "hT"); tmr = Rot(ph, 2, [128, G4], F32, "tm"); rsr = Rot(ph, 1, [128, G4], F32, "rs")
            acr = Rot(ph, 1, [128, FC, G4], BF16, "act")
            ssp = Rot(ph, 1, [128, G4], F32, "ssp", psum=True)
            gup = Rot(ph, 6, [128, G4], F32, "gup", psum=True)
            for (j, t0, G) in token_groups(NT, G4):
                if last and j == 1:
                    continue
                xg, xb = xgr.next()
                ph.dma("sync", xg[:, :, 0:G], xT[:, :, t0:t0 + G].rearrange("k p t -> p k t"), w=[xb])
                ac, acb = acr.next()
                sq, sqb = ac, acb
                ph.act(sq[:, 0:KC, 0:G], xg[:, :, 0:G], AF.Square, r=[xb], w=[sqb])
                sp, spb = ssp.next()
                for k in range(KC):
                    ph.mm(sp[:, 0:G], ones_bf[:], sq[:, k, 0:G], start=(k == 0), stop=(k == KC - 1), r=[b_cb, sqb], w=[spb])
                rs, rsb = rsr.next()
                ph.act(rs[:, 0:G], sp[:, 0:G], AF.Ln, bias=EPS, scale=1.0 / D, r=[spb], w=[rsb])
                ph.act(rs[:, 0:G], rs[:, 0:G], AF.Exp, scale=-0.5, r=[rsb], w=[rsb])
                hT, hb = hr.next()
                for k in range(KC):
                    tm, tmb = tmr.next()
                    ph.stt("vector", tm[:, 0:G], xg[:, k, 0:G], A2[:, j, k:k + 1], rs[:, 0:G], ALU.mult, ALU.mult, r=[xb, b_A2, rsb], w=[tmb])
                    ph.act(hT[:, k, 0:G], tm[:, 0:G], AF.Identity, bias=mods[:, 24 + k, j:j + 1], r=[tmb, b_mods], w=[hb])
                for f in range(FC):
                    pg, pgb = gup.next(); pu, pub = gup.next()
                    for k in range(KC):
                        ph.mm(pg[:, 0:G], wgu[:, k, f * 128:(f + 1) * 128], hT[:, k, 0:G], start=(k == 0), stop=(k == KC - 1), r=[b_wg[k], hb], w=[pgb])
                    for k in range(KC):
                        ph.mm(pu[:, 0:G], wgu[:, k, DFF + f * 128:DFF + (f + 1) * 128], hT[:, k, 0:G], start=(k == 0), stop=(k == KC - 1), r=[b_wg[k], hb], w=[pub])
                    tm, tmb = tmr.next()
                    ph.act(tm[:, 0:G], pg[:, 0:G], AF.Silu, r=[pgb], w=[tmb])
                    ph.tt("vector", ac[:, f, 0:G], pu[:, 0:G], tm[:, 0:G], ALU.mult, r=[pub, tmb], w=[acb])
                for m in range(KC):
                    pd, pdb = gup.next()
                    for f in range(FC):
                        ph.mm(pd[:, 0:G], wdn[:, f, m * 128:(m + 1) * 128], ac[:, f, 0:G], start=(f == 0), stop=(f == FC - 1), r=[b_wd, acb], w=[pdb])
                    ph.stt("vector", xg[:, m, 0:G], pd[:, 0:G], mods[:, 40 + m, j:j + 1], xg[:, m, 0:G], ALU.mult, ALU.add,
                           r=[pdb, b_mods], w=[xb])
                ph.dma("sync", xT[:, :, t0:t0 + G].rearrange("k p t -> p k t"), xg[:, :, 0:G], r=[xb])
        if done("p4_%d" % l):
            return nc

    with Phase(nc, "pf") as ph:
        cst = ph.sb([128, NCST, 128], F32, "cst"); b_cst = Buf()
        ph.dma("sync", cst[:], cst_d, w=[b_cst])
        ones_bf = ph.sb([128, 128], BF16); b_cb = Buf()
        ph.cp("vector", ones_bf[:], cst[:, 1, :], r=[b_cst], w=[b_cb])
        fng = ph.sb([128, KC], F32); b_fng = Buf()
        ph.dma("sync", fng[:], fng_d, w=[b_fng])
        GF = 512
        xgr = Rot(ph, 2, [128, KC, GF], F32, "xg"); sqr = Rot(ph, 1, [128, KC, GF], BF16, "sq"); rsr = Rot(ph, 1, [128, GF], F32, "rs")
        yr = Rot(ph, 2, [128, KC, GF], F32, "y"); otr = Rot(ph, 3, [128, D], F32, "ot")
        ssp = Rot(ph, 1, [128, GF], F32, "ssp", psum=True)
        trp = Rot(ph, 3, [128, KC, 128], F32, "trp", psum=True)
        out_waits = []
        for (j, t0, G) in token_groups(NT, GF):
            if j == 1:
                continue
            xg, xb = xgr.next()
            ph.dma("sync", xg[:], xT[:, :, t0:t0 + G].rearrange("k p t -> p k t"), w=[xb])
            sq, sqb = sqr.next()
            ph.act(sq[:], xg[:], AF.Square, r=[xb], w=[sqb])
            sp, spb = ssp.next()
            for k in range(KC):
                ph.mm(sp[:], ones_bf[:], sq[:, k, :], start=(k == 0), stop=(k == KC - 1), r=[b_cb, sqb], w=[spb])
            rs, rsb = rsr.next()
            ph.act(rs[:], sp[:], AF.Ln, bias=EPS, scale=1.0 / D, r=[spb], w=[rsb])
            ph.act(rs[:], rs[:], AF.Exp, scale=-0.5, r=[rsb], w=[rsb])
            y, yb = yr.next()
            for k in range(KC):
                ph.stt("vector", y[:, k, :], xg[:, k, :], fng[:, k:k + 1], rs[:], ALU.mult, ALU.mult,
                       r=[xb, b_fng, rsb], w=[yb])
            for s_ in range(G // 128):
                tp, tpb = trp.next()
                for k in range(KC):
                    ph.mm(tp[:, k, :], y[:, k, s_ * 128:(s_ + 1) * 128], cst[:, 0, :], r=[yb, b_cst], w=[tpb])
                ot, otb = otr.next()
                ph.cp("scalar" if s_ % 2 == 0 else "vector", ot[:].rearrange("p (k f) -> p k f", k=KC), tp[:], r=[tpb], w=[otb])
                ph.dma("scalar" if s_ % 2 == 0 else "sync", out_d[t0 - NCTX + s_ * 128:t0 - NCTX + (s_ + 1) * 128, :], ot[:], r=[otb])

    return nc


POOL_WINDOWS = (2, 4, 8, 16)


def _consts(NT):
    idx = np.arange(128)
    cst = np.zeros((128, NCST, 128), np.float32)
    cst[:, 0, :] = np.eye(128)
    cst[:, 1, :] = 1.0
    cst[:, 2, :] = (idx[:, None] <= idx[None, :])
    cst[:, 3, :] = (idx[:, None] >= idx[None, :])
    cst[:, 4, :] = np.where(idx[None, :] >= idx[:, None], 0.0, NEG)
    cst[:, 5, :] = np.where(idx[None, :] <= idx[:, None], 0.0, NEG)
    cst[:, 6, :] = (idx[None, :] > idx[:, None])
    cst[:, 7, :] = (idx[None, :] < idx[:, None])

    def blk(sz):
        return (idx[:, None] // sz == idx[None, :] // sz).astype(np.float32)
    cst[:, 8, :] = blk(8)
    for n, sz in enumerate((8, 16, 32, 64)):
        cst[:, 9 + n, :] = blk(2 * sz) - blk(sz)

    def cnt1d(n, w):
        lo = w // 2
        hi = w - 1 - lo
        pos = np.arange(n)
        return (np.clip(pos + hi + 1, 0, n) - np.clip(pos - lo, 0, n)).astype(np.float64)
    rows = NT // GW
    cnt_lat = np.zeros((2, 128, NT), np.float32)
    cnt_ctx = np.zeros((2, 128, NCTX), np.float32)
    for c in range(2):
        for half in range(2):
            w = POOL_WINDOWS[2 * c + half]
            cl = 1.0 / (cnt1d(rows, w)[:, None] * cnt1d(GW, w)[None, :])
            cnt_lat[c, half * 64:(half + 1) * 64, :] = cl.reshape(1, NT)
            cnt_ctx[c, half * 64:(half + 1) * 64, :] = (1.0 / cnt1d(NCTX, w))[None, :]
    return cst, cnt_lat, cnt_ctx


def col(v):
    v = np.asarray(v, np.float32)
    return np.ascontiguousarray(v.reshape(-1, 128).T)


def prep_shared(inp, NT):
    f = lambda a: np.ascontiguousarray(np.asarray(a, np.float32))
    cst, cnt_lat, cnt_ctx = _consts(NT)
    sh = {
        "w_ada": f(inp["w_ada"]),
        "b_adaT": np.stack([col(inp["b_ada"][l]) for l in range(2)]),
        "n1g": np.stack([col(inp["norm1_g"][l]) for l in range(2)]),
        "n2g": np.stack([col(inp["norm2_g"][l]) for l in range(2)]),
        "fng": col(inp["final_norm_g"]),
        "w_in": f(inp["w_in"]),
        "dcw": np.stack([np.stack([col(np.asarray(inp["dn_conv_w"])[l, t]) for t in range(3)], axis=-1) for l in range(2)]),
        "alog": np.stack([np.broadcast_to(np.asarray(inp["dn_a_log"], np.float32)[l].reshape(1, 8), (128, 8)) for l in range(2)]).copy(),
        "dtb": np.stack([np.broadcast_to(np.asarray(inp["dn_dt_bias"], np.float32)[l].reshape(1, 8), (128, 8)) for l in range(2)]).copy(),
        "dng": np.asarray(inp["dn_norm_g"], np.float32).reshape(2, 128, 1).copy(),
        "poolw": f(inp["pool_w"]),
        "pscale": np.stack([col(inp["pool_scale"][l]) for l in range(2)]),
        "scw": np.stack([np.stack([col(np.asarray(inp["sc_conv_w"])[l, t]) for t in range(3)], axis=-1) for l in range(2)]),
        "w_br_a": f(inp["w_br_a"]), "w_br_b": f(inp["w_br_b"]), "w_br_c": f(inp["w_br_c"]),
        "w_o": f(inp["w_o"]), "w_gu": f(inp["w_gu"]), "w_down": f(inp["w_down"]),
        "cst": cst, "cnt_lat": cnt_lat, "cnt_ctx": cnt_ctx,
    }
    return sh


def prep_core(inp, sh, b):
    m = dict(sh)
    m["x"] = np.ascontiguousarray(np.asarray(inp["x"], np.float32)[b])
    m["ctx"] = np.ascontiguousarray(np.asarray(inp["ctx"], np.float32)[b])
    m["cc"] = np.ascontiguousarray(np.stack([col(np.asarray(inp["c"])[b]), col(inp["c_ctx"])], axis=-1))
    return m


def kernel(**inputs):
    x = np.asarray(inputs["x"])
    B, NT, _ = x.shape
    nc = build(NT)
    sh = prep_shared(inputs, NT)
    in_maps = [prep_core(inputs, sh, c % B) for c in range(8)]
    res = run_bass_kernel_spmd(nc, in_maps, core_ids=list(range(8)))
    return np.stack([res.results[b]["out"] for b in range(B)]).astype(np.float32)
```

```python
import numpy as np
import ml_dtypes
from contextlib import ExitStack
import concourse.bass as bass
import concourse.mybir as mybir
from concourse.bass_utils import run_bass_kernel_spmd

F32 = mybir.dt.float32
BF16 = mybir.dt.bfloat16
AF = mybir.ActivationFunctionType
ALU = mybir.AluOpType

D = 1024
KC = 8
NCTX = 256
GW = 64
DFF = 2816
FC = DFF // 128
N_IN = 6160
EPS = 1e-6
NEG = -30000.0
ENGS = ("tensor", "vector", "scalar", "gpsimd", "sync")
NDS = 12
NCST = 13


class Buf:
    __slots__ = ("w", "r")

    def __init__(self):
        self.w = None
        self.r = []


class Sched:
    def __init__(self, nc, pname=""):
        self.nc = nc
        self.pname = pname
        self.ops = {e: [] for e in ENGS}
        self.keys = list(ENGS) + ["dma_%s_%d" % (q, i) for q in ("sync", "gpsimd", "scalar") for i in range(NDS)]
        self.cnt = {k: 0 for k in self.keys}
        self.seen = {e: {k: 0 for k in self.keys} for e in ENGS}
        self.dma_rr = {"sync": 0, "gpsimd": 0, "scalar": 0}
        self.sems = {}

    def alloc(self):
        for k in self.keys:
            self.sems[k] = self.nc.alloc_semaphore(name="s_%s_%s" % (self.pname, k))

    def _deps(self, eng, reads, writes):
        need = {}

        def add(tok):
            if tok is None:
                return
            p, c = tok
            if need.get(p, 0) < c:
                need[p] = c
        for b in reads:
            add(b.w)
        for b in writes:
            add(b.w)
            for t in b.r:
                add(t)
        waits = []
        for p, c in need.items():
            if eng == "tensor" and p == "tensor":
                continue
            if self.seen[eng][p] < c:
                self.seen[eng][p] = c
                waits.append((p, c))
        return waits

    def _commit(self, tok, reads, writes):
        for b in reads:
            b.r.append(tok)
        for b in writes:
            b.w = tok
            b.r = []

    def op(self, eng, fn, reads=(), writes=()):
        waits = self._deps(eng, reads, writes)
        self.cnt[eng] += 1
        tok = (eng, self.cnt[eng])
        self.ops[eng].append((waits, fn, (eng, 1)))
        self._commit(tok, reads, writes)
        return tok

    def dma(self, q, fn, reads=(), writes=()):
        waits = self._deps(q, reads, writes)
        key = "dma_%s_%d" % (q, self.dma_rr[q])
        self.dma_rr[q] = (self.dma_rr[q] + 1) % NDS
        if self.cnt[key] > 0 and self.seen[q][key] < self.cnt[key]:
            self.seen[q][key] = self.cnt[key]
            waits.append((key, self.cnt[key]))
        self.cnt[key] += 1
        tok = (key, self.cnt[key])
        self.ops[q].append((waits, fn, (key, 16)))
        self._commit(tok, reads, writes)
        return tok

    def emit(self):
        nc = self.nc
        mult = {k: (16 if k.startswith("dma_") else 1) for k in self.keys}
        waited = {k: set() for k in self.keys}
        for en in ENGS:
            for waits, fn, sk in self.ops[en]:
                for p, c in waits:
                    waited[p].add(c)
        for k in self.keys:
            waited[k].add(self.cnt[k])
        with nc.Block() as block:
            def run(engname):
                def body(e):
                    idx = 0
                    last = 0
                    for waits, fn, (sk, inc) in self.ops[engname]:
                        for p, c in waits:
                            e.wait_ge(self.sems[p], c * mult[p])
                        ins = fn(e)
                        if sk == engname:
                            idx += 1
                            if idx in waited[engname]:
                                ins.then_inc(self.sems[sk], idx - last)
                                last = idx
                        else:
                            ins.then_inc(self.sems[sk], inc)
                    if engname == "sync":
                        for k in self.keys:
                            if self.cnt[k] > 0:
                                e.wait_ge(self.sems[k], self.cnt[k] * mult[k])
                return body
            block.tensor(run("tensor"))
            block.vector(run("vector"))
            block.scalar(run("scalar"))
            block.gpsimd(run("gpsimd"))
            block.sync(run("sync"))


class Phase:
    def __init__(self, nc, name):
        self.nc = nc
        self.name = name
        self.es = ExitStack()
        self.S = Sched(nc, name)
        self.n = 0

    def __enter__(self):
        self.es.enter_context(self.nc.cleanup_on_exit())
        self.S.alloc()
        return self

    def __exit__(self, *a):
        if a[0] is None:
            self.S.emit()
        self.es.close()
        return False

    def sb(self, shape, dt, name=None):
        self.n += 1
        return self.es.enter_context(self.nc.sbuf_tensor("%s_%s%d" % (self.name, name or "t", self.n), list(shape), dt))

    def ps(self, shape, dt, name=None):
        self.n += 1
        return self.es.enter_context(self.nc.psum_tensor("%s_%s%d" % (self.name, name or "p", self.n), list(shape), dt))

    def dma(self, q, out, in_, r=(), w=()):
        return self.S.dma(q, lambda e: e.dma_start(out=out, in_=in_), r, w)

    def mm(self, out, lhsT, rhs, start=True, stop=True, r=(), w=()):
        return self.S.op("tensor", lambda e: e.matmul(out, lhsT=lhsT, rhs=rhs, start=start, stop=stop), r, w)

    def tr(self, out, in_, ident, r=(), w=()):
        return self.S.op("tensor", lambda e: e.transpose(out, in_, ident), r, w)

    def act(self, out, in_, func, bias=None, scale=None, r=(), w=()):
        kw = {}
        if bias is not None:
            kw["bias"] = bias
        if scale is not None:
            kw["scale"] = scale
        return self.S.op("scalar", lambda e: e.activation(out=out, in_=in_, func=func, **kw), r, w)

    def tt(self, eng, out, in0, in1, op, r=(), w=()):
        return self.S.op(eng, lambda e: e.tensor_tensor(out=out, in0=in0, in1=in1, op=op), r, w)

    def ts(self, eng, out, in0, s1, op0, s2=None, op1=None, r=(), w=()):
        if op1 is None:
            return self.S.op(eng, lambda e: e.tensor_scalar(out=out, in0=in0, scalar1=s1, scalar2=None, op0=op0), r, w)
        return self.S.op(eng, lambda e: e.tensor_scalar(out=out, in0=in0, scalar1=s1, scalar2=s2, op0=op0, op1=op1), r, w)

    def stt(self, eng, out, in0, scalar, in1, op0, op1, r=(), w=()):
        return self.S.op(eng, lambda e: e.scalar_tensor_tensor(out=out, in0=in0, scalar=scalar, in1=in1, op0=op0, op1=op1), r, w)

    def cp(self, eng, out, in_, r=(), w=()):
        if eng == "scalar":
            return self.S.op(eng, lambda e: e.copy(out=out, in_=in_), r, w)
        return self.S.op(eng, lambda e: e.tensor_copy(out=out, in_=in_), r, w)

    def memset(self, eng, ap, val, r=(), w=()):
        return self.S.op(eng, lambda e: e.memset(ap, val), r, w)


class Rot:
    def __init__(self, ph, n, shape, dt, name, psum=False):
        mk = ph.ps if psum else ph.sb
        self.t = [mk(shape, dt, name) for _ in range(n)]
        self.b = [Buf() for _ in range(n)]
        self.i = -1

    def next(self):
        self.i = (self.i + 1) % len(self.t)
        return self.t[self.i], self.b[self.i]


def token_groups(NT, G):
    gs = [(1, 0, NCTX)]
    t = NCTX
    while t < NCTX + NT:
        gs.append((0, t, G))
        t += G
    return gs


def build(NT, stop_after=None, dbg=()):
    NTA = NCTX + NT
    NTILE = NTA // 128
    ROWS = NT // GW
    nc = bass.Bass("TRN2", target_bir_lowering=False)

    def din(name, shape, dt=F32):
        return nc.dram_tensor(name, list(shape), dt, kind="ExternalInput").ap()

    def scratch(name, shape, dt=F32):
        kind = "ExternalOutput" if name in dbg else "Internal"
        return nc.dram_tensor(name, list(shape), dt, kind=kind).ap()

    x_d = din("x", [NT, D]); ctx_d = din("ctx", [NCTX, D]); cc_d = din("cc", [128, KC, 2])
    w_ada_d = din("w_ada", [2, D, 6 * D]); b_adaT_d = din("b_adaT", [2, 128, 48])
    n1g_d = din("n1g", [2, 128, KC]); n2g_d = din("n2g", [2, 128, KC]); fng_d = din("fng", [128, KC])
    w_in_d = din("w_in", [2, D, N_IN]); dcw_d = din("dcw", [2, 128, 12, 3])
    alog_d = din("alog", [2, 128, 8]); dtb_d = din("dtb", [2, 128, 8]); dng_d = din("dng", [2, 128, 1])
    poolw_d = din("poolw", [2, 4, 64, 64]); pscale_d = din("pscale", [2, 128, 2]); scw_d = din("scw", [2, 128, 2, 3])
    wbra_d = din("w_br_a", [2, 512, D]); wbrb_d = din("w_br_b", [2, 256, D]); wbrc_d = din("w_br_c", [2, 256, D])
    wo_d = din("w_o", [2, D, D]); wgu_d = din("w_gu", [2, D, 2 * DFF]); wdn_d = din("w_down", [2, DFF, D])
    cst_d = din("cst", [128, NCST, 128]); cntl_d = din("cnt_lat", [2, 128, NT]); cntc_d = din("cnt_ctx", [2, 128, NCTX])
    out_d = nc.dram_tensor("out", [NT, D], F32, kind="ExternalOutput").ap()

    xT = scratch("xT", [KC, 128, NTA])
    pqkvT = scratch("pqkvT", [12, 128, NTA], BF16); szT = scratch("szT", [4, 128, NTA], BF16)
    ppoolT = scratch("ppoolT", [2, 128, NTA], BF16); pscT = scratch("pscT", [6, 128, NTA], BF16)
    gatesT = scratch("gatesT", [24, 128, NTA], BF16)
    abS = scratch("abS", [NTILE, 128, 24])
    qnT = scratch("qnT", [4, 128, NTA], BF16); knT = scratch("knT", [4, 128, NTA], BF16)
    kTM = scratch("kTM", [NTILE, 128, 512], BF16); vTM = scratch("vTM", [NTILE, 128, 512], BF16)
    oTd = [scratch("oTf", [4, 128, NTA], BF16), scratch("oTb", [4, 128, NTA], BF16)]
    dT = scratch("dT", [2, 128, NTA], BF16)
    modd = scratch("modd", [2, 128, 6 * KC, 2])
    sfin = scratch("sfin", [2, 2, 4, 128, 128])
    dbgbuf = scratch("dbgbuf", [2, 128, 8, 128])

    def done(tag):
        return stop_after == tag

    with Phase(nc, "p0") as ph:
        cst = ph.sb([128, NCST, 128], F32, "cst"); b_cst = Buf()
        ph.dma("sync", cst[:], cst_d, w=[b_cst])
        cc = ph.sb([128, KC, 2], F32); b_cc = Buf()
        ph.dma("sync", cc[:], cc_d, w=[b_cc])
        scc = ph.sb([128, KC, 2], F32); b_scc = Buf()
        ph.act(scc[:], cc[:], AF.Silu, r=[b_cc], w=[b_scc])
        wrot = Rot(ph, 2, [128, KC, 768], F32, "wada")
        modps = ph.ps([128, 48, 2], F32); b_modps = Buf()
        for l in range(2):
            badaT = ph.sb([128, 48], F32); b_bada = Buf()
            ph.dma("sync", badaT[:], b_adaT_d[l], w=[b_bada])
            for pc in range(8):
                wt, wb = wrot.next()
                ph.dma("sync" if pc % 2 == 0 else "scalar", wt[:], w_ada_d[l, :, pc * 768:(pc + 1) * 768].rearrange("(k p) n -> p k n", p=128), w=[wb])
                for mi in range(6):
                    m = pc * 6 + mi
                    for k in range(KC):
                        ph.mm(modps[:, m, :], wt[:, k, mi * 128:(mi + 1) * 128], scc[:, k, :], start=(k == 0), stop=(k == KC - 1),
                              r=[wb, b_scc], w=[b_modps])
            mods = ph.sb([128, 48, 2], F32); b_mods = Buf()
            for j in range(2):
                ph.tt("vector", mods[:, :, j], modps[:, :, j], badaT[:], ALU.add, r=[b_modps, b_bada], w=[b_mods])
            ph.dma("sync", modd[l], mods[:], r=[b_mods])
        xrot = Rot(ph, 3, [128, D], F32, "xin")
        trps = Rot(ph, 2, [128, KC, 128], F32, "trps", psum=True)
        orot = Rot(ph, 3, [128, KC, 128], F32, "xo")
        for ti in range(NTILE):
            xt, xb = xrot.next()
            src = ctx_d[ti * 128:(ti + 1) * 128, :] if ti < NCTX // 128 else x_d[ti * 128 - NCTX:(ti + 1) * 128 - NCTX, :]
            ph.dma("sync", xt[:], src, w=[xb])
            pt, pb = trps.next()
            for k in range(KC):
                ph.mm(pt[:, k, :], xt[:, k * 128:(k + 1) * 128], cst[:, 0, :], r=[xb, b_cst], w=[pb])
            ot, ob = orot.next()
            ph.cp("vector" if ti % 2 == 0 else "scalar", ot[:], pt[:], r=[pb], w=[ob])
            ph.dma("sync", xT[:, :, ti * 128:(ti + 1) * 128].rearrange("k p t -> p k t"), ot[:], r=[ob])
    if done("p0"):
        return nc

    for l in range(2):
        last = (l == 1)
        with Phase(nc, "p1_%d" % l) as ph:
            cst = ph.sb([128, NCST, 128], F32, "cst"); b_cst = Buf()
            ph.dma("sync", cst[:], cst_d, w=[b_cst])
            ones_bf = ph.sb([128, 128], BF16); b_ones = Buf()
            ph.cp("vector", ones_bf[:], cst[:, 1, :], r=[b_cst], w=[b_ones])
            win = ph.sb([128, KC, N_IN], BF16, "win"); b_win = [Buf() for _ in range(KC)]
            for k in range(KC):
                ph.dma("gpsimd", win[:, k, :], w_in_d[l, k * 128:(k + 1) * 128, :], w=[b_win[k]])
            mods = ph.sb([128, 48, 2], F32); b_mods = Buf()
            ph.dma("sync", mods[:], modd[l], w=[b_mods])
            n1g = ph.sb([128, KC], F32); b_n1g = Buf()
            ph.dma("sync", n1g[:], n1g_d[l], w=[b_n1g])
            A1 = ph.sb([128, 2, KC], F32); b_A1 = Buf()
            for j in range(2):
                ph.stt("vector", A1[:, j, :], mods[:, 8:16, j], 1.0, n1g[:], ALU.add, ALU.mult, r=[b_mods, b_n1g], w=[b_A1])
            alog = ph.sb([128, 8], F32); dtb = ph.sb([128, 8], F32); b_al = Buf(); b_dtb = Buf()
            ph.dma("sync", alog[:], alog_d[l], w=[b_al])
            ph.dma("sync", dtb[:], dtb_d[l], w=[b_dtb])
            negea = ph.sb([128, 8], F32); b_negea = Buf()
            ph.act(negea[:], alog[:], AF.Exp, r=[b_al], w=[b_negea])
            ph.ts("vector", negea[:], negea[:], -1.0, ALU.mult, r=[b_negea], w=[b_negea])

            xrot = Rot(ph, 2, [128, KC, 512], F32, "xg")
            sqrot = Rot(ph, 1, [128, KC, 512], BF16, "sq")
            hrot = Rot(ph, 2, [128, KC, 512], BF16, "hT")
            tmprot = Rot(ph, 2, [128, 512], F32, "tmp")
            rsrot = Rot(ph, 2, [128, 512], F32, "rstd")
            stf = Rot(ph, 4, [128, 512], BF16, "stf")
            stb = Rot(ph, 4, [128, 512], BF16, "stb")
            abrot = Rot(ph, 2, [128, 24], F32, "ab")
            abt = Rot(ph, 2, [128, 8], F32, "abt")
            ssps = Rot(ph, 1, [128, 512], F32, "ssps", psum=True)
            accps = Rot(ph, 5, [128, 512], F32, "acc", psum=True)
            abps = Rot(ph, 2, [128, 16], F32, "abps", psum=True)

            chunks = []
            for c in range(12):
                chunks.append((c * 128, "copy", pqkvT, c))
            for c in range(4):
                chunks.append((1536 + c * 128, "silu", szT, c))
            for c in range(2):
                chunks.append((2064 + c * 128, "copy", ppoolT, c))
            for c in range(6):
                chunks.append((2320 + c * 128, "copy", pscT, c))
            for c in range(24):
                chunks.append((3088 + c * 128, "sigm", gatesT, c))

            for (j, t0, G) in token_groups(NT, 512):
                xg, xb = xrot.next()
                ph.dma("sync", xg[:, :, 0:G], xT[:, :, t0:t0 + G].rearrange("k p t -> p k t"), w=[xb])
                sq, sqb = sqrot.next()
                ph.act(sq[:, :, 0:G], xg[:, :, 0:G], AF.Square, r=[xb], w=[sqb])
                sp, spb = ssps.next()
                for k in range(KC):
                    ph.mm(sp[:, 0:G], ones_bf[:], sq[:, k, 0:G], start=(k == 0), stop=(k == KC - 1), r=[b_ones, sqb], w=[spb])
                rs, rsb = rsrot.next()
                ph.act(rs[:, 0:G], sp[:, 0:G], AF.Ln, bias=EPS, scale=1.0 / D, r=[spb], w=[rsb])
                ph.act(rs[:, 0:G], rs[:, 0:G], AF.Exp, scale=-0.5, r=[rsb], w=[rsb])
                hT, hb = hrot.next()
                for k in range(KC):
                    tm, tmb = tmprot.next()
                    ph.stt("vector", tm[:, 0:G], xg[:, k, 0:G], A1[:, j, k:k + 1], rs[:, 0:G], ALU.mult, ALU.mult,
                           r=[xb, b_A1, rsb], w=[tmb])
                    ph.act(hT[:, k, 0:G], tm[:, 0:G], AF.Identity, bias=mods[:, k, j:j + 1], r=[tmb, b_mods], w=[hb])
                for s in range(G // 128):
                    ap_, apb = abps.next()
                    for k in range(KC):
                        ph.mm(ap_[:], hT[:, k, s * 128:(s + 1) * 128], win[:, k, 2048:2064], start=(k == 0), stop=(k == KC - 1),
                              r=[hb, b_win[k]], w=[apb])
                    ab, abb = abrot.next()
                    at, atb = abt.next()
                    ph.tt("vector", at[:], ap_[:, 0:8], dtb[:], ALU.add, r=[apb, b_dtb], w=[atb])
                    ph.act(at[:], at[:], AF.Exp, r=[atb], w=[atb])
                    ph.act(at[:], at[:], AF.Ln, bias=1.0, r=[atb], w=[atb])
                    ph.tt("vector", ab[:, 0:8], at[:], negea[:], ALU.mult, r=[atb, b_negea], w=[abb])
                    ph.act(ab[:, 8:16], ap_[:, 8:16], AF.Sigmoid, w=[apb, abb])
                    ph.ts("vector", ab[:, 16:24], ab[:, 8:16], -1.0, ALU.mult, r=[abb], w=[abb])
                    ph.dma("sync", abS[(t0 // 128) + s], ab[:], r=[abb])
                for ci, (c0, kind, dst, dc) in enumerate(chunks):
                    acc, accb = accps.next()
                    for k in range(KC):
                        ph.mm(acc[:, 0:G], win[:, k, c0:c0 + 128], hT[:, k, 0:G], start=(k == 0), stop=(k == KC - 1),
                              r=[hb, b_win[k]], w=[accb])
                    if kind == "copy":
                        st, sb_ = stf.next()
                        ph.cp("vector", st[:, 0:G], acc[:, 0:G], r=[accb], w=[sb_])
                        ph.dma("sync", dst[dc, :, t0:t0 + G], st[:, 0:G], r=[sb_])
                    else:
                        st, sb_ = stb.next()
                        ph.act(st[:, 0:G], acc[:, 0:G], AF.Silu if kind == "silu" else AF.Sigmoid, r=[accb], w=[sb_])
                        ph.dma("scalar", dst[dc, :, t0:t0 + G], st[:, 0:G], r=[sb_])
        if done("p1_%d" % l):
            return nc


        with Phase(nc, "p2a_%d" % l) as ph:
            cst = ph.sb([128, NCST, 128], F32, "cst"); b_cst = Buf()
            ph.dma("sync", cst[:], cst_d, w=[b_cst])
            ones_bf = ph.sb([128, 128], BF16); ident_bf = ph.sb([128, 128], BF16); b_cb = Buf()
            ph.cp("vector", ones_bf[:], cst[:, 1, :], r=[b_cst], w=[b_cb])
            ph.cp("vector", ident_bf[:], cst[:, 0, :], r=[b_cst], w=[b_cb])
            cw = ph.sb([128, 12, 3], F32); b_cw = Buf()
            ph.dma("sync", cw[:], dcw_d[l], w=[b_cw])
            pqrot = Rot(ph, 2, [128, 12, 514], BF16, "pq")
            srot = Rot(ph, 1, [128, 8, 512], F32, "s")
            vrot = Rot(ph, 2, [128, 4, 512], BF16, "vT")
            qkrot = Rot(ph, 2, [128, 8, 512], BF16, "qkn")
            tmrot = Rot(ph, 3, [128, 512], F32, "tm")
            sqrot = Rot(ph, 2, [128, 512], BF16, "sq")
            rrot = Rot(ph, 2, [128, 512], F32, "r")
            ssps = Rot(ph, 2, [128, 512], F32, "ss", psum=True)
            trps = Rot(ph, 4, [128, 4, 128], BF16, "trp", psum=True)
            tmo = Rot(ph, 4, [128, 512], BF16, "tmo")
            QB = float(np.log(128.0 ** -0.5))
            for (j, t0, G) in token_groups(NT, 512):
                s_lo, s_hi = (0, NCTX) if j == 1 else (NCTX, NTA)
                pq, pqb = pqrot.next()
                a = max(t0 - 1, s_lo); b = min(t0 + G + 1, s_hi)
                if a > t0 - 1:
                    ph.memset("gpsimd", pq[:, :, 0:1], 0.0, w=[pqb])
                if b < t0 + G + 1:
                    ph.memset("gpsimd", pq[:, :, G + 1:G + 2], 0.0, w=[pqb])
                ph.dma("sync", pq[:, :, a - (t0 - 1):b - (t0 - 1)], pqkvT[:, :, a:b].rearrange("c p t -> p c t"), w=[pqb])
                st, sb_ = srot.next()
                vT_, vb = vrot.next()
                qk, qkb = qkrot.next()
                for c in range(12):
                    tm, tmb = tmrot.next()
                    ph.act(tm[:, 0:G], pq[:, c, 1:G + 1], AF.Identity, scale=cw[:, c, 1:2], r=[pqb, b_cw], w=[tmb])
                    ph.stt("vector", tm[:, 0:G], pq[:, c, 0:G], cw[:, c, 0:1], tm[:, 0:G], ALU.mult, ALU.add, r=[pqb, b_cw, tmb], w=[tmb])
                    ph.stt("vector", tm[:, 0:G], pq[:, c, 2:G + 2], cw[:, c, 2:3], tm[:, 0:G], ALU.mult, ALU.add, r=[pqb, b_cw, tmb], w=[tmb])
                    if c < 8:
                        ph.act(st[:, c, 0:G], tm[:, 0:G], AF.Silu, r=[tmb], w=[sb_])
                    else:
                        ph.act(vT_[:, c - 8, 0:G], tm[:, 0:G], AF.Silu, r=[tmb], w=[vb])
                for c in range(8):
                    sq, sqb = sqrot.next()
                    ph.tt("gpsimd", sq[:, 0:G], st[:, c, 0:G], st[:, c, 0:G], ALU.mult, r=[sb_], w=[sqb])
                    sp, spb = ssps.next()
                    ph.mm(sp[:, 0:G], ones_bf[:], sq[:, 0:G], r=[b_cb, sqb], w=[spb])
                    rr, rb = rrot.next()
                    ph.act(rr[:, 0:G], sp[:, 0:G], AF.Ln, bias=EPS, r=[spb], w=[rb])
                    ph.act(rr[:, 0:G], rr[:, 0:G], AF.Exp, scale=-0.5, bias=(QB if c < 4 else 0.0), r=[rb], w=[rb])
                    ph.tt("vector", qk[:, c, 0:G], st[:, c, 0:G], rr[:, 0:G], ALU.mult, r=[sb_, rb], w=[qkb])
                ph.dma("sync", qnT[:, :, t0:t0 + G].rearrange("h p t -> p h t"), qk[:, 0:4, 0:G], r=[qkb])
                ph.dma("sync", knT[:, :, t0:t0 + G].rearrange("h p t -> p h t"), qk[:, 4:8, 0:G], r=[qkb])
                for s in range(G // 128):
                    for which in range(2):
                        tp, tpb = trps.next()
                        for h in range(4):
                            src = qk[:, 4 + h, s * 128:(s + 1) * 128] if which == 0 else vT_[:, h, s * 128:(s + 1) * 128]
                            ph.tr(tp[:, h, :], src, ident_bf[:], r=[qkb if which == 0 else vb, b_cb], w=[tpb])
                        to, tob = tmo.next()
                        ph.cp("scalar" if which == 0 else "vector", to[:].rearrange("p (h t) -> p h t", h=4), tp[:], r=[tpb], w=[tob])
                        ph.dma("sync", (kTM if which == 0 else vTM)[t0 // 128 + s], to[:], r=[tob])
        if done("p2a_%d" % l):
            return nc

        with Phase(nc, "p2c_%d" % l) as ph:
            for (j, c0, Rr, Wd, cnt_src) in ((1, 0, 1, NCTX, cntc_d), (0, NCTX, ROWS, GW, cntl_d)):
                n_tok = Rr * Wd
                RP = Rr + 16 if Rr > 1 else 1
                WP = Wd + 16
                X = ph.sb([128, n_tok], BF16, "pX"); PA = ph.sb([128, RP, WP], F32, "pA"); PB = ph.sb([128, RP, WP], F32, "pB")
                CN = ph.sb([128, n_tok], F32, "pC"); DO = ph.sb([128, n_tok], BF16, "pD")
                bX = Buf(); bC = Buf(); bA = [Buf(), Buf()]; bB = [Buf(), Buf()]; bD = [Buf(), Buf()]
                r0 = 8 if Rr > 1 else 0
                for c in range(2):
                    ph.dma("sync", X[:], ppoolT[c, :, c0:c0 + n_tok], w=[bX])
                    ph.dma("sync", CN[:], cnt_src[c], w=[bC])
                    ph.memset("gpsimd", PA[:], 0.0, w=bA)
                    ph.cp("vector", PA[:, r0:r0 + Rr, 8:8 + Wd], X[:].rearrange("p (r w) -> p r w", w=Wd), r=[bX], w=bA)
                    for half in range(2):
                        eng = "gpsimd" if half == 0 else "vector"
                        w_ = POOL_WINDOWS[2 * c + half]
                        lo = w_ // 2
                        nl = int(np.log2(w_))
                        pr = slice(half * 64, (half + 1) * 64)
                        src, dst, bs, bd = PA, PB, bA[half], bB[half]
                        for lv in range(nl):
                            sft = 1 << lv
                            ph.tt(eng, dst[pr, :, 0:WP - sft], src[pr, :, 0:WP - sft], src[pr, :, sft:WP], ALU.add, r=[bs], w=[bd])
                            src, dst, bs, bd = dst, src, bd, bs
                        if Rr > 1:
                            for lv in range(nl):
                                sft = 1 << lv
                                ph.tt(eng, dst[pr, 0:RP - sft, :], src[pr, 0:RP - sft, :], src[pr, sft:RP, :], ALU.add, r=[bs], w=[bd])
                                src, dst, bs, bd = dst, src, bd, bs
                        ro = r0 - lo if Rr > 1 else 0
                        co = 8 - lo
                        Mv = src[pr, ro:ro + Rr, co:co + Wd]
                        tflat = dst[pr].rearrange("p r w -> p (r w)")[:, 0:n_tok]
                        ph.tt(eng, tflat.rearrange("p (r w) -> p r w", w=Wd), Mv, CN[pr].rearrange("p (r w) -> p r w", w=Wd), ALU.mult,
                              r=[bs, bC], w=[bd])
                        ph.tt(eng, DO[pr], tflat, X[pr], ALU.subtract, r=[bd, bX], w=[bD[half]])
                    ph.dma("sync", dT[c, :, c0:c0 + n_tok], DO[:], r=bD)
        if done("p2c_%d" % l):
            return nc


        with Phase(nc, "p2b_%d" % l) as ph:
            cst = ph.sb([128, NCST, 128], F32, "cst"); b_cst = Buf()
            ph.dma("sync", cst[:], cst_d, w=[b_cst])
            ident_bf = ph.sb([128, 128], BF16); b_ib = Buf()
            ph.cp("vector", ident_bf[:], cst[:, 0, :], r=[b_cst], w=[b_ib])
            S32 = [ph.sb([128, 4, 128], F32, "S32") for _ in range(2)]
            Sbf = [ph.sb([128, 4, 128], BF16, "Sbf") for _ in range(2)]
            bS32 = [[Buf() for _ in range(4)] for _ in range(2)]
            bSbf = [[Buf() for _ in range(4)] for _ in range(2)]
            for d in range(2):
                ph.memset("vector", S32[d][:], 0.0, w=bS32[d])
                ph.memset("gpsimd", Sbf[d][:], 0.0, w=bSbf[d])
            banks = Rot(ph, 8, [128, 512], F32, "bank", psum=True)

            def trbank():
                t_, b_ = banks.next()
                return t_[:].bitcast(BF16), b_
            T = {}
            TB = {}

            def tl(name, d, par, shape, dt):
                key = (name, d, par)
                if key not in T:
                    T[key] = ph.sb(shape, dt, name)
                return T[key]

            def tb(name, d, par, h=0):
                return TB.setdefault((name, d, par, h), Buf())
            order = [list(range(NTILE)), [1, 0] + list(range(NTILE - 1, 1, -1))]

            REC = ("ab", "sm", "wTp", "u", "kst", "qdecT", "aqkT", "vnew", "oTs")

            def prep(steps):
                C = {}
                for slot, step in enumerate(steps):
                    for d in range(2):
                        ti = order[d][step]
                        tc = slice(ti * 128, (ti + 1) * 128)
                        rk = step % 4

                        def mk(name, shape, dt, _d=d, _slot=slot, _rk=rk):
                            return tl(name, _d, ("r", _rk) if name in REC else ("p", _slot), shape, dt)

                        def B(name, h=0, _d=d, _slot=slot, _rk=rk):
                            return tb(name, _d, ("r", _rk) if name in REC else ("p", _slot), h)
                        ab = mk("ab", [128, 24], F32)
                        c = dict(step=step, ti=ti, tc=tc, B=B, ab=ab, sm=mk("sm", [128, 16], F32),
                                 qn=mk("qn", [128, 4, 128], BF16), kn=mk("kn", [128, 4, 128], BF16),
                                 kT=mk("kT", [128, 512], BF16), vT=mk("vT", [128, 512], BF16),
                                 Gbc=mk("Gbc", [128, 4, 128], F32), EQ=mk("EQ", [128, 4, 128], F32), E2T=mk("E2T", [128, 4, 128], F32),
                                 EsT=mk("EsT", [128, 4, 128], F32), M0t=mk("M0t", [128, 4, 128], BF16),
                                 A0t=mk("A0t", [128, 4, 128], BF16), AMb=mk("AMb", [128, 4, 2, 128], BF16),
                                 AM1=mk("AM1", [128, 4, 2, 128], BF16), A2t=mk("A2t", [128, 4, 128], BF16),
                                 PP=[mk("Pa", [128, 4, 128], BF16), mk("Pb", [128, 4, 128], BF16)],
                                 PT=mk("PT", [128, 4, 128], BF16), Xt=mk("Xt", [128, 4, 128], BF16),
                                 aqkT=mk("aqkT", [128, 4, 128], BF16), qdecT=mk("qdecT", [128, 4, 128], BF16),
                                 kegc=mk("kegc", [128, 4, 128], BF16), kst=mk("kst", [128, 4, 128], BF16),
                                 wTp=mk("wTp", [128, 4, 128], BF16), u=mk("u", [128, 4, 128], F32),
                                 vnew=mk("vnew", [128, 4, 128], BF16), oTs=mk("oTs", [128, 4, 128], BF16),
                                 b4=ab[:, 8 + d * 4:12 + d * 4], nb4=ab[:, 16 + d * 4:20 + d * 4])
                        C[(slot, d)] = c
                        ph.dma("sync", c["qn"][:], qnT[:, :, tc].rearrange("h p t -> p h t"), w=[B("qn")])
                        ph.dma("sync", c["kn"][:], knT[:, :, tc].rearrange("h p t -> p h t"), w=[B("kn")])
                        ph.dma("sync", c["kT"][:], kTM[ti], w=[B("kT")])
                        ph.dma("sync", c["vT"][:], vTM[ti], w=[B("vT")])
                        ph.dma("sync", ab[:], abS[ti], w=[B("ab")])
                        sm = c["sm"]
                        bsm = B("sm"); bab = B("ab")
                        pS, bpS = banks.next()
                        ph.mm(pS[:, 0:8], cst[:, 2 + d, :], ab[:, 0:8], r=[b_cst, bab], w=[bpS])
                        ph.mm(pS[:, 8:16], cst[:, 1, :], ab[:, 0:8], r=[b_cst, bab], w=[bpS])
                        gcs = pS[:, d * 4:d * 4 + 4]
                        gls = pS[:, 8 + d * 4:12 + d * 4]
                        ph.ts("vector", sm[:, 0:4], gcs, -1.0, ALU.mult, w=[bpS, bsm])
                        ph.act(sm[:, 4:8], gcs, AF.Exp, w=[bpS, bsm])
                        ph.act(sm[:, 8:12], gls, AF.Exp, w=[bpS, bsm])
                        ph.tt("vector", sm[:, 12:16], gls, sm[:, 0:4], ALU.add, w=[bpS, bsm])
                        ph.act(sm[:, 12:16], sm[:, 12:16], AF.Exp, w=[bsm])
                DHS = [(slot, d, h) for slot in range(len(steps)) for d in range(2) for h in range(4)]
                bk = {}

                def every(fn):
                    for ch in DHS:
                        fn(C[(ch[0], ch[1])], *ch)

                def stage(pe, cons):
                    for i in range(0, len(DHS), 8):
                        for ch in DHS[i:i + 8]:
                            pe(C[(ch[0], ch[1])], *ch)
                        for ch in DHS[i:i + 8]:
                            cons(C[(ch[0], ch[1])], *ch)

                def bank(ch, tr_=False):
                    bk[ch] = trbank() if tr_ else banks.next()
                    return bk[ch]

                every(lambda c, slot, d, h: ph.act(c["Gbc"][:, h, :], cst[:, 1, :], AF.Identity, scale=c["ab"][:, d * 4 + h:d * 4 + h + 1],
                                                   r=[b_cst, c["B"]("ab")], w=[c["B"]("Gbc", h)]))

                def pe_R(c, slot, d, h):
                    pR, bR = bank((slot, d, h))
                    ph.mm(pR[:, 0:128], c["Gbc"][:, h, :], cst[:, 2 + d, :], r=[c["B"]("Gbc", h), b_cst], w=[bR])
                    ph.mm(pR[:, 128:256], c["Gbc"][:, h, :], cst[:, 2 + d, :], start=True, stop=False, r=[c["B"]("Gbc", h), b_cst], w=[bR])
                    ph.mm(pR[:, 128:256], cst[:, 0, :], cst[:, 4 + d, :], start=False, stop=True, r=[b_cst], w=[bR])

                def co_R(c, slot, d, h):
                    pR, bR = bk[(slot, d, h)]
                    ph.act(c["EQ"][:, h, :], pR[:, 0:128], AF.Exp, w=[bR, c["B"]("EQ", h)])
                    ph.act(c["E2T"][:, h, :], pR[:, 128:256], AF.Exp, bias=c["sm"][:, h:h + 1], r=[c["B"]("sm")], w=[bR, c["B"]("E2T", h)])
                stage(pe_R, co_R)

                def po_E(c, slot, d, h):
                    ph.tt("gpsimd", c["EsT"][:, h, :], c["E2T"][:, h, :], cst[:, 6 + d, :], ALU.mult, r=[c["B"]("E2T", h), b_cst], w=[c["B"]("EsT", h)])
                    ph.tt("gpsimd", c["qdecT"][:, h, :], c["qn"][:, h, :], c["EQ"][:, h, :], ALU.mult, r=[c["B"]("qn"), c["B"]("EQ", h)], w=[c["B"]("qdecT", h)])
                every(po_E)

                def pe_G(c, slot, d, h):
                    pG, bG = bank((slot, d, h))
                    ph.mm(pG[:, 0:128], c["kn"][:, h, :], c["kn"][:, h, :], r=[c["B"]("kn")], w=[bG])
                    ph.mm(pG[:, 128:256], c["kn"][:, h, :], c["qn"][:, h, :], r=[c["B"]("kn"), c["B"]("qn")], w=[bG])

                def co_G(c, slot, d, h):
                    pG, bG = bk[(slot, d, h)]
                    ph.stt("vector", c["M0t"][:, h, :], pG[:, 0:128], c["b4"][:, h:h + 1], c["EsT"][:, h, :], ALU.mult, ALU.mult,
                           r=[c["B"]("ab"), c["B"]("EsT", h)], w=[bG, c["B"]("M0t", h)])
                    ph.tt("vector", c["aqkT"][:, h, :], pG[:, 128:256], c["E2T"][:, h, :], ALU.mult,
                          r=[c["B"]("E2T", h)], w=[bG, c["B"]("aqkT", h)])
                stage(pe_G, co_G)

                def pe_T(c, slot, d, h):
                    pT_, bT_ = bank((slot, d, h), True)
                    ph.tr(pT_[:, 0:128], c["M0t"][:, h, :], ident_bf[:], r=[c["B"]("M0t", h), b_ib], w=[bT_])

                def co_T(c, slot, d, h):
                    pT_, bT_ = bk[(slot, d, h)]
                    ph.cp("scalar", c["A0t"][:, h, :], pT_[:, 0:128], w=[bT_, c["B"]("A0t", h)])
                stage(pe_T, co_T)

                def po_M(c, slot, d, h):
                    ph.tt("gpsimd", c["AMb"][:, h, 1, :], c["M0t"][:, h, :], cst[:, 8, :], ALU.mult, r=[c["B"]("M0t", h), b_cst], w=[c["B"]("AMb", h)])
                    ph.tt("gpsimd", c["AMb"][:, h, 0, :], c["A0t"][:, h, :], cst[:, 8, :], ALU.mult, r=[c["B"]("A0t", h), b_cst], w=[c["B"]("AMb", h)])
                    ph.tt("gpsimd", c["PP"][0][:, h, :], cst[:, 0, :], c["AMb"][:, h, 1, :], ALU.subtract, r=[b_cst, c["B"]("AMb", h)], w=[c["B"]("P0", h)])
                every(po_M)

                def pe_A1(c, slot, d, h):
                    pA, bA_ = bank((slot, d, h))
                    ph.mm(pA[:, 0:128], c["AMb"][:, h, 1, :], c["AMb"][:, h, 0, :], r=[c["B"]("AMb", h)], w=[bA_])
                    ph.mm(pA[:, 128:256], c["AMb"][:, h, 0, :], c["AMb"][:, h, 1, :], r=[c["B"]("AMb", h)], w=[bA_])

                def co_A1(c, slot, d, h):
                    pA, bA_ = bk[(slot, d, h)]
                    ph.cp("scalar", c["AM1"][:, h, :, :], pA[:, 0:256].rearrange("p (a b) -> p a b", a=2), w=[bA_, c["B"]("AM1", h)])
                stage(pe_A1, co_A1)

                def pe_P1(c, slot, d, h):
                    pP, bP_ = bank((slot, d, h))
                    ph.mm(pP[:, 0:128], c["AM1"][:, h, 0, :], c["PP"][0][:, h, :], r=[c["B"]("AM1", h), c["B"]("P0", h)], w=[bP_])

                def co_P1(c, slot, d, h):
                    pP, bP_ = bk[(slot, d, h)]
                    ph.tt("vector", c["PP"][1][:, h, :], pP[:, 0:128], c["PP"][0][:, h, :], ALU.add, r=[c["B"]("P0", h)], w=[bP_, c["B"]("P1", h)])
                stage(pe_P1, co_P1)

                def pe_A2(c, slot, d, h):
                    pA, bA_ = bank((slot, d, h))
                    ph.mm(pA[:, 0:128], c["AM1"][:, h, 1, :], c["AM1"][:, h, 0, :], r=[c["B"]("AM1", h)], w=[bA_])

                def co_A2(c, slot, d, h):
                    pA, bA_ = bk[(slot, d, h)]
                    ph.cp("scalar", c["A2t"][:, h, :], pA[:, 0:128], w=[bA_, c["B"]("A2t", h)])
                stage(pe_A2, co_A2)

                def pe_P2(c, slot, d, h):
                    pP, bP_ = bank((slot, d, h))
                    ph.mm(pP[:, 0:128], c["A2t"][:, h, :], c["PP"][1][:, h, :], r=[c["B"]("A2t", h), c["B"]("P1", h)], w=[bP_])

                def co_P2(c, slot, d, h):
                    pP, bP_ = bk[(slot, d, h)]
                    ph.tt("vector", c["PP"][0][:, h, :], pP[:, 0:128], c["PP"][1][:, h, :], ALU.add, r=[c["B"]("P1", h)], w=[bP_, c["B"]("P0", h)])
                stage(pe_P2, co_P2)

                for mi in range(4):
                    cur = mi % 2
                    nxt = 1 - cur

                    def pe_PT(c, slot, d, h, cur=cur):
                        pT_, bT_ = bank((slot, d, h), True)
                        ph.tr(pT_[:, 0:128], c["PP"][cur][:, h, :], ident_bf[:], r=[c["B"]("P%d" % cur, h), b_ib], w=[bT_])

                    def co_PT(c, slot, d, h):
                        pT_, bT_ = bk[(slot, d, h)]
                        ph.cp("scalar", c["PT"][:, h, :], pT_[:, 0:128], w=[bT_, c["B"]("PT", h)])
                    stage(pe_PT, co_PT)

                    def pe_X(c, slot, d, h, cur=cur):
                        pX, bX_ = bank((slot, d, h))
                        ph.mm(pX[:, 0:128], c["A0t"][:, h, :], c["PP"][cur][:, h, :], r=[c["B"]("A0t", h), c["B"]("P%d" % cur, h)], w=[bX_])

                    def co_X(c, slot, d, h, mi=mi):
                        pX, bX_ = bk[(slot, d, h)]
                        ph.tt("vector", c["Xt"][:, h, :], pX[:, 0:128], cst[:, 9 + mi, :], ALU.mult, r=[b_cst], w=[bX_, c["B"]("Xt", h)])
                    stage(pe_X, co_X)

                    def pe_Y(c, slot, d, h):
                        pY, bY_ = bank((slot, d, h))
                        ph.mm(pY[:, 0:128], c["PT"][:, h, :], c["Xt"][:, h, :], r=[c["B"]("PT", h), c["B"]("Xt", h)], w=[bY_])

                    def co_Y(c, slot, d, h, cur=cur, nxt=nxt):
                        pY, bY_ = bk[(slot, d, h)]
                        ph.tt("vector", c["PP"][nxt][:, h, :], c["PP"][cur][:, h, :], pY[:, 0:128], ALU.subtract,
                              r=[c["B"]("P%d" % cur, h)], w=[bY_, c["B"]("P%d" % nxt, h)])
                    stage(pe_Y, co_Y)

                def ac_K(c, slot, d, h):
                    hc = slice(h * 128, (h + 1) * 128)
                    ph.act(c["kegc"][:, h, :], c["kT"][:, hc], AF.Identity, scale=c["sm"][:, 4 + h:5 + h], r=[c["B"]("kT"), c["B"]("sm")], w=[c["B"]("kegc", h)])
                    ph.act(c["kst"][:, h, :], c["kT"][:, hc], AF.Identity, scale=c["sm"][:, 12 + h:13 + h], r=[c["B"]("kT"), c["B"]("sm")], w=[c["B"]("kst", h)])
                every(ac_K)

                def pe_W(c, slot, d, h):
                    pW, bW = bank((slot, d, h))
                    ph.mm(pW[:, 0:128], c["kegc"][:, h, :], c["PP"][0][:, h, :], r=[c["B"]("kegc", h), c["B"]("P0", h)], w=[bW])
                    ph.mm(pW[:, 128:256], c["PP"][0][:, h, :], c["vT"][:, h * 128:(h + 1) * 128], r=[c["B"]("P0", h), c["B"]("vT")], w=[bW])

                def co_W(c, slot, d, h):
                    pW, bW = bk[(slot, d, h)]
                    ph.cp("scalar", c["wTp"][:, h, :], pW[:, 0:128], w=[bW, c["B"]("wTp", h)])
                    ph.ts("vector", c["u"][:, h, :], pW[:, 128:256], c["b4"][:, h:h + 1], ALU.mult, r=[c["B"]("ab")], w=[bW, c["B"]("u", h)])
                stage(pe_W, co_W)
                return C

            def rec(C, slot):
                DH2 = [(d, h) for d in range(2) for h in range(4)]
                bk = {}
                bk2 = {}
                for (d, h) in DH2:
                    c = C[(slot, d)]
                    pW, bW = banks.next(); bk[(d, h)] = (pW, bW)
                    ph.mm(pW[:, 0:128], c["wTp"][:, h, :], Sbf[d][:, h, :], r=[c["B"]("wTp", h), bSbf[d][h]], w=[bW])
                for (d, h) in DH2:
                    c = C[(slot, d)]; pW, bW = bk[(d, h)]
                    ph.stt("vector", c["vnew"][:, h, :], pW[:, 0:128], c["nb4"][:, h:h + 1], c["u"][:, h, :], ALU.mult, ALU.add,
                           r=[c["B"]("ab"), c["B"]("u", h)], w=[bW, c["B"]("vnew", h)])
                for (d, h) in DH2:
                    c = C[(slot, d)]
                    pO, bO = banks.next(); bk2[(d, h)] = (pO, bO)
                    ph.mm(pO[:, 0:128], Sbf[d][:, h, :], c["qdecT"][:, h, :], start=True, stop=False, r=[bSbf[d][h], c["B"]("qdecT", h)], w=[bO])
                    ph.mm(pO[:, 0:128], c["vnew"][:, h, :], c["aqkT"][:, h, :], start=False, stop=True, r=[c["B"]("vnew", h), c["B"]("aqkT", h)], w=[bO])
                    ph.mm(pO[:, 128:256], c["kst"][:, h, :], c["vnew"][:, h, :], r=[c["B"]("kst", h), c["B"]("vnew", h)], w=[bO])
                for (d, h) in DH2:
                    c = C[(slot, d)]; pO, bO = bk2[(d, h)]
                    ph.stt("vector", S32[d][:, h, :], S32[d][:, h, :], c["sm"][:, 8 + h:9 + h], pO[:, 128:256], ALU.mult, ALU.add,
                           r=[c["B"]("sm")], w=[bO, bS32[d][h]])
                    ph.cp("scalar", c["oTs"][:, h, :], pO[:, 0:128], w=[bO, c["B"]("oTs", h)])
                for (d, h) in DH2:
                    ph.cp("gpsimd", Sbf[d][:, h, :], S32[d][:, h, :], r=[bS32[d][h]], w=[bSbf[d][h]])
                for d in range(2):
                    c = C[(slot, d)]
                    ph.dma("sync", oTd[d][:, :, c["tc"]].rearrange("h p t -> p h t"), c["oTs"][:], r=[c["B"]("oTs", h) for h in range(4)])

            pairs = [(s_, s_ + 1) for s_ in range(0, NTILE, 2)]
            prev = prep(pairs[0])
            for pi in range(len(pairs)):
                nxt_ = prep(pairs[pi + 1]) if pi + 1 < len(pairs) else None
                for slot in range(2):
                    rec(prev, slot)
                prev = nxt_
            if "sfin" in dbg:
                for d in range(2):
                    ph.dma("sync", sfin[l, d].rearrange("h p t -> p h t"), S32[d][:], r=bS32[d])
        if done("p2b_%d" % l):
            return nc


        with Phase(nc, "p3_%d" % l) as ph:
            cst = ph.sb([128, NCST, 128], F32, "cst"); b_cst = Buf()
            ph.dma("sync", cst[:], cst_d, w=[b_cst])
            ones_bf = ph.sb([128, 128], BF16); b_cb = Buf()
            ph.cp("vector", ones_bf[:], cst[:, 1, :], r=[b_cst], w=[b_cb])
            wa = ph.sb([128, 4, D], BF16, "wa"); wbb = ph.sb([128, 2, D], BF16, "wb"); wc = ph.sb([128, 2, D], BF16, "wc")
            wo = ph.sb([128, KC, D], BF16, "wo"); b_w = Buf()
            ph.dma("gpsimd", wa[:], wbra_d[l].rearrange("(k p) n -> p k n", p=128), w=[b_w])
            ph.dma("gpsimd", wbb[:], wbrb_d[l].rearrange("(k p) n -> p k n", p=128), w=[b_w])
            ph.dma("gpsimd", wc[:], wbrc_d[l].rearrange("(k p) n -> p k n", p=128), w=[b_w])
            ph.dma("gpsimd", wo[:], wo_d[l].rearrange("(k p) n -> p k n", p=128), w=[b_w])
            pw = ph.sb([128, 2, 128], BF16, "pw"); b_pw = Buf()
            ph.memset("vector", pw[:], 0.0, w=[b_pw])
            for c in range(2):
                for half in range(2):
                    ph.dma("gpsimd", pw[half * 64:(half + 1) * 64, c, half * 64:(half + 1) * 64], poolw_d[l, 2 * c + half], w=[b_pw])
            psc_ = ph.sb([128, 2], F32); scw = ph.sb([128, 2, 3], F32); dng = ph.sb([128, 1], F32); b_sm = Buf()
            ph.dma("sync", psc_[:], pscale_d[l], w=[b_sm])
            ph.dma("sync", scw[:], scw_d[l], w=[b_sm])
            ph.dma("sync", dng[:], dng_d[l], w=[b_sm])
            mods = ph.sb([128, 48, 2], F32); b_mods = Buf()
            ph.dma("sync", mods[:], modd[l], w=[b_mods])

            G3 = 512
            ofr = Rot(ph, 1, [128, 4, G3], BF16, "of"); obr = Rot(ph, 1, [128, 4, G3], BF16, "ob"); osr = Rot(ph, 1, [128, 4, G3], F32, "os")
            szr = Rot(ph, 1, [128, 4, G3], BF16, "sz"); onr = Rot(ph, 1, [128, 4, G3], BF16, "on")
            sqr = Rot(ph, 2, [128, G3], BF16, "sq"); rr_ = Rot(ph, 2, [128, G3], F32, "r"); tmr = Rot(ph, 3, [128, G3], F32, "tm")
            dr = Rot(ph, 1, [128, 2, G3], BF16, "d"); ybr = Rot(ph, 1, [128, 2, G3], BF16, "yb")
            scr = Rot(ph, 1, [128, 6, G3 + 2], BF16, "sc"); cxr = Rot(ph, 1, [128, 2, G3 + 2], F32, "cx"); ycr = Rot(ph, 1, [128, 2, G3], BF16, "yc")
            gtr = Rot(ph, 1, [128, 24, G3], BF16, "gt"); xgr = Rot(ph, 1, [128, KC, G3], F32, "xg")
            yr = Rot(ph, 1, [128, KC, G3], BF16, "y"); xor_ = Rot(ph, 1, [128, KC, G3], F32, "xo")
            ps1 = Rot(ph, 2, [128, G3], F32, "ps1", psum=True)
            ps3 = Rot(ph, 5, [128, G3], F32, "ps3", psum=True)
            for (j, t0, G) in token_groups(NT, G3):
                if last and j == 1:
                    continue
                s_lo, s_hi = (0, NCTX) if j == 1 else (NCTX, NTA)
                of_, ofb = ofr.next(); ob_, obb = obr.next(); sz, szb = szr.next(); on, onb = onr.next()
                ph.dma("scalar", of_[:, :, 0:G], oTd[0][:, :, t0:t0 + G].rearrange("h p t -> p h t"), w=[ofb])
                ph.dma("scalar", ob_[:, :, 0:G], oTd[1][:, :, t0:t0 + G].rearrange("h p t -> p h t"), w=[obb])
                ph.dma("sync", sz[:, :, 0:G], szT[:, :, t0:t0 + G].rearrange("h p t -> p h t"), w=[szb])
                osum, osb = osr.next()
                ph.tt("gpsimd", osum[:, :, 0:G], of_[:, :, 0:G], ob_[:, :, 0:G], ALU.add, r=[ofb, obb], w=[osb])
                of_, ofb = osum, osb
                for h in range(4):
                    sq, sqb = sqr.next()
                    ph.act(sq[:, 0:G], of_[:, h, 0:G], AF.Square, r=[ofb], w=[sqb])
                    p1, p1b = ps1.next()
                    ph.mm(p1[:, 0:G], ones_bf[:], sq[:, 0:G], r=[b_cb, sqb], w=[p1b])
                    rr, rb = rr_.next()
                    ph.act(rr[:, 0:G], p1[:, 0:G], AF.Ln, bias=EPS, scale=1.0 / 128, r=[p1b], w=[rb])
                    ph.act(rr[:, 0:G], rr[:, 0:G], AF.Exp, scale=-0.5, r=[rb], w=[rb])
                    tm, tmb = tmr.next()
                    ph.stt("vector", tm[:, 0:G], of_[:, h, 0:G], dng[:, 0:1], rr[:, 0:G], ALU.mult, ALU.mult, r=[ofb, b_sm, rb], w=[tmb])
                    ph.tt("gpsimd", on[:, h, 0:G], tm[:, 0:G], sz[:, h, 0:G], ALU.mult, r=[tmb, szb], w=[onb])
                dd, ddb = dr.next(); yb, ybb = ybr.next()
                ph.dma("sync", dd[:, :, 0:G], dT[:, :, t0:t0 + G].rearrange("c p t -> p c t"), w=[ddb])
                for c in range(2):
                    p1, p1b = ps1.next()
                    ph.mm(p1[:, 0:G], pw[:, c, :], dd[:, c, 0:G], r=[b_pw, ddb], w=[p1b])
                    ph.ts("vector", yb[:, c, 0:G], p1[:, 0:G], psc_[:, c:c + 1], ALU.mult, r=[p1b, b_sm], w=[ybb])
                sc, scb = scr.next(); cx, cxb = cxr.next(); yc, ycb = ycr.next()
                a = max(t0 - 1, s_lo); b = min(t0 + G + 1, s_hi)
                if a > t0 - 1:
                    ph.memset("gpsimd", sc[:, :, 0:1], 0.0, w=[scb])
                if b < t0 + G + 1:
                    ph.memset("gpsimd", sc[:, :, G + 1:G + 2], 0.0, w=[scb])
                ph.dma("sync", sc[:, :, a - (t0 - 1):b - (t0 - 1)], pscT[:, :, a:b].rearrange("c p t -> p c t"), w=[scb])
                ph.tt("gpsimd", cx[:, :, 0:G + 2], sc[:, 4:6, 0:G + 2], sc[:, 0:2, 0:G + 2], ALU.mult, r=[scb], w=[cxb])
                for c in range(2):
                    tm, tmb = tmr.next()
                    ph.act(tm[:, 0:G], cx[:, c, 1:G + 1], AF.Identity, scale=scw[:, c, 1:2], r=[cxb, b_sm], w=[tmb])
                    ph.stt("vector", tm[:, 0:G], cx[:, c, 0:G], scw[:, c, 0:1], tm[:, 0:G], ALU.mult, ALU.add, r=[cxb, b_sm, tmb], w=[tmb])
                    ph.stt("vector", tm[:, 0:G], cx[:, c, 2:G + 2], scw[:, c, 2:3], tm[:, 0:G], ALU.mult, ALU.add, r=[cxb, b_sm, tmb], w=[tmb])
                    ph.tt("gpsimd", yc[:, c, 0:G], tm[:, 0:G], sc[:, 2 + c, 1:G + 1], ALU.mult, r=[tmb, scb], w=[ycb])
                gt, gtb = gtr.next(); xg, xb = xgr.next(); y, yb_ = yr.next(); xo, xob = xor_.next()
                ph.dma("scalar", gt[:, :, 0:G], gatesT[:, :, t0:t0 + G].rearrange("c p t -> p c t"), w=[gtb])
                ph.dma("sync", xg[:, :, 0:G], xT[:, :, t0:t0 + G].rearrange("k p t -> p k t"), w=[xb])
                for m in range(KC):
                    mc = slice(m * 128, (m + 1) * 128)
                    pa, pab = ps3.next(); pb_, pbb = ps3.next(); pc, pcb = ps3.next()
                    for h in range(4):
                        ph.mm(pa[:, 0:G], wa[:, h, mc], on[:, h, 0:G], start=(h == 0), stop=(h == 3), r=[b_w, onb], w=[pab])
                    for c in range(2):
                        ph.mm(pb_[:, 0:G], wbb[:, c, mc], yb[:, c, 0:G], start=(c == 0), stop=(c == 1), r=[b_w, ybb], w=[pbb])
                    for c in range(2):
                        ph.mm(pc[:, 0:G], wc[:, c, mc], yc[:, c, 0:G], start=(c == 0), stop=(c == 1), r=[b_w, ycb], w=[pcb])
                    t1, t1b = tmr.next(); t2, t2b = tmr.next(); t3, t3b = tmr.next()
                    ph.tt("vector", t1[:, 0:G], pa[:, 0:G], gt[:, m, 0:G], ALU.mult, r=[pab, gtb], w=[t1b])
                    ph.tt("vector", t2[:, 0:G], pb_[:, 0:G], gt[:, 8 + m, 0:G], ALU.mult, r=[pbb, gtb], w=[t2b])
                    ph.tt("vector", t3[:, 0:G], pc[:, 0:G], gt[:, 16 + m, 0:G], ALU.mult, r=[pcb, gtb], w=[t3b])
                    ph.tt("gpsimd", t1[:, 0:G], t1[:, 0:G], t2[:, 0:G], ALU.add, r=[t1b, t2b], w=[t1b])
                    ph.tt("gpsimd", y[:, m, 0:G], t1[:, 0:G], t3[:, 0:G], ALU.add, r=[t1b, t3b], w=[yb_])
                for m in range(KC):
                    mc = slice(m * 128, (m + 1) * 128)
                    pa, pab = ps3.next()
                    for k in range(KC):
                        ph.mm(pa[:, 0:G], wo[:, k, mc], y[:, k, 0:G], start=(k == 0), stop=(k == KC - 1), r=[b_w, yb_], w=[pab])
                    ph.stt("vector", xo[:, m, 0:G], pa[:, 0:G], mods[:, 16 + m, j:j + 1], xg[:, m, 0:G], ALU.mult, ALU.add,
                           r=[pab, b_mods, xb], w=[xob])
                ph.dma("sync", xT[:, :, t0:t0 + G].rearrange("k p t -> p k t"), xo[:, :, 0:G], r=[xob])
        if done("p3_%d" % l):
            return nc

        with Phase(nc, "p4_%d" % l) as ph:
            onesf = ph.sb([128, 128], F32, "onesf"); b_cst = Buf()
            ph.dma("sync", onesf[:], cst_d[:, 1, :], w=[b_cst])
            ones_bf = ph.sb([128, 128], BF16); b_cb = Buf()
            ph.cp("vector", ones_bf[:], onesf[:], r=[b_cst], w=[b_cb])
            wgu = ph.sb([128, KC, 2 * DFF], BF16, "wgu"); wdn = ph.sb([128, FC, D], BF16, "wdn"); b_wg = [Buf() for _ in range(KC)]; b_wd = Buf()
            for k in range(KC):
                ph.dma("gpsimd", wgu[:, k, :], wgu_d[l, k * 128:(k + 1) * 128, :], w=[b_wg[k]])
            for f0 in range(0, FC, 11):
                ph.dma("gpsimd", wdn[:, f0:f0 + 11, :], wdn_d[l, f0 * 128:(f0 + 11) * 128, :].rearrange("(f p) n -> p f n", p=128), w=[b_wd])
            mods = ph.sb([128, 48, 2], F32); b_mods = Buf()
            ph.dma("sync", mods[:], modd[l], w=[b_mods])
            n2g = ph.sb([128, KC], F32); b_n2g = Buf()
            ph.dma("sync", n2g[:], n2g_d[l], w=[b_n2g])
            A2 = ph.sb([128, 2, KC], F32); b_A2 = Buf()
            for j in range(2):
                ph.stt("vector", A2[:, j, :], mods[:, 32:40, j], 1.0, n2g[:], ALU.add, ALU.mult, r=[b_mods, b_n2g], w=[b_A2])
            G4 = 512
            xgr = Rot(ph, 2, [128, KC, G4], F32, "xg")
            hr = Rot(ph, 1, [128, KC, G4], BF16, "hT"); tmr = Rot(ph, 2, [128, G4], F32, "tm"); rsr = Rot(ph, 1, [128, G4], F32, "rs")
            acr = Rot(ph, 1, [128, FC, G4], BF16, "act")
            ssp = Rot(ph, 1, [128, G4], F32, "ssp", psum=True)
            gup = Rot(ph, 6, [128, G4], F32, "gup", psum=True)
            for (j, t0, G) in token_groups(NT, G4):
                if last and j == 1:
                    continue
                xg, xb = xgr.next()
                ph.dma("sync", xg[:, :, 0:G], xT[:, :, t0:t0 + G].rearrange("k p t -> p k t"), w=[xb])
                ac, acb = acr.next()
                sq, sqb = ac, acb
                ph.act(sq[:, 0:KC, 0:G], xg[:, :, 0:G], AF.Square, r=[xb], w=[sqb])
                sp, spb = ssp.next()
                for k in range(KC):
                    ph.mm(sp[:, 0:G], ones_bf[:], sq[:, k, 0:G], start=(k == 0), stop=(k == KC - 1), r=[b_cb, sqb], w=[spb])
                rs, rsb = rsr.next()
                ph.act(rs[:, 0:G], sp[:, 0:G], AF.Ln, bias=EPS, scale=1.0 / D, r=[spb], w=[rsb])
                ph.act(rs[:, 0:G], rs[:, 0:G], AF.Exp, scale=-0.5, r=[rsb], w=[rsb])
                hT, hb = hr.next()
                for k in range(KC):
                    tm, tmb = tmr.next()
                    ph.stt("vector", tm[:, 0:G], xg[:, k, 0:G], A2[:, j, k:k + 1], rs[:, 0:G], ALU.mult, ALU.mult, r=[xb, b_A2, rsb], w=[tmb])
                    ph.act(hT[:, k, 0:G], tm[:, 0:G], AF.Identity, bias=mods[:, 24 + k, j:j + 1], r=[tmb, b_mods], w=[hb])
                for f in range(FC):
                    pg, pgb = gup.next(); pu, pub = gup.next()
                    for k in range(KC):
                        ph.mm(pg[:, 0:G], wgu[:, k, f * 128:(f + 1) * 128], hT[:, k, 0:G], start=(k == 0), stop=(k == KC - 1), r=[b_wg[k], hb], w=[pgb])
                    for k in range(KC):
                        ph.mm(pu[:, 0:G], wgu[:, k, DFF + f * 128:DFF + (f + 1) * 128], hT[:, k, 0:G], start=(k == 0), stop=(k == KC - 1), r=[b_wg[k], hb], w=[pub])
                    tm, tmb = tmr.next()
                    ph.act(tm[:, 0:G], pg[:, 0:G], AF.Silu, r=[pgb], w=[tmb])
                    ph.tt("vector", ac[:, f, 0:G], pu[:, 0:G], tm[:, 0:G], ALU.mult, r=[pub, tmb], w=[acb])
                for m in range(KC):
                    pd, pdb = gup.next()
                    for f in range(FC):
                        ph.mm(pd[:, 0:G], wdn[:, f, m * 128:(m + 1) * 128], ac[:, f, 0:G], start=(f == 0), stop=(f == FC - 1), r=[b_wd, acb], w=[pdb])
                    ph.stt("vector", xg[:, m, 0:G], pd[:, 0:G], mods[:, 40 + m, j:j + 1], xg[:, m, 0:G], ALU.mult, ALU.add,
                           r=[pdb, b_mods], w=[xb])
                ph.dma("sync", xT[:, :, t0:t0 + G].rearrange("k p t -> p k t"), xg[:, :, 0:G], r=[xb])
        if done("p4_%d" % l):
            return nc

    with Phase(nc, "pf") as ph:
        cst = ph.sb([128, NCST, 128], F32, "cst"); b_cst = Buf()
        ph.dma("sync", cst[:], cst_d, w=[b_cst])
        ones_bf = ph.sb([128, 128], BF16); b_cb = Buf()
        ph.cp("vector", ones_bf[:], cst[:, 1, :], r=[b_cst], w=[b_cb])
        fng = ph.sb([128, KC], F32); b_fng = Buf()
        ph.dma("sync", fng[:], fng_d, w=[b_fng])
        GF = 512
        xgr = Rot(ph, 2, [128, KC, GF], F32, "xg"); sqr = Rot(ph, 1, [128, KC, GF], BF16, "sq"); rsr = Rot(ph, 1, [128, GF], F32, "rs")
        yr = Rot(ph, 2, [128, KC, GF], F32, "y"); otr = Rot(ph, 3, [128, D], F32, "ot")
        ssp = Rot(ph, 1, [128, GF], F32, "ssp", psum=True)
        trp = Rot(ph, 3, [128, KC, 128], F32, "trp", psum=True)
        out_waits = []
        for (j, t0, G) in token_groups(NT, GF):
            if j == 1:
                continue
            xg, xb = xgr.next()
            ph.dma("sync", xg[:], xT[:, :, t0:t0 + G].rearrange("k p t -> p k t"), w=[xb])
            sq, sqb = sqr.next()
            ph.act(sq[:], xg[:], AF.Square, r=[xb], w=[sqb])
            sp, spb = ssp.next()
            for k in range(KC):
                ph.mm(sp[:], ones_bf[:], sq[:, k, :], start=(k == 0), stop=(k == KC - 1), r=[b_cb, sqb], w=[spb])
            rs, rsb = rsr.next()
            ph.act(rs[:], sp[:], AF.Ln, bias=EPS, scale=1.0 / D, r=[spb], w=[rsb])
            ph.act(rs[:], rs[:], AF.Exp, scale=-0.5, r=[rsb], w=[rsb])
            y, yb = yr.next()
            for k in range(KC):
                ph.stt("vector", y[:, k, :], xg[:, k, :], fng[:, k:k + 1], rs[:], ALU.mult, ALU.mult,
                       r=[xb, b_fng, rsb], w=[yb])
            for s_ in range(G // 128):
                tp, tpb = trp.next()
                for k in range(KC):
                    ph.mm(tp[:, k, :], y[:, k, s_ * 128:(s_ + 1) * 128], cst[:, 0, :], r=[yb, b_cst], w=[tpb])
                ot, otb = otr.next()
                ph.cp("scalar" if s_ % 2 == 0 else "vector", ot[:].rearrange("p (k f) -> p k f", k=KC), tp[:], r=[tpb], w=[otb])
                ph.dma("scalar" if s_ % 2 == 0 else "sync", out_d[t0 - NCTX + s_ * 128:t0 - NCTX + (s_ + 1) * 128, :], ot[:], r=[otb])

    return nc


POOL_WINDOWS = (2, 4, 8, 16)


def _consts(NT):
    idx = np.arange(128)
    cst = np.zeros((128, NCST, 128), np.float32)
    cst[:, 0, :] = np.eye(128)
    cst[:, 1, :] = 1.0
    cst[:, 2, :] = (idx[:, None] <= idx[None, :])
    cst[:, 3, :] = (idx[:, None] >= idx[None, :])
    cst[:, 4, :] = np.where(idx[None, :] >= idx[:, None], 0.0, NEG)
    cst[:, 5, :] = np.where(idx[None, :] <= idx[:, None], 0.0, NEG)
    cst[:, 6, :] = (idx[None, :] > idx[:, None])
    cst[:, 7, :] = (idx[None, :] < idx[:, None])

    def blk(sz):
        return (idx[:, None] // sz == idx[None, :] // sz).astype(np.float32)
    cst[:, 8, :] = blk(8)
    for n, sz in enumerate((8, 16, 32, 64)):
        cst[:, 9 + n, :] = blk(2 * sz) - blk(sz)

    def cnt1d(n, w):
        lo = w // 2
        hi = w - 1 - lo
        pos = np.arange(n)
        return (np.clip(pos + hi + 1, 0, n) - np.clip(pos - lo, 0, n)).astype(np.float64)
    rows = NT // GW
    cnt_lat = np.zeros((2, 128, NT), np.float32)
    cnt_ctx = np.zeros((2, 128, NCTX), np.float32)
    for c in range(2):
        for half in range(2):
            w = POOL_WINDOWS[2 * c + half]
            cl = 1.0 / (cnt1d(rows, w)[:, None] * cnt1d(GW, w)[None, :])
            cnt_lat[c, half * 64:(half + 1) * 64, :] = cl.reshape(1, NT)
            cnt_ctx[c, half * 64:(half + 1) * 64, :] = (1.0 / cnt1d(NCTX, w))[None, :]
    return cst, cnt_lat, cnt_ctx


def col(v):
    v = np.asarray(v, np.float32)
    return np.ascontiguousarray(v.reshape(-1, 128).T)


def prep_shared(inp, NT):
    f = lambda a: np.ascontiguousarray(np.asarray(a, np.float32))
    cst, cnt_lat, cnt_ctx = _consts(NT)
    sh = {
        "w_ada": f(inp["w_ada"]),
        "b_adaT": np.stack([col(inp["b_ada"][l]) for l in range(2)]),
        "n1g": np.stack([col(inp["norm1_g"][l]) for l in range(2)]),
        "n2g": np.stack([col(inp["norm2_g"][l]) for l in range(2)]),
        "fng": col(inp["final_norm_g"]),
        "w_in": f(inp["w_in"]),
        "dcw": np.stack([np.stack([col(np.asarray(inp["dn_conv_w"])[l, t]) for t in range(3)], axis=-1) for l in range(2)]),
        "alog": np.stack([np.broadcast_to(np.asarray(inp["dn_a_log"], np.float32)[l].reshape(1, 8), (128, 8)) for l in range(2)]).copy(),
        "dtb": np.stack([np.broadcast_to(np.asarray(inp["dn_dt_bias"], np.float32)[l].reshape(1, 8), (128, 8)) for l in range(2)]).copy(),
        "dng": np.asarray(inp["dn_norm_g"], np.float32).reshape(2, 128, 1).copy(),
        "poolw": f(inp["pool_w"]),
        "pscale": np.stack([col(inp["pool_scale"][l]) for l in range(2)]),
        "scw": np.stack([np.stack([col(np.asarray(inp["sc_conv_w"])[l, t]) for t in range(3)], axis=-1) for l in range(2)]),
        "w_br_a": f(inp["w_br_a"]), "w_br_b": f(inp["w_br_b"]), "w_br_c": f(inp["w_br_c"]),
        "w_o": f(inp["w_o"]), "w_gu": f(inp["w_gu"]), "w_down": f(inp["w_down"]),
        "cst": cst, "cnt_lat": cnt_lat, "cnt_ctx": cnt_ctx,
    }
    return sh


def prep_core(inp, sh, b):
    m = dict(sh)
    m["x"] = np.ascontiguousarray(np.asarray(inp["x"], np.float32)[b])
    m["ctx"] = np.ascontiguousarray(np.asarray(inp["ctx"], np.float32)[b])
    m["cc"] = np.ascontiguousarray(np.stack([col(np.asarray(inp["c"])[b]), col(inp["c_ctx"])], axis=-1))
    return m


def kernel(**inputs):
    x = np.asarray(inputs["x"])
    B, NT, _ = x.shape
    nc = build(NT)
    sh = prep_shared(inputs, NT)
    in_maps = [prep_core(inputs, sh, c % B) for c in range(8)]
    res = run_bass_kernel_spmd(nc, in_maps, core_ids=list(range(8)))
    return np.stack([res.results[b]["out"] for b in range(B)]).astype(np.float32)
```

```python
import numpy as np
import ml_dtypes
from contextlib import ExitStack
import concourse.bass as bass
import concourse.mybir as mybir
from concourse.bass_utils import run_bass_kernel_spmd

F32 = mybir.dt.float32
BF16 = mybir.dt.bfloat16
AF = mybir.ActivationFunctionType
ALU = mybir.AluOpType

D = 1024
KC = 8
NCTX = 256
GW = 64
DFF = 2816
FC = DFF // 128
N_IN = 6160
EPS = 1e-6
NEG = -30000.0
ENGS = ("tensor", "vector", "scalar", "gpsimd", "sync")
NDS = 12
NCST = 13


class Buf:
    __slots__ = ("w", "r")

    def __init__(self):
        self.w = None
        self.r = []


class Sched:
    def __init__(self, nc, pname=""):
        self.nc = nc
        self.pname = pname
        self.ops = {e: [] for e in ENGS}
        self.keys = list(ENGS) + ["dma_%s_%d" % (q, i) for q in ("sync", "gpsimd", "scalar") for i in range(NDS)]
        self.cnt = {k: 0 for k in self.keys}
        self.seen = {e: {k: 0 for k in self.keys} for e in ENGS}
        self.dma_rr = {"sync": 0, "gpsimd": 0, "scalar": 0}
        self.sems = {}

    def alloc(self):
        for k in self.keys:
            self.sems[k] = self.nc.alloc_semaphore(name="s_%s_%s" % (self.pname, k))

    def _deps(self, eng, reads, writes):
        need = {}

        def add(tok):
            if tok is None:
                return
            p, c = tok
            if need.get(p, 0) < c:
                need[p] = c
        for b in reads:
            add(b.w)
        for b in writes:
            add(b.w)
            for t in b.r:
                add(t)
        waits = []
        for p, c in need.items():
            if eng == "tensor" and p == "tensor":
                continue
            if self.seen[eng][p] < c:
                self.seen[eng][p] = c
                waits.append((p, c))
        return waits

    def _commit(self, tok, reads, writes):
        for b in reads:
            b.r.append(tok)
        for b in writes:
            b.w = tok
            b.r = []

    def op(self, eng, fn, reads=(), writes=()):
        waits = self._deps(eng, reads, writes)
        self.cnt[eng] += 1
        tok = (eng, self.cnt[eng])
        self.ops[eng].append((waits, fn, (eng, 1)))
        self._commit(tok, reads, writes)
        return tok

    def dma(self, q, fn, reads=(), writes=()):
        waits = self._deps(q, reads, writes)
        key = "dma_%s_%d" % (q, self.dma_rr[q])
        self.dma_rr[q] = (self.dma_rr[q] + 1) % NDS
        if self.cnt[key] > 0 and self.seen[q][key] < self.cnt[key]:
            self.seen[q][key] = self.cnt[key]
            waits.append((key, self.cnt[key]))
        self.cnt[key] += 1
        tok = (key, self.cnt[key])
        self.ops[q].append((waits, fn, (key, 16)))
        self._commit(tok, reads, writes)
        return tok

    def emit(self):
        nc = self.nc
        mult = {k: (16 if k.startswith("dma_") else 1) for k in self.keys}
        waited = {k: set() for k in self.keys}
        for en in ENGS:
            for waits, fn, sk in self.ops[en]:
                for p, c in waits:
                    waited[p].add(c)
        for k in self.keys:
            waited[k].add(self.cnt[k])
        with nc.Block() as block:
            def run(engname):
                def body(e):
                    idx = 0
                    last = 0
                    for waits, fn, (sk, inc) in self.ops[engname]:
                        for p, c in waits:
                            e.wait_ge(self.sems[p], c * mult[p])
                        ins = fn(e)
                        if sk == engname:
                            idx += 1
                            if idx in waited[engname]:
                                ins.then_inc(self.sems[sk], idx - last)
                                last = idx
                        else:
                            ins.then_inc(self.sems[sk], inc)
                    if engname == "sync":
                        for k in self.keys:
                            if self.cnt[k] > 0:
                                e.wait_ge(self.sems[k], self.cnt[k] * mult[k])
                return body
            block.tensor(run("tensor"))
            block.vector(run("vector"))
            block.scalar(run("scalar"))
            block.gpsimd(run("gpsimd"))
            block.sync(run("sync"))


class Phase:
    def __init__(self, nc, name):
        self.nc = nc
        self.name = name
        self.es = ExitStack()
        self.S = Sched(nc, name)
        self.n = 0

    def __enter__(self):
        self.es.enter_context(self.nc.cleanup_on_exit())
        self.S.alloc()
        return self

    def __exit__(self, *a):
        if a[0] is None:
            self.S.emit()
        self.es.close()
        return False

    def sb(self, shape, dt, name=None):
        self.n += 1
        return self.es.enter_context(self.nc.sbuf_tensor("%s_%s%d" % (self.name, name or "t", self.n), list(shape), dt))

    def ps(self, shape, dt, name=None):
        self.n += 1
        return self.es.enter_context(self.nc.psum_tensor("%s_%s%d" % (self.name, name or "p", self.n), list(shape), dt))

    def dma(self, q, out, in_, r=(), w=()):
        return self.S.dma(q, lambda e: e.dma_start(out=out, in_=in_), r, w)

    def mm(self, out, lhsT, rhs, start=True, stop=True, r=(), w=()):
        return self.S.op("tensor", lambda e: e.matmul(out, lhsT=lhsT, rhs=rhs, start=start, stop=stop), r, w)

    def tr(self, out, in_, ident, r=(), w=()):
        return self.S.op("tensor", lambda e: e.transpose(out, in_, ident), r, w)

    def act(self, out, in_, func, bias=None, scale=None, r=(), w=()):
        kw = {}
        if bias is not None:
            kw["bias"] = bias
        if scale is not None:
            kw["scale"] = scale
        return self.S.op("scalar", lambda e: e.activation(out=out, in_=in_, func=func, **kw), r, w)

    def tt(self, eng, out, in0, in1, op, r=(), w=()):
        return self.S.op(eng, lambda e: e.tensor_tensor(out=out, in0=in0, in1=in1, op=op), r, w)

    def ts(self, eng, out, in0, s1, op0, s2=None, op1=None, r=(), w=()):
        if op1 is None:
            return self.S.op(eng, lambda e: e.tensor_scalar(out=out, in0=in0, scalar1=s1, scalar2=None, op0=op0), r, w)
        return self.S.op(eng, lambda e: e.tensor_scalar(out=out, in0=in0, scalar1=s1, scalar2=s2, op0=op0, op1=op1), r, w)

    def stt(self, eng, out, in0, scalar, in1, op0, op1, r=(), w=()):
        return self.S.op(eng, lambda e: e.scalar_tensor_tensor(out=out, in0=in0, scalar=scalar, in1=in1, op0=op0, op1=op1), r, w)

    def cp(self, eng, out, in_, r=(), w=()):
        if eng == "scalar":
            return self.S.op(eng, lambda e: e.copy(out=out, in_=in_), r, w)
        return self.S.op(eng, lambda e: e.tensor_copy(out=out, in_=in_), r, w)

    def memset(self, eng, ap, val, r=(), w=()):
        return self.S.op(eng, lambda e: e.memset(ap, val), r, w)


class Rot:
    def __init__(self, ph, n, shape, dt, name, psum=False):
        mk = ph.ps if psum else ph.sb
        self.t = [mk(shape, dt, name) for _ in range(n)]
        self.b = [Buf() for _ in range(n)]
        self.i = -1

    def next(self):
        self.i = (self.i + 1) % len(self.t)
        return self.t[self.i], self.b[self.i]


def token_groups(NT, G):
    gs = [(1, 0, NCTX)]
    t = NCTX
    while t < NCTX + NT:
        gs.append((0, t, G))
        t += G
    return gs


def build(NT, stop_after=None, dbg=()):
    NTA = NCTX + NT
    NTILE = NTA // 128
    ROWS = NT // GW
    nc = bass.Bass("TRN2", target_bir_lowering=False)

    def din(name, shape, dt=F32):
        return nc.dram_tensor(name, list(shape), dt, kind="ExternalInput").ap()

    def scratch(name, shape, dt=F32):
        kind = "ExternalOutput" if name in dbg else "Internal"
        return nc.dram_tensor(name, list(shape), dt, kind=kind).ap()

    x_d = din("x", [NT, D]); ctx_d = din("ctx", [NCTX, D]); cc_d = din("cc", [128, KC, 2])
    w_ada_d = din("w_ada", [2, D, 6 * D]); b_adaT_d = din("b_adaT", [2, 128, 48])
    n1g_d = din("n1g", [2, 128, KC]); n2g_d = din("n2g", [2, 128, KC]); fng_d = din("fng", [128, KC])
    w_in_d = din("w_in", [2, D, N_IN]); dcw_d = din("dcw", [2, 128, 12, 3])
    alog_d = din("alog", [2, 128, 8]); dtb_d = din("dtb", [2, 128, 8]); dng_d = din("dng", [2, 128, 1])
    poolw_d = din("poolw", [2, 4, 64, 64]); pscale_d = din("pscale", [2, 128, 2]); scw_d = din("scw", [2, 128, 2, 3])
    wbra_d = din("w_br_a", [2, 512, D]); wbrb_d = din("w_br_b", [2, 256, D]); wbrc_d = din("w_br_c", [2, 256, D])
    wo_d = din("w_o", [2, D, D]); wgu_d = din("w_gu", [2, D, 2 * DFF]); wdn_d = din("w_down", [2, DFF, D])
    cst_d = din("cst", [128, NCST, 128]); cntl_d = din("cnt_lat", [2, 128, NT]); cntc_d = din("cnt_ctx", [2, 128, NCTX])
    out_d = nc.dram_tensor("out", [NT, D], F32, kind="ExternalOutput").ap()

    xT = scratch("xT", [KC, 128, NTA])
    pqkvT = scratch("pqkvT", [12, 128, NTA], BF16); szT = scratch("szT", [4, 128, NTA], BF16)
    ppoolT = scratch("ppoolT", [2, 128, NTA], BF16); pscT = scratch("pscT", [6, 128, NTA], BF16)
    gatesT = scratch("gatesT", [24, 128, NTA], BF16)
    abS = scratch("abS", [NTILE, 128, 24])
    qnT = scratch("qnT", [4, 128, NTA], BF16); knT = scratch("knT", [4, 128, NTA], BF16)
    kTM = scratch("kTM", [NTILE, 128, 512], BF16); vTM = scratch("vTM", [NTILE, 128, 512], BF16)
    oTd = [scratch("oTf", [4, 128, NTA], BF16), scratch("oTb", [4, 128, NTA], BF16)]
    dT = scratch("dT", [2, 128, NTA], BF16)
    modd = scratch("modd", [2, 128, 6 * KC, 2])
    sfin = scratch("sfin", [2, 2, 4, 128, 128])
    dbgbuf = scratch("dbgbuf", [2, 128, 8, 128])

    def done(tag):
        return stop_after == tag

    with Phase(nc, "p0") as ph:
        cst = ph.sb([128, NCST, 128], F32, "cst"); b_cst = Buf()
        ph.dma("sync", cst[:], cst_d, w=[b_cst])
        cc = ph.sb([128, KC, 2], F32); b_cc = Buf()
        ph.dma("sync", cc[:], cc_d, w=[b_cc])
        scc = ph.sb([128, KC, 2], F32); b_scc = Buf()
        ph.act(scc[:], cc[:], AF.Silu, r=[b_cc], w=[b_scc])
        wrot = Rot(ph, 2, [128, KC, 768], F32, "wada")
        modps_full = ph.ps([128, 512], F32); b_modps = Buf()
        modps = modps_full[:, 0:96].rearrange("p (m j) -> p m j", j=2)
        for l in range(2):
            badaT = ph.sb([128, 48], F32); b_bada = Buf()
            ph.dma("sync", badaT[:], b_adaT_d[l], w=[b_bada])
            for pc in range(8):
                wt, wb = wrot.next()
                ph.dma("sync" if pc % 2 == 0 else "scalar", wt[:], w_ada_d[l, :, pc * 768:(pc + 1) * 768].rearrange("(k p) n -> p k n", p=128), w=[wb])
                for mi in range(6):
                    m = pc * 6 + mi
                    for k in range(KC):
                        ph.mm(modps[:, m, :], wt[:, k, mi * 128:(mi + 1) * 128], scc[:, k, :], start=(k == 0), stop=(k == KC - 1),
                              r=[wb, b_scc], w=[b_modps])
            mods = ph.sb([128, 48, 2], F32); b_mods = Buf()
            for j in range(2):
                ph.tt("vector", mods[:, :, j], modps[:, :, j], badaT[:], ALU.add, r=[b_modps, b_bada], w=[b_mods])
            ph.dma("sync", modd[l], mods[:], r=[b_mods])
        xrot = Rot(ph, 3, [128, D], F32, "xin")
        trps = Rot(ph, 2, [128, KC, 128], F32, "trps", psum=True)
        orot = Rot(ph, 3, [128, KC, 128], F32, "xo")
        for ti in range(NTILE):
            xt, xb = xrot.next()
            src = ctx_d[ti * 128:(ti + 1) * 128, :] if ti < NCTX // 128 else x_d[ti * 128 - NCTX:(ti + 1) * 128 - NCTX, :]
            ph.dma("sync", xt[:], src, w=[xb])
            pt, pb = trps.next()
            for k in range(KC):
                ph.mm(pt[:, k, :], xt[:, k * 128:(k + 1) * 128], cst[:, 0, :], r=[xb, b_cst], w=[pb])
            ot, ob = orot.next()
            ph.cp("vector" if ti % 2 == 0 else "scalar", ot[:], pt[:], r=[pb], w=[ob])
            ph.dma("sync", xT[:, :, ti * 128:(ti + 1) * 128].rearrange("k p t -> p k t"), ot[:], r=[ob])
    if done("p0"):
        return nc

    for l in range(2):
        last = (l == 1)
        with Phase(nc, "p1_%d" % l) as ph:
            cst = ph.sb([128, NCST, 128], F32, "cst"); b_cst = Buf()
            ph.dma("sync", cst[:], cst_d, w=[b_cst])
            ones_bf = ph.sb([128, 128], BF16); b_ones = Buf()
            ph.cp("vector", ones_bf[:], cst[:, 1, :], r=[b_cst], w=[b_ones])
            win = ph.sb([128, KC, N_IN], BF16, "win"); b_win = [Buf() for _ in range(KC)]
            for k in range(KC):
                ph.dma("gpsimd", win[:, k, :], w_in_d[l, k * 128:(k + 1) * 128, :], w=[b_win[k]])
            mods = ph.sb([128, 48, 2], F32); b_mods = Buf()
            ph.dma("sync", mods[:], modd[l], w=[b_mods])
            n1g = ph.sb([128, KC], F32); b_n1g = Buf()
            ph.dma("sync", n1g[:], n1g_d[l], w=[b_n1g])
            A1 = ph.sb([128, 2, KC], F32); b_A1 = Buf()
            for j in range(2):
                ph.stt("vector", A1[:, j, :], mods[:, 8:16, j], 1.0, n1g[:], ALU.add, ALU.mult, r=[b_mods, b_n1g], w=[b_A1])
            alog = ph.sb([128, 8], F32); dtb = ph.sb([128, 8], F32); b_al = Buf(); b_dtb = Buf()
            ph.dma("sync", alog[:], alog_d[l], w=[b_al])
            ph.dma("sync", dtb[:], dtb_d[l], w=[b_dtb])
            negea = ph.sb([128, 8], F32); b_negea = Buf()
            ph.act(negea[:], alog[:], AF.Exp, r=[b_al], w=[b_negea])
            ph.ts("vector", negea[:], negea[:], -1.0, ALU.mult, r=[b_negea], w=[b_negea])

            xrot = Rot(ph, 2, [128, KC, 512], F32, "xg")
            sqrot = Rot(ph, 1, [128, KC, 512], BF16, "sq")
            hrot = Rot(ph, 2, [128, KC, 512], BF16, "hT")
            tmprot = Rot(ph, 2, [128, 512], F32, "tmp")
            rsrot = Rot(ph, 2, [128, 512], F32, "rstd")
            stf = Rot(ph, 4, [128, 512], BF16, "stf")
            stb = Rot(ph, 4, [128, 512], BF16, "stb")
            abrot = Rot(ph, 2, [128, 24], F32, "ab")
            abt = Rot(ph, 2, [128, 8], F32, "abt")
            ssps = Rot(ph, 1, [128, 512], F32, "ssps", psum=True)
            accps = Rot(ph, 5, [128, 512], F32, "acc", psum=True)
            abps = Rot(ph, 2, [128, 512], F32, "abps", psum=True)

            chunks = []
            for c in range(12):
                chunks.append((c * 128, "copy", pqkvT, c))
            for c in range(4):
                chunks.append((1536 + c * 128, "silu", szT, c))
            for c in range(2):
                chunks.append((2064 + c * 128, "copy", ppoolT, c))
            for c in range(6):
                chunks.append((2320 + c * 128, "copy", pscT, c))
            for c in range(24):
                chunks.append((3088 + c * 128, "sigm", gatesT, c))

            for (j, t0, G) in token_groups(NT, 512):
                xg, xb = xrot.next()
                ph.dma("sync", xg[:, :, 0:G], xT[:, :, t0:t0 + G].rearrange("k p t -> p k t"), w=[xb])
                sq, sqb = sqrot.next()
                ph.act(sq[:, :, 0:G], xg[:, :, 0:G], AF.Square, r=[xb], w=[sqb])
                sp, spb = ssps.next()
                for k in range(KC):
                    ph.mm(sp[:, 0:G], ones_bf[:], sq[:, k, 0:G], start=(k == 0), stop=(k == KC - 1), r=[b_ones, sqb], w=[spb])
                rs, rsb = rsrot.next()
                ph.act(rs[:, 0:G], sp[:, 0:G], AF.Ln, bias=EPS, scale=1.0 / D, r=[spb], w=[rsb])
                ph.act(rs[:, 0:G], rs[:, 0:G], AF.Exp, scale=-0.5, r=[rsb], w=[rsb])
                hT, hb = hrot.next()
                for k in range(KC):
                    tm, tmb = tmprot.next()
                    ph.stt("vector", tm[:, 0:G], xg[:, k, 0:G], A1[:, j, k:k + 1], rs[:, 0:G], ALU.mult, ALU.mult,
                           r=[xb, b_A1, rsb], w=[tmb])
                    ph.act(hT[:, k, 0:G], tm[:, 0:G], AF.Identity, bias=mods[:, k, j:j + 1], r=[tmb, b_mods], w=[hb])
                for s in range(G // 128):
                    ap_, apb = abps.next()
                    for k in range(KC):
                        ph.mm(ap_[:, 0:16], hT[:, k, s * 128:(s + 1) * 128], win[:, k, 2048:2064], start=(k == 0), stop=(k == KC - 1),
                              r=[hb, b_win[k]], w=[apb])
                    ab, abb = abrot.next()
                    at, atb = abt.next()
                    ph.tt("vector", at[:], ap_[:, 0:8], dtb[:], ALU.add, r=[apb, b_dtb], w=[atb])
                    ph.act(at[:], at[:], AF.Exp, r=[atb], w=[atb])
                    ph.act(at[:], at[:], AF.Ln, bias=1.0, r=[atb], w=[atb])
                    ph.tt("vector", ab[:, 0:8], at[:], negea[:], ALU.mult, r=[atb, b_negea], w=[abb])
                    ph.act(ab[:, 8:16], ap_[:, 8:16], AF.Sigmoid, w=[apb, abb])
                    ph.ts("vector", ab[:, 16:24], ab[:, 8:16], -1.0, ALU.mult, r=[abb], w=[abb])
                    ph.dma("sync", abS[(t0 // 128) + s], ab[:], r=[abb])
                for ci, (c0, kind, dst, dc) in enumerate(chunks):
                    acc, accb = accps.next()
                    for k in range(KC):
                        ph.mm(acc[:, 0:G], win[:, k, c0:c0 + 128], hT[:, k, 0:G], start=(k == 0), stop=(k == KC - 1),
                              r=[hb, b_win[k]], w=[accb])
                    if kind == "copy":
                        st, sb_ = stf.next()
                        ph.cp("vector", st[:, 0:G], acc[:, 0:G], r=[accb], w=[sb_])
                        ph.dma("sync", dst[dc, :, t0:t0 + G], st[:, 0:G], r=[sb_])
                    else:
                        st, sb_ = stb.next()
                        ph.act(st[:, 0:G], acc[:, 0:G], AF.Silu if kind == "silu" else AF.Sigmoid, r=[accb], w=[sb_])
                        ph.dma("scalar", dst[dc, :, t0:t0 + G], st[:, 0:G], r=[sb_])
        if done("p1_%d" % l):
            return nc


        with Phase(nc, "p2a_%d" % l) as ph:
            cst = ph.sb([128, NCST, 128], F32, "cst"); b_cst = Buf()
            ph.dma("sync", cst[:], cst_d, w=[b_cst])
            ones_bf = ph.sb([128, 128], BF16); ident_bf = ph.sb([128, 128], BF16); b_cb = Buf()
            ph.cp("vector", ones_bf[:], cst[:, 1, :], r=[b_cst], w=[b_cb])
            ph.cp("vector", ident_bf[:], cst[:, 0, :], r=[b_cst], w=[b_cb])
            cw = ph.sb([128, 12, 3], F32); b_cw = Buf()
            ph.dma("sync", cw[:], dcw_d[l], w=[b_cw])
            pqrot = Rot(ph, 2, [128, 12, 514], BF16, "pq")
            srot = Rot(ph, 1, [128, 8, 512], F32, "s")
            vrot = Rot(ph, 2, [128, 4, 512], BF16, "vT")
            qkrot = Rot(ph, 2, [128, 8, 512], BF16, "qkn")
            tmrot = Rot(ph, 3, [128, 512], F32, "tm")
            sqrot = Rot(ph, 2, [128, 512], BF16, "sq")
            rrot = Rot(ph, 2, [128, 512], F32, "r")
            ssps = Rot(ph, 2, [128, 512], F32, "ss", psum=True)
            trps = Rot(ph, 4, [128, 512], F32, "trp", psum=True)
            tmo = Rot(ph, 4, [128, 512], BF16, "tmo")
            QB = float(np.log(128.0 ** -0.5))
            for (j, t0, G) in token_groups(NT, 512):
                s_lo, s_hi = (0, NCTX) if j == 1 else (NCTX, NTA)
                pq, pqb = pqrot.next()
                a = max(t0 - 1, s_lo); b = min(t0 + G + 1, s_hi)
                if a > t0 - 1:
                    ph.memset("gpsimd", pq[:, :, 0:1], 0.0, w=[pqb])
                if b < t0 + G + 1:
                    ph.memset("gpsimd", pq[:, :, G + 1:G + 2], 0.0, w=[pqb])
                ph.dma("sync", pq[:, :, a - (t0 - 1):b - (t0 - 1)], pqkvT[:, :, a:b].rearrange("c p t -> p c t"), w=[pqb])
                st, sb_ = srot.next()
                vT_, vb = vrot.next()
                qk, qkb = qkrot.next()
                for c in range(12):
                    tm, tmb = tmrot.next()
                    ph.act(tm[:, 0:G], pq[:, c, 1:G + 1], AF.Identity, scale=cw[:, c, 1:2], r=[pqb, b_cw], w=[tmb])
                    ph.stt("vector", tm[:, 0:G], pq[:, c, 0:G], cw[:, c, 0:1], tm[:, 0:G], ALU.mult, ALU.add, r=[pqb, b_cw, tmb], w=[tmb])
                    ph.stt("vector", tm[:, 0:G], pq[:, c, 2:G + 2], cw[:, c, 2:3], tm[:, 0:G], ALU.mult, ALU.add, r=[pqb, b_cw, tmb], w=[tmb])
                    if c < 8:
                        ph.act(st[:, c, 0:G], tm[:, 0:G], AF.Silu, r=[tmb], w=[sb_])
                    else:
                        ph.act(vT_[:, c - 8, 0:G], tm[:, 0:G], AF.Silu, r=[tmb], w=[vb])
                for c in range(8):
                    sq, sqb = sqrot.next()
                    ph.tt("gpsimd", sq[:, 0:G], st[:, c, 0:G], st[:, c, 0:G], ALU.mult, r=[sb_], w=[sqb])
                    sp, spb = ssps.next()
                    ph.mm(sp[:, 0:G], ones_bf[:], sq[:, 0:G], r=[b_cb, sqb], w=[spb])
                    rr, rb = rrot.next()
                    ph.act(rr[:, 0:G], sp[:, 0:G], AF.Ln, bias=EPS, r=[spb], w=[rb])
                    ph.act(rr[:, 0:G], rr[:, 0:G], AF.Exp, scale=-0.5, bias=(QB if c < 4 else 0.0), r=[rb], w=[rb])
                    ph.tt("vector", qk[:, c, 0:G], st[:, c, 0:G], rr[:, 0:G], ALU.mult, r=[sb_, rb], w=[qkb])
                ph.dma("sync", qnT[:, :, t0:t0 + G].rearrange("h p t -> p h t"), qk[:, 0:4, 0:G], r=[qkb])
                ph.dma("sync", knT[:, :, t0:t0 + G].rearrange("h p t -> p h t"), qk[:, 4:8, 0:G], r=[qkb])
                for s in range(G // 128):
                    for which in range(2):
                        tp_, tpb = trps.next()
                        tp = tp_[:].bitcast(BF16)
                        for h in range(4):
                            src = qk[:, 4 + h, s * 128:(s + 1) * 128] if which == 0 else vT_[:, h, s * 128:(s + 1) * 128]
                            ph.tr(tp[:, h * 128:(h + 1) * 128], src, ident_bf[:], r=[qkb if which == 0 else vb, b_cb], w=[tpb])
                        to, tob = tmo.next()
                        ph.cp("scalar" if which == 0 else "vector", to[:], tp[:, 0:512], r=[tpb], w=[tob])
                        ph.dma("sync", (kTM if which == 0 else vTM)[t0 // 128 + s], to[:], r=[tob])
        if done("p2a_%d" % l):
            return nc

        with Phase(nc, "p2c_%d" % l) as ph:
            for (j, c0, Rr, Wd, cnt_src) in ((1, 0, 1, NCTX, cntc_d), (0, NCTX, ROWS, GW, cntl_d)):
                n_tok = Rr * Wd
                RP = Rr + 16 if Rr > 1 else 1
                WP = Wd + 16
                X = ph.sb([128, n_tok], BF16, "pX"); PA = ph.sb([128, RP, WP], F32, "pA"); PB = ph.sb([128, RP, WP], F32, "pB")
                CN = ph.sb([128, n_tok], F32, "pC"); DO = ph.sb([128, n_tok], BF16, "pD")
                bX = Buf(); bC = Buf(); bA = [Buf(), Buf()]; bB = [Buf(), Buf()]; bD = [Buf(), Buf()]
                r0 = 8 if Rr > 1 else 0
                for c in range(2):
                    ph.dma("sync", X[:], ppoolT[c, :, c0:c0 + n_tok], w=[bX])
                    ph.dma("sync", CN[:], cnt_src[c], w=[bC])
                    ph.memset("gpsimd", PA[:], 0.0, w=bA)
                    ph.memset("vector", PB[:], 0.0, w=bB)
                    ph.cp("vector", PA[:, r0:r0 + Rr, 8:8 + Wd], X[:].rearrange("p (r w) -> p r w", w=Wd), r=[bX], w=bA)
                    for half in range(2):
                        eng = "gpsimd" if half == 0 else "vector"
                        w_ = POOL_WINDOWS[2 * c + half]
                        lo = w_ // 2
                        nl = int(np.log2(w_))
                        pr = slice(half * 64, (half + 1) * 64)
                        src, dst, bs, bd = PA, PB, bA[half], bB[half]
                        for lv in range(nl):
                            sft = 1 << lv
                            ph.tt(eng, dst[pr, :, 0:WP - sft], src[pr, :, 0:WP - sft], src[pr, :, sft:WP], ALU.add, r=[bs], w=[bd])
                            src, dst, bs, bd = dst, src, bd, bs
                        if Rr > 1:
                            for lv in range(nl):
                                sft = 1 << lv
                                ph.tt(eng, dst[pr, 0:RP - sft, :], src[pr, 0:RP - sft, :], src[pr, sft:RP, :], ALU.add, r=[bs], w=[bd])
                                src, dst, bs, bd = dst, src, bd, bs
                        ro = r0 - lo if Rr > 1 else 0
                        co = 8 - lo
                        Mv = src[pr, ro:ro + Rr, co:co + Wd]
                        tflat = dst[pr].rearrange("p r w -> p (r w)")[:, 0:n_tok]
                        ph.tt(eng, tflat.rearrange("p (r w) -> p r w", w=Wd), Mv, CN[pr].rearrange("p (r w) -> p r w", w=Wd), ALU.mult,
                              r=[bs, bC], w=[bd])
                        ph.tt(eng, DO[pr], tflat, X[pr], ALU.subtract, r=[bd, bX], w=[bD[half]])
                    ph.dma("sync", dT[c, :, c0:c0 + n_tok], DO[:], r=bD)
        if done("p2c_%d" % l):
            return nc


        with Phase(nc, "p2b_%d" % l) as ph:
            cst = ph.sb([128, NCST, 128], F32, "cst"); b_cst = Buf()
            ph.dma("sync", cst[:], cst_d, w=[b_cst])
            ident_bf = ph.sb([128, 128], BF16); b_ib = Buf()
            ph.cp("vector", ident_bf[:], cst[:, 0, :], r=[b_cst], w=[b_ib])
            S32 = [ph.sb([128, 4, 128], F32, "S32") for _ in range(2)]
            Sbf = [ph.sb([128, 4, 128], BF16, "Sbf") for _ in range(2)]
            bS32 = [[Buf() for _ in range(4)] for _ in range(2)]
            bSbf = [[Buf() for _ in range(4)] for _ in range(2)]
            for d in range(2):
                ph.memset("vector", S32[d][:], 0.0, w=bS32[d])
                ph.memset("gpsimd", Sbf[d][:], 0.0, w=bSbf[d])
            banks = Rot(ph, 8, [128, 512], F32, "bank", psum=True)

            def trbank():
                t_, b_ = banks.next()
                return t_[:].bitcast(BF16), b_
            T = {}
            TB = {}

            def tl(name, d, par, shape, dt):
                key = (name, d, par)
                if key not in T:
                    T[key] = ph.sb(shape, dt, name)
                return T[key]

            def tb(name, d, par, h=0):
                return TB.setdefault((name, d, par, h), Buf())
            order = [list(range(NTILE)), [1, 0] + list(range(NTILE - 1, 1, -1))]

            REC = ("ab", "sm", "wTp", "u", "kst", "qdecT", "aqkT", "vnew", "oTs")

            def prep(steps):
                C = {}
                for slot, step in enumerate(steps):
                    for d in range(2):
                        ti = order[d][step]
                        tc = slice(ti * 128, (ti + 1) * 128)
                        rk = step % 4

                        def mk(name, shape, dt, _d=d, _slot=slot, _rk=rk):
                            return tl(name, _d, ("r", _rk) if name in REC else ("p", _slot), shape, dt)

                        def B(name, h=0, _d=d, _slot=slot, _rk=rk):
                            return tb(name, _d, ("r", _rk) if name in REC else ("p", _slot), h)
                        ab = mk("ab", [128, 24], F32)
                        c = dict(step=step, ti=ti, tc=tc, B=B, ab=ab, sm=mk("sm", [128, 16], F32),
                                 qn=mk("qn", [128, 4, 128], BF16), kn=mk("kn", [128, 4, 128], BF16),
                                 kT=mk("kT", [128, 512], BF16), vT=mk("vT", [128, 512], BF16),
                                 Gbc=mk("Gbc", [128, 4, 128], F32), EQ=mk("EQ", [128, 4, 128], F32), E2T=mk("E2T", [128, 4, 128], F32),
                                 EsT=mk("EsT", [128, 4, 128], F32), M0t=mk("M0t", [128, 4, 128], BF16),
                                 A0t=mk("A0t", [128, 4, 128], BF16), AMb=mk("AMb", [128, 4, 2, 128], BF16),
                                 AM1=mk("AM1", [128, 4, 2, 128], BF16), A2t=mk("A2t", [128, 4, 128], BF16),
                                 PP=[mk("Pa", [128, 4, 128], BF16), mk("Pb", [128, 4, 128], BF16)],
                                 PT=mk("PT", [128, 4, 128], BF16), Xt=mk("Xt", [128, 4, 128], BF16),
                                 aqkT=mk("aqkT", [128, 4, 128], BF16), qdecT=mk("qdecT", [128, 4, 128], BF16),
                                 kegc=mk("kegc", [128, 4, 128], BF16), kst=mk("kst", [128, 4, 128], BF16),
                                 wTp=mk("wTp", [128, 4, 128], BF16), u=mk("u", [128, 4, 128], F32),
                                 vnew=mk("vnew", [128, 4, 128], BF16), oTs=mk("oTs", [128, 4, 128], BF16),
                                 b4=ab[:, 8 + d * 4:12 + d * 4], nb4=ab[:, 16 + d * 4:20 + d * 4])
                        C[(slot, d)] = c
                        ph.dma("sync", c["qn"][:], qnT[:, :, tc].rearrange("h p t -> p h t"), w=[B("qn")])
                        ph.dma("sync", c["kn"][:], knT[:, :, tc].rearrange("h p t -> p h t"), w=[B("kn")])
                        ph.dma("sync", c["kT"][:], kTM[ti], w=[B("kT")])
                        ph.dma("sync", c["vT"][:], vTM[ti], w=[B("vT")])
                        ph.dma("sync", ab[:], abS[ti], w=[B("ab")])
                        sm = c["sm"]
                        bsm = B("sm"); bab = B("ab")
                        pS, bpS = banks.next()
                        ph.mm(pS[:, 0:8], cst[:, 2 + d, :], ab[:, 0:8], r=[b_cst, bab], w=[bpS])
                        ph.mm(pS[:, 8:16], cst[:, 1, :], ab[:, 0:8], r=[b_cst, bab], w=[bpS])
                        gcs = pS[:, d * 4:d * 4 + 4]
                        gls = pS[:, 8 + d * 4:12 + d * 4]
                        ph.ts("vector", sm[:, 0:4], gcs, -1.0, ALU.mult, w=[bpS, bsm])
                        ph.act(sm[:, 4:8], gcs, AF.Exp, w=[bpS, bsm])
                        ph.act(sm[:, 8:12], gls, AF.Exp, w=[bpS, bsm])
                        ph.tt("vector", sm[:, 12:16], gls, sm[:, 0:4], ALU.add, w=[bpS, bsm])
                        ph.act(sm[:, 12:16], sm[:, 12:16], AF.Exp, w=[bsm])
                DHS = [(slot, d, h) for slot in range(len(steps)) for d in range(2) for h in range(4)]
                bk = {}

                def every(fn):
                    for ch in DHS:
                        fn(C[(ch[0], ch[1])], *ch)

                def stage(pe, cons):
                    for i in range(0, len(DHS), 8):
                        for ch in DHS[i:i + 8]:
                            pe(C[(ch[0], ch[1])], *ch)
                        for ch in DHS[i:i + 8]:
                            cons(C[(ch[0], ch[1])], *ch)

                def bank(ch, tr_=False):
                    bk[ch] = trbank() if tr_ else banks.next()
                    return bk[ch]

                every(lambda c, slot, d, h: ph.act(c["Gbc"][:, h, :], cst[:, 1, :], AF.Identity, scale=c["ab"][:, d * 4 + h:d * 4 + h + 1],
                                                   r=[b_cst, c["B"]("ab")], w=[c["B"]("Gbc", h)]))

                def pe_R(c, slot, d, h):
                    pR, bR = bank((slot, d, h))
                    ph.mm(pR[:, 0:128], c["Gbc"][:, h, :], cst[:, 2 + d, :], r=[c["B"]("Gbc", h), b_cst], w=[bR])
                    ph.mm(pR[:, 128:256], c["Gbc"][:, h, :], cst[:, 2 + d, :], start=True, stop=False, r=[c["B"]("Gbc", h), b_cst], w=[bR])
                    ph.mm(pR[:, 128:256], cst[:, 0, :], cst[:, 4 + d, :], start=False, stop=True, r=[b_cst], w=[bR])

                def co_R(c, slot, d, h):
                    pR, bR = bk[(slot, d, h)]
                    ph.act(c["EQ"][:, h, :], pR[:, 0:128], AF.Exp, w=[bR, c["B"]("EQ", h)])
                    ph.act(c["E2T"][:, h, :], pR[:, 128:256], AF.Exp, bias=c["sm"][:, h:h + 1], r=[c["B"]("sm")], w=[bR, c["B"]("E2T", h)])
                stage(pe_R, co_R)

                def po_E(c, slot, d, h):
                    ph.tt("gpsimd", c["EsT"][:, h, :], c["E2T"][:, h, :], cst[:, 6 + d, :], ALU.mult, r=[c["B"]("E2T", h), b_cst], w=[c["B"]("EsT", h)])
                    ph.tt("gpsimd", c["qdecT"][:, h, :], c["qn"][:, h, :], c["EQ"][:, h, :], ALU.mult, r=[c["B"]("qn"), c["B"]("EQ", h)], w=[c["B"]("qdecT", h)])
                every(po_E)

                def pe_G(c, slot, d, h):
                    pG, bG = bank((slot, d, h))
                    ph.mm(pG[:, 0:128], c["kn"][:, h, :], c["kn"][:, h, :], r=[c["B"]("kn")], w=[bG])
                    ph.mm(pG[:, 128:256], c["kn"][:, h, :], c["qn"][:, h, :], r=[c["B"]("kn"), c["B"]("qn")], w=[bG])

                def co_G(c, slot, d, h):
                    pG, bG = bk[(slot, d, h)]
                    ph.stt("vector", c["M0t"][:, h, :], pG[:, 0:128], c["b4"][:, h:h + 1], c["EsT"][:, h, :], ALU.mult, ALU.mult,
                           r=[c["B"]("ab"), c["B"]("EsT", h)], w=[bG, c["B"]("M0t", h)])
                    ph.tt("vector", c["aqkT"][:, h, :], pG[:, 128:256], c["E2T"][:, h, :], ALU.mult,
                          r=[c["B"]("E2T", h)], w=[bG, c["B"]("aqkT", h)])
                stage(pe_G, co_G)

                def pe_T(c, slot, d, h):
                    pT_, bT_ = bank((slot, d, h), True)
                    ph.tr(pT_[:, 0:128], c["M0t"][:, h, :], ident_bf[:], r=[c["B"]("M0t", h), b_ib], w=[bT_])

                def co_T(c, slot, d, h):
                    pT_, bT_ = bk[(slot, d, h)]
                    ph.cp("scalar", c["A0t"][:, h, :], pT_[:, 0:128], w=[bT_, c["B"]("A0t", h)])
                stage(pe_T, co_T)

                def po_M(c, slot, d, h):
                    ph.tt("gpsimd", c["AMb"][:, h, 1, :], c["M0t"][:, h, :], cst[:, 8, :], ALU.mult, r=[c["B"]("M0t", h), b_cst], w=[c["B"]("AMb", h)])
                    ph.tt("gpsimd", c["AMb"][:, h, 0, :], c["A0t"][:, h, :], cst[:, 8, :], ALU.mult, r=[c["B"]("A0t", h), b_cst], w=[c["B"]("AMb", h)])
                    ph.tt("gpsimd", c["PP"][0][:, h, :], cst[:, 0, :], c["AMb"][:, h, 1, :], ALU.subtract, r=[b_cst, c["B"]("AMb", h)], w=[c["B"]("P0", h)])
                every(po_M)

                def pe_A1(c, slot, d, h):
                    pA, bA_ = bank((slot, d, h))
                    ph.mm(pA[:, 0:128], c["AMb"][:, h, 1, :], c["AMb"][:, h, 0, :], r=[c["B"]("AMb", h)], w=[bA_])
                    ph.mm(pA[:, 128:256], c["AMb"][:, h, 0, :], c["AMb"][:, h, 1, :], r=[c["B"]("AMb", h)], w=[bA_])

                def co_A1(c, slot, d, h):
                    pA, bA_ = bk[(slot, d, h)]
                    ph.cp("scalar", c["AM1"][:, h, :, :], pA[:, 0:256].rearrange("p (a b) -> p a b", a=2), w=[bA_, c["B"]("AM1", h)])
                stage(pe_A1, co_A1)

                def pe_P1(c, slot, d, h):
                    pP, bP_ = bank((slot, d, h))
                    ph.mm(pP[:, 0:128], c["AM1"][:, h, 0, :], c["PP"][0][:, h, :], r=[c["B"]("AM1", h), c["B"]("P0", h)], w=[bP_])

                def co_P1(c, slot, d, h):
                    pP, bP_ = bk[(slot, d, h)]
                    ph.tt("vector", c["PP"][1][:, h, :], pP[:, 0:128], c["PP"][0][:, h, :], ALU.add, r=[c["B"]("P0", h)], w=[bP_, c["B"]("P1", h)])
                stage(pe_P1, co_P1)

                def pe_A2(c, slot, d, h):
                    pA, bA_ = bank((slot, d, h))
                    ph.mm(pA[:, 0:128], c["AM1"][:, h, 1, :], c["AM1"][:, h, 0, :], r=[c["B"]("AM1", h)], w=[bA_])

                def co_A2(c, slot, d, h):
                    pA, bA_ = bk[(slot, d, h)]
                    ph.cp("scalar", c["A2t"][:, h, :], pA[:, 0:128], w=[bA_, c["B"]("A2t", h)])
                stage(pe_A2, co_A2)

                def pe_P2(c, slot, d, h):
                    pP, bP_ = bank((slot, d, h))
                    ph.mm(pP[:, 0:128], c["A2t"][:, h, :], c["PP"][1][:, h, :], r=[c["B"]("A2t", h), c["B"]("P1", h)], w=[bP_])

                def co_P2(c, slot, d, h):
                    pP, bP_ = bk[(slot, d, h)]
                    ph.tt("vector", c["PP"][0][:, h, :], pP[:, 0:128], c["PP"][1][:, h, :], ALU.add, r=[c["B"]("P1", h)], w=[bP_, c["B"]("P0", h)])
                stage(pe_P2, co_P2)

                for mi in range(4):
                    cur = mi % 2
                    nxt = 1 - cur

                    def pe_PT(c, slot, d, h, cur=cur):
                        pT_, bT_ = bank((slot, d, h), True)
                        ph.tr(pT_[:, 0:128], c["PP"][cur][:, h, :], ident_bf[:], r=[c["B"]("P%d" % cur, h), b_ib], w=[bT_])

                    def co_PT(c, slot, d, h):
                        pT_, bT_ = bk[(slot, d, h)]
                        ph.cp("scalar", c["PT"][:, h, :], pT_[:, 0:128], w=[bT_, c["B"]("PT", h)])
                    stage(pe_PT, co_PT)

                    def pe_X(c, slot, d, h, cur=cur):
                        pX, bX_ = bank((slot, d, h))
                        ph.mm(pX[:, 0:128], c["A0t"][:, h, :], c["PP"][cur][:, h, :], r=[c["B"]("A0t", h), c["B"]("P%d" % cur, h)], w=[bX_])

                    def co_X(c, slot, d, h, mi=mi):
                        pX, bX_ = bk[(slot, d, h)]
                        ph.tt("vector", c["Xt"][:, h, :], pX[:, 0:128], cst[:, 9 + mi, :], ALU.mult, r=[b_cst], w=[bX_, c["B"]("Xt", h)])
                    stage(pe_X, co_X)

                    def pe_Y(c, slot, d, h):
                        pY, bY_ = bank((slot, d, h))
                        ph.mm(pY[:, 0:128], c["PT"][:, h, :], c["Xt"][:, h, :], r=[c["B"]("PT", h), c["B"]("Xt", h)], w=[bY_])

                    def co_Y(c, slot, d, h, cur=cur, nxt=nxt):
                        pY, bY_ = bk[(slot, d, h)]
                        ph.tt("vector", c["PP"][nxt][:, h, :], c["PP"][cur][:, h, :], pY[:, 0:128], ALU.subtract,
                              r=[c["B"]("P%d" % cur, h)], w=[bY_, c["B"]("P%d" % nxt, h)])
                    stage(pe_Y, co_Y)

                def ac_K(c, slot, d, h):
                    hc = slice(h * 128, (h + 1) * 128)
                    ph.act(c["kegc"][:, h, :], c["kT"][:, hc], AF.Identity, scale=c["sm"][:, 4 + h:5 + h], r=[c["B"]("kT"), c["B"]("sm")], w=[c["B"]("kegc", h)])
                    ph.act(c["kst"][:, h, :], c["kT"][:, hc], AF.Identity, scale=c["sm"][:, 12 + h:13 + h], r=[c["B"]("kT"), c["B"]("sm")], w=[c["B"]("kst", h)])
                every(ac_K)

                def pe_W(c, slot, d, h):
                    pW, bW = bank((slot, d, h))
                    ph.mm(pW[:, 0:128], c["kegc"][:, h, :], c["PP"][0][:, h, :], r=[c["B"]("kegc", h), c["B"]("P0", h)], w=[bW])
                    ph.mm(pW[:, 128:256], c["PP"][0][:, h, :], c["vT"][:, h * 128:(h + 1) * 128], r=[c["B"]("P0", h), c["B"]("vT")], w=[bW])

                def co_W(c, slot, d, h):
                    pW, bW = bk[(slot, d, h)]
                    ph.cp("scalar", c["wTp"][:, h, :], pW[:, 0:128], w=[bW, c["B"]("wTp", h)])
                    ph.ts("vector", c["u"][:, h, :], pW[:, 128:256], c["b4"][:, h:h + 1], ALU.mult, r=[c["B"]("ab")], w=[bW, c["B"]("u", h)])
                stage(pe_W, co_W)
                return C

            def rec(C, slot):
                DH2 = [(d, h) for d in range(2) for h in range(4)]
                bk = {}
                bk2 = {}
                for (d, h) in DH2:
                    c = C[(slot, d)]
                    pW, bW = banks.next(); bk[(d, h)] = (pW, bW)
                    ph.mm(pW[:, 0:128], c["wTp"][:, h, :], Sbf[d][:, h, :], r=[c["B"]("wTp", h), bSbf[d][h]], w=[bW])
                for (d, h) in DH2:
                    c = C[(slot, d)]; pW, bW = bk[(d, h)]
                    ph.stt("vector", c["vnew"][:, h, :], pW[:, 0:128], c["nb4"][:, h:h + 1], c["u"][:, h, :], ALU.mult, ALU.add,
                           r=[c["B"]("ab"), c["B"]("u", h)], w=[bW, c["B"]("vnew", h)])
                for (d, h) in DH2:
                    c = C[(slot, d)]
                    pO, bO = banks.next(); bk2[(d, h)] = (pO, bO)
                    ph.mm(pO[:, 0:128], Sbf[d][:, h, :], c["qdecT"][:, h, :], start=True, stop=False, r=[bSbf[d][h], c["B"]("qdecT", h)], w=[bO])
                    ph.mm(pO[:, 0:128], c["vnew"][:, h, :], c["aqkT"][:, h, :], start=False, stop=True, r=[c["B"]("vnew", h), c["B"]("aqkT", h)], w=[bO])
                    ph.mm(pO[:, 128:256], c["kst"][:, h, :], c["vnew"][:, h, :], r=[c["B"]("kst", h), c["B"]("vnew", h)], w=[bO])
                for (d, h) in DH2:
                    c = C[(slot, d)]; pO, bO = bk2[(d, h)]
                    ph.stt("vector", S32[d][:, h, :], S32[d][:, h, :], c["sm"][:, 8 + h:9 + h], pO[:, 128:256], ALU.mult, ALU.add,
                           r=[c["B"]("sm")], w=[bO, bS32[d][h]])
                    ph.cp("scalar", c["oTs"][:, h, :], pO[:, 0:128], w=[bO, c["B"]("oTs", h)])
                for (d, h) in DH2:
                    ph.cp("gpsimd", Sbf[d][:, h, :], S32[d][:, h, :], r=[bS32[d][h]], w=[bSbf[d][h]])
                for d in range(2):
                    c = C[(slot, d)]
                    ph.dma("sync", oTd[d][:, :, c["tc"]].rearrange("h p t -> p h t"), c["oTs"][:], r=[c["B"]("oTs", h) for h in range(4)])

            pairs = [(s_, s_ + 1) for s_ in range(0, NTILE, 2)]
            prev = prep(pairs[0])
            for pi in range(len(pairs)):
                nxt_ = prep(pairs[pi + 1]) if pi + 1 < len(pairs) else None
                for slot in range(2):
                    rec(prev, slot)
                prev = nxt_
            if "sfin" in dbg:
                for d in range(2):
                    ph.dma("sync", sfin[l, d].rearrange("h p t -> p h t"), S32[d][:], r=bS32[d])
        if done("p2b_%d" % l):
            return nc


        with Phase(nc, "p3_%d" % l) as ph:
            cst = ph.sb([128, NCST, 128], F32, "cst"); b_cst = Buf()
            ph.dma("sync", cst[:], cst_d, w=[b_cst])
            ones_bf = ph.sb([128, 128], BF16); b_cb = Buf()
            ph.cp("vector", ones_bf[:], cst[:, 1, :], r=[b_cst], w=[b_cb])
            wa = ph.sb([128, 4, D], BF16, "wa"); wbb = ph.sb([128, 2, D], BF16, "wb"); wc = ph.sb([128, 2, D], BF16, "wc")
            wo = ph.sb([128, KC, D], BF16, "wo"); b_w = Buf()
            ph.dma("gpsimd", wa[:], wbra_d[l].rearrange("(k p) n -> p k n", p=128), w=[b_w])
            ph.dma("gpsimd", wbb[:], wbrb_d[l].rearrange("(k p) n -> p k n", p=128), w=[b_w])
            ph.dma("gpsimd", wc[:], wbrc_d[l].rearrange("(k p) n -> p k n", p=128), w=[b_w])
            ph.dma("gpsimd", wo[:], wo_d[l].rearrange("(k p) n -> p k n", p=128), w=[b_w])
            pw = ph.sb([128, 2, 128], BF16, "pw"); b_pw = Buf()
            ph.memset("vector", pw[:], 0.0, w=[b_pw])
            for c in range(2):
                for half in range(2):
                    ph.dma("gpsimd", pw[half * 64:(half + 1) * 64, c, half * 64:(half + 1) * 64], poolw_d[l, 2 * c + half], w=[b_pw])
            psc_ = ph.sb([128, 2], F32); scw = ph.sb([128, 2, 3], F32); dng = ph.sb([128, 1], F32); b_sm = Buf()
            ph.dma("sync", psc_[:], pscale_d[l], w=[b_sm])
            ph.dma("sync", scw[:], scw_d[l], w=[b_sm])
            ph.dma("sync", dng[:], dng_d[l], w=[b_sm])
            mods = ph.sb([128, 48, 2], F32); b_mods = Buf()
            ph.dma("sync", mods[:], modd[l], w=[b_mods])

            G3 = 512
            ofr = Rot(ph, 1, [128, 4, G3], BF16, "of"); obr = Rot(ph, 1, [128, 4, G3], BF16, "ob"); osr = Rot(ph, 1, [128, 4, G3], F32, "os")
            szr = Rot(ph, 1, [128, 4, G3], BF16, "sz"); onr = Rot(ph, 1, [128, 4, G3], BF16, "on")
            sqr = Rot(ph, 2, [128, G3], BF16, "sq"); rr_ = Rot(ph, 2, [128, G3], F32, "r"); tmr = Rot(ph, 3, [128, G3], F32, "tm")
            dr = Rot(ph, 1, [128, 2, G3], BF16, "d"); ybr = Rot(ph, 1, [128, 2, G3], BF16, "yb")
            scr = Rot(ph, 1, [128, 6, G3 + 2], BF16, "sc"); cxr = Rot(ph, 1, [128, 2, G3 + 2], F32, "cx"); ycr = Rot(ph, 1, [128, 2, G3], BF16, "yc")
            gtr = Rot(ph, 1, [128, 24, G3], BF16, "gt"); xgr = Rot(ph, 1, [128, KC, G3], F32, "xg")
            yr = Rot(ph, 1, [128, KC, G3], BF16, "y"); xor_ = Rot(ph, 1, [128, KC, G3], F32, "xo")
            ps1 = Rot(ph, 2, [128, G3], F32, "ps1", psum=True)
            ps3 = Rot(ph, 5, [128, G3], F32, "ps3", psum=True)
            for (j, t0, G) in token_groups(NT, G3):
                if last and j == 1:
                    continue
                s_lo, s_hi = (0, NCTX) if j == 1 else (NCTX, NTA)
                of_, ofb = ofr.next(); ob_, obb = obr.next(); sz, szb = szr.next(); on, onb = onr.next()
                ph.dma("scalar", of_[:, :, 0:G], oTd[0][:, :, t0:t0 + G].rearrange("h p t -> p h t"), w=[ofb])
                ph.dma("scalar", ob_[:, :, 0:G], oTd[1][:, :, t0:t0 + G].rearrange("h p t -> p h t"), w=[obb])
                ph.dma("sync", sz[:, :, 0:G], szT[:, :, t0:t0 + G].rearrange("h p t -> p h t"), w=[szb])
                osum, osb = osr.next()
                ph.tt("gpsimd", osum[:, :, 0:G], of_[:, :, 0:G], ob_[:, :, 0:G], ALU.add, r=[ofb, obb], w=[osb])
                of_, ofb = osum, osb
                for h in range(4):
                    sq, sqb = sqr.next()
                    ph.act(sq[:, 0:G], of_[:, h, 0:G], AF.Square, r=[ofb], w=[sqb])
                    p1, p1b = ps1.next()
                    ph.mm(p1[:, 0:G], ones_bf[:], sq[:, 0:G], r=[b_cb, sqb], w=[p1b])
                    rr, rb = rr_.next()
                    ph.act(rr[:, 0:G], p1[:, 0:G], AF.Ln, bias=EPS, scale=1.0 / 128, r=[p1b], w=[rb])
                    ph.act(rr[:, 0:G], rr[:, 0:G], AF.Exp, scale=-0.5, r=[rb], w=[rb])
                    tm, tmb = tmr.next()
                    ph.stt("vector", tm[:, 0:G], of_[:, h, 0:G], dng[:, 0:1], rr[:, 0:G], ALU.mult, ALU.mult, r=[ofb, b_sm, rb], w=[tmb])
                    ph.tt("gpsimd", on[:, h, 0:G], tm[:, 0:G], sz[:, h, 0:G], ALU.mult, r=[tmb, szb], w=[onb])
                dd, ddb = dr.next(); yb, ybb = ybr.next()
                ph.dma("sync", dd[:, :, 0:G], dT[:, :, t0:t0 + G].rearrange("c p t -> p c t"), w=[ddb])
                for c in range(2):
                    p1, p1b = ps1.next()
                    ph.mm(p1[:, 0:G], pw[:, c, :], dd[:, c, 0:G], r=[b_pw, ddb], w=[p1b])
                    ph.ts("vector", yb[:, c, 0:G], p1[:, 0:G], psc_[:, c:c + 1], ALU.mult, r=[p1b, b_sm], w=[ybb])
                sc, scb = scr.next(); cx, cxb = cxr.next(); yc, ycb = ycr.next()
                a = max(t0 - 1, s_lo); b = min(t0 + G + 1, s_hi)
                if a > t0 - 1:
                    ph.memset("gpsimd", sc[:, :, 0:1], 0.0, w=[scb])
                if b < t0 + G + 1:
                    ph.memset("gpsimd", sc[:, :, G + 1:G + 2], 0.0, w=[scb])
                ph.dma("sync", sc[:, :, a - (t0 - 1):b - (t0 - 1)], pscT[:, :, a:b].rearrange("c p t -> p c t"), w=[scb])
                ph.tt("gpsimd", cx[:, :, 0:G + 2], sc[:, 4:6, 0:G + 2], sc[:, 0:2, 0:G + 2], ALU.mult, r=[scb], w=[cxb])
                for c in range(2):
                    tm, tmb = tmr.next()
                    ph.act(tm[:, 0:G], cx[:, c, 1:G + 1], AF.Identity, scale=scw[:, c, 1:2], r=[cxb, b_sm], w=[tmb])
                    ph.stt("vector", tm[:, 0:G], cx[:, c, 0:G], scw[:, c, 0:1], tm[:, 0:G], ALU.mult, ALU.add, r=[cxb, b_sm, tmb], w=[tmb])
                    ph.stt("vector", tm[:, 0:G], cx[:, c, 2:G + 2], scw[:, c, 2:3], tm[:, 0:G], ALU.mult, ALU.add, r=[cxb, b_sm, tmb], w=[tmb])
                    ph.tt("gpsimd", yc[:, c, 0:G], tm[:, 0:G], sc[:, 2 + c, 1:G + 1], ALU.mult, r=[tmb, scb], w=[ycb])
                gt, gtb = gtr.next(); xg, xb = xgr.next(); y, yb_ = yr.next(); xo, xob = xor_.next()
                ph.dma("scalar", gt[:, :, 0:G], gatesT[:, :, t0:t0 + G].rearrange("c p t -> p c t"), w=[gtb])
                ph.dma("sync", xg[:, :, 0:G], xT[:, :, t0:t0 + G].rearrange("k p t -> p k t"), w=[xb])
                for m in range(KC):
                    mc = slice(m * 128, (m + 1) * 128)
                    pa, pab = ps3.next(); pb_, pbb = ps3.next(); pc, pcb = ps3.next()
                    for h in range(4):
                        ph.mm(pa[:, 0:G], wa[:, h, mc], on[:, h, 0:G], start=(h == 0), stop=(h == 3), r=[b_w, onb], w=[pab])
                    for c in range(2):
                        ph.mm(pb_[:, 0:G], wbb[:, c, mc], yb[:, c, 0:G], start=(c == 0), stop=(c == 1), r=[b_w, ybb], w=[pbb])
                    for c in range(2):
                        ph.mm(pc[:, 0:G], wc[:, c, mc], yc[:, c, 0:G], start=(c == 0), stop=(c == 1), r=[b_w, ycb], w=[pcb])
                    t1, t1b = tmr.next(); t2, t2b = tmr.next(); t3, t3b = tmr.next()
                    ph.tt("vector", t1[:, 0:G], pa[:, 0:G], gt[:, m, 0:G], ALU.mult, r=[pab, gtb], w=[t1b])
                    ph.tt("vector", t2[:, 0:G], pb_[:, 0:G], gt[:, 8 + m, 0:G], ALU.mult, r=[pbb, gtb], w=[t2b])
                    ph.tt("vector", t3[:, 0:G], pc[:, 0:G], gt[:, 16 + m, 0:G], ALU.mult, r=[pcb, gtb], w=[t3b])
                    ph.tt("gpsimd", t1[:, 0:G], t1[:, 0:G], t2[:, 0:G], ALU.add, r=[t1b, t2b], w=[t1b])
                    ph.tt("gpsimd", y[:, m, 0:G], t1[:, 0:G], t3[:, 0:G], ALU.add, r=[t1b, t3b], w=[yb_])
                for m in range(KC):
                    mc = slice(m * 128, (m + 1) * 128)
                    pa, pab = ps3.next()
                    for k in range(KC):
                        ph.mm(pa[:, 0:G], wo[:, k, mc], y[:, k, 0:G], start=(k == 0), stop=(k == KC - 1), r=[b_w, yb_], w=[pab])
                    ph.stt("vector", xo[:, m, 0:G], pa[:, 0:G], mods[:, 16 + m, j:j + 1], xg[:, m, 0:G], ALU.mult, ALU.add,
                           r=[pab, b_mods, xb], w=[xob])
                ph.dma("sync", xT[:, :, t0:t0 + G].rearrange("k p t -> p k t"), xo[:, :, 0:G], r=[xob])
        if done("p3_%d" % l):
            return nc

        with Phase(nc, "p4_%d" % l) as ph:
            onesf = ph.sb([128, 128], F32, "onesf"); b_cst = Buf()
            ph.dma("sync", onesf[:], cst_d[:, 1, :], w=[b_cst])
            ones_bf = ph.sb([128, 128], BF16); b_cb = Buf()
            ph.cp("vector", ones_bf[:], onesf[:], r=[b_cst], w=[b_cb])
            wgu = ph.sb([128, KC, 2 * DFF], BF16, "wgu"); wdn = ph.sb([128, FC, D], BF16, "wdn"); b_wg = [Buf() for _ in range(KC)]; b_wd = Buf()
            for k in range(KC):
                ph.dma("gpsimd", wgu[:, k, :], wgu_d[l, k * 128:(k + 1) * 128, :], w=[b_wg[k]])
            for f0 in range(0, FC, 11):
                ph.dma("gpsimd", wdn[:, f0:f0 + 11, :], wdn_d[l, f0 * 128:(f0 + 11) * 128, :].rearrange("(f p) n -> p f n", p=128), w=[b_wd])
            mods = ph.sb([128, 48, 2], F32); b_mods = Buf()
            ph.dma("sync", mods[:], modd[l], w=[b_mods])
            n2g = ph.sb([128, KC], F32); b_n2g = Buf()
            ph.dma("sync", n2g[:], n2g_d[l], w=[b_n2g])
            A2 = ph.sb([128, 2, KC], F32); b_A2 = Buf()
            for j in range(2):
                ph.stt("vector", A2[:, j, :], mods[:, 32:40, j], 1.0, n2g[:], ALU.add, ALU.mult, r=[b_mods, b_n2g], w=[b_A2])
            G4 = 512
            xgr = Rot(ph, 2, [128, KC, G4], F32, "xg")
            hr = Rot(ph, 1, [128, KC, G4], BF16, "hT"); tmr = Rot(ph, 2, [128, G4], F32, "tm"); rsr = Rot(ph, 1, [128, G4], F32, "rs")
            acr = Rot(ph, 1, [128, FC, G4], BF16, "act")
            ssp = Rot(ph, 1, [128, G4], F32, "ssp", psum=True)
            gup = Rot(ph, 6, [128, G4], F32, "gup", psum=True)
            for (j, t0, G) in token_groups(NT, G4):
                if last and j == 1:
                    continue
                xg, xb = xgr.next()
                ph.dma("sync", xg[:, :, 0:G], xT[:, :, t0:t0 + G].rearrange("k p t -> p k t"), w=[xb])
                ac, acb = acr.next()
                sq, sqb = ac, acb
                ph.act(sq[:, 0:KC, 0:G], xg[:, :, 0:G], AF.Square, r=[xb], w=[sqb])
                sp, spb = ssp.next()
                for k in range(KC):
                    ph.mm(sp[:, 0:G], ones_bf[:], sq[:, k, 0:G], start=(k == 0), stop=(k == KC - 1), r=[b_cb, sqb], w=[spb])
                rs, rsb = rsr.next()
                ph.act(rs[:, 0:G], sp[:, 0:G], AF.Ln, bias=EPS, scale=1.0 / D, r=[spb], w=[rsb])
                ph.act(rs[:, 0:G], rs[:, 0:G], AF.Exp, scale=-0.5, r=[rsb], w=[rsb])
                hT, hb = hr.next()
                for k in range(KC):
                    tm, tmb = tmr.next()
                    ph.stt("vector", tm[:, 0:G], xg[:, k, 0:G], A2[:, j, k:k + 1], rs[:, 0:G], ALU.mult, ALU.mult, r=[xb, b_A2, rsb], w=[tmb])
                    ph.act(hT[:, k, 0:G], tm[:, 0:G], AF.Identity, bias=mods[:, 24 + k, j:j + 1], r=[tmb, b_mods], w=[hb])
                for f in range(FC):
                    pg, pgb = gup.next(); pu, pub = gup.next()
                    for k in range(KC):
                        ph.mm(pg[:, 0:G], wgu[:, k, f * 128:(f + 1) * 128], hT[:, k, 0:G], start=(k == 0), stop=(k == KC - 1), r=[b_wg[k], hb], w=[pgb])
                    for k in range(KC):
                        ph.mm(pu[:, 0:G], wgu[:, k, DFF + f * 128:DFF + (f + 1) * 128], hT[:, k, 0:G], start=(k == 0), stop=(k == KC - 1), r=[b_wg[k], hb], w=[pub])
                    tm, tmb = tmr.next()
                    ph.act(tm[:, 0:G], pg[:, 0:G], AF.Silu, r=[pgb], w=[tmb])
                    ph.tt("vector", ac[:, f, 0:G], pu[:, 0:G], tm[:, 0:G], ALU.mult, r=[pub, tmb], w=[acb])
                for m in range(KC):
                    pd, pdb = gup.next()
                    for f in range(FC):
                        ph.mm(pd[:, 0:G], wdn[:, f, m * 128:(m + 1) * 128], ac[:, f, 0:G], start=(f == 0), stop=(f == FC - 1), r=[b_wd, acb], w=[pdb])
                    ph.stt("vector", xg[:, m, 0:G], pd[:, 0:G], mods[:, 40 + m, j:j + 1], xg[:, m, 0:G], ALU.mult, ALU.add,
                           r=[pdb, b_mods], w=[xb])
                ph.dma("sync", xT[:, :, t0:t0 + G].rearrange("k p t -> p k t"), xg[:, :, 0:G], r=[xb])
        if done("p4_%d" % l):
            return nc

    with Phase(nc, "pf") as ph:
        cst = ph.sb([128, NCST, 128], F32, "cst"); b_cst = Buf()
        ph.dma("sync", cst[:], cst_d, w=[b_cst])
        ones_bf = ph.sb([128, 128], BF16); b_cb = Buf()
        ph.cp("vector", ones_bf[:], cst[:, 1, :], r=[b_cst], w=[b_cb])
        fng = ph.sb([128, KC], F32); b_fng = Buf()
        ph.dma("sync", fng[:], fng_d, w=[b_fng])
        GF = 512
        xgr = Rot(ph, 2, [128, KC, GF], F32, "xg"); sqr = Rot(ph, 1, [128, KC, GF], BF16, "sq"); rsr = Rot(ph, 1, [128, GF], F32, "rs")
        yr = Rot(ph, 2, [128, KC, GF], F32, "y"); otr = Rot(ph, 3, [128, D], F32, "ot")
        ssp = Rot(ph, 1, [128, GF], F32, "ssp", psum=True)
        trp = Rot(ph, 3, [128, KC, 128], F32, "trp", psum=True)
        out_waits = []
        for (j, t0, G) in token_groups(NT, GF):
            if j == 1:
                continue
            xg, xb = xgr.next()
            ph.dma("sync", xg[:], xT[:, :, t0:t0 + G].rearrange("k p t -> p k t"), w=[xb])
            sq, sqb = sqr.next()
            ph.act(sq[:], xg[:], AF.Square, r=[xb], w=[sqb])
            sp, spb = ssp.next()
            for k in range(KC):
                ph.mm(sp[:], ones_bf[:], sq[:, k, :], start=(k == 0), stop=(k == KC - 1), r=[b_cb, sqb], w=[spb])
            rs, rsb = rsr.next()
            ph.act(rs[:], sp[:], AF.Ln, bias=EPS, scale=1.0 / D, r=[spb], w=[rsb])
            ph.act(rs[:], rs[:], AF.Exp, scale=-0.5, r=[rsb], w=[rsb])
            y, yb = yr.next()
            for k in range(KC):
                ph.stt("vector", y[:, k, :], xg[:, k, :], fng[:, k:k + 1], rs[:], ALU.mult, ALU.mult,
                       r=[xb, b_fng, rsb], w=[yb])
            for s_ in range(G // 128):
                tp, tpb = trp.next()
                for k in range(KC):
                    ph.mm(tp[:, k, :], y[:, k, s_ * 128:(s_ + 1) * 128], cst[:, 0, :], r=[yb, b_cst], w=[tpb])
                ot, otb = otr.next()
                ph.cp("scalar" if s_ % 2 == 0 else "vector", ot[:].rearrange("p (k f) -> p k f", k=KC), tp[:], r=[tpb], w=[otb])
                ph.dma("scalar" if s_ % 2 == 0 else "sync", out_d[t0 - NCTX + s_ * 128:t0 - NCTX + (s_ + 1) * 128, :], ot[:], r=[otb])

    return nc


POOL_WINDOWS = (2, 4, 8, 16)


def _consts(NT):
    idx = np.arange(128)
    cst = np.zeros((128, NCST, 128), np.float32)
    cst[:, 0, :] = np.eye(128)
    cst[:, 1, :] = 1.0
    cst[:, 2, :] = (idx[:, None] <= idx[None, :])
    cst[:, 3, :] = (idx[:, None] >= idx[None, :])
    cst[:, 4, :] = np.where(idx[None, :] >= idx[:, None], 0.0, NEG)
    cst[:, 5, :] = np.where(idx[None, :] <= idx[:, None], 0.0, NEG)
    cst[:, 6, :] = (idx[None, :] > idx[:, None])
    cst[:, 7, :] = (idx[None, :] < idx[:, None])

    def blk(sz):
        return (idx[:, None] // sz == idx[None, :] // sz).astype(np.float32)
    cst[:, 8, :] = blk(8)
    for n, sz in enumerate((8, 16, 32, 64)):
        cst[:, 9 + n, :] = blk(2 * sz) - blk(sz)

    def cnt1d(n, w):
        lo = w // 2
        hi = w - 1 - lo
        pos = np.arange(n)
        return (np.clip(pos + hi + 1, 0, n) - np.clip(pos - lo, 0, n)).astype(np.float64)
    rows = NT // GW
    cnt_lat = np.zeros((2, 128, NT), np.float32)
    cnt_ctx = np.zeros((2, 128, NCTX), np.float32)
    for c in range(2):
        for half in range(2):
            w = POOL_WINDOWS[2 * c + half]
            cl = 1.0 / (cnt1d(rows, w)[:, None] * cnt1d(GW, w)[None, :])
            cnt_lat[c, half * 64:(half + 1) * 64, :] = cl.reshape(1, NT)
            cnt_ctx[c, half * 64:(half + 1) * 64, :] = (1.0 / cnt1d(NCTX, w))[None, :]
    return cst, cnt_lat, cnt_ctx


def col(v):
    v = np.asarray(v, np.float32)
    return np.ascontiguousarray(v.reshape(-1, 128).T)


def prep_shared(inp, NT):
    f = lambda a: np.ascontiguousarray(np.asarray(a, np.float32))
    cst, cnt_lat, cnt_ctx = _consts(NT)
    sh = {
        "w_ada": f(inp["w_ada"]),
        "b_adaT": np.stack([col(inp["b_ada"][l]) for l in range(2)]),
        "n1g": np.stack([col(inp["norm1_g"][l]) for l in range(2)]),
        "n2g": np.stack([col(inp["norm2_g"][l]) for l in range(2)]),
        "fng": col(inp["final_norm_g"]),
        "w_in": f(inp["w_in"]),
        "dcw": np.stack([np.stack([col(np.asarray(inp["dn_conv_w"])[l, t]) for t in range(3)], axis=-1) for l in range(2)]),
        "alog": np.stack([np.broadcast_to(np.asarray(inp["dn_a_log"], np.float32)[l].reshape(1, 8), (128, 8)) for l in range(2)]).copy(),
        "dtb": np.stack([np.broadcast_to(np.asarray(inp["dn_dt_bias"], np.float32)[l].reshape(1, 8), (128, 8)) for l in range(2)]).copy(),
        "dng": np.asarray(inp["dn_norm_g"], np.float32).reshape(2, 128, 1).copy(),
        "poolw": f(inp["pool_w"]),
        "pscale": np.stack([col(inp["pool_scale"][l]) for l in range(2)]),
        "scw": np.stack([np.stack([col(np.asarray(inp["sc_conv_w"])[l, t]) for t in range(3)], axis=-1) for l in range(2)]),
        "w_br_a": f(inp["w_br_a"]), "w_br_b": f(inp["w_br_b"]), "w_br_c": f(inp["w_br_c"]),
        "w_o": f(inp["w_o"]), "w_gu": f(inp["w_gu"]), "w_down": f(inp["w_down"]),
        "cst": cst, "cnt_lat": cnt_lat, "cnt_ctx": cnt_ctx,
    }
    return sh


def prep_core(inp, sh, b):
    m = dict(sh)
    m["x"] = np.ascontiguousarray(np.asarray(inp["x"], np.float32)[b])
    m["ctx"] = np.ascontiguousarray(np.asarray(inp["ctx"], np.float32)[b])
    m["cc"] = np.ascontiguousarray(np.stack([col(np.asarray(inp["c"])[b]), col(inp["c_ctx"])], axis=-1))
    return m


def kernel(**inputs):
    x = np.asarray(inputs["x"])
    B, NT, _ = x.shape
    nc = build(NT)
    sh = prep_shared(inputs, NT)
    in_maps = [prep_core(inputs, sh, c % B) for c in range(8)]
    res = run_bass_kernel_spmd(nc, in_maps, core_ids=list(range(8)))
    return np.stack([res.results[b]["out"] for b in range(B)]).astype(np.float32)
```

```python
import numpy as np
import ml_dtypes
from contextlib import ExitStack
import concourse.bass as bass
import concourse.mybir as mybir
from concourse.bass_utils import run_bass_kernel_spmd

F32 = mybir.dt.float32
BF16 = mybir.dt.bfloat16
AF = mybir.ActivationFunctionType
ALU = mybir.AluOpType

D = 1024
KC = 8
NCTX = 256
GW = 64
DFF = 2816
FC = DFF // 128
N_IN = 6160
EPS = 1e-6
NEG = -30000.0
ENGS = ("tensor", "vector", "scalar", "gpsimd", "sync")
NDS = 12
NCST = 13


class Buf:
    __slots__ = ("w", "r")

    def __init__(self):
        self.w = None
        self.r = []


class Sched:
    def __init__(self, nc, pname=""):
        self.nc = nc
        self.pname = pname
        self.ops = {e: [] for e in ENGS}
        self.keys = list(ENGS) + ["dma_%s_%d" % (q, i) for q in ("sync", "gpsimd", "scalar") for i in range(NDS)]
        self.cnt = {k: 0 for k in self.keys}
        self.seen = {e: {k: 0 for k in self.keys} for e in ENGS}
        self.dma_rr = {"sync": 0, "gpsimd": 0, "scalar": 0}
        self.sems = {}

    def alloc(self):
        for k in self.keys:
            self.sems[k] = self.nc.alloc_semaphore(name="s_%s_%s" % (self.pname, k))

    def _deps(self, eng, reads, writes):
        need = {}

        def add(tok):
            if tok is None:
                return
            p, c = tok
            if need.get(p, 0) < c:
                need[p] = c
        for b in reads:
            add(b.w)
        for b in writes:
            add(b.w)
            for t in b.r:
                add(t)
        waits = []
        for p, c in need.items():
            if eng == "tensor" and p == "tensor":
                continue
            if self.seen[eng][p] < c:
                self.seen[eng][p] = c
                waits.append((p, c))
        return waits

    def _commit(self, tok, reads, writes):
        for b in reads:
            b.r.append(tok)
        for b in writes:
            b.w = tok
            b.r = []

    def op(self, eng, fn, reads=(), writes=()):
        waits = self._deps(eng, reads, writes)
        self.cnt[eng] += 1
        tok = (eng, self.cnt[eng])
        self.ops[eng].append((waits, fn, (eng, 1)))
        self._commit(tok, reads, writes)
        return tok

    def dma(self, q, fn, reads=(), writes=()):
        waits = self._deps(q, reads, writes)
        key = "dma_%s_%d" % (q, self.dma_rr[q])
        self.dma_rr[q] = (self.dma_rr[q] + 1) % NDS
        if self.cnt[key] > 0 and self.seen[q][key] < self.cnt[key]:
            self.seen[q][key] = self.cnt[key]
            waits.append((key, self.cnt[key]))
        self.cnt[key] += 1
        tok = (key, self.cnt[key])
        self.ops[q].append((waits, fn, (key, 16)))
        self._commit(tok, reads, writes)
        return tok

    def emit(self):
        nc = self.nc
        mult = {k: (16 if k.startswith("dma_") else 1) for k in self.keys}
        waited = {k: set() for k in self.keys}
        for en in ENGS:
            for waits, fn, sk in self.ops[en]:
                for p, c in waits:
                    waited[p].add(c)
        for k in self.keys:
            waited[k].add(self.cnt[k])
        with nc.Block() as block:
            def run(engname):
                def body(e):
                    idx = 0
                    last = 0
                    for waits, fn, (sk, inc) in self.ops[engname]:
                        for p, c in waits:
                            e.wait_ge(self.sems[p], c * mult[p])
                        ins = fn(e)
                        if sk == engname:
                            idx += 1
                            if idx in waited[engname]:
                                ins.then_inc(self.sems[sk], idx - last)
                                last = idx
                        else:
                            ins.then_inc(self.sems[sk], inc)
                    if engname == "sync":
                        for k in self.keys:
                            if self.cnt[k] > 0:
                                e.wait_ge(self.sems[k], self.cnt[k] * mult[k])
                return body
            block.tensor(run("tensor"))
            block.vector(run("vector"))
            block.scalar(run("scalar"))
            block.gpsimd(run("gpsimd"))
            block.sync(run("sync"))


class Phase:
    def __init__(self, nc, name):
        self.nc = nc
        self.name = name
        self.es = ExitStack()
        self.S = Sched(nc, name)
        self.n = 0

    def __enter__(self):
        self.es.enter_context(self.nc.cleanup_on_exit())
        self.S.alloc()
        return self

    def __exit__(self, *a):
        if a[0] is None:
            self.S.emit()
        self.es.close()
        return False

    def sb(self, shape, dt, name=None):
        self.n += 1
        return self.es.enter_context(self.nc.sbuf_tensor("%s_%s%d" % (self.name, name or "t", self.n), list(shape), dt))

    def ps(self, shape, dt, name=None):
        self.n += 1
        return self.es.enter_context(self.nc.psum_tensor("%s_%s%d" % (self.name, name or "p", self.n), list(shape), dt))

    def dma(self, q, out, in_, r=(), w=()):
        return self.S.dma(q, lambda e: e.dma_start(out=out, in_=in_), r, w)

    def mm(self, out, lhsT, rhs, start=True, stop=True, r=(), w=()):
        return self.S.op("tensor", lambda e: e.matmul(out, lhsT=lhsT, rhs=rhs, start=start, stop=stop), r, w)

    def tr(self, out, in_, ident, r=(), w=()):
        return self.S.op("tensor", lambda e: e.transpose(out, in_, ident), r, w)

    def act(self, out, in_, func, bias=None, scale=None, r=(), w=()):
        kw = {}
        if bias is not None:
            kw["bias"] = bias
        if scale is not None:
            kw["scale"] = scale
        return self.S.op("scalar", lambda e: e.activation(out=out, in_=in_, func=func, **kw), r, w)

    def tt(self, eng, out, in0, in1, op, r=(), w=()):
        return self.S.op(eng, lambda e: e.tensor_tensor(out=out, in0=in0, in1=in1, op=op), r, w)

    def ts(self, eng, out, in0, s1, op0, s2=None, op1=None, r=(), w=()):
        if op1 is None:
            return self.S.op(eng, lambda e: e.tensor_scalar(out=out, in0=in0, scalar1=s1, scalar2=None, op0=op0), r, w)
        return self.S.op(eng, lambda e: e.tensor_scalar(out=out, in0=in0, scalar1=s1, scalar2=s2, op0=op0, op1=op1), r, w)

    def stt(self, eng, out, in0, scalar, in1, op0, op1, r=(), w=()):
        return self.S.op(eng, lambda e: e.scalar_tensor_tensor(out=out, in0=in0, scalar=scalar, in1=in1, op0=op0, op1=op1), r, w)

    def cp(self, eng, out, in_, r=(), w=()):
        if eng == "scalar":
            return self.S.op(eng, lambda e: e.copy(out=out, in_=in_), r, w)
        return self.S.op(eng, lambda e: e.tensor_copy(out=out, in_=in_), r, w)

    def memset(self, eng, ap, val, r=(), w=()):
        return self.S.op(eng, lambda e: e.memset(ap, val), r, w)


class Rot:
    def __init__(self, ph, n, shape, dt, name, psum=False):
        mk = ph.ps if psum else ph.sb
        self.t = [mk(shape, dt, name) for _ in range(n)]
        self.b = [Buf() for _ in range(n)]
        self.i = -1

    def next(self):
        self.i = (self.i + 1) % len(self.t)
        return self.t[self.i], self.b[self.i]


def token_groups(NT, G):
    gs = [(1, 0, NCTX)]
    t = NCTX
    while t < NCTX + NT:
        gs.append((0, t, G))
        t += G
    return gs


def build(NT, stop_after=None, dbg=()):
    NTA = NCTX + NT
    NTILE = NTA // 128
    ROWS = NT // GW
    nc = bass.Bass("TRN2", target_bir_lowering=False)

    def din(name, shape, dt=F32):
        return nc.dram_tensor(name, list(shape), dt, kind="ExternalInput").ap()

    def scratch(name, shape, dt=F32):
        kind = "ExternalOutput" if name in dbg else "Internal"
        return nc.dram_tensor(name, list(shape), dt, kind=kind).ap()

    x_d = din("x", [NT, D]); ctx_d = din("ctx", [NCTX, D]); cc_d = din("cc", [128, KC, 2])
    w_ada_d = din("w_ada", [2, D, 6 * D]); b_adaT_d = din("b_adaT", [2, 128, 48])
    n1g_d = din("n1g", [2, 128, KC]); n2g_d = din("n2g", [2, 128, KC]); fng_d = din("fng", [128, KC])
    w_in_d = din("w_in", [2, D, N_IN]); dcw_d = din("dcw", [2, 128, 12, 3])
    alog_d = din("alog", [2, 128, 8]); dtb_d = din("dtb", [2, 128, 8]); dng_d = din("dng", [2, 128, 1])
    poolw_d = din("poolw", [2, 4, 64, 64]); pscale_d = din("pscale", [2, 128, 2]); scw_d = din("scw", [2, 128, 2, 3])
    wbra_d = din("w_br_a", [2, 512, D]); wbrb_d = din("w_br_b", [2, 256, D]); wbrc_d = din("w_br_c", [2, 256, D])
    wo_d = din("w_o", [2, D, D]); wgu_d = din("w_gu", [2, D, 2 * DFF]); wdn_d = din("w_down", [2, DFF, D])
    cst_d = din("cst", [128, NCST, 128]); cntl_d = din("cnt_lat", [2, 128, NT]); cntc_d = din("cnt_ctx", [2, 128, NCTX])
    out_d = nc.dram_tensor("out", [NT, D], F32, kind="ExternalOutput").ap()

    xT = scratch("xT", [KC, 128, NTA])
    pqkvT = scratch("pqkvT", [12, 128, NTA], BF16); szT = scratch("szT", [4, 128, NTA], BF16)
    ppoolT = scratch("ppoolT", [2, 128, NTA], BF16); pscT = scratch("pscT", [6, 128, NTA], BF16)
    gatesT = scratch("gatesT", [24, 128, NTA], BF16)
    abS = scratch("abS", [NTILE, 128, 24])
    qnT = scratch("qnT", [4, 128, NTA], BF16); knT = scratch("knT", [4, 128, NTA], BF16)
    kTM = scratch("kTM", [NTILE, 128, 512], BF16); vTM = scratch("vTM", [NTILE, 128, 512], BF16)
    oTd = [scratch("oTf", [4, 128, NTA], BF16), scratch("oTb", [4, 128, NTA], BF16)]
    dT = scratch("dT", [2, 128, NTA], BF16)
    modd = scratch("modd", [2, 128, 6 * KC, 2])
    sfin = scratch("sfin", [2, 2, 4, 128, 128])
    dbgbuf = scratch("dbgbuf", [2, 128, 8, 128])

    def done(tag):
        return stop_after == tag

    with Phase(nc, "p0") as ph:
        cst = ph.sb([128, NCST, 128], F32, "cst"); b_cst = Buf()
        ph.dma("sync", cst[:], cst_d, w=[b_cst])
        cc = ph.sb([128, KC, 2], F32); b_cc = Buf()
        ph.dma("sync", cc[:], cc_d, w=[b_cc])
        scc = ph.sb([128, KC, 2], F32); b_scc = Buf()
        ph.act(scc[:], cc[:], AF.Silu, r=[b_cc], w=[b_scc])
        wrot = Rot(ph, 3, [128, KC, 768], F32, "wada")
        modps_full = ph.ps([128, 512], F32); b_modps = Buf()
        modps = modps_full[:, 0:96].rearrange("p (m j) -> p m j", j=2)
        for l in range(2):
            badaT = ph.sb([128, 48], F32); b_bada = Buf()
            ph.dma("sync", badaT[:], b_adaT_d[l], w=[b_bada])
            for pc in range(8):
                wt, wb = wrot.next()
                ph.dma(("sync", "scalar", "gpsimd")[pc % 3], wt[:], w_ada_d[l, :, pc * 768:(pc + 1) * 768].rearrange("(k p) n -> p k n", p=128), w=[wb])
                for mi in range(6):
                    m = pc * 6 + mi
                    for k in range(KC):
                        ph.mm(modps[:, m, :], wt[:, k, mi * 128:(mi + 1) * 128], scc[:, k, :], start=(k == 0), stop=(k == KC - 1),
                              r=[wb, b_scc], w=[b_modps])
            mods = ph.sb([128, 48, 2], F32); b_mods = Buf()
            for j in range(2):
                ph.tt("vector", mods[:, :, j], modps[:, :, j], badaT[:], ALU.add, r=[b_modps, b_bada], w=[b_mods])
            ph.dma("sync", modd[l], mods[:], r=[b_mods])
        xrot = Rot(ph, 3, [128, D], F32, "xin")
        trps = Rot(ph, 2, [128, KC, 128], F32, "trps", psum=True)
        orot = Rot(ph, 3, [128, KC, 128], F32, "xo")
        for ti in range(NTILE):
            xt, xb = xrot.next()
            src = ctx_d[ti * 128:(ti + 1) * 128, :] if ti < NCTX // 128 else x_d[ti * 128 - NCTX:(ti + 1) * 128 - NCTX, :]
            ph.dma("scalar" if ti % 2 == 0 else "gpsimd", xt[:], src, w=[xb])
            pt, pb = trps.next()
            for k in range(KC):
                ph.mm(pt[:, k, :], xt[:, k * 128:(k + 1) * 128], cst[:, 0, :], r=[xb, b_cst], w=[pb])
            ot, ob = orot.next()
            ph.cp("vector" if ti % 2 == 0 else "scalar", ot[:], pt[:], r=[pb], w=[ob])
            ph.dma("sync", xT[:, :, ti * 128:(ti + 1) * 128].rearrange("k p t -> p k t"), ot[:], r=[ob])
    if done("p0"):
        return nc

    for l in range(2):
        last = (l == 1)
        with Phase(nc, "p1_%d" % l) as ph:
            cst = ph.sb([128, NCST, 128], F32, "cst"); b_cst = Buf()
            ph.dma("sync", cst[:], cst_d, w=[b_cst])
            ones_bf = ph.sb([128, 128], BF16); b_ones = Buf()
            ph.cp("vector", ones_bf[:], cst[:, 1, :], r=[b_cst], w=[b_ones])
            win = ph.sb([128, KC, N_IN], BF16, "win"); b_win = [Buf() for _ in range(KC)]
            for k in range(KC):
                ph.dma("gpsimd", win[:, k, :], w_in_d[l, k * 128:(k + 1) * 128, :], w=[b_win[k]])
            mods = ph.sb([128, 48, 2], F32); b_mods = Buf()
            ph.dma("sync", mods[:], modd[l], w=[b_mods])
            n1g = ph.sb([128, KC], F32); b_n1g = Buf()
            ph.dma("sync", n1g[:], n1g_d[l], w=[b_n1g])
            A1 = ph.sb([128, 2, KC], F32); b_A1 = Buf()
            for j in range(2):
                ph.stt("vector", A1[:, j, :], mods[:, 8:16, j], 1.0, n1g[:], ALU.add, ALU.mult, r=[b_mods, b_n1g], w=[b_A1])
            alog = ph.sb([128, 8], F32); dtb = ph.sb([128, 8], F32); b_al = Buf(); b_dtb = Buf()
            ph.dma("sync", alog[:], alog_d[l], w=[b_al])
            ph.dma("sync", dtb[:], dtb_d[l], w=[b_dtb])
            negea = ph.sb([128, 8], F32); b_negea = Buf()
            ph.act(negea[:], alog[:], AF.Exp, r=[b_al], w=[b_negea])
            ph.ts("vector", negea[:], negea[:], -1.0, ALU.mult, r=[b_negea], w=[b_negea])

            xrot = Rot(ph, 2, [128, KC, 512], F32, "xg")
            sqrot = Rot(ph, 1, [128, KC, 512], BF16, "sq")
            hrot = Rot(ph, 2, [128, KC, 512], BF16, "hT")
            tmprot = Rot(ph, 2, [128, 512], F32, "tmp")
            rsrot = Rot(ph, 2, [128, 512], F32, "rstd")
            stf = Rot(ph, 4, [128, 512], BF16, "stf")
            stb = Rot(ph, 4, [128, 512], BF16, "stb")
            abrot = Rot(ph, 2, [128, 24], F32, "ab")
            abt = Rot(ph, 2, [128, 8], F32, "abt")
            ssps = Rot(ph, 1, [128, 512], F32, "ssps", psum=True)
            accps = Rot(ph, 5, [128, 512], F32, "acc", psum=True)
            abps = Rot(ph, 2, [128, 512], F32, "abps", psum=True)

            chunks = []
            for c in range(12):
                chunks.append((c * 128, "copy", pqkvT, c))
            for c in range(4):
                chunks.append((1536 + c * 128, "silu", szT, c))
            for c in range(2):
                chunks.append((2064 + c * 128, "copy", ppoolT, c))
            for c in range(6):
                chunks.append((2320 + c * 128, "copy", pscT, c))
            for c in range(24):
                chunks.append((3088 + c * 128, "sigm", gatesT, c))

            for (j, t0, G) in token_groups(NT, 512):
                xg, xb = xrot.next()
                ph.dma("sync", xg[:, :, 0:G], xT[:, :, t0:t0 + G].rearrange("k p t -> p k t"), w=[xb])
                sq, sqb = sqrot.next()
                ph.act(sq[:, :, 0:G], xg[:, :, 0:G], AF.Square, r=[xb], w=[sqb])
                sp, spb = ssps.next()
                for k in range(KC):
                    ph.mm(sp[:, 0:G], ones_bf[:], sq[:, k, 0:G], start=(k == 0), stop=(k == KC - 1), r=[b_ones, sqb], w=[spb])
                rs, rsb = rsrot.next()
                ph.act(rs[:, 0:G], sp[:, 0:G], AF.Ln, bias=EPS, scale=1.0 / D, r=[spb], w=[rsb])
                ph.act(rs[:, 0:G], rs[:, 0:G], AF.Exp, scale=-0.5, r=[rsb], w=[rsb])
                hT, hb = hrot.next()
                for k in range(KC):
                    tm, tmb = tmprot.next()
                    ph.stt("vector", tm[:, 0:G], xg[:, k, 0:G], A1[:, j, k:k + 1], rs[:, 0:G], ALU.mult, ALU.mult,
                           r=[xb, b_A1, rsb], w=[tmb])
                    ph.act(hT[:, k, 0:G], tm[:, 0:G], AF.Identity, bias=mods[:, k, j:j + 1], r=[tmb, b_mods], w=[hb])
                for s in range(G // 128):
                    ap_, apb = abps.next()
                    for k in range(KC):
                        ph.mm(ap_[:, 0:16], hT[:, k, s * 128:(s + 1) * 128], win[:, k, 2048:2064], start=(k == 0), stop=(k == KC - 1),
                              r=[hb, b_win[k]], w=[apb])
                    ab, abb = abrot.next()
                    at, atb = abt.next()
                    ph.tt("vector", at[:], ap_[:, 0:8], dtb[:], ALU.add, r=[apb, b_dtb], w=[atb])
                    ph.act(at[:], at[:], AF.Exp, r=[atb], w=[atb])
                    ph.act(at[:], at[:], AF.Ln, bias=1.0, r=[atb], w=[atb])
                    ph.tt("vector", ab[:, 0:8], at[:], negea[:], ALU.mult, r=[atb, b_negea], w=[abb])
                    ph.act(ab[:, 8:16], ap_[:, 8:16], AF.Sigmoid, w=[apb, abb])
                    ph.ts("vector", ab[:, 16:24], ab[:, 8:16], -1.0, ALU.mult, r=[abb], w=[abb])
                    ph.dma("sync", abS[(t0 // 128) + s], ab[:], r=[abb])
                for ci, (c0, kind, dst, dc) in enumerate(chunks):
                    acc, accb = accps.next()
                    for k in range(KC):
                        ph.mm(acc[:, 0:G], win[:, k, c0:c0 + 128], hT[:, k, 0:G], start=(k == 0), stop=(k == KC - 1),
                              r=[hb, b_win[k]], w=[accb])
                    if kind == "copy":
                        st, sb_ = stf.next()
                        ph.cp("vector", st[:, 0:G], acc[:, 0:G], r=[accb], w=[sb_])
                        ph.dma("sync", dst[dc, :, t0:t0 + G], st[:, 0:G], r=[sb_])
                    else:
                        st, sb_ = stb.next()
                        ph.act(st[:, 0:G], acc[:, 0:G], AF.Silu if kind == "silu" else AF.Sigmoid, r=[accb], w=[sb_])
                        ph.dma("scalar", dst[dc, :, t0:t0 + G], st[:, 0:G], r=[sb_])
        if done("p1_%d" % l):
            return nc


        with Phase(nc, "p2a_%d" % l) as ph:
            cst = ph.sb([128, NCST, 128], F32, "cst"); b_cst = Buf()
            ph.dma("sync", cst[:], cst_d, w=[b_cst])
            ones_bf = ph.sb([128, 128], BF16); ident_bf = ph.sb([128, 128], BF16); b_cb = Buf()
            ph.cp("vector", ones_bf[:], cst[:, 1, :], r=[b_cst], w=[b_cb])
            ph.cp("vector", ident_bf[:], cst[:, 0, :], r=[b_cst], w=[b_cb])
            cw = ph.sb([128, 12, 3], F32); b_cw = Buf()
            ph.dma("sync", cw[:], dcw_d[l], w=[b_cw])
            pqrot = Rot(ph, 2, [128, 12, 514], BF16, "pq")
            srot = Rot(ph, 1, [128, 8, 512], F32, "s")
            vrot = Rot(ph, 2, [128, 4, 512], BF16, "vT")
            qkrot = Rot(ph, 2, [128, 8, 512], BF16, "qkn")
            tmrot = Rot(ph, 3, [128, 512], F32, "tm")
            sqrot = Rot(ph, 2, [128, 512], BF16, "sq")
            rrot = Rot(ph, 2, [128, 512], F32, "r")
            ssps = Rot(ph, 2, [128, 512], F32, "ss", psum=True)
            trps = Rot(ph, 4, [128, 512], F32, "trp", psum=True)
            tmo = Rot(ph, 4, [128, 512], BF16, "tmo")
            QB = float(np.log(128.0 ** -0.5))
            for (j, t0, G) in token_groups(NT, 512):
                s_lo, s_hi = (0, NCTX) if j == 1 else (NCTX, NTA)
                pq, pqb = pqrot.next()
                a = max(t0 - 1, s_lo); b = min(t0 + G + 1, s_hi)
                if a > t0 - 1:
                    ph.memset("gpsimd", pq[:, :, 0:1], 0.0, w=[pqb])
                if b < t0 + G + 1:
                    ph.memset("gpsimd", pq[:, :, G + 1:G + 2], 0.0, w=[pqb])
                ph.dma("sync", pq[:, :, a - (t0 - 1):b - (t0 - 1)], pqkvT[:, :, a:b].rearrange("c p t -> p c t"), w=[pqb])
                st, sb_ = srot.next()
                vT_, vb = vrot.next()
                qk, qkb = qkrot.next()
                for c in range(12):
                    tm, tmb = tmrot.next()
                    ph.act(tm[:, 0:G], pq[:, c, 1:G + 1], AF.Identity, scale=cw[:, c, 1:2], r=[pqb, b_cw], w=[tmb])
                    ph.stt("vector", tm[:, 0:G], pq[:, c, 0:G], cw[:, c, 0:1], tm[:, 0:G], ALU.mult, ALU.add, r=[pqb, b_cw, tmb], w=[tmb])
                    ph.stt("vector", tm[:, 0:G], pq[:, c, 2:G + 2], cw[:, c, 2:3], tm[:, 0:G], ALU.mult, ALU.add, r=[pqb, b_cw, tmb], w=[tmb])
                    if c < 8:
                        ph.act(st[:, c, 0:G], tm[:, 0:G], AF.Silu, r=[tmb], w=[sb_])
                    else:
                        ph.act(vT_[:, c - 8, 0:G], tm[:, 0:G], AF.Silu, r=[tmb], w=[vb])
                for c in range(8):
                    sq, sqb = sqrot.next()
                    ph.tt("gpsimd", sq[:, 0:G], st[:, c, 0:G], st[:, c, 0:G], ALU.mult, r=[sb_], w=[sqb])
                    sp, spb = ssps.next()
                    ph.mm(sp[:, 0:G], ones_bf[:], sq[:, 0:G], r=[b_cb, sqb], w=[spb])
                    rr, rb = rrot.next()
                    ph.act(rr[:, 0:G], sp[:, 0:G], AF.Ln, bias=EPS, r=[spb], w=[rb])
                    ph.act(rr[:, 0:G], rr[:, 0:G], AF.Exp, scale=-0.5, bias=(QB if c < 4 else 0.0), r=[rb], w=[rb])
                    ph.tt("vector", qk[:, c, 0:G], st[:, c, 0:G], rr[:, 0:G], ALU.mult, r=[sb_, rb], w=[qkb])
                ph.dma("sync", qnT[:, :, t0:t0 + G].rearrange("h p t -> p h t"), qk[:, 0:4, 0:G], r=[qkb])
                ph.dma("sync", knT[:, :, t0:t0 + G].rearrange("h p t -> p h t"), qk[:, 4:8, 0:G], r=[qkb])
                for s in range(G // 128):
                    for which in range(2):
                        tp_, tpb = trps.next()
                        tp = tp_[:].bitcast(BF16)
                        for h in range(4):
                            src = qk[:, 4 + h, s * 128:(s + 1) * 128] if which == 0 else vT_[:, h, s * 128:(s + 1) * 128]
                            ph.tr(tp[:, h * 128:(h + 1) * 128], src, ident_bf[:], r=[qkb if which == 0 else vb, b_cb], w=[tpb])
                        to, tob = tmo.next()
                        ph.cp("scalar" if which == 0 else "vector", to[:], tp[:, 0:512], r=[tpb], w=[tob])
                        ph.dma("sync", (kTM if which == 0 else vTM)[t0 // 128 + s], to[:], r=[tob])
        if done("p2a_%d" % l):
            return nc

        with Phase(nc, "p2c_%d" % l) as ph:
            for (j, c0, Rr, Wd, cnt_src) in ((1, 0, 1, NCTX, cntc_d), (0, NCTX, ROWS, GW, cntl_d)):
                n_tok = Rr * Wd
                RP = Rr + 16 if Rr > 1 else 1
                WP = Wd + 16
                X = ph.sb([128, n_tok], BF16, "pX"); PA = ph.sb([128, RP, WP], F32, "pA"); PB = ph.sb([128, RP, WP], F32, "pB")
                CN = ph.sb([128, n_tok], F32, "pC"); DO = ph.sb([128, n_tok], BF16, "pD")
                bX = Buf(); bC = Buf(); bA = [Buf(), Buf()]; bB = [Buf(), Buf()]; bD = [Buf(), Buf()]
                r0 = 8 if Rr > 1 else 0
                for c in range(2):
                    ph.dma("sync", X[:], ppoolT[c, :, c0:c0 + n_tok], w=[bX])
                    ph.dma("sync", CN[:], cnt_src[c], w=[bC])
                    ph.memset("gpsimd", PA[:], 0.0, w=bA)
                    ph.memset("vector", PB[:], 0.0, w=bB)
                    ph.cp("vector", PA[:, r0:r0 + Rr, 8:8 + Wd], X[:].rearrange("p (r w) -> p r w", w=Wd), r=[bX], w=bA)
                    for half in range(2):
                        eng = "gpsimd" if half == 0 else "vector"
                        w_ = POOL_WINDOWS[2 * c + half]
                        lo = w_ // 2
                        nl = int(np.log2(w_))
                        pr = slice(half * 64, (half + 1) * 64)
                        src, dst, bs, bd = PA, PB, bA[half], bB[half]
                        for lv in range(nl):
                            sft = 1 << lv
                            ph.tt(eng, dst[pr, :, 0:WP - sft], src[pr, :, 0:WP - sft], src[pr, :, sft:WP], ALU.add, r=[bs], w=[bd])
                            src, dst, bs, bd = dst, src, bd, bs
                        if Rr > 1:
                            for lv in range(nl):
                                sft = 1 << lv
                                ph.tt(eng, dst[pr, 0:RP - sft, :], src[pr, 0:RP - sft, :], src[pr, sft:RP, :], ALU.add, r=[bs], w=[bd])
                                src, dst, bs, bd = dst, src, bd, bs
                        ro = r0 - lo if Rr > 1 else 0
                        co = 8 - lo
                        Mv = src[pr, ro:ro + Rr, co:co + Wd]
                        tflat = dst[pr].rearrange("p r w -> p (r w)")[:, 0:n_tok]
                        ph.tt(eng, tflat.rearrange("p (r w) -> p r w", w=Wd), Mv, CN[pr].rearrange("p (r w) -> p r w", w=Wd), ALU.mult,
                              r=[bs, bC], w=[bd])
                        ph.tt(eng, DO[pr], tflat, X[pr], ALU.subtract, r=[bd, bX], w=[bD[half]])
                    ph.dma("sync", dT[c, :, c0:c0 + n_tok], DO[:], r=bD)
        if done("p2c_%d" % l):
            return nc


        with Phase(nc, "p2b_%d" % l) as ph:
            cst = ph.sb([128, NCST, 128], F32, "cst"); b_cst = Buf()
            ph.dma("sync", cst[:], cst_d, w=[b_cst])
            ident_bf = ph.sb([128, 128], BF16); b_ib = Buf()
            ph.cp("vector", ident_bf[:], cst[:, 0, :], r=[b_cst], w=[b_ib])
            tri2 = ph.sb([128, 2, 2, 128], BF16, "tri2"); negm_bf = ph.sb([128, 2, 128], BF16, "negm")
            for d in range(2):
                for rp_ in range(2):
                    ph.cp("vector", tri2[:, d, rp_, :], cst[:, 2 + d, :], r=[b_cst], w=[b_ib])
                ph.cp("vector", negm_bf[:, d, :], cst[:, 4 + d, :], r=[b_cst], w=[b_ib])
            S32 = [ph.sb([128, 4, 128], F32, "S32") for _ in range(2)]
            Sbf = [ph.sb([128, 4, 128], BF16, "Sbf") for _ in range(2)]
            bS32 = [[Buf() for _ in range(4)] for _ in range(2)]
            bSbf = [[Buf() for _ in range(4)] for _ in range(2)]
            for d in range(2):
                ph.memset("vector", S32[d][:], 0.0, w=bS32[d])
                ph.memset("gpsimd", Sbf[d][:], 0.0, w=bSbf[d])
            banks = Rot(ph, 8, [128, 512], F32, "bank", psum=True)

            def trbank():
                t_, b_ = banks.next()
                return t_[:].bitcast(BF16), b_
            T = {}
            TB = {}

            def tl(name, d, par, shape, dt):
                key = (name, d, par)
                if key not in T:
                    T[key] = ph.sb(shape, dt, name)
                return T[key]

            def tb(name, d, par, h=0):
                return TB.setdefault((name, d, par, h), Buf())
            order = [list(range(NTILE)), [1, 0] + list(range(NTILE - 1, 1, -1))]

            REC = ("ab", "sm", "wTp", "u", "kst", "qdecT", "aqkT", "vnew", "oTs")

            def prep(steps):
                C = {}
                for slot, step in enumerate(steps):
                    for d in range(2):
                        ti = order[d][step]
                        tc = slice(ti * 128, (ti + 1) * 128)
                        rk = step % 4

                        def mk(name, shape, dt, _d=d, _slot=slot, _rk=rk):
                            return tl(name, _d, ("r", _rk) if name in REC else ("p", _slot), shape, dt)

                        def B(name, h=0, _d=d, _slot=slot, _rk=rk):
                            return tb(name, _d, ("r", _rk) if name in REC else ("p", _slot), h)
                        ab = mk("ab", [128, 24], F32)
                        c = dict(step=step, ti=ti, tc=tc, B=B, ab=ab, sm=mk("sm", [128, 16], F32),
                                 qn=mk("qn", [128, 4, 128], BF16), kn=mk("kn", [128, 4, 128], BF16),
                                 kT=mk("kT", [128, 512], BF16), vT=mk("vT", [128, 512], BF16),
                                 Gbc=mk("Gbc", [128, 4, 2, 128], BF16), ghb=mk("ghb", [128, 8], BF16), gsp=mk("gsp", [128, 16], F32), EQ=mk("EQ", [128, 4, 128], F32), E2T=mk("E2T", [128, 4, 128], F32),
                                 EsT=mk("EsT", [128, 4, 128], F32), M0t=mk("M0t", [128, 4, 128], BF16),
                                 A0t=mk("A0t", [128, 4, 128], BF16), AMb=mk("AMb", [128, 4, 2, 128], BF16),
                                 AM1=mk("AM1", [128, 4, 2, 128], BF16), A2t=mk("A2t", [128, 4, 128], BF16),
                                 PP=[mk("Pa", [128, 4, 128], BF16), mk("Pb", [128, 4, 128], BF16)],
                                 PT=mk("PT", [128, 4, 128], BF16), Xt=mk("Xt", [128, 4, 128], BF16),
                                 aqkT=mk("aqkT", [128, 4, 128], BF16), qdecT=mk("qdecT", [128, 4, 128], BF16),
                                 kegc=mk("kegc", [128, 4, 128], BF16), kst=mk("kst", [128, 4, 128], BF16),
                                 wTp=mk("wTp", [128, 4, 128], BF16), u=mk("u", [128, 4, 128], F32),
                                 vnew=mk("vnew", [128, 4, 128], BF16), oTs=mk("oTs", [128, 4, 128], BF16),
                                 b4=ab[:, 8 + d * 4:12 + d * 4], nb4=ab[:, 16 + d * 4:20 + d * 4])
                        C[(slot, d)] = c
                        ph.dma("sync", c["qn"][:], qnT[:, :, tc].rearrange("h p t -> p h t"), w=[B("qn")])
                        ph.dma("sync", c["kn"][:], knT[:, :, tc].rearrange("h p t -> p h t"), w=[B("kn")])
                        ph.dma("sync", c["kT"][:], kTM[ti], w=[B("kT")])
                        ph.dma("sync", c["vT"][:], vTM[ti], w=[B("vT")])
                        ph.dma("sync", ab[:], abS[ti], w=[B("ab")])
                        sm = c["sm"]
                        bsm = B("sm"); bab = B("ab")
                        pS, bpS = banks.next()
                        ph.mm(pS[:, 0:8], cst[:, 2 + d, :], ab[:, 0:8], r=[b_cst, bab], w=[bpS])
                        ph.mm(pS[:, 8:16], cst[:, 1, :], ab[:, 0:8], r=[b_cst, bab], w=[bpS])
                        gcs = pS[:, d * 4:d * 4 + 4]
                        gls = pS[:, 8 + d * 4:12 + d * 4]
                        ph.ts("vector", sm[:, 0:4], gcs, -1.0, ALU.mult, w=[bpS, bsm])
                        ph.act(sm[:, 4:8], gcs, AF.Exp, w=[bpS, bsm])
                        ph.act(sm[:, 8:12], gls, AF.Exp, w=[bpS, bsm])
                        ph.tt("vector", sm[:, 12:16], gls, sm[:, 0:4], ALU.add, w=[bpS, bsm])
                        ph.act(sm[:, 12:16], sm[:, 12:16], AF.Exp, w=[bsm])
                        ph.cp("vector", c["ghb"][:], ab[:, 0:8], r=[bab], w=[B("ghb")])
                        ph.cp("vector", c["gsp"][:, 0:8], c["ghb"][:], r=[B("ghb")], w=[B("gsp")])
                        ph.tt("vector", c["gsp"][:, 8:16], ab[:, 0:8], c["gsp"][:, 0:8], ALU.subtract, r=[bab, B("gsp")], w=[B("gsp")])
                DHS = [(slot, d, h) for slot in range(len(steps)) for d in range(2) for h in range(4)]
                bk = {}

                def every(fn):
                    for ch in DHS:
                        fn(C[(ch[0], ch[1])], *ch)

                def stage(pe, cons):
                    for i in range(0, len(DHS), 8):
                        for ch in DHS[i:i + 8]:
                            pe(C[(ch[0], ch[1])], *ch)
                        for ch in DHS[i:i + 8]:
                            cons(C[(ch[0], ch[1])], *ch)

                def bank(ch, tr_=False):
                    bk[ch] = trbank() if tr_ else banks.next()
                    return bk[ch]

                def ac_G(c, slot, d, h):
                    for part in range(2):
                        ph.act(c["Gbc"][:, h, part, :], cst[:, 1, :], AF.Identity, scale=c["gsp"][:, part * 8 + d * 4 + h:part * 8 + d * 4 + h + 1],
                               r=[b_cst, c["B"]("gsp")], w=[c["B"]("Gbc", h)])
                every(ac_G)

                def pe_R(c, slot, d, h):
                    pR, bR = bank((slot, d, h))
                    t2 = tri2[:, d, :, :].rearrange("p a b -> p (a b)")
                    ph.mm(pR[:, 0:256], c["Gbc"][:, h, 0, :], t2, start=True, stop=False, r=[c["B"]("Gbc", h), b_ib], w=[bR])
                    ph.mm(pR[:, 0:256], c["Gbc"][:, h, 1, :], t2, start=False, stop=False, r=[c["B"]("Gbc", h), b_ib], w=[bR])
                    ph.mm(pR[:, 128:256], ident_bf[:], negm_bf[:, d, :], start=False, stop=True, r=[b_ib], w=[bR])

                def co_R(c, slot, d, h):
                    pR, bR = bk[(slot, d, h)]
                    ph.act(c["EQ"][:, h, :], pR[:, 0:128], AF.Exp, w=[bR, c["B"]("EQ", h)])
                    ph.act(c["E2T"][:, h, :], pR[:, 128:256], AF.Exp, bias=c["sm"][:, h:h + 1], r=[c["B"]("sm")], w=[bR, c["B"]("E2T", h)])
                stage(pe_R, co_R)

                def po_E(c, slot, d, h):
                    ph.tt("gpsimd", c["EsT"][:, h, :], c["E2T"][:, h, :], cst[:, 6 + d, :], ALU.mult, r=[c["B"]("E2T", h), b_cst], w=[c["B"]("EsT", h)])
                    ph.tt("gpsimd", c["qdecT"][:, h, :], c["qn"][:, h, :], c["EQ"][:, h, :], ALU.mult, r=[c["B"]("qn"), c["B"]("EQ", h)], w=[c["B"]("qdecT", h)])
                every(po_E)

                def pe_G(c, slot, d, h):
                    pG, bG = bank((slot, d, h))
                    ph.mm(pG[:, 0:128], c["kn"][:, h, :], c["kn"][:, h, :], r=[c["B"]("kn")], w=[bG])
                    ph.mm(pG[:, 128:256], c["kn"][:, h, :], c["qn"][:, h, :], r=[c["B"]("kn"), c["B"]("qn")], w=[bG])

                def co_G(c, slot, d, h):
                    pG, bG = bk[(slot, d, h)]
                    ph.stt("vector", c["M0t"][:, h, :], pG[:, 0:128], c["b4"][:, h:h + 1], c["EsT"][:, h, :], ALU.mult, ALU.mult,
                           r=[c["B"]("ab"), c["B"]("EsT", h)], w=[bG, c["B"]("M0t", h)])
                    ph.tt("vector", c["aqkT"][:, h, :], pG[:, 128:256], c["E2T"][:, h, :], ALU.mult,
                          r=[c["B"]("E2T", h)], w=[bG, c["B"]("aqkT", h)])
                stage(pe_G, co_G)

                def pe_T(c, slot, d, h):
                    pT_, bT_ = bank((slot, d, h), True)
                    ph.tr(pT_[:, 0:128], c["M0t"][:, h, :], ident_bf[:], r=[c["B"]("M0t", h), b_ib], w=[bT_])

                def co_T(c, slot, d, h):
                    pT_, bT_ = bk[(slot, d, h)]
                    ph.cp("scalar", c["A0t"][:, h, :], pT_[:, 0:128], w=[bT_, c["B"]("A0t", h)])
                stage(pe_T, co_T)

                def po_M(c, slot, d, h):
                    ph.tt("gpsimd", c["AMb"][:, h, 1, :], c["M0t"][:, h, :], cst[:, 8, :], ALU.mult, r=[c["B"]("M0t", h), b_cst], w=[c["B"]("AMb", h)])
                    ph.tt("gpsimd", c["AMb"][:, h, 0, :], c["A0t"][:, h, :], cst[:, 8, :], ALU.mult, r=[c["B"]("A0t", h), b_cst], w=[c["B"]("AMb", h)])
                    ph.tt("gpsimd", c["PP"][0][:, h, :], cst[:, 0, :], c["AMb"][:, h, 1, :], ALU.subtract, r=[b_cst, c["B"]("AMb", h)], w=[c["B"]("P0", h)])
                every(po_M)

                def pe_A1(c, slot, d, h):
                    pA, bA_ = bank((slot, d, h))
                    ph.mm(pA[:, 0:128], c["AMb"][:, h, 1, :], c["AMb"][:, h, 0, :], r=[c["B"]("AMb", h)], w=[bA_])
                    ph.mm(pA[:, 128:256], c["AMb"][:, h, 0, :], c["AMb"][:, h, 1, :], r=[c["B"]("AMb", h)], w=[bA_])

                def co_A1(c, slot, d, h):
                    pA, bA_ = bk[(slot, d, h)]
                    ph.cp("scalar", c["AM1"][:, h, :, :], pA[:, 0:256].rearrange("p (a b) -> p a b", a=2), w=[bA_, c["B"]("AM1", h)])
                stage(pe_A1, co_A1)

                def pe_P1(c, slot, d, h):
                    pP, bP_ = bank((slot, d, h))
                    ph.mm(pP[:, 0:128], c["AM1"][:, h, 0, :], c["PP"][0][:, h, :], r=[c["B"]("AM1", h), c["B"]("P0", h)], w=[bP_])

                def co_P1(c, slot, d, h):
                    pP, bP_ = bk[(slot, d, h)]
                    ph.tt("vector", c["PP"][1][:, h, :], pP[:, 0:128], c["PP"][0][:, h, :], ALU.add, r=[c["B"]("P0", h)], w=[bP_, c["B"]("P1", h)])
                stage(pe_P1, co_P1)

                def pe_A2(c, slot, d, h):
                    pA, bA_ = bank((slot, d, h))
                    ph.mm(pA[:, 0:128], c["AM1"][:, h, 1, :], c["AM1"][:, h, 0, :], r=[c["B"]("AM1", h)], w=[bA_])

                def co_A2(c, slot, d, h):
                    pA, bA_ = bk[(slot, d, h)]
                    ph.cp("scalar", c["A2t"][:, h, :], pA[:, 0:128], w=[bA_, c["B"]("A2t", h)])
                stage(pe_A2, co_A2)

                def pe_P2(c, slot, d, h):
                    pP, bP_ = bank((slot, d, h))
                    ph.mm(pP[:, 0:128], c["A2t"][:, h, :], c["PP"][1][:, h, :], r=[c["B"]("A2t", h), c["B"]("P1", h)], w=[bP_])

                def co_P2(c, slot, d, h):
                    pP, bP_ = bk[(slot, d, h)]
                    ph.tt("vector", c["PP"][0][:, h, :], pP[:, 0:128], c["PP"][1][:, h, :], ALU.add, r=[c["B"]("P1", h)], w=[bP_, c["B"]("P0", h)])
                stage(pe_P2, co_P2)

                for mi in range(4):
                    cur = mi % 2
                    nxt = 1 - cur

                    def pe_PT(c, slot, d, h, cur=cur):
                        pT_, bT_ = bank((slot, d, h), True)
                        ph.tr(pT_[:, 0:128], c["PP"][cur][:, h, :], ident_bf[:], r=[c["B"]("P%d" % cur, h), b_ib], w=[bT_])

                    def co_PT(c, slot, d, h):
                        pT_, bT_ = bk[(slot, d, h)]
                        ph.cp("scalar", c["PT"][:, h, :], pT_[:, 0:128], w=[bT_, c["B"]("PT", h)])
                    stage(pe_PT, co_PT)

                    def pe_X(c, slot, d, h, cur=cur):
                        pX, bX_ = bank((slot, d, h))
                        ph.mm(pX[:, 0:128], c["A0t"][:, h, :], c["PP"][cur][:, h, :], r=[c["B"]("A0t", h), c["B"]("P%d" % cur, h)], w=[bX_])

                    def co_X(c, slot, d, h, mi=mi):
                        pX, bX_ = bk[(slot, d, h)]
                        ph.tt("vector", c["Xt"][:, h, :], pX[:, 0:128], cst[:, 9 + mi, :], ALU.mult, r=[b_cst], w=[bX_, c["B"]("Xt", h)])
                    stage(pe_X, co_X)

                    def pe_Y(c, slot, d, h):
                        pY, bY_ = bank((slot, d, h))
                        ph.mm(pY[:, 0:128], c["PT"][:, h, :], c["Xt"][:, h, :], r=[c["B"]("PT", h), c["B"]("Xt", h)], w=[bY_])

                    def co_Y(c, slot, d, h, cur=cur, nxt=nxt):
                        pY, bY_ = bk[(slot, d, h)]
                        ph.tt("vector", c["PP"][nxt][:, h, :], c["PP"][cur][:, h, :], pY[:, 0:128], ALU.subtract,
                              r=[c["B"]("P%d" % cur, h)], w=[bY_, c["B"]("P%d" % nxt, h)])
                    stage(pe_Y, co_Y)

                def ac_K(c, slot, d, h):
                    hc = slice(h * 128, (h + 1) * 128)
                    ph.act(c["kegc"][:, h, :], c["kT"][:, hc], AF.Identity, scale=c["sm"][:, 4 + h:5 + h], r=[c["B"]("kT"), c["B"]("sm")], w=[c["B"]("kegc", h)])
                    ph.act(c["kst"][:, h, :], c["kT"][:, hc], AF.Identity, scale=c["sm"][:, 12 + h:13 + h], r=[c["B"]("kT"), c["B"]("sm")], w=[c["B"]("kst", h)])
                every(ac_K)

                def pe_W(c, slot, d, h):
                    pW, bW = bank((slot, d, h))
                    ph.mm(pW[:, 0:128], c["kegc"][:, h, :], c["PP"][0][:, h, :], r=[c["B"]("kegc", h), c["B"]("P0", h)], w=[bW])
                    ph.mm(pW[:, 128:256], c["PP"][0][:, h, :], c["vT"][:, h * 128:(h + 1) * 128], r=[c["B"]("P0", h), c["B"]("vT")], w=[bW])

                def co_W(c, slot, d, h):
                    pW, bW = bk[(slot, d, h)]
                    ph.cp("scalar", c["wTp"][:, h, :], pW[:, 0:128], w=[bW, c["B"]("wTp", h)])
                    ph.ts("vector", c["u"][:, h, :], pW[:, 128:256], c["b4"][:, h:h + 1], ALU.mult, r=[c["B"]("ab")], w=[bW, c["B"]("u", h)])
                stage(pe_W, co_W)
                return C

            def rec(C, slot):
                DH2 = [(d, h) for d in range(2) for h in range(4)]
                bk = {}
                bk2 = {}
                for (d, h) in DH2:
                    c = C[(slot, d)]
                    pW, bW = banks.next(); bk[(d, h)] = (pW, bW)
                    ph.mm(pW[:, 0:128], c["wTp"][:, h, :], Sbf[d][:, h, :], r=[c["B"]("wTp", h), bSbf[d][h]], w=[bW])
                for (d, h) in DH2:
                    c = C[(slot, d)]; pW, bW = bk[(d, h)]
                    ph.stt("vector", c["vnew"][:, h, :], pW[:, 0:128], c["nb4"][:, h:h + 1], c["u"][:, h, :], ALU.mult, ALU.add,
                           r=[c["B"]("ab"), c["B"]("u", h)], w=[bW, c["B"]("vnew", h)])
                for (d, h) in DH2:
                    c = C[(slot, d)]
                    pO, bO = banks.next(); bk2[(d, h)] = (pO, bO)
                    ph.mm(pO[:, 0:128], Sbf[d][:, h, :], c["qdecT"][:, h, :], start=True, stop=False, r=[bSbf[d][h], c["B"]("qdecT", h)], w=[bO])
                    ph.mm(pO[:, 0:128], c["vnew"][:, h, :], c["aqkT"][:, h, :], start=False, stop=True, r=[c["B"]("vnew", h), c["B"]("aqkT", h)], w=[bO])
                    ph.mm(pO[:, 128:256], c["kst"][:, h, :], c["vnew"][:, h, :], r=[c["B"]("kst", h), c["B"]("vnew", h)], w=[bO])
                for (d, h) in DH2:
                    c = C[(slot, d)]; pO, bO = bk2[(d, h)]
                    ph.stt("vector", S32[d][:, h, :], S32[d][:, h, :], c["sm"][:, 8 + h:9 + h], pO[:, 128:256], ALU.mult, ALU.add,
                           r=[c["B"]("sm")], w=[bO, bS32[d][h]])
                    ph.cp("scalar", c["oTs"][:, h, :], pO[:, 0:128], w=[bO, c["B"]("oTs", h)])
                for (d, h) in DH2:
                    ph.cp("gpsimd", Sbf[d][:, h, :], S32[d][:, h, :], r=[bS32[d][h]], w=[bSbf[d][h]])
                for d in range(2):
                    c = C[(slot, d)]
                    ph.dma("sync", oTd[d][:, :, c["tc"]].rearrange("h p t -> p h t"), c["oTs"][:], r=[c["B"]("oTs", h) for h in range(4)])

            pairs = [(s_, s_ + 1) for s_ in range(0, NTILE, 2)]
            prev = prep(pairs[0])
            for pi in range(len(pairs)):
                nxt_ = prep(pairs[pi + 1]) if pi + 1 < len(pairs) else None
                for slot in range(2):
                    rec(prev, slot)
                prev = nxt_
            if "sfin" in dbg:
                for d in range(2):
                    ph.dma("sync", sfin[l, d].rearrange("h p t -> p h t"), S32[d][:], r=bS32[d])
        if done("p2b_%d" % l):
            return nc


        with Phase(nc, "p3_%d" % l) as ph:
            cst = ph.sb([128, NCST, 128], F32, "cst"); b_cst = Buf()
            ph.dma("sync", cst[:], cst_d, w=[b_cst])
            ones_bf = ph.sb([128, 128], BF16); b_cb = Buf()
            ph.cp("vector", ones_bf[:], cst[:, 1, :], r=[b_cst], w=[b_cb])
            wa = ph.sb([128, 4, D], BF16, "wa"); wbb = ph.sb([128, 2, D], BF16, "wb"); wc = ph.sb([128, 2, D], BF16, "wc")
            wo = ph.sb([128, KC, D], BF16, "wo"); b_w = Buf()
            ph.dma("gpsimd", wa[:], wbra_d[l].rearrange("(k p) n -> p k n", p=128), w=[b_w])
            ph.dma("gpsimd", wbb[:], wbrb_d[l].rearrange("(k p) n -> p k n", p=128), w=[b_w])
            ph.dma("gpsimd", wc[:], wbrc_d[l].rearrange("(k p) n -> p k n", p=128), w=[b_w])
            ph.dma("gpsimd", wo[:], wo_d[l].rearrange("(k p) n -> p k n", p=128), w=[b_w])
            pw = ph.sb([128, 2, 128], BF16, "pw"); b_pw = Buf()
            ph.memset("vector", pw[:], 0.0, w=[b_pw])
            for c in range(2):
                for half in range(2):
                    ph.dma("gpsimd", pw[half * 64:(half + 1) * 64, c, half * 64:(half + 1) * 64], poolw_d[l, 2 * c + half], w=[b_pw])
            psc_ = ph.sb([128, 2], F32); scw = ph.sb([128, 2, 3], F32); dng = ph.sb([128, 1], F32); b_sm = Buf()
            ph.dma("sync", psc_[:], pscale_d[l], w=[b_sm])
            ph.dma("sync", scw[:], scw_d[l], w=[b_sm])
            ph.dma("sync", dng[:], dng_d[l], w=[b_sm])
            mods = ph.sb([128, 48, 2], F32); b_mods = Buf()
            ph.dma("sync", mods[:], modd[l], w=[b_mods])

            G3 = 512
            ofr = Rot(ph, 1, [128, 4, G3], BF16, "of"); obr = Rot(ph, 1, [128, 4, G3], BF16, "ob"); osr = Rot(ph, 1, [128, 4, G3], F32, "os")
            szr = Rot(ph, 1, [128, 4, G3], BF16, "sz"); onr = Rot(ph, 1, [128, 4, G3], BF16, "on")
            sqr = Rot(ph, 2, [128, G3], BF16, "sq"); rr_ = Rot(ph, 2, [128, G3], F32, "r"); tmr = Rot(ph, 3, [128, G3], F32, "tm")
            dr = Rot(ph, 1, [128, 2, G3], BF16, "d"); ybr = Rot(ph, 1, [128, 2, G3], BF16, "yb")
            scr = Rot(ph, 1, [128, 6, G3 + 2], BF16, "sc"); cxr = Rot(ph, 1, [128, 2, G3 + 2], F32, "cx"); ycr = Rot(ph, 1, [128, 2, G3], BF16, "yc")
            gtr = Rot(ph, 1, [128, 24, G3], BF16, "gt"); xgr = Rot(ph, 1, [128, KC, G3], F32, "xg")
            yr = Rot(ph, 1, [128, KC, G3], BF16, "y"); xor_ = Rot(ph, 1, [128, KC, G3], F32, "xo")
            ps1 = Rot(ph, 2, [128, G3], F32, "ps1", psum=True)
            ps3 = Rot(ph, 5, [128, G3], F32, "ps3", psum=True)
            for (j, t0, G) in token_groups(NT, G3):
                if last and j == 1:
                    continue
                s_lo, s_hi = (0, NCTX) if j == 1 else (NCTX, NTA)
                of_, ofb = ofr.next(); ob_, obb = obr.next(); sz, szb = szr.next(); on, onb = onr.next()
                ph.dma("scalar", of_[:, :, 0:G], oTd[0][:, :, t0:t0 + G].rearrange("h p t -> p h t"), w=[ofb])
                ph.dma("scalar", ob_[:, :, 0:G], oTd[1][:, :, t0:t0 + G].rearrange("h p t -> p h t"), w=[obb])
                ph.dma("sync", sz[:, :, 0:G], szT[:, :, t0:t0 + G].rearrange("h p t -> p h t"), w=[szb])
                osum, osb = osr.next()
                ph.tt("gpsimd", osum[:, :, 0:G], of_[:, :, 0:G], ob_[:, :, 0:G], ALU.add, r=[ofb, obb], w=[osb])
                of_, ofb = osum, osb
                for h in range(4):
                    sq, sqb = sqr.next()
                    ph.act(sq[:, 0:G], of_[:, h, 0:G], AF.Square, r=[ofb], w=[sqb])
                    p1, p1b = ps1.next()
                    ph.mm(p1[:, 0:G], ones_bf[:], sq[:, 0:G], r=[b_cb, sqb], w=[p1b])
                    rr, rb = rr_.next()
                    ph.act(rr[:, 0:G], p1[:, 0:G], AF.Ln, bias=EPS, scale=1.0 / 128, r=[p1b], w=[rb])
                    ph.act(rr[:, 0:G], rr[:, 0:G], AF.Exp, scale=-0.5, r=[rb], w=[rb])
                    tm, tmb = tmr.next()
                    ph.stt("vector", tm[:, 0:G], of_[:, h, 0:G], dng[:, 0:1], rr[:, 0:G], ALU.mult, ALU.mult, r=[ofb, b_sm, rb], w=[tmb])
                    ph.tt("gpsimd", on[:, h, 0:G], tm[:, 0:G], sz[:, h, 0:G], ALU.mult, r=[tmb, szb], w=[onb])
                dd, ddb = dr.next(); yb, ybb = ybr.next()
                ph.dma("sync", dd[:, :, 0:G], dT[:, :, t0:t0 + G].rearrange("c p t -> p c t"), w=[ddb])
                for c in range(2):
                    p1, p1b = ps1.next()
                    ph.mm(p1[:, 0:G], pw[:, c, :], dd[:, c, 0:G], r=[b_pw, ddb], w=[p1b])
                    ph.ts("vector", yb[:, c, 0:G], p1[:, 0:G], psc_[:, c:c + 1], ALU.mult, r=[p1b, b_sm], w=[ybb])
                sc, scb = scr.next(); cx, cxb = cxr.next(); yc, ycb = ycr.next()
                a = max(t0 - 1, s_lo); b = min(t0 + G + 1, s_hi)
                if a > t0 - 1:
                    ph.memset("gpsimd", sc[:, :, 0:1], 0.0, w=[scb])
                if b < t0 + G + 1:
                    ph.memset("gpsimd", sc[:, :, G + 1:G + 2], 0.0, w=[scb])
                ph.dma("sync", sc[:, :, a - (t0 - 1):b - (t0 - 1)], pscT[:, :, a:b].rearrange("c p t -> p c t"), w=[scb])
                ph.tt("gpsimd", cx[:, :, 0:G + 2], sc[:, 4:6, 0:G + 2], sc[:, 0:2, 0:G + 2], ALU.mult, r=[scb], w=[cxb])
                for c in range(2):
                    tm, tmb = tmr.next()
                    ph.act(tm[:, 0:G], cx[:, c, 1:G + 1], AF.Identity, scale=scw[:, c, 1:2], r=[cxb, b_sm], w=[tmb])
                    ph.stt("vector", tm[:, 0:G], cx[:, c, 0:G], scw[:, c, 0:1], tm[:, 0:G], ALU.mult, ALU.add, r=[cxb, b_sm, tmb], w=[tmb])
                    ph.stt("vector", tm[:, 0:G], cx[:, c, 2:G + 2], scw[:, c, 2:3], tm[:, 0:G], ALU.mult, ALU.add, r=[cxb, b_sm, tmb], w=[tmb])
                    ph.tt("gpsimd", yc[:, c, 0:G], tm[:, 0:G], sc[:, 2 + c, 1:G + 1], ALU.mult, r=[tmb, scb], w=[ycb])
                gt, gtb = gtr.next(); xg, xb = xgr.next(); y, yb_ = yr.next(); xo, xob = xor_.next()
                ph.dma("scalar", gt[:, :, 0:G], gatesT[:, :, t0:t0 + G].rearrange("c p t -> p c t"), w=[gtb])
                ph.dma("sync", xg[:, :, 0:G], xT[:, :, t0:t0 + G].rearrange("k p t -> p k t"), w=[xb])
                for m in range(KC):
                    mc = slice(m * 128, (m + 1) * 128)
                    pa, pab = ps3.next(); pb_, pbb = ps3.next(); pc, pcb = ps3.next()
                    for h in range(4):
                        ph.mm(pa[:, 0:G], wa[:, h, mc], on[:, h, 0:G], start=(h == 0), stop=(h == 3), r=[b_w, onb], w=[pab])
                    for c in range(2):
                        ph.mm(pb_[:, 0:G], wbb[:, c, mc], yb[:, c, 0:G], start=(c == 0), stop=(c == 1), r=[b_w, ybb], w=[pbb])
                    for c in range(2):
                        ph.mm(pc[:, 0:G], wc[:, c, mc], yc[:, c, 0:G], start=(c == 0), stop=(c == 1), r=[b_w, ycb], w=[pcb])
                    t1, t1b = tmr.next(); t2, t2b = tmr.next(); t3, t3b = tmr.next()
                    ph.tt("vector", t1[:, 0:G], pa[:, 0:G], gt[:, m, 0:G], ALU.mult, r=[pab, gtb], w=[t1b])
                    ph.tt("vector", t2[:, 0:G], pb_[:, 0:G], gt[:, 8 + m, 0:G], ALU.mult, r=[pbb, gtb], w=[t2b])
                    ph.tt("vector", t3[:, 0:G], pc[:, 0:G], gt[:, 16 + m, 0:G], ALU.mult, r=[pcb, gtb], w=[t3b])
                    ph.tt("gpsimd", t1[:, 0:G], t1[:, 0:G], t2[:, 0:G], ALU.add, r=[t1b, t2b], w=[t1b])
                    ph.tt("gpsimd", y[:, m, 0:G], t1[:, 0:G], t3[:, 0:G], ALU.add, r=[t1b, t3b], w=[yb_])
                for m in range(KC):
                    mc = slice(m * 128, (m + 1) * 128)
                    pa, pab = ps3.next()
                    for k in range(KC):
                        ph.mm(pa[:, 0:G], wo[:, k, mc], y[:, k, 0:G], start=(k == 0), stop=(k == KC - 1), r=[b_w, yb_], w=[pab])
                    ph.stt("vector", xo[:, m, 0:G], pa[:, 0:G], mods[:, 16 + m, j:j + 1], xg[:, m, 0:G], ALU.mult, ALU.add,
                           r=[pab, b_mods, xb], w=[xob])
                ph.dma("sync", xT[:, :, t0:t0 + G].rearrange("k p t -> p k t"), xo[:, :, 0:G], r=[xob])
        if done("p3_%d" % l):
            return nc

        with Phase(nc, "p4_%d" % l) as ph:
            onesf = ph.sb([128, 128], F32, "onesf"); b_cst = Buf()
            ph.dma("sync", onesf[:], cst_d[:, 1, :], w=[b_cst])
            ones_bf = ph.sb([128, 128], BF16); b_cb = Buf()
            ph.cp("vector", ones_bf[:], onesf[:], r=[b_cst], w=[b_cb])
            wgu = ph.sb([128, KC, 2 * DFF], BF16, "wgu"); wdn = ph.sb([128, FC, D], BF16, "wdn"); b_wg = [Buf() for _ in range(KC)]; b_wd = Buf()
            for k in range(KC):
                ph.dma("gpsimd", wgu[:, k, :], wgu_d[l, k * 128:(k + 1) * 128, :], w=[b_wg[k]])
            for f0 in range(0, FC, 11):
                ph.dma("gpsimd", wdn[:, f0:f0 + 11, :], wdn_d[l, f0 * 128:(f0 + 11) * 128, :].rearrange("(f p) n -> p f n", p=128), w=[b_wd])
            mods = ph.sb([128, 48, 2], F32); b_mods = Buf()
            ph.dma("sync", mods[:], modd[l], w=[b_mods])
            n2g = ph.sb([128, KC], F32); b_n2g = Buf()
            ph.dma("sync", n2g[:], n2g_d[l], w=[b_n2g])
            A2 = ph.sb([128, 2, KC], F32); b_A2 = Buf()
            for j in range(2):
                ph.stt("vector", A2[:, j, :], mods[:, 32:40, j], 1.0, n2g[:], ALU.add, ALU.mult, r=[b_mods, b_n2g], w=[b_A2])
            G4 = 512
            xgr = Rot(ph, 2, [128, KC, G4], F32, "xg")
            hr = Rot(ph, 1, [128, KC, G4], BF16, "hT"); tmr = Rot(ph, 2, [128, G4], F32, "tm"); rsr = Rot(ph, 1, [128, G4], F32, "rs")
            acr = Rot(ph, 1, [128, FC, G4], BF16, "act")
            ssp = Rot(ph, 1, [128, G4], F32, "ssp", psum=True)
            gup = Rot(ph, 6, [128, G4], F32, "gup", psum=True)
            for (j, t0, G) in token_groups(NT, G4):
                if last and j == 1:
                    continue
                xg, xb = xgr.next()
                ph.dma("sync", xg[:, :, 0:G], xT[:, :, t0:t0 + G].rearrange("k p t -> p k t"), w=[xb])
                ac, acb = acr.next()
                sq, sqb = ac, acb
                ph.act(sq[:, 0:KC, 0:G], xg[:, :, 0:G], AF.Square, r=[xb], w=[sqb])
                sp, spb = ssp.next()
                for k in range(KC):
                    ph.mm(sp[:, 0:G], ones_bf[:], sq[:, k, 0:G], start=(k == 0), stop=(k == KC - 1), r=[b_cb, sqb], w=[spb])
                rs, rsb = rsr.next()
                ph.act(rs[:, 0:G], sp[:, 0:G], AF.Ln, bias=EPS, scale=1.0 / D, r=[spb], w=[rsb])
                ph.act(rs[:, 0:G], rs[:, 0:G], AF.Exp, scale=-0.5, r=[rsb], w=[rsb])
                hT, hb = hr.next()
                for k in range(KC):
                    tm, tmb = tmr.next()
                    ph.stt("vector", tm[:, 0:G], xg[:, k, 0:G], A2[:, j, k:k + 1], rs[:, 0:G], ALU.mult, ALU.mult, r=[xb, b_A2, rsb], w=[tmb])
                    ph.act(hT[:, k, 0:G], tm[:, 0:G], AF.Identity, bias=mods[:, 24 + k, j:j + 1], r=[tmb, b_mods], w=[hb])
                for f in range(FC):
                    pg, pgb = gup.next(); pu, pub = gup.next()
                    for k in range(KC):
                        ph.mm(pg[:, 0:G], wgu[:, k, f * 128:(f + 1) * 128], hT[:, k, 0:G], start=(k == 0), stop=(k == KC - 1), r=[b_wg[k], hb], w=[pgb])
                    for k in range(KC):
                        ph.mm(pu[:, 0:G], wgu[:, k, DFF + f * 128:DFF + (f + 1) * 128], hT[:, k, 0:G], start=(k == 0), stop=(k == KC - 1), r=[b_wg[k], hb], w=[pub])
                    tm, tmb = tmr.next()
                    ph.act(tm[:, 0:G], pg[:, 0:G], AF.Silu, r=[pgb], w=[tmb])
                    ph.tt("vector", ac[:, f, 0:G], pu[:, 0:G], tm[:, 0:G], ALU.mult, r=[pub, tmb], w=[acb])
                for m in range(KC):
                    pd, pdb = gup.next()
                    for f in range(FC):
                        ph.mm(pd[:, 0:G], wdn[:, f, m * 128:(m + 1) * 128], ac[:, f, 0:G], start=(f == 0), stop=(f == FC - 1), r=[b_wd, acb], w=[pdb])
                    ph.stt("vector", xg[:, m, 0:G], pd[:, 0:G], mods[:, 40 + m, j:j + 1], xg[:, m, 0:G], ALU.mult, ALU.add,
                           r=[pdb, b_mods], w=[xb])
                ph.dma("sync", xT[:, :, t0:t0 + G].rearrange("k p t -> p k t"), xg[:, :, 0:G], r=[xb])
        if done("p4_%d" % l):
            return nc

    with Phase(nc, "pf") as ph:
        cst = ph.sb([128, NCST, 128], F32, "cst"); b_cst = Buf()
        ph.dma("sync", cst[:], cst_d, w=[b_cst])
        ones_bf = ph.sb([128, 128], BF16); b_cb = Buf()
        ph.cp("vector", ones_bf[:], cst[:, 1, :], r=[b_cst], w=[b_cb])
        fng = ph.sb([128, KC], F32); b_fng = Buf()
        ph.dma("sync", fng[:], fng_d, w=[b_fng])
        GF = 512
        xgr = Rot(ph, 2, [128, KC, GF], F32, "xg"); sqr = Rot(ph, 1, [128, KC, GF], BF16, "sq"); rsr = Rot(ph, 1, [128, GF], F32, "rs")
        yr = Rot(ph, 2, [128, KC, GF], F32, "y"); otr = Rot(ph, 3, [128, D], F32, "ot")
        ssp = Rot(ph, 1, [128, GF], F32, "ssp", psum=True)
        trp = Rot(ph, 3, [128, KC, 128], F32, "trp", psum=True)
        out_waits = []
        for (j, t0, G) in token_groups(NT, GF):
            if j == 1:
                continue
            xg, xb = xgr.next()
            ph.dma("gpsimd", xg[:], xT[:, :, t0:t0 + G].rearrange("k p t -> p k t"), w=[xb])
            sq, sqb = sqr.next()
            ph.act(sq[:], xg[:], AF.Square, r=[xb], w=[sqb])
            sp, spb = ssp.next()
            for k in range(KC):
                ph.mm(sp[:], ones_bf[:], sq[:, k, :], start=(k == 0), stop=(k == KC - 1), r=[b_cb, sqb], w=[spb])
            rs, rsb = rsr.next()
            ph.act(rs[:], sp[:], AF.Ln, bias=EPS, scale=1.0 / D, r=[spb], w=[rsb])
            ph.act(rs[:], rs[:], AF.Exp, scale=-0.5, r=[rsb], w=[rsb])
            y, yb = yr.next()
            for k in range(KC):
                ph.stt("vector", y[:, k, :], xg[:, k, :], fng[:, k:k + 1], rs[:], ALU.mult, ALU.mult,
                       r=[xb, b_fng, rsb], w=[yb])
            for s_ in range(G // 128):
                tp, tpb = trp.next()
                for k in range(KC):
                    ph.mm(tp[:, k, :], y[:, k, s_ * 128:(s_ + 1) * 128], cst[:, 0, :], r=[yb, b_cst], w=[tpb])
                ot, otb = otr.next()
                ph.cp("scalar" if s_ % 2 == 0 else "vector", ot[:].rearrange("p (k f) -> p k f", k=KC), tp[:], r=[tpb], w=[otb])
                ph.dma("scalar" if s_ % 2 == 0 else "sync", out_d[t0 - NCTX + s_ * 128:t0 - NCTX + (s_ + 1) * 128, :], ot[:], r=[otb])

    return nc


POOL_WINDOWS = (2, 4, 8, 16)


def _consts(NT):
    idx = np.arange(128)
    cst = np.zeros((128, NCST, 128), np.float32)
    cst[:, 0, :] = np.eye(128)
    cst[:, 1, :] = 1.0
    cst[:, 2, :] = (idx[:, None] <= idx[None, :])
    cst[:, 3, :] = (idx[:, None] >= idx[None, :])
    cst[:, 4, :] = np.where(idx[None, :] >= idx[:, None], 0.0, NEG)
    cst[:, 5, :] = np.where(idx[None, :] <= idx[:, None], 0.0, NEG)
    cst[:, 6, :] = (idx[None, :] > idx[:, None])
    cst[:, 7, :] = (idx[None, :] < idx[:, None])

    def blk(sz):
        return (idx[:, None] // sz == idx[None, :] // sz).astype(np.float32)
    cst[:, 8, :] = blk(8)
    for n, sz in enumerate((8, 16, 32, 64)):
        cst[:, 9 + n, :] = blk(2 * sz) - blk(sz)

    def cnt1d(n, w):
        lo = w // 2
        hi = w - 1 - lo
        pos = np.arange(n)
        return (np.clip(pos + hi + 1, 0, n) - np.clip(pos - lo, 0, n)).astype(np.float64)
    rows = NT // GW
    cnt_lat = np.zeros((2, 128, NT), np.float32)
    cnt_ctx = np.zeros((2, 128, NCTX), np.float32)
    for c in range(2):
        for half in range(2):
            w = POOL_WINDOWS[2 * c + half]
            cl = 1.0 / (cnt1d(rows, w)[:, None] * cnt1d(GW, w)[None, :])
            cnt_lat[c, half * 64:(half + 1) * 64, :] = cl.reshape(1, NT)
            cnt_ctx[c, half * 64:(half + 1) * 64, :] = (1.0 / cnt1d(NCTX, w))[None, :]
    return cst, cnt_lat, cnt_ctx


def col(v):
    v = np.asarray(v, np.float32)
    return np.ascontiguousarray(v.reshape(-1, 128).T)


def prep_shared(inp, NT):
    f = lambda a: np.ascontiguousarray(np.asarray(a, np.float32))
    cst, cnt_lat, cnt_ctx = _consts(NT)
    sh = {
        "w_ada": f(inp["w_ada"]),
        "b_adaT": np.stack([col(inp["b_ada"][l]) for l in range(2)]),
        "n1g": np.stack([col(inp["norm1_g"][l]) for l in range(2)]),
        "n2g": np.stack([col(inp["norm2_g"][l]) for l in range(2)]),
        "fng": col(inp["final_norm_g"]),
        "w_in": f(inp["w_in"]),
        "dcw": np.stack([np.stack([col(np.asarray(inp["dn_conv_w"])[l, t]) for t in range(3)], axis=-1) for l in range(2)]),
        "alog": np.stack([np.broadcast_to(np.asarray(inp["dn_a_log"], np.float32)[l].reshape(1, 8), (128, 8)) for l in range(2)]).copy(),
        "dtb": np.stack([np.broadcast_to(np.asarray(inp["dn_dt_bias"], np.float32)[l].reshape(1, 8), (128, 8)) for l in range(2)]).copy(),
        "dng": np.asarray(inp["dn_norm_g"], np.float32).reshape(2, 128, 1).copy(),
        "poolw": f(inp["pool_w"]),
        "pscale": np.stack([col(inp["pool_scale"][l]) for l in range(2)]),
        "scw": np.stack([np.stack([col(np.asarray(inp["sc_conv_w"])[l, t]) for t in range(3)], axis=-1) for l in range(2)]),
        "w_br_a": f(inp["w_br_a"]), "w_br_b": f(inp["w_br_b"]), "w_br_c": f(inp["w_br_c"]),
        "w_o": f(inp["w_o"]), "w_gu": f(inp["w_gu"]), "w_down": f(inp["w_down"]),
        "cst": cst, "cnt_lat": cnt_lat, "cnt_ctx": cnt_ctx,
    }
    return sh


def prep_core(inp, sh, b):
    m = dict(sh)
    m["x"] = np.ascontiguousarray(np.asarray(inp["x"], np.float32)[b])
    m["ctx"] = np.ascontiguousarray(np.asarray(inp["ctx"], np.float32)[b])
    m["cc"] = np.ascontiguousarray(np.stack([col(np.asarray(inp["c"])[b]), col(inp["c_ctx"])], axis=-1))
    return m


def kernel(**inputs):
    x = np.asarray(inputs["x"])
    B, NT, _ = x.shape
    nc = build(NT)
    sh = prep_shared(inputs, NT)
    in_maps = [prep_core(inputs, sh, c % B) for c in range(8)]
    res = run_bass_kernel_spmd(nc, in_maps, core_ids=list(range(8)))
    return np.stack([res.results[b]["out"] for b in range(B)]).astype(np.float32)
```
